# Optimizing a Trainium2 kernel written in Bass

```python
import math
import jax
import jax.numpy as jnp
from jax import lax
import numpy as np

D_MODEL = 1024
BATCH = 4
SEQ = 4096
DEPTH = 4
DEC_BATCH = 128
DEC_SEQ = 4
PAST_LEN = 8192
PAGE_SIZE = 128

BRANCH_WIDTH = D_MODEL
N_BRANCH = 4
SSD_D_INNER = BRANCH_WIDTH
SSD_HEAD_DIM = 64
SSD_HEADS = SSD_D_INNER // SSD_HEAD_DIM
SSD_GROUPS = 4
SSD_STATE = 64
SSD_CONV = 4
SSD_CHUNK = 128
SSD_CONV_DIM = SSD_D_INNER + 2 * SSD_GROUPS * SSD_STATE
SC_WIDTH = BRANCH_WIDTH
SC_CONV = 3
ATTN_HEAD_DIM = 64
ATTN_HEADS = BRANCH_WIDTH // ATTN_HEAD_DIM
ATTN_KV_HEADS = 4
ATTN_GROUP = ATTN_HEADS // ATTN_KV_HEADS
WINDOW = 128
GM_WIDTH = BRANCH_WIDTH
GM_GROUPS = 8
GM_CHUNK = 128
D_FF = 2816
FFN_CONV = 3
EPS = 1e-6

IN_SIZES = (SSD_D_INNER, SSD_CONV_DIM, SSD_HEADS, 3 * SC_WIDTH, ATTN_HEADS * ATTN_HEAD_DIM,
            ATTN_KV_HEADS * ATTN_HEAD_DIM, ATTN_KV_HEADS * ATTN_HEAD_DIM, 2 * GM_WIDTH, N_BRANCH * D_MODEL)
IN_OFFSETS = tuple(sum(IN_SIZES[:i + 1]) for i in range(len(IN_SIZES) - 1))
D_IN = sum(IN_SIZES)

kernel_name = 'hybrid_gated_ssd_conv_swa_gmlp_decoder_step'


def rmsnorm(x, g):
    xf = x.astype(jnp.float32)
    xf = xf * lax.rsqrt(jnp.mean(xf * xf, axis=-1, keepdims=True) + EPS)
    return (xf * g.astype(jnp.float32)).astype(x.dtype)


def layernorm(x, g, b):
    xf = x.astype(jnp.float32)
    xc = xf - jnp.mean(xf, axis=-1, keepdims=True)
    xf = xc * lax.rsqrt(jnp.mean(xc * xc, axis=-1, keepdims=True) + EPS)
    return (xf * g.astype(jnp.float32) + b.astype(jnp.float32)).astype(x.dtype)


def causal_dwconv(x, w, prev, b=None):
    K = w.shape[0]
    L = x.shape[1]
    xp = jnp.concatenate([prev.astype(x.dtype), x], axis=1)
    y = xp[:, 0:L] * w[0]
    for j in range(1, K):
        y = y + xp[:, j:j + L] * w[j]
    if b is not None:
        y = y + b
    return y, xp[:, L:]


def ssd_scan(xdt, dtA, Bm, Cm, s0):
    f32 = jnp.float32
    b, l, h, p = xdt.shape
    g, n = Bm.shape[2], Bm.shape[3]
    r = h // g
    T = min(SSD_CHUNK, l)
    c = l // T
    X = xdt.astype(f32).reshape(b, c, T, g, r, p)
    Bc = Bm.astype(f32).reshape(b, c, T, g, n)
    Cc = Cm.astype(f32).reshape(b, c, T, g, n)
    A_cs = jnp.cumsum(dtA.astype(f32).reshape(b, c, T, g, r), axis=2)
    diff = A_cs[:, :, :, None] - A_cs[:, :, None, :]
    causal = jnp.tril(jnp.ones((T, T), dtype=bool))[None, None, :, :, None, None]
    Lmat = jnp.exp(jnp.where(causal, diff, -jnp.inf))
    CB = jnp.einsum('bctgn,bcsgn->bctsg', Cc, Bc)
    y_diag = jnp.einsum('bctsgr,bcsgrp->bctgrp', CB[..., None] * Lmat, X)
    decay_s = jnp.exp(A_cs[:, :, -1:] - A_cs)
    states = jnp.einsum('bcsgn,bcsgr,bcsgrp->bcgrpn', Bc, decay_s, X)
    chunk_decay = jnp.exp(A_cs[:, :, -1])

    def step(S, inp):
        st, dec = inp
        return S * dec[..., None, None] + st, S

    S0 = s0.astype(f32).reshape(b, g, r, p, n)
    S_fin, S_in = lax.scan(step, S0, (jnp.moveaxis(states, 1, 0), jnp.moveaxis(chunk_decay, 1, 0)))
    S_in = jnp.moveaxis(S_in, 0, 1)
    y_off = jnp.einsum('bctgn,bcgrpn,bctgr->bctgrp', Cc, S_in, jnp.exp(A_cs))
    y = (y_diag + y_off).reshape(b, l, h, p)
    return y, S_fin.reshape(b, h, p, n)


def ssd_mixer(z, xbc, dtr, conv_prev, ssm_prev, conv_w, conv_b, dt_bias, a_log, d_skip, norm_g):
    f32 = jnp.float32
    b, l = xbc.shape[:2]
    xbc, conv_new = causal_dwconv(xbc, conv_w, conv_prev, conv_b)
    xbc = jax.nn.silu(xbc)
    xs, Bm, Cm = jnp.split(xbc, [SSD_D_INNER, SSD_D_INNER + SSD_GROUPS * SSD_STATE], axis=-1)
    xh = xs.reshape(b, l, SSD_HEADS, SSD_HEAD_DIM).astype(f32)
    Bm = Bm.reshape(b, l, SSD_GROUPS, SSD_STATE)
    Cm = Cm.reshape(b, l, SSD_GROUPS, SSD_STATE)
    dt = jax.nn.softplus(dtr.astype(f32) + dt_bias.astype(f32))
    A = -jnp.exp(a_log.astype(f32))
    y, s_new = ssd_scan(xh * dt[..., None], dt * A, Bm, Cm, ssm_prev)
    y = y + xh * d_skip.astype(f32)[:, None]
    y = y.reshape(b, l, SSD_D_INNER) * jax.nn.silu(z.astype(f32))
    yg = y.reshape(b, l, SSD_GROUPS, SSD_D_INNER // SSD_GROUPS)
    yg = yg * lax.rsqrt(jnp.mean(yg * yg, axis=-1, keepdims=True) + EPS)
    y = yg.reshape(b, l, SSD_D_INNER) * norm_g.astype(f32)
    return y.astype(z.dtype), conv_new, s_new.astype(ssm_prev.dtype)


def shortconv_mixer(bcx, conv_prev, conv_w):
    Bg, Cg, xs = jnp.split(bcx, 3, axis=-1)
    y, conv_new = causal_dwconv(Cg * xs, conv_w, conv_prev)
    return Bg * y, conv_new


def alibi_slopes():
    return 2.0 ** (-8.0 * jnp.arange(1, ATTN_HEADS + 1, dtype=jnp.float32) / ATTN_HEADS)


def attn_core(q, k, v, dist, valid, sinks):
    f32 = jnp.float32
    slopes = alibi_slopes().reshape(ATTN_KV_HEADS, ATTN_GROUP)[:, :, None, None]
    s = jnp.einsum('bnqhgd,bnkhd->bnhgqk', q.astype(f32), k.astype(f32)) * (ATTN_HEAD_DIM ** -0.5)
    s = s - slopes * dist[:, None, None]
    s = jnp.where(valid[:, None, None], s, -jnp.inf)
    sink = sinks.astype(f32).reshape(ATTN_KV_HEADS, ATTN_GROUP)[:, :, None, None]
    m = jnp.maximum(jnp.max(s, axis=-1, keepdims=True), sink)
    pr = jnp.exp(s - m)
    pr = pr / (jnp.sum(pr, axis=-1, keepdims=True) + jnp.exp(sink - m))
    return jnp.einsum('bnhgqk,bnkhd->bnqhgd', pr, v.astype(f32))


def swa_prompt(q, k, v, sinks):
    b, l = q.shape[:2]
    nb = l // WINDOW
    qb = q.reshape(b, nb, WINDOW, ATTN_KV_HEADS, ATTN_GROUP, ATTN_HEAD_DIM)
    kb = k.reshape(b, nb, WINDOW, ATTN_KV_HEADS, ATTN_HEAD_DIM)
    vb = v.reshape(b, nb, WINDOW, ATTN_KV_HEADS, ATTN_HEAD_DIM)
    pad = ((0, 0), (1, 0), (0, 0), (0, 0), (0, 0))
    kk = jnp.concatenate([jnp.pad(kb, pad)[:, :-1], kb], axis=2)
    vv = jnp.concatenate([jnp.pad(vb, pad)[:, :-1], vb], axis=2)
    qpos = jnp.arange(WINDOW)
    kpos = jnp.arange(2 * WINDOW) - WINDOW
    dist = qpos[:, None] - kpos[None, :]
    band = (dist >= 0) & (dist <= WINDOW)
    blk = jnp.arange(nb)
    valid = band[None] & ((blk[:, None, None] > 0) | (kpos[None, None, :] >= 0))
    o = attn_core(qb, kk, vv, dist[None].astype(jnp.float32), valid, sinks)
    nbuf = min(WINDOW, l)
    return o.reshape(b, l, ATTN_HEADS * ATTN_HEAD_DIM).astype(q.dtype), k[:, l - nbuf:], v[:, l - nbuf:]


def swa_sample(q, k, v, k_buf, v_buf, sinks):
    b, l = q.shape[:2]
    nbuf = k_buf.shape[1]
    kk = jnp.concatenate([k_buf.astype(k.dtype), k], axis=1)
    vv = jnp.concatenate([v_buf.astype(v.dtype), v], axis=1)
    dist = (nbuf + jnp.arange(l))[:, None] - jnp.arange(nbuf + l)[None, :]
    valid = (dist >= 0) & (dist <= WINDOW)
    o = attn_core(q[:, None], kk[:, None], vv[:, None], dist[None].astype(jnp.float32), valid[None], sinks)
    return o.reshape(b, l, ATTN_HEADS * ATTN_HEAD_DIM).astype(q.dtype), kk[:, l:], vv[:, l:]


def gmlp_mixer(uv, ln_g, ln_b, w_s, b_s):
    uv = jax.nn.gelu(uv)
    u, v = jnp.split(uv, 2, axis=-1)
    v = layernorm(v, ln_g, ln_b)
    b, l = v.shape[:2]
    T = min(GM_CHUNK, l)
    c = l // T
    vg = v.reshape(b, c, T, GM_GROUPS, GM_WIDTH // GM_GROUPS)
    W = jnp.tril(w_s[:, :T, :T])
    mixed = jnp.einsum('gts,bcsgf->bctgf', W, vg) + b_s[:, :T].T[None, None, :, :, None]
    return u * mixed.reshape(b, l, GM_WIDTH).astype(u.dtype), v


def conv_ffn(h, conv_prev, w_up, conv_w, conv_b, w_down):
    up, conv_new = causal_dwconv(h @ w_up, conv_w, conv_prev, conv_b)
    a, g = jnp.split(up, 2, axis=-1)
    return (jax.nn.silu(a) * g) @ w_down, conv_new


def run_group(x, c, is_prompt, ssm_st, ssd_conv_st, sc_conv_st, k_st, v_st, ffn_conv_st,
              w_ada, b_ada, g_norm_mix, w_in, ssd_conv_w, ssd_conv_b, ssd_dt_bias, ssd_a_log, ssd_d,
              ssd_norm_g, sc_conv_w, attn_sinks, gm_ln_g, gm_ln_b, gm_w_s, gm_b_s, w_branch, w_o,
              g_norm_ffn, ffn_w_up, ffn_conv_w, ffn_conv_b, ffn_w_down, g_final):
    b, l = x.shape[:2]
    dtp = x.dtype
    new = [[] for _ in range(6 if is_prompt else 7)]
    for i in range(DEPTH):
        mod = (jax.nn.silu(c) @ w_ada[i] + b_ada[i])[:, None, :]
        sh1, sc1, ga1, sh2, sc2, ga2 = jnp.split(mod, 6, axis=-1)
        h = rmsnorm(x, g_norm_mix[i]) * (1 + sc1) + sh1
        z, xbc, dtr, bcx, q, k, v, uv, gates = jnp.split(h @ w_in[i], IN_OFFSETS, axis=-1)
        if is_prompt:
            ssm0 = jnp.zeros((b, SSD_HEADS, SSD_HEAD_DIM, SSD_STATE), dtp)
            ssdc0 = jnp.zeros((b, SSD_CONV - 1, SSD_CONV_DIM), dtp)
            scc0 = jnp.zeros((b, SC_CONV - 1, SC_WIDTH), dtp)
            ffc0 = jnp.zeros((b, FFN_CONV - 1, 2 * D_FF), dtp)
        else:
            ssm0, ssdc0, scc0, ffc0 = ssm_st[i], ssd_conv_st[i], sc_conv_st[i], ffn_conv_st[i]
        ya, ssdc1, ssm1 = ssd_mixer(z, xbc, dtr, ssdc0, ssm0, ssd_conv_w[i], ssd_conv_b[i],
                                    ssd_dt_bias[i], ssd_a_log[i], ssd_d[i], ssd_norm_g[i])
        yb, scc1 = shortconv_mixer(bcx, scc0, sc_conv_w[i])
        q = q.reshape(b, l, ATTN_KV_HEADS, ATTN_GROUP, ATTN_HEAD_DIM)
        k = k.reshape(b, l, ATTN_KV_HEADS, ATTN_HEAD_DIM)
        v = v.reshape(b, l, ATTN_KV_HEADS, ATTN_HEAD_DIM)
        if is_prompt:
            yc, k1, v1 = swa_prompt(q, k, v, attn_sinks[i])
        else:
            yc, k1, v1 = swa_sample(q, k, v, k_st[i], v_st[i], attn_sinks[i])
        yd, v_rows = gmlp_mixer(uv, gm_ln_g[i], gm_ln_b[i], gm_w_s[i], gm_b_s[i])
        branches = jnp.stack([ya, yb.astype(dtp), yc, yd], axis=2)
        proj = jnp.einsum('blif,ifd->blid', branches, w_branch[i])
        gate = jax.nn.sigmoid(gates.astype(jnp.float32)).reshape(b, l, N_BRANCH, D_MODEL)
        merged = jnp.sum(gate * proj, axis=2).astype(dtp)
        x = x + ga1 * (merged @ w_o[i])
        h2 = rmsnorm(x, g_norm_ffn[i]) * (1 + sc2) + sh2
        yf, ffc1 = conv_ffn(h2, ffc0, ffn_w_up[i], ffn_conv_w[i], ffn_conv_b[i], ffn_w_down[i])
        x = x + ga2 * yf
        vals = (ssm1, ssdc1, scc1, k1, v1, ffc1) if is_prompt else (ssm1, ssdc1, scc1, k1, v1, ffc1, v_rows)
        for lst, val in zip(new, vals):
            lst.append(val)
    return rmsnorm(x, g_final), tuple(jnp.stack(lst, axis=0) for lst in new)


def setup_inputs(seed: int = 0) -> dict:
    key = jax.random.key(seed)
    keys = iter(jax.random.split(key, 64))

    def nrm(shape, scale):
        return scale * jax.random.normal(next(keys), shape, jnp.float32)

    n_buf = min(WINDOW, PAST_LEN)
    dt0 = jnp.exp(jax.random.uniform(next(keys), (DEPTH, SSD_HEADS), jnp.float32,
                                     math.log(1e-3), math.log(1e-1)))
    a0 = jax.random.uniform(next(keys), (DEPTH, SSD_HEADS), jnp.float32, 1.0, 16.0)
    return {
        'x_prompt': nrm((BATCH, SEQ, D_MODEL), 1.0),
        'x_sample': nrm((DEC_BATCH, DEC_SEQ, D_MODEL), 1.0),
        'c_prompt': nrm((BATCH, D_MODEL), 1.0),
        'c_sample': nrm((DEC_BATCH, D_MODEL), 1.0),
        'state_ssm': nrm((DEPTH, DEC_BATCH, SSD_HEADS, SSD_HEAD_DIM, SSD_STATE), 0.1),
        'state_ssd_conv': nrm((DEPTH, DEC_BATCH, SSD_CONV - 1, SSD_CONV_DIM), 1.0),
        'state_sc_conv': nrm((DEPTH, DEC_BATCH, SC_CONV - 1, SC_WIDTH), 1.0),
        'cache_k': nrm((DEPTH, DEC_BATCH, n_buf, ATTN_KV_HEADS, ATTN_HEAD_DIM), 1.0),
        'cache_v': nrm((DEPTH, DEC_BATCH, n_buf, ATTN_KV_HEADS, ATTN_HEAD_DIM), 1.0),
        'state_ffn_conv': nrm((DEPTH, DEC_BATCH, FFN_CONV - 1, 2 * D_FF), 1.0),
        'w_ada': nrm((DEPTH, D_MODEL, 6 * D_MODEL), 0.5 * D_MODEL ** -0.5),
        'b_ada': nrm((DEPTH, 6 * D_MODEL), 0.02),
        'g_norm_mix': 1.0 + nrm((DEPTH, D_MODEL), 0.02),
        'w_in': nrm((DEPTH, D_MODEL, D_IN), D_MODEL ** -0.5),
        'ssd_conv_w': nrm((DEPTH, SSD_CONV, SSD_CONV_DIM), SSD_CONV ** -0.5),
        'ssd_conv_b': nrm((DEPTH, SSD_CONV_DIM), 0.02),
        'ssd_dt_bias': dt0 + jnp.log(-jnp.expm1(-dt0)),
        'ssd_a_log': jnp.log(a0),
        'ssd_d': 1.0 + nrm((DEPTH, SSD_HEADS), 0.1),
        'ssd_norm_g': 1.0 + nrm((DEPTH, SSD_D_INNER), 0.02),
        'sc_conv_w': nrm((DEPTH, SC_CONV, SC_WIDTH), SC_CONV ** -0.5),
        'attn_sinks': nrm((DEPTH, ATTN_HEADS), 0.5),
        'gm_ln_g': 1.0 + nrm((DEPTH, GM_WIDTH), 0.02),
        'gm_ln_b': nrm((DEPTH, GM_WIDTH), 0.02),
        'gm_w_s': nrm((DEPTH, GM_GROUPS, GM_CHUNK, GM_CHUNK), GM_CHUNK ** -0.5),
        'gm_b_s': 1.0 + nrm((DEPTH, GM_GROUPS, GM_CHUNK), 0.1),
        'w_branch': nrm((DEPTH, N_BRANCH, BRANCH_WIDTH, D_MODEL), BRANCH_WIDTH ** -0.5),
        'w_o': nrm((DEPTH, D_MODEL, D_MODEL), D_MODEL ** -0.5),
        'g_norm_ffn': 1.0 + nrm((DEPTH, D_MODEL), 0.02),
        'ffn_w_up': nrm((DEPTH, D_MODEL, 2 * D_FF), D_MODEL ** -0.5),
        'ffn_conv_w': nrm((DEPTH, FFN_CONV, 2 * D_FF), FFN_CONV ** -0.5),
        'ffn_conv_b': nrm((DEPTH, 2 * D_FF), 0.02),
        'ffn_w_down': nrm((DEPTH, D_FF, D_MODEL), D_FF ** -0.5),
        'g_final': 1.0 + nrm((D_MODEL,), 0.02),
    }


def reference(x_prompt, x_sample, c_prompt, c_sample, state_ssm, state_ssd_conv, state_sc_conv,
              cache_k, cache_v, state_ffn_conv, w_ada, b_ada, g_norm_mix, w_in, ssd_conv_w, ssd_conv_b,
              ssd_dt_bias, ssd_a_log, ssd_d, ssd_norm_g, sc_conv_w, attn_sinks, gm_ln_g, gm_ln_b,
              gm_w_s, gm_b_s, w_branch, w_o, g_norm_ffn, ffn_w_up, ffn_conv_w, ffn_conv_b, ffn_w_down,
              g_final):
    weights = (w_ada, b_ada, g_norm_mix, w_in, ssd_conv_w, ssd_conv_b, ssd_dt_bias, ssd_a_log, ssd_d,
               ssd_norm_g, sc_conv_w, attn_sinks, gm_ln_g, gm_ln_b, gm_w_s, gm_b_s, w_branch, w_o,
               g_norm_ffn, ffn_w_up, ffn_conv_w, ffn_conv_b, ffn_w_down, g_final)
    y_prompt, (p_ssm, p_ssd_conv, p_sc_conv, p_k, p_v, p_ffn_conv) = run_group(
        x_prompt, c_prompt, True, None, None, None, None, None, None, *weights)
    y_sample, (s_ssm, s_ssd_conv, s_sc_conv, s_k, s_v, s_ffn_conv, s_gm_v) = run_group(
        x_sample, c_sample, False, state_ssm, state_ssd_conv, state_sc_conv, cache_k, cache_v,
        state_ffn_conv, *weights)
    return (y_prompt, y_sample, p_ssm, p_ssd_conv, p_sc_conv, p_k, p_v, p_ffn_conv,
            s_ssm, s_ssd_conv, s_sc_conv, s_k, s_v, s_ffn_conv, s_gm_v)
```

```python
import os
import numpy as np
import concourse.bass as bass
import concourse.mybir as mybir
from concourse.bass_utils import run_bass_kernel_spmd

F32 = mybir.dt.float32
BF16 = mybir.dt.bfloat16
AF = mybir.ActivationFunctionType
ALU = mybir.AluOpType
AX = mybir.AxisListType

D = 1024
TT = 256
NCH = TT // 128
WBC = 256
SLOPES = [2.0 ** (-8.0 * (h + 1) / 16) for h in range(16)]
EPS = 1e-6
XBC0, DT0, BCX0, Q0, K0, V0, UV0, GT0, DIN = 1024, 2560, 2576, 5648, 6672, 6928, 7184, 9232, 13328
DFF = 2816
PP_BADA, PP_GMIX, PP_GFFN, PP_SCW, PP_SCB, PP_SHW, PP_FW, PP_FB, PP_GFIN = 0, 48, 56, 64, 96, 104, 128, 260, 304
NPP = 312
NEG = -30000.0
DBG_STOP = int(os.environ.get('DBG_STOP', '99'))
DBG_ATT = int(os.environ.get('DBG_ATT', '99'))
DBG_ATTP = int(os.environ.get('DBG_ATTP', '99'))
DBG_S = int(os.environ.get('DBG_S', '99'))


class Res:
    __slots__ = ("name", "w", "r", "excl")

    def __init__(self, name, excl=False):
        self.name = name
        self.w = None
        self.r = {}
        self.excl = excl


class DSem:
    __slots__ = ("sem", "tot", "key", "keep")

    def __init__(self, nc, name):
        self.sem = nc.alloc_semaphore(name)
        self.tot = 0
        self.key = name
        self.keep = False


class Sched:
    def __init__(self, nc):
        self.nc = nc
        self.eng = {}
        self.seen = {}
        self.dsems = []
        self.dkeys = {}
        for name, h in (("pe", nc.tensor), ("act", nc.scalar), ("dve", nc.vector),
                        ("pool", nc.gpsimd), ("sp", nc.sync)):
            self.eng[name] = [h, nc.alloc_semaphore("s_" + name), 0]

    def dsem(self, name):
        d = DSem(self.nc, "d_" + name)
        self.dsems.append(d)
        self.dkeys[d.key] = d
        return d

    def _waits(self, eng, reads, writes):
        need = {}

        def add(ent):
            key, sem, val = ent
            if key in self.dkeys:
                val = self.dkeys[key].tot
            if key not in need or need[key][1] < val:
                need[key] = (sem, val)

        for r in reads:
            if r.w is not None:
                add(r.w)
            if r.excl:
                for key, (sem, val) in r.r.items():
                    if key != eng:
                        add((key, sem, val))
        for w in writes:
            if w.w is not None:
                add(w.w)
            for key, (sem, val) in w.r.items():
                add((key, sem, val))
        h = self.eng[eng][0]
        for key, (sem, val) in need.items():
            if self.seen.get((eng, key), 0) < val:
                h.wait_ge(sem, val)
                self.seen[(eng, key)] = val

    def op(self, eng, reads, writes, fn):
        E = self.eng[eng]
        self._waits(eng, reads, writes)
        inst = fn()
        E[2] += 1
        inst.then_inc(E[1], 1)
        for r in reads:
            r.r[eng] = (E[1], E[2])
        for w in writes:
            w.w = (eng, E[1], E[2])
            w.r = {}
        return inst

    def _auto_dsem(self, reads, writes):
        if writes:
            name = "ld_" + writes[0].name
        elif reads:
            name = "st_" + reads[0].name
        else:
            name = "dd"
        d = self.dkeys.get("d_" + name)
        if d is None:
            d = self.dsem(name)
        return d

    def dma(self, q, ds, reads, writes, fn):
        if not getattr(ds, "keep", False):
            ds = self._auto_dsem(reads, writes)
        self._waits(q, reads, writes)
        inst = fn()
        ds.tot += 16
        inst.then_inc(ds.sem, 16)
        for r in reads:
            r.r[ds.key] = (ds.sem, ds.tot)
        for w in writes:
            w.w = (ds.key, ds.sem, ds.tot)
            w.r = {}
        return inst

    def finish(self, eng="sp"):
        h = self.eng[eng][0]
        for d in self.dsems:
            if d.tot > 0:
                h.wait_ge(d.sem, d.tot)


def build(NT, L, SB, dbg=False):
    nc = bass.Bass("TRN2", target_bir_lowering=False)
    S = Sched(nc)
    NTOK = NT * TT

    def din(name, shape):
        return nc.dram_tensor(name, list(shape), F32, kind="ExternalInput").ap()

    def dout(name, shape):
        return nc.dram_tensor(name, list(shape), F32, kind="ExternalOutput").ap()

    xT_d = din("xT", [D, NTOK])
    cT_d = din("cT", [128, 8, 1])
    w_ada_d = din("w_ada", [L, D, 6 * D])
    w_in_d = din("w_in", [L, D, DIN])
    w_br_d = din("w_branch", [L, 4, D, D])
    w_o_d = din("w_o", [L, D, D])
    w_up_d = din("ffn_w_up", [L, D, 2 * DFF])
    w_dn_d = din("ffn_w_down", [L, DFF, D])
    pp_d = din("pp", [128, L, NPP])
    pp64_d = din("pp64", [64, L, 40])
    rows_d = din("rows", [128, L, 64])
    rowb_d = din("rowb", [L, 128, 4096])
    gmw_d = din("gm_w_s", [L, 8, 128, 128])
    cst_d = din("cst", [128, 772])

    yT_d = dout("yT", [D, NTOK])
    pssm_d = dout("p_ssm", [L, 64, 4, 256])
    pxbc_d = dout("p_xbc", [L, 128, 8, 4])
    pbc_d = dout("p_bc", [L, 64, 8, 4])
    psc_d = dout("p_sc", [L, 128, 8, 2])
    pk_d = dout("p_k", [L, 128, 256])
    pv_d = dout("p_v", [L, 128, 256])
    pffn_d = dout("p_ffn", [L, 128, 48, 2])

    if SB:
        xsT_d = din("xsT", [D, 4 * SB])
        csT_d = din("csT", [128, 8, SB])
        sssm_d = din("s_ssm_in", [L, SB, 64, 4, 256])
        sxbc_d = din("s_xbc_in", [L, 128, 8, SB, 3])
        sbc_d = din("s_bc_in", [L, 64, 8, SB, 3])
        ssc_d = din("s_sc_in", [L, 128, 8, SB, 2])
        sffn_d = din("s_ffn_in", [L, 128, 44, SB, 2])
        ckT_d = din("ckT", [L, SB, 128, 4, 128])
        ck_d = din("ck", [L, SB, 128, 256])
        cv_d = din("cv", [L, SB, 128, 256])
        ysT_d = dout("ysT", [D, 4 * SB])
        sssm_o = dout("s_ssm_o", [L, SB, 64, 4, 256])
        sxbc_o = dout("s_xbc_o", [L, 128, 8, SB, 3])
        sbc_o = dout("s_bc_o", [L, 64, 8, SB, 3])
        ssc_o = dout("s_sc_o", [L, 128, 8, SB, 2])
        sffn_o = dout("s_ffn_o", [L, 128, 44, SB, 2])
        sk_o = dout("s_k_o", [L, SB, 128, 256])
        sv_o = dout("s_v_o", [L, SB, 128, 256])
        sgmv_o = dout("s_gmv_o", [L, SB, 4, 1024])

    def sb(name, shape, dt=F32):
        return nc.alloc_sbuf_tensor(name, list(shape), dt)

    xT = sb("xTt", [128, 8, TT]); r_x = Res("x")
    mod = sb("mod", [128, L, 48, 1]); r_mod = Res("mod")
    gm = sb("gm", [128, L, 2, 8, 1]); r_gm = Res("gm")
    pp = sb("ppt", [128, L, NPP]); r_pp = Res("pp")
    pp64 = sb("pp64t", [64, L, 40]); r_pp64 = Res("pp64")
    rows = sb("rowst", [128, L, 64]); r_rows = Res("rows")
    rowb = sb("rowbt", [128, 4096]); r_rowb = Res("rowb")
    cst = sb("cstt", [128, 772]); r_cst = Res("cst")
    identb = sb("identb", [128, 128], BF16)
    onesb = sb("onesb", [128, 128], BF16)
    ident = cst[:, 0:128]
    Umat = cst[:, 128:256]
    NEGM = cst[:, 256:384]
    distm = cst[:, 384:640]
    dists = cst[:, 640:772]
    gmwT = sb("gmwT", [128, 8, 128], BF16); r_gmw = Res("gmwT")
    gmw_raw = sb("gmw_raw", [128, 8, 128], BF16); r_gmraw = Res("gmw_raw")
    ST = sb("ST", [64, L, 4, 256]); r_ST = [Res("ST%d" % l) for l in range(L)]
    STb = sb("STb", [64, 4, 256], BF16); r_STb1 = Res("STb"); r_STb = [r_STb1] * L
    cx = sb("cx", [128, L, 8, 4]); r_cx = [Res("cx%d" % l) for l in range(L)]
    cbc = sb("cbc", [64, L, 8, 4]); r_cbc = [Res("cbc%d" % l) for l in range(L)]
    csc = sb("csc", [128, L, 8, 2]); r_csc = [Res("csc%d" % l) for l in range(L)]
    cff = sb("cff", [128, L, 48, 2]); r_cff = [Res("cff%d" % l) for l in range(L)]
    kprev = sb("kprev", [128, L, 4, 128], BF16); r_kprev = [Res("kp%d" % l) for l in range(L)]
    vprev = sb("vprev", [128, L, 256], BF16); r_vprev = [Res("vp%d" % l) for l in range(L)]
    G = [sb("G%d" % i, [128, 8, TT], BF16) for i in range(6)]
    r_G = [Res("G%d" % i) for i in range(6)]
    hB, r_h = G[0], r_G[0]
    stg = [sb("stg%d" % i, [128, TT + 4]) for i in range(3)]
    r_stg = [Res("stg%d" % i) for i in range(3)]
    acc = [sb("acc%d" % i, [128, TT]) for i in range(3)]
    r_acc = [Res("acc%d" % i) for i in range(3)]
    BCT = G[2][0:64, :, :]; r_BCT = r_G[2]
    kT = sb("kT", [128, 4, TT], BF16); r_kT = Res("kT")
    vtok = sb("vtok", [128, NCH, 256], BF16); r_vtok = Res("vtok")
    tA = sb("tA", [128, 1024]); r_tA = Res("tA")
    tB = sb("tB", [128, 1024]); r_tB = Res("tB")
    tC = sb("tC", [128, 1024], BF16); r_tC = Res("tC")
    tD = sb("tD", [128, 1024], BF16); r_tD = Res("tD")
    tE = sb("tE", [128, 1024], BF16); r_tE = Res("tE")
    Eh = sb("Eh", [128, 16, 128], BF16); r_Eh = Res("Eh")
    Mh, r_Mh = Eh, r_Eh
    CBs = sb("CBs", [128, 4, 128], BF16); r_CBs = Res("CBs")
    Btok = sb("Btok", [128, 4, 64], BF16); r_Btok = Res("Btok")
    sm = sb("sm", [128, 256]); r_sm = Res("sm")
    ktok, r_ktok = tB, r_tB

    d_in = d_par = d_rowb = d_out = d_x = None

    PS = [nc.alloc_psum_tensor("ps%d" % i, [128, 512], F32) for i in range(8)]
    r_PS = [Res("ps%d" % i, excl=True) for i in range(8)]
    psi = [0]

    def psum():
        i = psi[0]
        psi[0] = (i + 1) % 6
        return PS[i], r_PS[i]

    NSLOT = 8
    WS = [sb("ws%d" % i, [128, 2048], BF16) for i in range(NSLOT)]
    r_WS = [Res("ws%d" % i) for i in range(NSLOT)]
    d_WS = [S.dsem("ws%d" % i) for i in range(NSLOT)]
    for d_ in d_WS:
        d_.keep = True
    wsi = [0]

    def wload(src, K, C):
        i = wsi[0]
        wsi[0] = (i + 1) % NSLOT
        view = WS[i][:, 0:K * C].rearrange("p (k c) -> p k c", k=K)
        S.dma("pool", d_WS[i], [], [r_WS[i]], lambda: nc.gpsimd.dma_start(out=view, in_=src))
        return view, r_WS[i]

    def wcols(wd, l, c0, C):
        return wd[l, :, c0:c0 + C].rearrange("(k p) c -> p k c", p=128)

    ev = [0]

    def eng2():
        ev[0] ^= 1
        return "act" if ev[0] else "dve"

    def copy(eng, out, in_, reads, writes):
        if eng == "act":
            return S.op("act", reads, writes, lambda: nc.scalar.copy(out=out, in_=in_))
        return S.op(eng, reads, writes, lambda: (nc.vector if eng == "dve" else nc.gpsimd).tensor_copy(out=out, in_=in_))

    S.dma("sp", d_par, [], [r_pp], lambda: nc.sync.dma_start(out=pp[:], in_=pp_d))
    S.dma("sp", d_par, [], [r_pp64], lambda: nc.sync.dma_start(out=pp64[:], in_=pp64_d))
    S.dma("sp", d_par, [], [r_rows], lambda: nc.sync.dma_start(out=rows[:], in_=rows_d))
    S.dma("sp", d_par, [], [r_cst], lambda: nc.sync.dma_start(out=cst[:], in_=cst_d))
    S.op("dve", [r_cst], [r_cst], lambda: nc.vector.tensor_copy(out=identb[:], in_=ident))
    S.op("dve", [], [r_cst], lambda: nc.vector.memset(onesb[:], 1.0))
    dupI = sb("dupI", [64, 128], BF16)
    S.op("dve", [r_cst], [r_cst], lambda: nc.vector.tensor_copy(out=dupI[:, 0:64], in_=ident[0:64, 0:64]))
    S.op("dve", [r_cst], [r_cst], lambda: nc.vector.tensor_copy(out=dupI[:, 64:128], in_=ident[0:64, 0:64]))
    ones32 = sb("ones32", [128, 128])
    S.op("dve", [], [r_cst], lambda: nc.vector.memset(ones32[:], 1.0))
    S.op("act", [r_rows], [r_rows], lambda: nc.scalar.activation(out=rows[:, :, 16:32], in_=rows[:, :, 16:32], func=AF.Exp))
    S.op("dve", [r_rows], [r_rows], lambda: nc.vector.tensor_scalar(out=rows[:, :, 16:32], in0=rows[:, :, 16:32], scalar1=-1.0, scalar2=None, op0=ALU.mult))
    for t_, r_ in ((ST, r_ST), (cx, r_cx), (cbc, r_cbc), (csc, r_csc), (cff, r_cff)):
        S.op("dve", [], list(r_), lambda t_=t_: nc.vector.memset(t_[:], 0.0))

    NCc = 1
    cTt = sb("cTt", [128, 8, NCc]); r_cT = Res("cT")
    cTb = sb("cTb", [128, 8, NCc], BF16)
    S.dma("sp", d_par, [], [r_cT], lambda: nc.sync.dma_start(out=cTt[:], in_=cT_d))
    S.op("act", [r_cT], [r_cT], lambda: nc.scalar.activation(out=cTb[:], in_=cTt[:], func=AF.Silu))

    def ada_layer(l, cb, r_cb, ncol, mod_t, r_mod_t, gm_t, r_gm_t, lidx):
        for blk in range(24):
            wv, wr = wload(wcols(w_ada_d, l, blk * 256, 256), 8, 256)
            ps, pr = psum()

            def f(wv=wv, ps=ps):
                for i in range(2):
                    for k in range(8):
                        last = nc.tensor.matmul(ps[:, i * ncol:(i + 1) * ncol], lhsT=wv[:, k, i * 128:(i + 1) * 128],
                                                rhs=cb[:, k, :], start=(k == 0), stop=(k == 7))
                return last
            S.op("pe", [wr, r_cb], [pr], f)
            S.op("dve", [pr, r_pp], [r_mod_t], lambda ps=ps, blk=blk: nc.vector.tensor_tensor(
                out=mod_t[:, lidx, blk * 2:blk * 2 + 2, :], in0=ps[:, 0:2 * ncol].rearrange("p (i j) -> p i j", i=2),
                in1=pp[:, l, PP_BADA + blk * 2:PP_BADA + blk * 2 + 2].unsqueeze(2).to_broadcast([128, 2, ncol]), op=ALU.add))
        for which, (sc_i, gcol) in enumerate(((1, PP_GMIX), (4, PP_GFFN))):
            S.op("dve", [r_mod_t, r_pp], [r_gm_t], lambda which=which, sc_i=sc_i, gcol=gcol: nc.vector.scalar_tensor_tensor(
                out=gm_t[:, lidx, which, :, :], in0=mod_t[:, lidx, sc_i * 8:sc_i * 8 + 8, :], scalar=1.0,
                in1=pp[:, l, gcol:gcol + 8].unsqueeze(2).to_broadcast([128, 8, ncol]), op0=ALU.add, op1=ALU.mult))

    for l in range(L):
        ada_layer(l, cTb, r_cT, 1, mod, r_mod, gm, r_gm, l)

    epsc = sb("epsc", [128, 1])
    S.op("dve", [], [r_cst], lambda: nc.vector.memset(epsc[:], EPS))

    def rstd_of(ss_ap, n, out_ap, reads, writes, scale):
        S.op("act", reads, writes, lambda: nc.scalar.activation(out=out_ap, in_=ss_ap, func=AF.Ln, bias=epsc[0:n, :], scale=scale))
        S.op("act", writes, writes, lambda: nc.scalar.activation(out=out_ap, in_=out_ap, func=AF.Exp, scale=-0.5))

    def mod_norm(l, which, N):
        sh_i = 0 if which == 0 else 3
        ps, pr = psum()
        sq = G[5]
        S.op("act", [r_x], [r_G[5]], lambda: nc.scalar.activation(out=sq[:, :, 0:N], in_=xT[:, :, 0:N], func=AF.Square))

        def f():
            for k in range(8):
                last = nc.tensor.matmul(ps[:, 0:N], lhsT=onesb[:], rhs=sq[:, k, 0:N], start=(k == 0), stop=(k == 7))
            return last
        S.op("pe", [r_G[5], r_cst], [pr], f)
        rs = acc[2]
        rstd_of(ps[:, 0:N], 128, rs[:, 0:N], [pr, r_cst], [r_acc[2]], 1.0 / D)
        for c in range(8):
            S.op("dve", [r_x, r_gm, r_acc[2]], [r_acc[c % 2]], lambda c=c: nc.vector.scalar_tensor_tensor(
                out=acc[c % 2][:, 0:N], in0=xT[:, c, 0:N], scalar=gm[:, l, which, c, 0:1], in1=rs[:, 0:N],
                op0=ALU.mult, op1=ALU.mult))
            S.op("act", [r_acc[c % 2], r_mod], [r_h], lambda c=c: nc.scalar.activation(
                out=hB[:, c, 0:N], in_=acc[c % 2][:, 0:N], func=AF.Identity, bias=mod[:, l, sh_i * 8 + c, 0:1], scale=1.0))

    def proj_ws(wv, wr, col, M, N, rhs=None, rres=None, nk=8):
        rhs = hB if rhs is None else rhs
        rres = r_h if rres is None else rres
        ps, pr = psum()

        def f():
            for k in range(nk):
                last = nc.tensor.matmul(ps[0:M, 0:N], lhsT=wv[:, k, col:col + M], rhs=rhs[:, k, 0:N],
                                        start=(k == 0), stop=(k == nk - 1))
            return last
        S.op("pe", [wr, rres], [pr], f)
        return ps, pr

    def proj_as(wlist, C, t0, T):
        ps, pr = psum()

        def f():
            for bi, (wv, wr) in enumerate(wlist):
                cw = min(256, C - bi * 256)
                for k in range(8):
                    last = nc.tensor.matmul(ps[0:T, bi * 256:bi * 256 + cw], lhsT=hB[:, k, t0:t0 + T], rhs=wv[:, k, 0:cw],
                                            start=(k == 0), stop=(k == 7))
            return last
        S.op("pe", [w[1] for w in wlist] + [r_h], [pr], f)
        return ps, pr

    def conv_fm(P, ps, pr, N, carry_ap, r_carry, taps, bias_ap, preads, si):
        K = len(taps)
        CW = K - 1
        st, rst = stg[si], r_stg[si]
        ac, rac = acc[si], r_acc[si]
        S.op("act", [pr], [rst], lambda: nc.scalar.copy(out=st[0:P, CW:CW + N], in_=ps[0:P, 0:N]))
        S.op("dve", [r_carry], [rst], lambda: nc.vector.tensor_copy(out=st[0:P, 0:CW], in_=carry_ap))
        S.op("dve", [rst], [r_carry], lambda: nc.vector.tensor_copy(out=carry_ap, in_=st[0:P, N:N + CW]))
        if bias_ap is not None:
            S.op("dve", [rst] + preads, [rac], lambda: nc.vector.tensor_scalar(
                out=ac[0:P, 0:N], in0=st[0:P, 0:N], scalar1=taps[0], scalar2=bias_ap, op0=ALU.mult, op1=ALU.add))
        else:
            S.op("dve", [rst] + preads, [rac], lambda: nc.vector.tensor_scalar(
                out=ac[0:P, 0:N], in0=st[0:P, 0:N], scalar1=taps[0], scalar2=None, op0=ALU.mult))
        for j in range(1, K):
            S.op("dve", [rst, rac] + preads, [rac], lambda j=j: nc.vector.scalar_tensor_tensor(
                out=ac[0:P, 0:N], in0=st[0:P, j:j + N], scalar=taps[j], in1=ac[0:P, 0:N], op0=ALU.mult, op1=ALU.add))
        return ac, rac

    def win(l, c0, C=256):
        return wload(wcols(w_in_d, l, c0, C), 8, C)

    def prompt_tile_layer(ti, l):
        N = TT
        last_tile = (ti == NT - 1)
        S.dma("sp", d_rowb, [], [r_rowb], lambda: nc.sync.dma_start(out=rowb[:], in_=rowb_d[l]))
        normg = rowb[:, 0:1024]
        lng = rowb[:, 1024:2048]
        lnb = rowb[:, 2048:3072]
        bsrow = rowb[:, 3072:4096]
        mod_norm(l, 0, N)
        xsT, r_xsT = G[3], r_G[3]
        for bi in range(4):
            wv, wr = win(l, XBC0 + bi * 256)
            for cc in range(2):
                c = bi * 2 + cc
                ps, pr = proj_ws(wv, wr, cc * 128, 128, N)
                taps = [pp[:, l, PP_SCW + j * 8 + c:PP_SCW + j * 8 + c + 1] for j in range(4)]
                ac, rac = conv_fm(128, ps, pr, N, cx[:, l, c, 0:3], r_cx[l], taps, pp[:, l, PP_SCB + c:PP_SCB + c + 1], [r_pp], c % 2)
                S.op("act", [rac], [r_xsT], lambda ac=ac, c=c: nc.scalar.activation(out=xsT[:, c, 0:N], in_=ac[:, 0:N], func=AF.Silu))
        for bi in range(2):
            wv, wr = win(l, XBC0 + 1024 + bi * 256)
            for ee in range(4):
                e = bi * 4 + ee
                ps, pr = proj_ws(wv, wr, ee * 64, 64, N)
                taps = [pp64[:, l, j * 8 + e:j * 8 + e + 1] for j in range(4)]
                ac, rac = conv_fm(64, ps, pr, N, cbc[:, l, e, 0:3], r_cbc[l], taps, pp64[:, l, 32 + e:33 + e], [r_pp64], e % 2)
                S.op("act", [rac], [r_BCT], lambda ac=ac, e=e: nc.scalar.activation(out=BCT[:, e, 0:N], in_=ac[0:64, 0:N], func=AF.Silu))
        if DBG_STOP <= 1:
            return
        wz = [win(l, i * 256) for i in range(4)]
        wdt = [win(l, DT0, 256)]
        S.op("act", [r_ST[l]], [r_STb1], lambda: nc.scalar.copy(out=STb[:, :, :], in_=ST[:, l, :, :]))
        yaT, r_yaT = G[1], r_G[1]
        for j in range(NCH):
            ssd_chunk(l, j * 128, 128, wz, wdt, xsT, r_xsT, yaT, r_yaT, normg)
        if last_tile:
            S.dma("sp", d_out, [r_ST[l]], [], lambda: nc.sync.dma_start(out=pssm_d[l], in_=ST[:, l, :, :]))
            S.dma("sp", d_out, [r_cx[l]], [], lambda: nc.sync.dma_start(out=pxbc_d[l], in_=cx[:, l, :, :]))
            S.dma("sp", d_out, [r_cbc[l]], [], lambda: nc.sync.dma_start(out=pbc_d[l], in_=cbc[:, l, :, :]))
        if DBG_STOP <= 2:
            return
        ybT, r_ybT = G[2], r_G[2]
        for bi in range(4):
            wb = [win(l, BCX0 + part * 1024 + bi * 256) for part in range(3)]
            for cc in range(2):
                c = bi * 2 + cc
                psB, prB = proj_ws(wb[0][0], wb[0][1], cc * 128, 128, N)
                psC, prC = proj_ws(wb[1][0], wb[1][1], cc * 128, 128, N)
                psX, prX = proj_ws(wb[2][0], wb[2][1], cc * 128, 128, N)
                st, rst = stg[2], r_stg[2]
                S.op("act", [prC], [r_acc[2]], lambda psC=psC: nc.scalar.copy(out=acc[2][:, 0:N], in_=psC[:, 0:N]))
                S.op("dve", [prX, r_acc[2]], [rst], lambda psX=psX: nc.vector.tensor_tensor(
                    out=st[:, 2:2 + N], in0=psX[:, 0:N], in1=acc[2][:, 0:N], op=ALU.mult))
                S.op("dve", [r_csc[l]], [rst], lambda c=c: nc.vector.tensor_copy(out=st[:, 0:2], in_=csc[:, l, c, :]))
                S.op("dve", [rst], [r_csc[l]], lambda c=c: nc.vector.tensor_copy(out=csc[:, l, c, :], in_=st[:, N:N + 2]))
                ac, rac = acc[c % 2], r_acc[c % 2]
                taps = [pp[:, l, PP_SHW + j * 8 + c:PP_SHW + j * 8 + c + 1] for j in range(3)]
                S.op("dve", [rst, r_pp], [rac], lambda ac=ac, taps=taps: nc.vector.tensor_scalar(
                    out=ac[:, 0:N], in0=st[:, 0:N], scalar1=taps[0], scalar2=None, op0=ALU.mult))
                for jj in (1, 2):
                    S.op("dve", [rst, rac, r_pp], [rac], lambda ac=ac, taps=taps, jj=jj: nc.vector.scalar_tensor_tensor(
                        out=ac[:, 0:N], in0=st[:, jj:jj + N], scalar=taps[jj], in1=ac[:, 0:N], op0=ALU.mult, op1=ALU.add))
                S.op("dve", [prB, rac], [r_ybT], lambda ac=ac, psB=psB, c=c: nc.vector.tensor_tensor(
                    out=ybT[:, c, 0:N], in0=psB[:, 0:N], in1=ac[:, 0:N], op=ALU.mult))
        if last_tile:
            S.dma("sp", d_out, [r_csc[l]], [], lambda: nc.sync.dma_start(out=psc_d[l], in_=csc[:, l, :, :]))
        if DBG_STOP <= 3:
            return
        qT, r_qT = G[5], r_G[5]
        wk = win(l, K0)
        wvv = win(l, V0)
        for hk in range(4):
            ps, pr = proj_ws(wk[0], wk[1], hk * 64, 64, N)
            ktmp, r_ktmp = tC, r_tC
            copy("act", ktmp[0:64, 0:N], ps[0:64, 0:N], [pr], [r_ktmp])
            ps2, pr2 = psum()
            S.op("pe", [r_ktmp, r_cst], [pr2], lambda ps2=ps2: nc.tensor.matmul(ps2[:, 0:N], lhsT=dupI[:, :], rhs=ktmp[0:64, 0:N], start=True, stop=True))
            copy("dve", kT[:, hk, 0:N], ps2[:, 0:N], [pr2], [r_kT])
        if DBG_ATTP <= 1:
            return
        for j in range(NCH):
            ps, pr = proj_as([wk, wvv], 512, j * 128, 128)
            if last_tile and j == NCH - 1:
                S.op("act", [pr], [r_ktok], lambda ps=ps: nc.scalar.copy(out=ktok[:, 0:512], in_=ps[:, :]))
                S.dma("sp", d_out, [r_ktok], [], lambda: nc.sync.dma_start(out=pk_d[l], in_=ktok[:, 0:256]))
                S.dma("sp", d_out, [r_ktok], [], lambda: nc.sync.dma_start(out=pv_d[l], in_=ktok[:, 256:512]))
            S.op("dve", [pr], [r_vtok], lambda j=j, ps=ps: nc.vector.tensor_copy(out=vtok[:, j, :], in_=ps[:, 256:512]))
        if DBG_ATTP <= 2:
            return
        for bi in range(4):
            wv, wr = win(l, Q0 + bi * 256)
            for cc in range(2):
                c = bi * 2 + cc
                ps, pr = proj_ws(wv, wr, cc * 128, 128, N)
                S.op("act", [pr], [r_qT], lambda ps=ps, c=c: nc.scalar.activation(out=qT[:, c, 0:N], in_=ps[:, 0:N], func=AF.Identity, scale=0.125))
        ycT, r_ycT = G[3], r_G[3]
        if DBG_ATTP <= 3:
            return
        for j in range(NCH):
            if DBG_ATT >= 2:
                attn_chunk(l, ti, j, qT, r_qT, ycT, r_ycT)
        S.op("dve", [r_kT], [r_kprev[l]], lambda: nc.vector.tensor_copy(out=kprev[:, l, :, :], in_=kT[:, :, N - 128:N]))
        S.op("dve", [r_vtok], [r_vprev[l]], lambda: nc.vector.tensor_copy(out=vprev[:, l, :], in_=vtok[:, NCH - 1, :]))
        if DBG_STOP <= 4:
            return
        ydT, r_ydT = G[4], r_G[4]
        S.dma("pool", d_in, [], [r_gmraw], lambda: nc.gpsimd.dma_start(out=gmw_raw[:], in_=gmw_d[l].rearrange("g t s -> t g s")))
        for g in range(8):
            ps, pr = psum()
            psb = ps[:, 0:64].bitcast(BF16)
            S.op("pe", [r_gmraw, r_cst], [pr], lambda psb=psb, g=g: nc.tensor.transpose(psb[:, 0:128], gmw_raw[:, g, :], identb[:]))
            S.op("dve", [pr, r_cst], [r_gmw], lambda psb=psb, g=g: nc.vector.tensor_tensor(
                out=gmwT[:, g, :], in0=psb[:, 0:128], in1=Umat, op=ALU.mult))
        for bi in range(4):
            wv, wr = win(l, UV0 + bi * 256)
            for cc in range(2):
                c = bi * 2 + cc
                ps, pr = proj_ws(wv, wr, cc * 128, 128, N)
                S.op("act", [pr], [r_ydT], lambda ps=ps, c=c: nc.scalar.activation(out=ydT[:, c, 0:N], in_=ps[:, 0:N], func=AF.Gelu_apprx_tanh))
        wv_ = [win(l, UV0 + 1024 + i * 256) for i in range(4)]
        for j in range(NCH):
            gmlp_chunk(l, j * 128, 128, wv_, ydT, r_ydT, lng, lnb, bsrow, gmwT, None)
        if DBG_STOP <= 5:
            return
        mT, r_mT = G[5], r_G[5]
        brs = ((G[1], r_G[1]), (G[2], r_G[2]), (G[3], r_G[3]), (G[4], r_G[4]))
        macc = (acc[0], acc[1])
        r_macc = (r_acc[0], r_acc[1])
        for bi in range(4):
            for i in range(4):
                wg = win(l, GT0 + i * 1024 + bi * 256)
                wb = wload(w_br_d[l, i, :, bi * 256:bi * 256 + 256].rearrange("(k p) c -> p k c", p=128), 8, 256)
                for cc in range(2):
                    c = bi * 2 + cc
                    psg, prg = proj_ws(wg[0], wg[1], cc * 128, 128, N)
                    psp, prp = proj_ws(wb[0], wb[1], cc * 128, 128, N, rhs=brs[i][0], rres=brs[i][1])
                    S.op("act", [prg], [r_stg[2]], lambda psg=psg: nc.scalar.activation(out=stg[2][:, 0:N], in_=psg[:, 0:N], func=AF.Sigmoid))
                    if i == 0:
                        S.op("dve", [prp, r_stg[2]], [r_macc[cc]], lambda psp=psp, cc=cc: nc.vector.tensor_tensor(
                            out=macc[cc][:, 0:N], in0=psp[:, 0:N], in1=stg[2][:, 0:N], op=ALU.mult))
                    else:
                        S.op("dve", [prp, r_stg[2]], [r_acc[2]], lambda psp=psp: nc.vector.tensor_tensor(
                            out=acc[2][:, 0:N], in0=psp[:, 0:N], in1=stg[2][:, 0:N], op=ALU.mult))
                        if i < 3:
                            S.op("dve", [r_macc[cc], r_acc[2]], [r_macc[cc]], lambda cc=cc: nc.vector.tensor_tensor(
                                out=macc[cc][:, 0:N], in0=macc[cc][:, 0:N], in1=acc[2][:, 0:N], op=ALU.add))
                        else:
                            S.op("dve", [r_macc[cc], r_acc[2]], [r_mT], lambda c=c, cc=cc: nc.vector.tensor_tensor(
                                out=mT[:, c, 0:N], in0=macc[cc][:, 0:N], in1=acc[2][:, 0:N], op=ALU.add))
        if DBG_STOP <= 6:
            return
        for bi in range(4):
            wv, wr = wload(wcols(w_o_d, l, bi * 256, 256), 8, 256)
            for cc in range(2):
                c = bi * 2 + cc
                ps, pr = proj_ws(wv, wr, cc * 128, 128, N, rhs=mT, rres=r_mT)
                S.op("dve", [pr, r_mod, r_x], [r_x], lambda ps=ps, c=c: nc.vector.scalar_tensor_tensor(
                    out=xT[:, c, 0:N], in0=ps[:, 0:N], scalar=mod[:, l, 16 + c, 0:1], in1=xT[:, c, 0:N], op0=ALU.mult, op1=ALU.add))
        if DBG_STOP <= 7:
            return
        mod_norm(l, 1, N)
        gat = (G[1], G[2], G[3])
        r_gat = (r_G[1], r_G[2], r_G[3])
        for blk in range(11):
            wa = wload(wcols(w_up_d, l, blk * 256, 256), 8, 256)
            wg_ = wload(wcols(w_up_d, l, DFF + blk * 256, 256), 8, 256)
            for cc in range(2):
                i = blk * 2 + cc
                outs = []
                for which, (wv, wr) in enumerate((wa, wg_)):
                    ci = which * 22 + i
                    ps, pr = proj_ws(wv, wr, cc * 128, 128, N)
                    taps = [pp[:, l, PP_FW + j * 44 + ci:PP_FW + j * 44 + ci + 1] for j in range(3)]
                    ac, rac = conv_fm(128, ps, pr, N, cff[:, l, ci, :], r_cff[l], taps, pp[:, l, PP_FB + ci:PP_FB + ci + 1], [r_pp], which)
                    outs.append((ac, rac))
                S.op("act", [outs[0][1]], [r_acc[2]], lambda a=outs[0][0]: nc.scalar.activation(out=acc[2][:, 0:N], in_=a[:, 0:N], func=AF.Silu))
                S.op("dve", [r_acc[2], outs[1][1]], [r_gat[i // 8]], lambda g_=outs[1][0], i=i: nc.vector.tensor_tensor(
                    out=gat[i // 8][:, i % 8, 0:N], in0=acc[2][:, 0:N], in1=g_[:, 0:N], op=ALU.mult))
        if last_tile:
            S.dma("sp", d_out, [r_cff[l]], [], lambda: nc.sync.dma_start(out=pffn_d[l], in_=cff[:, l, :, :]))
        for c in range(8):
            wh = [wload(w_dn_d[l, hh * 1408:(hh + 1) * 1408, c * 128:(c + 1) * 128].rearrange("(k p) c -> p k c", p=128), 11, 128) for hh in range(2)]
            ps, pr = psum()

            def f(wh=wh, ps=ps):
                for k in range(22):
                    last = nc.tensor.matmul(ps[:, 0:N], lhsT=wh[k // 11][0][:, k % 11, :], rhs=gat[k // 8][:, k % 8, 0:N], start=(k == 0), stop=(k == 21))
                return last
            S.op("pe", [wh[0][1], wh[1][1]] + list(r_gat), [pr], f)
            S.op("dve", [pr, r_mod, r_x], [r_x], lambda ps=ps, c=c: nc.vector.scalar_tensor_tensor(
                out=xT[:, c, 0:N], in0=ps[:, 0:N], scalar=mod[:, l, 40 + c, 0:1], in1=xT[:, c, 0:N], op0=ALU.mult, op1=ALU.add))

    def ssd_chunk(l, t0, T, wz, wdt, xsT, r_xsT, yaT, r_yaT, normg):
        dtb = rows[0:T, l, 0:16]
        Arow = rows[0:T, l, 16:32]
        Drow = rows[0:T, l, 32:48]
        xs_tok, r_xs = tC, r_tC
        for c in range(8):
            ps, pr = psum()
            psb = ps[:, 0:64].bitcast(BF16)
            S.op("pe", [r_xsT, r_cst], [pr], lambda c=c, psb=psb: nc.tensor.transpose(psb[0:T, 0:128], xsT[:, c, t0:t0 + T], identb[:]))
            copy(eng2(), xs_tok[0:T, c * 128:(c + 1) * 128], psb[0:T, 0:128], [pr], [r_xs])
        for g in range(4):
            ps, pr = psum()
            psb = ps[:, 0:64].bitcast(BF16)
            S.op("pe", [r_BCT, r_cst], [pr], lambda g=g, psb=psb: nc.tensor.transpose(psb[0:T, 0:64], BCT[:, g, t0:t0 + T], identb[0:64, 0:64]))
            copy(eng2(), Btok[0:T, g, :], psb[0:T, 0:64], [pr], [r_Btok])
        ps, pr = proj_as(wdt, 16, t0, T)
        dt = sm[0:T, 0:16]
        dtA = sm[0:T, 16:32]
        Acs = sm[0:T, 32:48]
        nAcs = sm[0:T, 48:64]
        eA = sm[0:T, 64:80]
        dec = sm[0:T, 80:96]
        wdec = sm[0:T, 96:112]
        S.op("dve", [pr, r_rows], [r_sm], lambda: nc.vector.tensor_tensor(out=dt, in0=ps[0:T, 0:16], in1=dtb, op=ALU.add))
        S.op("act", [r_sm], [r_sm], lambda: nc.scalar.activation(out=dt, in_=dt, func=AF.Exp))
        S.op("act", [r_sm], [r_sm], lambda: nc.scalar.activation(out=dt, in_=dt, func=AF.Ln, bias=1.0))
        S.op("dve", [r_sm, r_rows], [r_sm], lambda: nc.vector.tensor_tensor(out=dtA, in0=dt, in1=Arow, op=ALU.mult))
        ps1, pr1 = psum()

        def f1():
            nc.tensor.matmul(ps1[0:T, 0:16], lhsT=Umat[0:T, 0:T], rhs=dtA, start=True, stop=True)
            return nc.tensor.matmul(ps1[0:max(T, 64), 16:32], lhsT=ones32[0:T, 0:max(T, 64)], rhs=dtA, start=True, stop=True)
        S.op("pe", [r_sm, r_cst], [pr1], f1)
        S.op("dve", [pr1], [r_sm], lambda: nc.vector.tensor_copy(out=Acs, in_=ps1[0:T, 0:16]))
        S.op("dve", [pr1], [r_sm], lambda: nc.vector.tensor_scalar(out=nAcs, in0=ps1[0:T, 0:16], scalar1=-1.0, scalar2=None, op0=ALU.mult))
        S.op("act", [pr1], [r_sm], lambda: nc.scalar.activation(out=eA, in_=ps1[0:T, 0:16], func=AF.Exp))
        S.op("dve", [pr1, r_sm], [r_sm], lambda: nc.vector.tensor_tensor(out=dec, in0=ps1[0:T, 16:32], in1=Acs, op=ALU.subtract))
        cdec = sm[0:64, 112:128]
        S.op("act", [pr1], [r_sm], lambda: nc.scalar.activation(out=cdec, in_=ps1[0:64, 16:32], func=AF.Exp))
        S.op("act", [r_sm], [r_sm], lambda: nc.scalar.activation(out=dec, in_=dec, func=AF.Exp))
        S.op("dve", [r_sm], [r_sm], lambda: nc.vector.tensor_tensor(out=wdec, in0=dec, in1=dt, op=ALU.mult))
        for hg in range(4):
            ps, pr = psum()

            def fe(ps=ps, hg=hg):
                for hh in range(4):
                    h_ = hg * 4 + hh
                    nc.tensor.matmul(ps[0:T, hh * 128:hh * 128 + T], lhsT=dtA[:, h_:h_ + 1].to_broadcast([T, T]), rhs=Umat[0:T, 0:T], start=True, stop=False)
                    last = nc.tensor.matmul(ps[0:T, hh * 128:hh * 128 + T], lhsT=ident[0:T, 0:T], rhs=NEGM[0:T, 0:T], start=False, stop=True)
                return last
            S.op("pe", [r_sm, r_cst], [pr], fe)
            for hh in range(4):
                h_ = hg * 4 + hh
                S.op("act", [pr, r_sm], [r_Eh], lambda ps=ps, hh=hh, h_=h_: nc.scalar.activation(
                    out=Eh[0:T, h_, 0:T], in_=ps[0:T, hh * 128:hh * 128 + T], func=AF.Exp, bias=nAcs[:, h_:h_ + 1], scale=1.0))
        ps, pr = psum()

        def fcb(ps=ps):
            for g in range(4):
                last = nc.tensor.matmul(ps[0:T, g * 128:g * 128 + T], lhsT=BCT[:, g, t0:t0 + T], rhs=BCT[:, 4 + g, t0:t0 + T], start=True, stop=True)
            return last
        S.op("pe", [r_BCT], [pr], fcb)
        copy("act", CBs[0:T, :, 0:T], ps[0:T, :].rearrange("p (g t) -> p g t", g=4)[:, :, 0:T], [pr], [r_CBs])
        for g in range(4):
            S.op("dve", [r_Eh, r_CBs], [r_Mh], lambda g=g: nc.vector.tensor_tensor(
                out=Mh[0:T, g * 4:g * 4 + 4, 0:T], in0=Eh[0:T, g * 4:g * 4 + 4, 0:T],
                in1=CBs[0:T, g:g + 1, 0:T].to_broadcast([T, 4, T]), op=ALU.mult))
        xdt, r_xdt = tD, r_tD
        Xdd, r_Xdd = tE, r_tE
        S.op("dve", [r_xs, r_sm], [r_xdt], lambda: nc.vector.tensor_tensor(
            out=xdt[0:T, :].rearrange("p (h q) -> p h q", h=16), in0=xs_tok[0:T, :].rearrange("p (h q) -> p h q", h=16),
            in1=dt.unsqueeze(2).to_broadcast([T, 16, 64]), op=ALU.mult))
        S.op("dve", [r_xs, r_sm], [r_Xdd], lambda: nc.vector.tensor_tensor(
            out=Xdd[0:T, :].rearrange("p (h q) -> p h q", h=16), in0=xs_tok[0:T, :].rearrange("p (h q) -> p h q", h=16),
            in1=wdec.unsqueeze(2).to_broadcast([T, 16, 64]), op=ALU.mult))
        pso = [psum(), psum()]
        psd = [psum(), psum()]

        def foff():
            for g in range(4):
                last = nc.tensor.matmul(pso[g // 2][0][0:T, (g % 2) * 256:(g % 2) * 256 + 256], lhsT=BCT[:, 4 + g, t0:t0 + T],
                                        rhs=STb[:, g, :], start=True, stop=True)
            return last
        S.op("pe", [r_BCT, r_STb[l]], [pso[0][1], pso[1][1]], foff)

        def fdiag():
            for h_ in range(16):
                last = nc.tensor.matmul(psd[h_ // 8][0][0:T, (h_ % 8) * 64:(h_ % 8) * 64 + 64], lhsT=Mh[0:T, h_, 0:T],
                                        rhs=xdt[0:T, h_ * 64:(h_ + 1) * 64], start=True, stop=True)
            return last
        S.op("pe", [r_Mh, r_xdt], [psd[0][1], psd[1][1]], fdiag)
        y, r_y = tA, r_tA
        for hf in range(2):
            S.op("dve", [pso[hf][1], r_sm], [r_y], lambda hf=hf: nc.vector.tensor_tensor(
                out=y[0:T, hf * 512:(hf + 1) * 512].rearrange("p (h q) -> p h q", h=8),
                in0=pso[hf][0][0:T, :].rearrange("p (h q) -> p h q", h=8),
                in1=eA[:, hf * 8:hf * 8 + 8].unsqueeze(2).to_broadcast([T, 8, 64]), op=ALU.mult))
            S.op("dve", [psd[hf][1], r_y], [r_y], lambda hf=hf: nc.vector.tensor_tensor(
                out=y[0:T, hf * 512:(hf + 1) * 512], in0=psd[hf][0][0:T, :], in1=y[0:T, hf * 512:(hf + 1) * 512], op=ALU.add))
        t2, r_t2 = tB, r_tB
        S.op("dve", [r_xs, r_rows], [r_t2], lambda: nc.vector.tensor_tensor(
            out=t2[0:T, :].rearrange("p (h q) -> p h q", h=16), in0=xs_tok[0:T, :].rearrange("p (h q) -> p h q", h=16),
            in1=Drow.unsqueeze(2).to_broadcast([T, 16, 64]), op=ALU.mult))
        S.op("dve", [r_t2, r_y], [r_y], lambda: nc.vector.tensor_tensor(out=y[0:T, :], in0=y[0:T, :], in1=t2[0:T, :], op=ALU.add))
        pss = [psum(), psum()]

        def fst():
            for g in range(4):
                last = nc.tensor.matmul(pss[g // 2][0][0:64, (g % 2) * 256:(g % 2) * 256 + 256], lhsT=Btok[0:T, g, :],
                                        rhs=Xdd[0:T, g * 256:(g + 1) * 256], start=True, stop=True)
            return last
        S.op("pe", [r_Btok, r_Xdd], [pss[0][1], pss[1][1]], fst)
        S.op("dve", [r_ST[l], r_sm, r_STb[l]], [r_ST[l]], lambda: nc.vector.tensor_tensor(
            out=ST[:, l, :, :].rearrange("p g (r q) -> p (g r) q", r=4), in0=ST[:, l, :, :].rearrange("p g (r q) -> p (g r) q", r=4),
            in1=cdec.unsqueeze(2).to_broadcast([64, 16, 64]), op=ALU.mult))
        for hf in range(2):
            S.op("dve", [pss[hf][1], r_ST[l]], [r_ST[l]], lambda hf=hf: nc.vector.tensor_tensor(
                out=ST[:, l, hf * 2:hf * 2 + 2, :], in0=ST[:, l, hf * 2:hf * 2 + 2, :],
                in1=pss[hf][0][0:64, :].rearrange("p (g q) -> p g q", g=2), op=ALU.add))
        S.op("act", [r_ST[l]], [r_STb[l]], lambda: nc.scalar.copy(out=STb[:, :, :], in_=ST[:, l, :, :]))
        for hf in range(2):
            ps, pr = proj_as(wz[hf * 2:hf * 2 + 2], 512, t0, T)
            S.op("act", [pr], [r_t2], lambda ps=ps, hf=hf: nc.scalar.activation(out=t2[0:T, hf * 512:(hf + 1) * 512], in_=ps[0:T, :], func=AF.Silu))
        S.op("dve", [r_t2, r_y], [r_y], lambda: nc.vector.tensor_tensor(out=y[0:T, :], in0=y[0:T, :], in1=t2[0:T, :], op=ALU.mult))
        ssq = sm[0:T, 128:132]
        for g in range(4):
            S.op("act", [r_y], [r_t2, r_sm], lambda g=g: nc.scalar.activation(
                out=t2[0:T, g * 256:(g + 1) * 256], in_=y[0:T, g * 256:(g + 1) * 256], func=AF.Square, accum_out=ssq[:, g:g + 1]))
        rstd_of(ssq, T, ssq, [r_sm, r_cst], [r_sm], 1.0 / 256)
        S.op("dve", [r_y, r_sm], [r_y], lambda: nc.vector.tensor_tensor(
            out=y[0:T, :].rearrange("p (g q) -> p g q", g=4), in0=y[0:T, :].rearrange("p (g q) -> p g q", g=4),
            in1=ssq.unsqueeze(2).to_broadcast([T, 4, 256]), op=ALU.mult))
        ya_tok, r_ya = tC, r_tC
        S.op("dve", [r_y, r_rowb], [r_ya], lambda: nc.vector.tensor_tensor(out=ya_tok[0:T, :], in0=y[0:T, :], in1=normg[0:T, :], op=ALU.mult))
        for c in range(8):
            ps, pr = psum()
            psb = ps[:, 0:64].bitcast(BF16)
            S.op("pe", [r_ya, r_cst], [pr], lambda c=c, psb=psb: nc.tensor.transpose(psb[:, 0:T], ya_tok[0:T, c * 128:(c + 1) * 128], identb[0:T, 0:T]))
            copy(eng2(), yaT[:, c, t0:t0 + T], psb[:, 0:T], [pr], [r_yaT])

    def attn_chunk(l, ti, j, qT, r_qT, ycT, r_ycT):
        T = 128
        t0 = j * 128
        first = (ti == 0 and j == 0)
        KC = 128 if first else 256
        boff = 128 if first else 0
        sink = rows[:, l, 48:64]
        o_ps = [(PS[6], r_PS[6]), (PS[7], r_PS[7])]
        rinv = sm[:, 136:152]
        for hk in range(4):
            pss_ = [psum(), psum()]

            def fs(hk=hk, pss_=pss_):
                for hh in range(4):
                    hd = hk * 4 + hh
                    pb = (hd % 2) * 64
                    qa = qT[pb:pb + 64, hd // 2, t0:t0 + T]
                    dst = pss_[hh % 2][0][:, (hh // 2) * 256:(hh // 2) * 256 + KC]
                    if first:
                        last = nc.tensor.matmul(dst, lhsT=qa, rhs=kT[pb:pb + 64, hk, t0:t0 + T], start=True, stop=True)
                    elif j == 0:
                        nc.tensor.matmul(dst[:, 0:128], lhsT=qa, rhs=kprev[pb:pb + 64, l, hk, :], start=True, stop=True)
                        last = nc.tensor.matmul(dst[:, 128:256], lhsT=qa, rhs=kT[pb:pb + 64, hk, t0:t0 + T], start=True, stop=True)
                    else:
                        last = nc.tensor.matmul(dst, lhsT=qa, rhs=kT[pb:pb + 64, hk, t0 - 128:t0 + T], start=True, stop=True)
                return last
            S.op("pe", [r_qT, r_kT, r_kprev[l]], [pss_[0][1], pss_[1][1]], fs)
            sc = tA[:, :].rearrange("p (h k) -> p h k", h=4)
            for hh in range(4):
                S.op("dve", [pss_[hh % 2][1], r_cst], [r_tA], lambda hh=hh, hk=hk, pss_=pss_: nc.vector.scalar_tensor_tensor(
                    out=sc[:, hh, 0:KC], in0=distm[:, boff:boff + KC], scalar=-SLOPES[hk * 4 + hh],
                    in1=pss_[hh % 2][0][:, (hh // 2) * 256:(hh // 2) * 256 + KC], op0=ALU.mult, op1=ALU.add))
            if DBG_ATT <= 2:
                continue
            mx = sm[:, 152:156]
            nmx = sm[:, 156:160]
            esk = sm[:, 160:164]
            rsum = sm[:, 164:168]
            S.op("dve", [r_tA], [r_sm], lambda: nc.vector.tensor_reduce(out=mx, in_=sc[:, :, 0:KC], axis=AX.X, op=ALU.max))
            S.op("dve", [r_sm, r_rows], [r_sm], lambda hk=hk: nc.vector.tensor_tensor(out=mx, in0=mx, in1=sink[:, hk * 4:hk * 4 + 4], op=ALU.max))
            S.op("dve", [r_sm], [r_sm], lambda: nc.vector.tensor_scalar(out=nmx, in0=mx, scalar1=-1.0, scalar2=None, op0=ALU.mult))
            S.op("dve", [r_sm, r_rows], [r_sm], lambda hk=hk: nc.vector.tensor_tensor(out=esk, in0=sink[:, hk * 4:hk * 4 + 4], in1=mx, op=ALU.subtract))
            S.op("act", [r_sm], [r_sm], lambda: nc.scalar.activation(out=esk, in_=esk, func=AF.Exp))
            Pm = tD[:, :].rearrange("p (h k) -> p h k", h=4)
            for hh in range(4):
                S.op("act", [r_tA, r_sm], [r_tD, r_sm], lambda hh=hh: nc.scalar.activation(
                    out=Pm[:, hh, 0:KC], in_=sc[:, hh, 0:KC], func=AF.Exp, bias=nmx[:, hh:hh + 1], scale=1.0, accum_out=rsum[:, hh:hh + 1]))
            S.op("dve", [r_sm], [r_sm], lambda: nc.vector.tensor_tensor(out=rsum, in0=rsum, in1=esk, op=ALU.add))
            S.op("dve", [r_sm], [r_sm], lambda hk=hk: nc.vector.reciprocal(out=rinv[:, hk * 4:hk * 4 + 4], in_=rsum))
            if DBG_ATT <= 3:
                continue
            PT = tE[:, :].rearrange("p (h k) -> p h k", h=4)
            nkb = KC // 128
            for hh in range(4):
                ps, pr = psum()
                psb = ps[:, 0:128].bitcast(BF16)

                def ft(hh=hh, psb=psb):
                    for kb in range(nkb):
                        last = nc.tensor.transpose(psb[:, kb * 128:(kb + 1) * 128], Pm[:, hh, kb * 128:(kb + 1) * 128], identb[:])
                    return last
                S.op("pe", [r_tD, r_cst], [pr], ft)
                copy(eng2(), PT[:, hh, 0:KC], psb[:, 0:KC], [pr], [r_tE])

            def fo(hk=hk):
                for hh in range(4):
                    hd = hk * 4 + hh
                    dst = o_ps[hd // 8][0][:, (hd % 8) * 64:(hd % 8) * 64 + 64]
                    if first:
                        last = nc.tensor.matmul(dst, lhsT=PT[:, hh, 0:128], rhs=vtok[:, j, hk * 64:hk * 64 + 64], start=True, stop=True)
                    else:
                        vp = vprev[:, l, hk * 64:hk * 64 + 64] if j == 0 else vtok[:, j - 1, hk * 64:hk * 64 + 64]
                        nc.tensor.matmul(dst, lhsT=PT[:, hh, 0:128], rhs=vp, start=True, stop=False)
                        last = nc.tensor.matmul(dst, lhsT=PT[:, hh, 128:256], rhs=vtok[:, j, hk * 64:hk * 64 + 64], start=False, stop=True)
                return last
            S.op("pe", [r_tE, r_vtok, r_vprev[l]], [o_ps[hk // 2][1]], fo)
        yc_tok = tC
        for hf in range(2):
            S.op("dve", [o_ps[hf][1], r_sm], [r_tC], lambda hf=hf: nc.vector.tensor_tensor(
                out=yc_tok[:, hf * 512:(hf + 1) * 512].rearrange("p (h q) -> p h q", h=8),
                in0=o_ps[hf][0][:, :].rearrange("p (h q) -> p h q", h=8),
                in1=rinv[:, hf * 8:hf * 8 + 8].unsqueeze(2).to_broadcast([128, 8, 64]), op=ALU.mult))
        for c in range(8):
            ps, pr = psum()
            psb = ps[:, 0:64].bitcast(BF16)
            S.op("pe", [r_tC, r_cst], [pr], lambda c=c, psb=psb: nc.tensor.transpose(psb[:, 0:T], yc_tok[:, c * 128:(c + 1) * 128], identb[:]))
            copy(eng2(), ycT[:, c, t0:t0 + T], psb[:, 0:T], [pr], [r_ycT])

    def gmlp_chunk(l, t0, T, wv_, ydT, r_ydT, lng, lnb, bsrow, wT, vout):
        v, r_v = tA, r_tA
        ssum = sm[0:T, 168:170]
        mean = sm[0:T, 170:171]
        ssq = sm[0:T, 171:173]
        rstd = sm[0:T, 173:174]
        for hf in range(2):
            ps, pr = proj_as(wv_[hf * 2:hf * 2 + 2], 512, t0, T)
            S.op("act", [pr], [r_v, r_sm], lambda ps=ps, hf=hf: nc.scalar.activation(
                out=v[0:T, hf * 512:(hf + 1) * 512], in_=ps[0:T, :], func=AF.Gelu_apprx_tanh, accum_out=ssum[:, hf:hf + 1]))
        S.op("dve", [r_sm], [r_sm], lambda: nc.vector.tensor_tensor(out=mean, in0=ssum[:, 0:1], in1=ssum[:, 1:2], op=ALU.add))
        S.op("dve", [r_sm], [r_sm], lambda: nc.vector.tensor_scalar(out=mean, in0=mean, scalar1=1.0 / 1024, scalar2=None, op0=ALU.mult))
        S.op("dve", [r_v, r_sm], [r_v], lambda: nc.vector.tensor_scalar(out=v[0:T, :], in0=v[0:T, :], scalar1=mean, scalar2=None, op0=ALU.subtract))
        for hf in range(2):
            S.op("act", [r_v], [r_tB, r_sm], lambda hf=hf: nc.scalar.activation(
                out=tB[0:T, hf * 512:(hf + 1) * 512], in_=v[0:T, hf * 512:(hf + 1) * 512], func=AF.Square, accum_out=ssq[:, hf:hf + 1]))
        S.op("dve", [r_sm], [r_sm], lambda: nc.vector.tensor_tensor(out=rstd, in0=ssq[:, 0:1], in1=ssq[:, 1:2], op=ALU.add))
        rstd_of(rstd, T, rstd, [r_sm, r_cst], [r_sm], 1.0 / 1024)
        S.op("dve", [r_v, r_sm, r_rowb], [r_v], lambda: nc.vector.scalar_tensor_tensor(
            out=v[0:T, :], in0=v[0:T, :], scalar=rstd, in1=lng[0:T, :], op0=ALU.mult, op1=ALU.mult))
        if vout is not None:
            S.op("dve", [r_v, r_rowb], [r_tB], lambda: nc.vector.tensor_tensor(out=tB[0:T, :], in0=v[0:T, :], in1=lnb[0:T, :], op=ALU.add))
            vout(tB, r_tB)
        vn, r_vn = tC, r_tC
        S.op("dve", [r_v, r_rowb], [r_vn], lambda: nc.vector.tensor_tensor(out=vn[0:T, :], in0=v[0:T, :], in1=lnb[0:T, :], op=ALU.add))
        for hf in range(2):
            ps, pr = psum()

            def fm(ps=ps, hf=hf):
                for gg in range(4):
                    g = hf * 4 + gg
                    last = nc.tensor.matmul(ps[:, gg * 128:gg * 128 + T], lhsT=vn[0:T, g * 128:(g + 1) * 128], rhs=wT[0:T, g, 0:T], start=True, stop=True)
                return last
            S.op("pe", [r_vn, r_gmw], [pr], fm)
            mix = tB[:, 0:512].rearrange("p (g t) -> p g t", g=4)
            S.op("dve", [pr, r_rowb], [r_tB], lambda ps=ps, hf=hf: nc.vector.tensor_tensor(
                out=mix[:, :, 0:T], in0=ps[:, :].rearrange("p (g t) -> p g t", g=4)[:, :, 0:T],
                in1=bsrow[:, hf * 512:(hf + 1) * 512].rearrange("p (g t) -> p g t", g=4)[:, :, 0:T], op=ALU.add))
            S.op("dve", [r_tB, r_ydT], [r_ydT], lambda hf=hf: nc.vector.tensor_tensor(
                out=ydT[:, hf * 4:hf * 4 + 4, t0:t0 + T], in0=ydT[:, hf * 4:hf * 4 + 4, t0:t0 + T], in1=mix[:, :, 0:T], op=ALU.mult))

    if SB:
        NS = 4 * SB
        mod_s = sb("mod_s", [128, 1, 48, SB]); r_mods = Res("mod_s")
        gm_s = sb("gm_s", [128, 1, 2, 8, SB]); r_gms = Res("gm_s")
        csT = sb("csT_t", [128, 8, SB]); r_csT = Res("csT")
        csb = sb("csb", [128, 8, SB], BF16)
        sxbc = sb("sxbc", [128, 8, SB, 3]); r_sxbc = Res("sxbc")
        sbc = sb("sbc", [64, 8, SB, 3]); r_sbc = Res("sbc")
        ssc = sb("ssc", [128, 8, SB, 2]); r_ssc = Res("ssc")
        sffn = sb("sffn", [128, 44, SB, 2]); r_sffn = Res("sffn")
        KKb = sb("KKb", [128, 4, 160], BF16); r_KKb = Res("KKb")
        Vb = sb("Vb", [128, 256], BF16); r_Vb = Res("Vb")
        vnew = sb("vnew", [4, 256], BF16); r_vnew = Res("vnew")
        PTn = sb("PTn", [4, 4, 4], BF16); r_PTn = Res("PTn")
        d_cs = d_kv = d_st = None

    def conv_s(P, ps, pr, taps, bias_ap, preads, st_ap, r_st, si):
        K = len(taps)
        CW = K - 1
        W = CW + 4
        st3 = stg[si][0:P, 0:SB * W].rearrange("p (b w) -> p b w", b=SB)
        ac3 = acc[si][0:P, 0:NS].rearrange("p (b w) -> p b w", b=SB)
        rst, rac = r_stg[si], r_acc[si]
        S.op("act", [pr], [rst], lambda: nc.scalar.copy(out=st3[:, :, CW:W], in_=ps[0:P, 0:NS].rearrange("p (b w) -> p b w", b=SB)))
        S.op("dve", [r_st], [rst], lambda: nc.vector.tensor_copy(out=st3[:, :, 0:CW], in_=st_ap))
        S.op("dve", [rst], [r_st], lambda: nc.vector.tensor_copy(out=st_ap, in_=st3[:, :, 4:W]))
        if bias_ap is not None:
            S.op("dve", [rst] + preads, [rac], lambda: nc.vector.tensor_scalar(
                out=ac3, in0=st3[:, :, 0:4], scalar1=taps[0], scalar2=bias_ap, op0=ALU.mult, op1=ALU.add))
        else:
            S.op("dve", [rst] + preads, [rac], lambda: nc.vector.tensor_scalar(
                out=ac3, in0=st3[:, :, 0:4], scalar1=taps[0], scalar2=None, op0=ALU.mult))
        for j in range(1, K):
            S.op("dve", [rst, rac] + preads, [rac], lambda j=j: nc.vector.scalar_tensor_tensor(
                out=ac3, in0=st3[:, :, j:j + 4], scalar=taps[j], in1=ac3, op0=ALU.mult, op1=ALU.add))
        return acc[si], rac

    def mod_norm_s(which):
        N = NS
        sh_i = 0 if which == 0 else 3
        ps, pr = psum()
        sq = G[5]
        S.op("act", [r_x], [r_G[5]], lambda: nc.scalar.activation(out=sq[:, :, 0:N], in_=xT[:, :, 0:N], func=AF.Square))

        def f():
            for k in range(8):
                last = nc.tensor.matmul(ps[:, 0:N], lhsT=onesb[:], rhs=sq[:, k, 0:N], start=(k == 0), stop=(k == 7))
            return last
        S.op("pe", [r_G[5], r_cst], [pr], f)
        rs = acc[2]
        rstd_of(ps[:, 0:N], 128, rs[:, 0:N], [pr, r_cst], [r_acc[2]], 1.0 / D)
        for c in range(8):
            a_ = acc[c % 2]
            a3 = a_[:, 0:N].rearrange("p (b w) -> p b w", b=SB)
            S.op("dve", [r_x, r_acc[2]], [r_acc[c % 2]], lambda c=c, a_=a_: nc.vector.tensor_tensor(
                out=a_[:, 0:N], in0=xT[:, c, 0:N], in1=rs[:, 0:N], op=ALU.mult))
            S.op("dve", [r_gms, r_acc[c % 2]], [r_acc[c % 2]], lambda c=c, a3=a3: nc.vector.tensor_tensor(
                out=a3, in0=a3, in1=gm_s[:, 0, which, c, :].unsqueeze(2).to_broadcast([128, SB, 4]), op=ALU.mult))
            S.op("dve", [r_mods, r_acc[c % 2]], [r_h], lambda c=c, a3=a3: nc.vector.tensor_tensor(
                out=hB[:, c, 0:N].rearrange("p (b w) -> p b w", b=SB), in0=a3,
                in1=mod_s[:, 0, sh_i * 8 + c, :].unsqueeze(2).to_broadcast([128, SB, 4]), op=ALU.add))

    def resid_s(ps, pr, c, gi):
        N = NS
        S.op("dve", [pr, r_mods], [r_acc[2]], lambda: nc.vector.tensor_tensor(
            out=acc[2][:, 0:N].rearrange("p (b w) -> p b w", b=SB), in0=ps[:, 0:N].rearrange("p (b w) -> p b w", b=SB),
            in1=mod_s[:, 0, gi * 8 + c, :].unsqueeze(2).to_broadcast([128, SB, 4]), op=ALU.mult))
        S.op("dve", [r_acc[2], r_x], [r_x], lambda: nc.vector.tensor_tensor(
            out=xT[:, c, 0:N], in0=xT[:, c, 0:N], in1=acc[2][:, 0:N], op=ALU.add))

    def attn_s(l, b, qT, r_qT, ycT, r_ycT, wk, wvv):
        T = 4
        t0 = 4 * b
        KC = 132
        sink = rows[0:T, l, 48:64]
        o_ps = [(PS[6], r_PS[6]), (PS[7], r_PS[7])]
        rinv = sm[0:T, 136:152]
        S.dma("pool", d_kv, [], [r_KKb], lambda: nc.gpsimd.dma_start(out=KKb[:, :, 0:128], in_=ckT_d[l, b]))
        S.dma("pool", d_kv, [], [r_Vb], lambda: nc.gpsimd.dma_start(out=Vb[:, :], in_=cv_d[l, b]))
        S.op("dve", [r_kT], [r_KKb], lambda: nc.vector.tensor_copy(out=KKb[:, :, 128:132], in_=kT[:, :, t0:t0 + T]))
        ps, pr = proj_as([wk, wvv], 512, t0, T)
        S.op("act", [pr], [r_ktok], lambda: nc.scalar.copy(out=ktok[0:T, 0:512], in_=ps[0:T, :]))
        S.op("dve", [r_ktok], [r_vnew], lambda: nc.vector.tensor_copy(out=vnew[:, :], in_=ktok[0:T, 256:512]))
        S.dma("sp", d_out, [r_ktok], [], lambda: nc.sync.dma_start(out=sk_o[l, b, 124:128, :], in_=ktok[0:T, 0:256]))
        S.dma("sp", d_out, [r_ktok], [], lambda: nc.sync.dma_start(out=sv_o[l, b, 124:128, :], in_=ktok[0:T, 256:512]))
        for hk in range(4):
            pss_ = [psum(), psum()]

            def fs(hk=hk, pss_=pss_):
                for hh in range(4):
                    hd = hk * 4 + hh
                    pb = (hd % 2) * 64
                    last = nc.tensor.matmul(pss_[hh % 2][0][0:T, (hh // 2) * 256:(hh // 2) * 256 + KC],
                                            lhsT=qT[pb:pb + 64, hd // 2, t0:t0 + T], rhs=KKb[pb:pb + 64, hk, 0:132], start=True, stop=True)
                return last
            S.op("pe", [r_qT, r_KKb], [pss_[0][1], pss_[1][1]], fs)
            sc = tA[0:T, :].rearrange("p (h k) -> p h k", h=4)
            for hh in range(4):
                S.op("dve", [pss_[hh % 2][1], r_cst], [r_tA], lambda hh=hh, hk=hk, pss_=pss_: nc.vector.scalar_tensor_tensor(
                    out=sc[:, hh, 0:KC], in0=dists[0:T, :], scalar=-SLOPES[hk * 4 + hh],
                    in1=pss_[hh % 2][0][0:T, (hh // 2) * 256:(hh // 2) * 256 + KC], op0=ALU.mult, op1=ALU.add))
            mx = sm[0:T, 152:156]
            nmx = sm[0:T, 156:160]
            esk = sm[0:T, 160:164]
            rsum = sm[0:T, 164:168]
            S.op("dve", [r_tA], [r_sm], lambda: nc.vector.tensor_reduce(out=mx, in_=sc[:, :, 0:KC], axis=AX.X, op=ALU.max))
            S.op("dve", [r_sm, r_rows], [r_sm], lambda hk=hk: nc.vector.tensor_tensor(out=mx, in0=mx, in1=sink[:, hk * 4:hk * 4 + 4], op=ALU.max))
            S.op("dve", [r_sm], [r_sm], lambda: nc.vector.tensor_scalar(out=nmx, in0=mx, scalar1=-1.0, scalar2=None, op0=ALU.mult))
            S.op("dve", [r_sm, r_rows], [r_sm], lambda hk=hk: nc.vector.tensor_tensor(out=esk, in0=sink[:, hk * 4:hk * 4 + 4], in1=mx, op=ALU.subtract))
            S.op("act", [r_sm], [r_sm], lambda: nc.scalar.activation(out=esk, in_=esk, func=AF.Exp))
            Pm = tD[0:T, :].rearrange("p (h k) -> p h k", h=4)
            for hh in range(4):
                S.op("act", [r_tA, r_sm], [r_tD, r_sm], lambda hh=hh: nc.scalar.activation(
                    out=Pm[:, hh, 0:KC], in_=sc[:, hh, 0:KC], func=AF.Exp, bias=nmx[:, hh:hh + 1], scale=1.0, accum_out=rsum[:, hh:hh + 1]))
            S.op("dve", [r_sm], [r_sm], lambda: nc.vector.tensor_tensor(out=rsum, in0=rsum, in1=esk, op=ALU.add))
            S.op("dve", [r_sm], [r_sm], lambda hk=hk: nc.vector.reciprocal(out=rinv[:, hk * 4:hk * 4 + 4], in_=rsum))
            PT = tE[:, :].rearrange("p (h k) -> p h k", h=4)
            ps, pr = psum()
            psb = ps[:, 0:64].bitcast(BF16)

            def ft(psb=psb):
                for hh in range(4):
                    nc.tensor.transpose(psb[:, hh * 8:hh * 8 + 4], Pm[:, hh, 0:128], identb[0:T, 0:T])
                    last = nc.tensor.transpose(psb[0:T, hh * 8 + 4:hh * 8 + 8], Pm[:, hh, 128:132], identb[0:T, 0:T])
                return last
            S.op("pe", [r_tD, r_cst], [pr], ft)
            pv = psb[:, 0:32].rearrange("p (h k) -> p h k", h=4)
            copy("dve", PT[:, :, 0:4], pv[:, :, 0:4], [pr], [r_tE])
            copy("dve", PTn[:, :, :], pv[0:T, :, 4:8], [pr], [r_PTn])

            def fo(hk=hk):
                for hh in range(4):
                    hd = hk * 4 + hh
                    dst = o_ps[hd // 8][0][0:T, (hd % 8) * 64:(hd % 8) * 64 + 64]
                    nc.tensor.matmul(dst, lhsT=PT[:, hh, 0:4], rhs=Vb[:, hk * 64:hk * 64 + 64], start=True, stop=False)
                    last = nc.tensor.matmul(dst, lhsT=PTn[:, hh, :], rhs=vnew[:, hk * 64:hk * 64 + 64], start=False, stop=True)
                return last
            S.op("pe", [r_tE, r_PTn, r_Vb, r_vnew], [o_ps[hk // 2][1]], fo)
        yc_tok = tC
        for hf in range(2):
            S.op("dve", [o_ps[hf][1], r_sm], [r_tC], lambda hf=hf: nc.vector.tensor_tensor(
                out=yc_tok[0:T, hf * 512:(hf + 1) * 512].rearrange("p (h q) -> p h q", h=8),
                in0=o_ps[hf][0][0:T, :].rearrange("p (h q) -> p h q", h=8),
                in1=rinv[:, hf * 8:hf * 8 + 8].unsqueeze(2).to_broadcast([T, 8, 64]), op=ALU.mult))
        for c in range(8):
            ps, pr = psum()
            psb = ps[:, 0:64].bitcast(BF16)
            S.op("pe", [r_tC, r_cst], [pr], lambda c=c, psb=psb: nc.tensor.transpose(psb[:, 0:T], yc_tok[0:T, c * 128:(c + 1) * 128], identb[0:T, 0:T]))
            copy(eng2(), ycT[:, c, t0:t0 + T], psb[:, 0:T], [pr], [r_ycT])

    def sample_tile_layer(l):
        N = NS
        S.dma("sp", d_rowb, [], [r_rowb], lambda: nc.sync.dma_start(out=rowb[:], in_=rowb_d[l]))
        normg = rowb[:, 0:1024]
        lng = rowb[:, 1024:2048]
        lnb = rowb[:, 2048:3072]
        bsrow = rowb[:, 3072:4096]
        S.dma("sp", d_cs, [], [r_sxbc], lambda: nc.sync.dma_start(out=sxbc[:], in_=sxbc_d[l]))
        S.dma("sp", d_cs, [], [r_sbc], lambda: nc.sync.dma_start(out=sbc[:], in_=sbc_d[l]))
        S.dma("sp", d_cs, [], [r_ssc], lambda: nc.sync.dma_start(out=ssc[:], in_=ssc_d[l]))
        S.dma("sp", d_cs, [], [r_sffn], lambda: nc.sync.dma_start(out=sffn[:], in_=sffn_d[l]))
        S.dma("sp", d_out, [], [], lambda: nc.sync.dma_start(out=sk_o[l, :, 0:124, :], in_=ck_d[l, :, 4:128, :]))
        S.dma("sp", d_out, [], [], lambda: nc.sync.dma_start(out=sv_o[l, :, 0:124, :], in_=cv_d[l, :, 4:128, :]))
        if DBG_S <= 0:
            return
        ada_layer(l, csb, r_csT, SB, mod_s, r_mods, gm_s, r_gms, 0)
        if DBG_S <= 1:
            return
        mod_norm_s(0)
        if DBG_S <= 2:
            return
        xsT, r_xsT = G[3], r_G[3]
        for bi in range(4):
            wv, wr = win(l, XBC0 + bi * 256)
            for cc in range(2):
                c = bi * 2 + cc
                ps, pr = proj_ws(wv, wr, cc * 128, 128, N)
                taps = [pp[:, l, PP_SCW + j * 8 + c:PP_SCW + j * 8 + c + 1] for j in range(4)]
                ac, rac = conv_s(128, ps, pr, taps, pp[:, l, PP_SCB + c:PP_SCB + c + 1], [r_pp], sxbc[:, c, :, :], r_sxbc, c % 2)
                S.op("act", [rac], [r_xsT], lambda ac=ac, c=c: nc.scalar.activation(out=xsT[:, c, 0:N], in_=ac[:, 0:N], func=AF.Silu))
        for bi in range(2):
            wv, wr = win(l, XBC0 + 1024 + bi * 256)
            for ee in range(4):
                e = bi * 4 + ee
                ps, pr = proj_ws(wv, wr, ee * 64, 64, N)
                taps = [pp64[:, l, j * 8 + e:j * 8 + e + 1] for j in range(4)]
                ac, rac = conv_s(64, ps, pr, taps, pp64[:, l, 32 + e:33 + e], [r_pp64], sbc[:, e, :, :], r_sbc, e % 2)
                S.op("act", [rac], [r_BCT], lambda ac=ac, e=e: nc.scalar.activation(out=BCT[:, e, 0:N], in_=ac[0:64, 0:N], func=AF.Silu))
        S.dma("sp", d_out, [r_sxbc], [], lambda: nc.sync.dma_start(out=sxbc_o[l], in_=sxbc[:]))
        S.dma("sp", d_out, [r_sbc], [], lambda: nc.sync.dma_start(out=sbc_o[l], in_=sbc[:]))
        if DBG_S <= 3:
            return
        wz = [win(l, i * 256) for i in range(4)]
        wdt = [win(l, DT0, 256)]
        yaT, r_yaT = G[1], r_G[1]
        for b in range(SB):
            S.dma("sp", d_st, [], [r_ST[l]], lambda b=b: nc.sync.dma_start(out=ST[:, l, :, :], in_=sssm_d[l, b]))
            S.op("act", [r_ST[l]], [r_STb1], lambda: nc.scalar.copy(out=STb[:, :, :], in_=ST[:, l, :, :]))
            ssd_chunk(l, 4 * b, 4, wz, wdt, xsT, r_xsT, yaT, r_yaT, normg)
            S.dma("sp", d_out, [r_ST[l]], [], lambda b=b: nc.sync.dma_start(out=sssm_o[l, b], in_=ST[:, l, :, :]))
        if DBG_S <= 4:
            return
        ybT, r_ybT = G[2], r_G[2]
        for bi in range(4):
            wb = [win(l, BCX0 + part * 1024 + bi * 256) for part in range(3)]
            for cc in range(2):
                c = bi * 2 + cc
                psB, prB = proj_ws(wb[0][0], wb[0][1], cc * 128, 128, N)
                psC, prC = proj_ws(wb[1][0], wb[1][1], cc * 128, 128, N)
                psX, prX = proj_ws(wb[2][0], wb[2][1], cc * 128, 128, N)
                st3 = stg[2][:, 0:SB * 6].rearrange("p (b w) -> p b w", b=SB)
                rst = r_stg[2]
                S.op("act", [prC], [r_acc[2]], lambda psC=psC: nc.scalar.copy(out=acc[2][:, 0:N], in_=psC[:, 0:N]))
                S.op("dve", [prX, r_acc[2]], [rst], lambda psX=psX, st3=st3: nc.vector.tensor_tensor(
                    out=st3[:, :, 2:6], in0=psX[:, 0:N].rearrange("p (b w) -> p b w", b=SB),
                    in1=acc[2][:, 0:N].rearrange("p (b w) -> p b w", b=SB), op=ALU.mult))
                S.op("dve", [r_ssc], [rst], lambda c=c, st3=st3: nc.vector.tensor_copy(out=st3[:, :, 0:2], in_=ssc[:, c, :, :]))
                S.op("dve", [rst], [r_ssc], lambda c=c, st3=st3: nc.vector.tensor_copy(out=ssc[:, c, :, :], in_=st3[:, :, 4:6]))
                ac, rac = acc[c % 2], r_acc[c % 2]
                ac3 = ac[:, 0:N].rearrange("p (b w) -> p b w", b=SB)
                taps = [pp[:, l, PP_SHW + j * 8 + c:PP_SHW + j * 8 + c + 1] for j in range(3)]
                S.op("dve", [rst, r_pp], [rac], lambda ac3=ac3, taps=taps, st3=st3: nc.vector.tensor_scalar(
                    out=ac3, in0=st3[:, :, 0:4], scalar1=taps[0], scalar2=None, op0=ALU.mult))
                for jj in (1, 2):
                    S.op("dve", [rst, rac, r_pp], [rac], lambda ac3=ac3, taps=taps, jj=jj, st3=st3: nc.vector.scalar_tensor_tensor(
                        out=ac3, in0=st3[:, :, jj:jj + 4], scalar=taps[jj], in1=ac3, op0=ALU.mult, op1=ALU.add))
                S.op("dve", [prB, rac], [r_ybT], lambda ac=ac, psB=psB, c=c: nc.vector.tensor_tensor(
                    out=ybT[:, c, 0:N], in0=psB[:, 0:N], in1=ac[:, 0:N], op=ALU.mult))
        S.dma("sp", d_out, [r_ssc], [], lambda: nc.sync.dma_start(out=ssc_o[l], in_=ssc[:]))
        if DBG_S <= 5:
            return
        qT, r_qT = G[5], r_G[5]
        wk = win(l, K0)
        wvv = win(l, V0)
        for hk in range(4):
            ps, pr = proj_ws(wk[0], wk[1], hk * 64, 64, N)
            ktmp, r_ktmp = tC, r_tC
            copy("act", ktmp[0:64, 0:N], ps[0:64, 0:N], [pr], [r_ktmp])
            ps2, pr2 = psum()
            S.op("pe", [r_ktmp, r_cst], [pr2], lambda ps2=ps2: nc.tensor.matmul(ps2[:, 0:N], lhsT=dupI[:, :], rhs=ktmp[0:64, 0:N], start=True, stop=True))
            copy("dve", kT[:, hk, 0:N], ps2[:, 0:N], [pr2], [r_kT])
        for bi in range(4):
            wv, wr = win(l, Q0 + bi * 256)
            for cc in range(2):
                c = bi * 2 + cc
                ps, pr = proj_ws(wv, wr, cc * 128, 128, N)
                S.op("act", [pr], [r_qT], lambda ps=ps, c=c: nc.scalar.activation(out=qT[:, c, 0:N], in_=ps[:, 0:N], func=AF.Identity, scale=0.125))
        ycT, r_ycT = G[3], r_G[3]
        for b in range(SB):
            attn_s(l, b, qT, r_qT, ycT, r_ycT, wk, wvv)
        if DBG_S <= 6:
            return
        ydT, r_ydT = G[4], r_G[4]
        S.dma("pool", d_in, [], [r_gmraw], lambda: nc.gpsimd.dma_start(out=gmw_raw[:], in_=gmw_d[l].rearrange("g t s -> t g s")))
        for g in range(8):
            ps, pr = psum()
            psb = ps[:, 0:64].bitcast(BF16)
            S.op("pe", [r_gmraw, r_cst], [pr], lambda psb=psb, g=g: nc.tensor.transpose(psb[:, 0:128], gmw_raw[:, g, :], identb[:]))
            S.op("dve", [pr, r_cst], [r_gmw], lambda psb=psb, g=g: nc.vector.tensor_tensor(
                out=gmwT[:, g, :], in0=psb[:, 0:128], in1=Umat, op=ALU.mult))
        for bi in range(4):
            wv, wr = win(l, UV0 + bi * 256)
            for cc in range(2):
                c = bi * 2 + cc
                ps, pr = proj_ws(wv, wr, cc * 128, 128, N)
                S.op("act", [pr], [r_ydT], lambda ps=ps, c=c: nc.scalar.activation(out=ydT[:, c, 0:N], in_=ps[:, 0:N], func=AF.Gelu_apprx_tanh))
        wv_ = [win(l, UV0 + 1024 + i * 256) for i in range(4)]
        for b in range(SB):
            def vout(tt, rtt, b=b):
                S.dma("sp", d_out, [rtt], [], lambda: nc.sync.dma_start(out=sgmv_o[l, b], in_=tt[0:4, :]))
            gmlp_chunk(l, 4 * b, 4, wv_, ydT, r_ydT, lng, lnb, bsrow, gmwT, vout)
        if DBG_S <= 7:
            return
        mT, r_mT = G[5], r_G[5]
        brs = ((G[1], r_G[1]), (G[2], r_G[2]), (G[3], r_G[3]), (G[4], r_G[4]))
        macc = (acc[0], acc[1])
        r_macc = (r_acc[0], r_acc[1])
        for bi in range(4):
            for i in range(4):
                wg = win(l, GT0 + i * 1024 + bi * 256)
                wb = wload(w_br_d[l, i, :, bi * 256:bi * 256 + 256].rearrange("(k p) c -> p k c", p=128), 8, 256)
                for cc in range(2):
                    c = bi * 2 + cc
                    psg, prg = proj_ws(wg[0], wg[1], cc * 128, 128, N)
                    psp, prp = proj_ws(wb[0], wb[1], cc * 128, 128, N, rhs=brs[i][0], rres=brs[i][1])
                    S.op("act", [prg], [r_stg[2]], lambda psg=psg: nc.scalar.activation(out=stg[2][:, 0:N], in_=psg[:, 0:N], func=AF.Sigmoid))
                    if i == 0:
                        S.op("dve", [prp, r_stg[2]], [r_macc[cc]], lambda psp=psp, cc=cc: nc.vector.tensor_tensor(
                            out=macc[cc][:, 0:N], in0=psp[:, 0:N], in1=stg[2][:, 0:N], op=ALU.mult))
                    else:
                        S.op("dve", [prp, r_stg[2]], [r_acc[2]], lambda psp=psp: nc.vector.tensor_tensor(
                            out=acc[2][:, 0:N], in0=psp[:, 0:N], in1=stg[2][:, 0:N], op=ALU.mult))
                        if i < 3:
                            S.op("dve", [r_macc[cc], r_acc[2]], [r_macc[cc]], lambda cc=cc: nc.vector.tensor_tensor(
                                out=macc[cc][:, 0:N], in0=macc[cc][:, 0:N], in1=acc[2][:, 0:N], op=ALU.add))
                        else:
                            S.op("dve", [r_macc[cc], r_acc[2]], [r_mT], lambda c=c, cc=cc: nc.vector.tensor_tensor(
                                out=mT[:, c, 0:N], in0=macc[cc][:, 0:N], in1=acc[2][:, 0:N], op=ALU.add))
        for bi in range(4):
            wv, wr = wload(wcols(w_o_d, l, bi * 256, 256), 8, 256)
            for cc in range(2):
                c = bi * 2 + cc
                ps, pr = proj_ws(wv, wr, cc * 128, 128, N, rhs=mT, rres=r_mT)
                resid_s(ps, pr, c, 2)
        if DBG_S <= 8:
            return
        mod_norm_s(1)
        gat = (G[1], G[2], G[3])
        r_gat = (r_G[1], r_G[2], r_G[3])
        for blk in range(11):
            wa = wload(wcols(w_up_d, l, blk * 256, 256), 8, 256)
            wg_ = wload(wcols(w_up_d, l, DFF + blk * 256, 256), 8, 256)
            for cc in range(2):
                i = blk * 2 + cc
                outs = []
                for which, (wv, wr) in enumerate((wa, wg_)):
                    ci = which * 22 + i
                    ps, pr = proj_ws(wv, wr, cc * 128, 128, N)
                    taps = [pp[:, l, PP_FW + j * 44 + ci:PP_FW + j * 44 + ci + 1] for j in range(3)]
                    ac, rac = conv_s(128, ps, pr, taps, pp[:, l, PP_FB + ci:PP_FB + ci + 1], [r_pp], sffn[:, ci, :, :], r_sffn, which)
                    outs.append((ac, rac))
                S.op("act", [outs[0][1]], [r_acc[2]], lambda a=outs[0][0]: nc.scalar.activation(out=acc[2][:, 0:N], in_=a[:, 0:N], func=AF.Silu))
                S.op("dve", [r_acc[2], outs[1][1]], [r_gat[i // 8]], lambda g_=outs[1][0], i=i: nc.vector.tensor_tensor(
                    out=gat[i // 8][:, i % 8, 0:N], in0=acc[2][:, 0:N], in1=g_[:, 0:N], op=ALU.mult))
        S.dma("sp", d_out, [r_sffn], [], lambda: nc.sync.dma_start(out=sffn_o[l], in_=sffn[:]))
        for c in range(8):
            wh = [wload(w_dn_d[l, hh * 1408:(hh + 1) * 1408, c * 128:(c + 1) * 128].rearrange("(k p) c -> p k c", p=128), 11, 128) for hh in range(2)]
            ps, pr = psum()

            def f(wh=wh, ps=ps):
                for k in range(22):
                    last = nc.tensor.matmul(ps[:, 0:N], lhsT=wh[k // 11][0][:, k % 11, :], rhs=gat[k // 8][:, k % 8, 0:N], start=(k == 0), stop=(k == 21))
                return last
            S.op("pe", [wh[0][1], wh[1][1]] + list(r_gat), [pr], f)
            resid_s(ps, pr, c, 5)

    for ti in range(NT):
        S.dma("sp", d_x, [], [r_x], lambda ti=ti: nc.sync.dma_start(
            out=xT[:], in_=xT_d[:, ti * TT:(ti + 1) * TT].rearrange("(c p) t -> p c t", p=128)))
        for l in range(L):
            prompt_tile_layer(ti, l)
        ps, pr = psum()
        sq = G[5]
        S.op("act", [r_x], [r_G[5]], lambda: nc.scalar.activation(out=sq[:, :, :], in_=xT[:, :, :], func=AF.Square))

        def ff(ps=ps):
            for k in range(8):
                last = nc.tensor.matmul(ps[:, 0:TT], lhsT=onesb[:], rhs=sq[:, k, :], start=(k == 0), stop=(k == 7))
            return last
        S.op("pe", [r_G[5], r_cst], [pr], ff)
        rstd_of(ps[:, 0:TT], 128, acc[2][:, 0:TT], [pr, r_cst], [r_acc[2]], 1.0 / D)
        for c in range(8):
            yo, ryo = acc[c % 2], r_acc[c % 2]
            S.op("dve", [r_x, r_pp, r_acc[2]], [ryo], lambda c=c, yo=yo: nc.vector.scalar_tensor_tensor(
                out=yo[:, 0:TT], in0=xT[:, c, :], scalar=pp[:, 0, PP_GFIN + c:PP_GFIN + c + 1], in1=acc[2][:, 0:TT], op0=ALU.mult, op1=ALU.mult))
            S.dma("sp", d_out, [ryo], [], lambda ti=ti, yo=yo, c=c: nc.sync.dma_start(
                out=yT_d[c * 128:(c + 1) * 128, ti * TT:(ti + 1) * TT], in_=yo[:, 0:TT]))
    if SB:
        NS = 4 * SB
        S.dma("sp", d_par, [], [r_csT], lambda: nc.sync.dma_start(out=csT[:], in_=csT_d))
        S.op("act", [r_csT], [r_csT], lambda: nc.scalar.activation(out=csb[:], in_=csT[:], func=AF.Silu))
        S.dma("sp", d_x, [], [r_x], lambda: nc.sync.dma_start(out=xT[:, :, 0:NS], in_=xsT_d.rearrange("(c p) t -> p c t", p=128)))
        for l in range(L):
            sample_tile_layer(l)
        ps, pr = psum()
        sq = G[5]
        S.op("act", [r_x], [r_G[5]], lambda: nc.scalar.activation(out=sq[:, :, 0:NS], in_=xT[:, :, 0:NS], func=AF.Square))

        def ffs(ps=ps):
            for k in range(8):
                last = nc.tensor.matmul(ps[:, 0:NS], lhsT=onesb[:], rhs=sq[:, k, 0:NS], start=(k == 0), stop=(k == 7))
            return last
        S.op("pe", [r_G[5], r_cst], [pr], ffs)
        rstd_of(ps[:, 0:NS], 128, acc[2][:, 0:NS], [pr, r_cst], [r_acc[2]], 1.0 / D)
        for c in range(8):
            yo, ryo = acc[c % 2], r_acc[c % 2]
            S.op("dve", [r_x, r_pp, r_acc[2]], [ryo], lambda c=c, yo=yo: nc.vector.scalar_tensor_tensor(
                out=yo[:, 0:NS], in0=xT[:, c, 0:NS], scalar=pp[:, 0, PP_GFIN + c:PP_GFIN + c + 1], in1=acc[2][:, 0:NS], op0=ALU.mult, op1=ALU.mult))
            S.dma("sp", d_out, [ryo], [], lambda yo=yo, c=c: nc.sync.dma_start(out=ysT_d[c * 128:(c + 1) * 128, :], in_=yo[:, 0:NS]))
    S.finish()
    return nc


def _consts():
    cst = np.zeros((128, 772), np.float32)
    jj = np.arange(132)[None, :]
    ds = 128 + np.arange(128)[:, None] - jj
    cst[:, 640:772] = np.where((ds >= 0) & (ds <= 128), ds, 1e6)
    cst[:, 0:128] = np.eye(128, dtype=np.float32)
    s_ = np.arange(128)[:, None]
    t_ = np.arange(128)[None, :]
    cst[:, 128:256] = (s_ <= t_).astype(np.float32)
    cst[:, 256:384] = np.where(s_ > t_, NEG, 0.0)
    kpos = np.arange(256)[None, :] - 128
    dist = s_ - kpos
    cst[:, 384:640] = np.where((dist >= 0) & (dist <= 128), dist, 1e6)
    return cst


def _pp(w, L):
    pp = np.zeros((128, L, NPP), np.float32)
    pp64 = np.zeros((64, L, 40), np.float32)
    for l in range(L):
        pp[:, l, PP_BADA:PP_BADA + 48] = w["b_ada"][l].reshape(48, 128).T
        pp[:, l, PP_GMIX:PP_GMIX + 8] = w["g_norm_mix"][l].reshape(8, 128).T
        pp[:, l, PP_GFFN:PP_GFFN + 8] = w["g_norm_ffn"][l].reshape(8, 128).T
        for j in range(4):
            pp[:, l, PP_SCW + j * 8:PP_SCW + j * 8 + 8] = w["ssd_conv_w"][l][j, :1024].reshape(8, 128).T
            pp64[:, l, j * 8:j * 8 + 8] = w["ssd_conv_w"][l][j, 1024:].reshape(8, 64).T
        pp[:, l, PP_SCB:PP_SCB + 8] = w["ssd_conv_b"][l][:1024].reshape(8, 128).T
        pp64[:, l, 32:40] = w["ssd_conv_b"][l][1024:].reshape(8, 64).T
        for j in range(3):
            pp[:, l, PP_SHW + j * 8:PP_SHW + j * 8 + 8] = w["sc_conv_w"][l][j].reshape(8, 128).T
            pp[:, l, PP_FW + j * 44:PP_FW + j * 44 + 44] = w["ffn_conv_w"][l][j].reshape(44, 128).T
        pp[:, l, PP_FB:PP_FB + 44] = w["ffn_conv_b"][l].reshape(44, 128).T
        pp[:, l, PP_GFIN:PP_GFIN + 8] = w["g_final"].reshape(8, 128).T
    return pp, pp64


def _rows(w, L):
    rows = np.zeros((128, L, 64), np.float32)
    rowb = np.zeros((L, 128, 4096), np.float32)
    for l in range(L):
        rows[:, l, 0:16] = w["ssd_dt_bias"][l][None]
        rows[:, l, 16:32] = w["ssd_a_log"][l][None]
        rows[:, l, 32:48] = w["ssd_d"][l][None]
        rows[:, l, 48:64] = w["attn_sinks"][l][None]
        rowb[l, :, 0:1024] = w["ssd_norm_g"][l][None]
        rowb[l, :, 1024:2048] = w["gm_ln_g"][l][None]
        rowb[l, :, 2048:3072] = w["gm_ln_b"][l][None]
        rowb[l, :, 3072:4096] = w["gm_b_s"][l].reshape(1024)[None]
    return rows, rowb


def shared_maps(w, L):
    pp, pp64 = _pp(w, L)
    rows, rowb = _rows(w, L)
    m = {"pp": pp, "pp64": pp64, "rows": rows, "rowb": rowb, "cst": _consts()}
    for k in ("w_ada", "w_in", "w_branch", "w_o", "ffn_w_up", "ffn_w_down", "gm_w_s"):
        m[k] = np.ascontiguousarray(w[k][:L], dtype=np.float32)
    return m


def core_map(shared, xp_b, cp_b):
    m = dict(shared)
    m["xT"] = np.ascontiguousarray(xp_b.T)
    m["cT"] = np.ascontiguousarray(cp_b.reshape(8, 128).T)[:, :, None].copy()
    return m


def unpack_prompt(r, L):
    o = {}
    o["y"] = np.ascontiguousarray(r["yT"].T)
    o["ssm"] = np.ascontiguousarray(r["p_ssm"].reshape(L, 64, 4, 4, 64).transpose(0, 2, 3, 4, 1)).reshape(L, 16, 64, 64)
    xs = r["p_xbc"][..., 0:3].transpose(0, 3, 2, 1).reshape(L, 3, 1024)
    bc = r["p_bc"][..., 0:3].transpose(0, 3, 2, 1).reshape(L, 3, 512)
    o["ssd_conv"] = np.concatenate([xs, bc], axis=2)
    o["sc_conv"] = r["p_sc"].transpose(0, 3, 2, 1).reshape(L, 2, 1024)
    o["k"] = r["p_k"].reshape(L, 128, 4, 64)
    o["v"] = r["p_v"].reshape(L, 128, 4, 64)
    o["ffn_conv"] = r["p_ffn"][:, :, 0:44, :].transpose(0, 3, 2, 1).reshape(L, 2, 5632)
    return o


def sample_map(m, inp, bs, L):
    SBn = bs.stop - bs.start
    m["xsT"] = np.ascontiguousarray(inp["x_sample"][bs].reshape(4 * SBn, D).T)
    m["csT"] = np.ascontiguousarray(inp["c_sample"][bs].reshape(SBn, 8, 128).transpose(2, 1, 0))
    st = inp["state_ssm"][:L, bs]
    m["s_ssm_in"] = np.ascontiguousarray(st.reshape(L, SBn, 4, 4, 64, 64).transpose(0, 1, 5, 2, 3, 4)).reshape(L, SBn, 64, 4, 256)
    sc = inp["state_ssd_conv"][:L, bs]
    m["s_xbc_in"] = np.ascontiguousarray(sc[..., :1024].reshape(L, SBn, 3, 8, 128).transpose(0, 4, 3, 1, 2))
    m["s_bc_in"] = np.ascontiguousarray(sc[..., 1024:].reshape(L, SBn, 3, 8, 64).transpose(0, 4, 3, 1, 2))
    m["s_sc_in"] = np.ascontiguousarray(inp["state_sc_conv"][:L, bs].reshape(L, SBn, 2, 8, 128).transpose(0, 4, 3, 1, 2))
    m["s_ffn_in"] = np.ascontiguousarray(inp["state_ffn_conv"][:L, bs].reshape(L, SBn, 2, 44, 128).transpose(0, 4, 3, 1, 2))
    ck = inp["cache_k"][:L, bs]
    kt = ck.transpose(0, 1, 4, 3, 2)
    m["ckT"] = np.ascontiguousarray(np.concatenate([kt, kt], axis=2))
    m["ck"] = np.ascontiguousarray(ck.reshape(L, SBn, 128, 256))
    m["cv"] = np.ascontiguousarray(inp["cache_v"][:L, bs].reshape(L, SBn, 128, 256))
    return m


def unpack_sample(r, L, SBn):
    o = {}
    o["y"] = np.ascontiguousarray(r["ysT"].T).reshape(SBn, 4, D)
    o["ssm"] = np.ascontiguousarray(r["s_ssm_o"].reshape(L, SBn, 64, 4, 4, 64).transpose(0, 1, 3, 4, 5, 2)).reshape(L, SBn, 16, 64, 64)
    xs = r["s_xbc_o"].transpose(0, 3, 4, 2, 1).reshape(L, SBn, 3, 1024)
    bc = r["s_bc_o"].transpose(0, 3, 4, 2, 1).reshape(L, SBn, 3, 512)
    o["ssd_conv"] = np.concatenate([xs, bc], axis=3)
    o["sc_conv"] = r["s_sc_o"].transpose(0, 3, 4, 2, 1).reshape(L, SBn, 2, 1024)
    o["k"] = r["s_k_o"].reshape(L, SBn, 128, 4, 64)
    o["v"] = r["s_v_o"].reshape(L, SBn, 128, 4, 64)
    o["ffn_conv"] = r["s_ffn_o"].transpose(0, 3, 4, 2, 1).reshape(L, SBn, 2, 5632)
    o["gm_v"] = r["s_gmv_o"].reshape(L, SBn, 4, 1024)
    return o


def kernel(**inputs):
    inp = {k: np.asarray(v) for k, v in inputs.items()}
    L = 4
    B = inp["x_prompt"].shape[0]
    SEQ = inp["x_prompt"].shape[1]
    NT = SEQ // TT
    DB = inp["x_sample"].shape[0]
    SBn = DB // 8
    shared = shared_maps(inp, L)
    nc = build(NT, L, SBn)
    in_maps = []
    for core in range(8):
        b = core % B
        m = core_map(shared, inp["x_prompt"][b], inp["c_prompt"][b])
        sample_map(m, inp, slice(core * SBn, (core + 1) * SBn), L)
        in_maps.append(m)
    res = run_bass_kernel_spmd(nc, in_maps, core_ids=list(range(8)))
    rs = [{k: np.asarray(v) for k, v in r.items()} for r in res.results]
    po = [unpack_prompt(rs[b], L) for b in range(B)]
    so = [unpack_sample(rs[c], L, SBn) for c in range(8)]
    f32 = np.float32
    y_prompt = np.stack([p["y"] for p in po], 0).astype(f32)
    y_sample = np.concatenate([s_["y"] for s_ in so], 0).astype(f32)

    def pst(k):
        return np.ascontiguousarray(np.stack([p[k] for p in po], 1)).astype(f32)

    def sst(k):
        return np.ascontiguousarray(np.concatenate([s_[k] for s_ in so], 1)).astype(f32)
    return (y_prompt, y_sample, pst("ssm"), pst("ssd_conv"), pst("sc_conv"), pst("k"), pst("v"), pst("ffn_conv"),
            sst("ssm"), sst("ssd_conv"), sst("sc_conv"), sst("k"), sst("v"), sst("ffn_conv"), sst("gm_v"))
```

```python
import os
import numpy as np
import concourse.bass as bass
import concourse.mybir as mybir
from concourse.bass_utils import run_bass_kernel_spmd

F32 = mybir.dt.float32
BF16 = mybir.dt.bfloat16
AF = mybir.ActivationFunctionType
ALU = mybir.AluOpType
AX = mybir.AxisListType

D = 1024
TT = 512
NCH = TT // 128
WBC = 256
SLOPES = [2.0 ** (-8.0 * (h + 1) / 16) for h in range(16)]
EPS = 1e-6
XBC0, DT0, BCX0, Q0, K0, V0, UV0, GT0, DIN = 1024, 2560, 2576, 5648, 6672, 6928, 7184, 9232, 13328
DFF = 2816
PP_BADA, PP_GMIX, PP_GFFN, PP_SCW, PP_SCB, PP_SHW, PP_FW, PP_FB, PP_GFIN = 0, 48, 56, 64, 96, 104, 128, 260, 304
NPP = 312
NEG = -30000.0
DBG_STOP = int(os.environ.get('DBG_STOP', '99'))
DBG_ATT = int(os.environ.get('DBG_ATT', '99'))
DBG_ATTP = int(os.environ.get('DBG_ATTP', '99'))
DBG_S = int(os.environ.get('DBG_S', '99'))


class Res:
    __slots__ = ("name", "w", "r", "excl")

    def __init__(self, name, excl=False):
        self.name = name
        self.w = None
        self.r = {}
        self.excl = excl


class DSem:
    __slots__ = ("sem", "tot", "key", "keep")

    def __init__(self, nc, name):
        self.sem = nc.alloc_semaphore(name)
        self.tot = 0
        self.key = name
        self.keep = False


class Sched:
    def __init__(self, nc):
        self.nc = nc
        self.eng = {}
        self.seen = {}
        self.dsems = []
        self.dkeys = {}
        for name, h in (("pe", nc.tensor), ("act", nc.scalar), ("dve", nc.vector),
                        ("pool", nc.gpsimd), ("sp", nc.sync)):
            self.eng[name] = [h, nc.alloc_semaphore("s_" + name), 0]

    def dsem(self, name):
        d = DSem(self.nc, "d_" + name)
        self.dsems.append(d)
        self.dkeys[d.key] = d
        return d

    def _waits(self, eng, reads, writes):
        need = {}

        def add(ent):
            key, sem, val = ent
            if key in self.dkeys:
                val = self.dkeys[key].tot
            if key not in need or need[key][1] < val:
                need[key] = (sem, val)

        for r in reads:
            if r.w is not None:
                add(r.w)
            if r.excl:
                for key, (sem, val) in r.r.items():
                    if key != eng:
                        add((key, sem, val))
        for w in writes:
            if w.w is not None:
                add(w.w)
            for key, (sem, val) in w.r.items():
                add((key, sem, val))
        h = self.eng[eng][0]
        for key, (sem, val) in need.items():
            if self.seen.get((eng, key), 0) < val:
                h.wait_ge(sem, val)
                self.seen[(eng, key)] = val

    def op(self, eng, reads, writes, fn):
        E = self.eng[eng]
        self._waits(eng, reads, writes)
        inst = fn()
        E[2] += 1
        inst.then_inc(E[1], 1)
        for r in reads:
            r.r[eng] = (E[1], E[2])
        for w in writes:
            w.w = (eng, E[1], E[2])
            w.r = {}
        return inst

    def _auto_dsem(self, reads, writes):
        if writes:
            name = "ld_" + writes[0].name
        elif reads:
            name = "st_" + reads[0].name
        else:
            name = "dd"
        d = self.dkeys.get("d_" + name)
        if d is None:
            d = self.dsem(name)
        return d

    def dma(self, q, ds, reads, writes, fn, extra=()):
        if not getattr(ds, "keep", False):
            ds = self._auto_dsem(reads, writes)
        self._waits(q, reads, writes)
        for (sem_, val_, key_) in extra:
            if self.seen.get((q, key_), 0) < val_:
                self.eng[q][0].wait_ge(sem_, val_)
                self.seen[(q, key_)] = val_
        inst = fn()
        ds.tot += 16
        inst.then_inc(ds.sem, 16)
        for r in reads:
            r.r[ds.key] = (ds.sem, ds.tot)
        for w in writes:
            w.w = (ds.key, ds.sem, ds.tot)
            w.r = {}
        return inst

    def finish(self, eng="sp"):
        h = self.eng[eng][0]
        for d in self.dsems:
            if d.tot > 0:
                h.wait_ge(d.sem, d.tot)


def build(NT, L, SB, dbg=False):
    nc = bass.Bass("TRN2", target_bir_lowering=False)
    S = Sched(nc)
    NTOK = NT * TT

    def din(name, shape):
        return nc.dram_tensor(name, list(shape), F32, kind="ExternalInput").ap()

    def dout(name, shape):
        return nc.dram_tensor(name, list(shape), F32, kind="ExternalOutput").ap()

    xT_d = din("xT", [D, NTOK])
    cT_d = din("cT", [128, 8, 1])
    w_ada_d = din("w_ada", [L, D, 6 * D])
    w_in_d = din("w_in", [L, D, DIN])
    w_br_d = din("w_branch", [L, 4, D, D])
    w_o_d = din("w_o", [L, D, D])
    w_up_d = din("ffn_w_up", [L, D, 2 * DFF])
    w_dn_d = din("ffn_w_down", [L, DFF, D])
    pp_d = din("pp", [128, L, NPP])
    pp64_d = din("pp64", [64, L, 40])
    rows_d = din("rows", [128, L, 64])
    rowb_d = din("rowb", [L, 128, 4096])
    gmw_d = din("gm_w_s", [L, 8, 128, 128])
    cst_d = din("cst", [128, 772])

    yT_d = dout("yT", [D, NTOK])
    pssm_d = dout("p_ssm", [L, 64, 4, 256])
    pxbc_d = dout("p_xbc", [L, 128, 8, 4])
    pbc_d = dout("p_bc", [L, 64, 8, 4])
    psc_d = dout("p_sc", [L, 128, 8, 2])
    pk_d = dout("p_k", [L, 128, 256])
    pv_d = dout("p_v", [L, 128, 256])
    pffn_d = dout("p_ffn", [L, 128, 48, 2])

    if SB:
        xsT_d = din("xsT", [D, 4 * SB])
        csT_d = din("csT", [128, 8, SB])
        sssm_d = din("s_ssm_in", [L, SB, 64, 4, 256])
        sxbc_d = din("s_xbc_in", [L, 128, 8, SB, 3])
        sbc_d = din("s_bc_in", [L, 64, 8, SB, 3])
        ssc_d = din("s_sc_in", [L, 128, 8, SB, 2])
        sffn_d = din("s_ffn_in", [L, 128, 44, SB, 2])
        ckT_d = din("ckT", [L, SB, 128, 4, 128])
        ck_d = din("ck", [L, SB, 128, 256])
        cv_d = din("cv", [L, SB, 128, 256])
        ysT_d = dout("ysT", [D, 4 * SB])
        sssm_o = dout("s_ssm_o", [L, SB, 64, 4, 256])
        sxbc_o = dout("s_xbc_o", [L, 128, 8, SB, 3])
        sbc_o = dout("s_bc_o", [L, 64, 8, SB, 3])
        ssc_o = dout("s_sc_o", [L, 128, 8, SB, 2])
        sffn_o = dout("s_ffn_o", [L, 128, 44, SB, 2])
        sk_o = dout("s_k_o", [L, SB, 128, 256])
        sv_o = dout("s_v_o", [L, SB, 128, 256])
        sgmv_o = dout("s_gmv_o", [L, SB, 4, 1024])

    def sb(name, shape, dt=F32):
        return nc.alloc_sbuf_tensor(name, list(shape), dt)

    xT = sb("xTt", [128, 8, TT]); r_x = Res("x")
    mod = sb("mod", [128, L, 48, 1]); r_mod = Res("mod")
    gm = sb("gm", [128, L, 2, 8, 1]); r_gm = Res("gm")
    pp = sb("ppt", [128, L, NPP]); r_pp = Res("pp")
    pp64 = sb("pp64t", [64, L, 40]); r_pp64 = Res("pp64")
    rows = sb("rowst", [128, L, 64]); r_rows = Res("rows")
    rowb = sb("rowbt", [128, 3072]); r_rowb = Res("rowb")
    cst = sb("cstt", [128, 772]); r_cst = Res("cst")
    identb = sb("identb", [128, 128], BF16)
    onesb = sb("onesb", [128, 128], BF16)
    ident = cst[:, 0:128]
    Umat = cst[:, 128:256]
    NEGM = cst[:, 256:384]
    distm = cst[:, 384:640]
    dists = cst[:, 640:772]
    gmwT = sb("gmwT", [128, 8, 128], BF16); r_gmw = Res("gmwT")
    gmw_raw = sb("gmw_raw", [128, 8, 128], BF16); r_gmraw = Res("gmw_raw")
    ST = sb("ST", [64, L, 4, 256]); r_ST = [Res("ST%d" % l) for l in range(L)]
    STb = sb("STb", [64, 4, 256], BF16); r_STb1 = Res("STb"); r_STb = [r_STb1] * L
    cx = sb("cx", [128, L, 8, 4]); r_cx = [Res("cx%d" % l) for l in range(L)]
    cbc = sb("cbc", [64, L, 8, 4]); r_cbc = [Res("cbc%d" % l) for l in range(L)]
    csc = sb("csc", [128, L, 8, 2]); r_csc = [Res("csc%d" % l) for l in range(L)]
    cff = sb("cff", [128, L, 48, 2]); r_cff = [Res("cff%d" % l) for l in range(L)]
    kprev = sb("kprev", [128, L, 4, 128], BF16); r_kprev = [Res("kp%d" % l) for l in range(L)]
    vprev = sb("vprev", [128, L, 256], BF16); r_vprev = [Res("vp%d" % l) for l in range(L)]
    G = [sb("G%d" % i, [128, 8, TT], BF16) for i in range(6)]
    r_G = [Res("G%d" % i) for i in range(6)]
    hB, r_h = G[0], r_G[0]
    stg = [sb("stg%d" % i, [128, TT + 4]) for i in range(3)]
    r_stg = [Res("stg%d" % i) for i in range(3)]
    acc = [sb("acc%d" % i, [128, TT]) for i in range(3)]
    r_acc = [Res("acc%d" % i) for i in range(3)]
    BCT = G[2][0:64, :, :]; r_BCT = r_G[2]
    kT = sb("kT", [128, 4, TT], BF16); r_kT = Res("kT")
    vtok = sb("vtok", [128, NCH, 256], BF16); r_vtok = Res("vtok")
    tA = sb("tA", [128, 1024]); r_tA = Res("tA")
    tB = sb("tB", [128, 1024]); r_tB = Res("tB")
    tC = sb("tC", [128, 1024], BF16); r_tC = Res("tC")
    tD = sb("tD", [128, 1024], BF16); r_tD = Res("tD")
    tE = sb("tE", [128, 1024], BF16); r_tE = Res("tE")
    Eh = sb("Eh", [128, 16, 128], BF16); r_Eh = Res("Eh")
    Mh, r_Mh = Eh, r_Eh
    CBs = sb("CBs", [128, 4, 128], BF16); r_CBs = Res("CBs")
    Btok = sb("Btok", [128, 4, 64], BF16); r_Btok = Res("Btok")
    sm = sb("sm", [128, 256]); r_sm = Res("sm")
    ktok, r_ktok = tB, r_tB

    d_in = d_par = d_rowb = d_out = d_x = None

    PS = [nc.alloc_psum_tensor("ps%d" % i, [128, 512], F32) for i in range(8)]
    r_PS = [Res("ps%d" % i, excl=True) for i in range(8)]
    psi = [0]

    def psum():
        i = psi[0]
        psi[0] = (i + 1) % 6
        return PS[i], r_PS[i]

    NSLOT = 7
    WS = [sb("ws%d" % i, [128, 2048], BF16) for i in range(NSLOT)]
    r_WS = [Res("ws%d" % i) for i in range(NSLOT)]
    d_WS = [S.dsem("ws%d" % i) for i in range(NSLOT)]
    for d_ in d_WS:
        d_.keep = True
    wsi = [0]

    NSCR = 111 * L + 2
    wscr = nc.dram_tensor("wscr", [NSCR, 128, 2048], BF16, kind="Internal").ap()
    scr = {}

    def wload(src, K, C, key=None):
        i = wsi[0]
        wsi[0] = (i + 1) % NSLOT
        flat = WS[i][:, 0:K * C]
        view = flat.rearrange("p (k c) -> p k c", k=K)
        if key is not None and key in scr:
            idx, ent = scr[key]
            S.dma("sp", d_WS[i], [], [r_WS[i]], lambda: nc.sync.dma_start(out=flat, in_=wscr[idx, :, 0:K * C]), extra=[ent])
            return view, r_WS[i]
        S.dma("pool", d_WS[i], [], [r_WS[i]], lambda: nc.gpsimd.dma_start(out=view, in_=src))
        if key is not None:
            idx = len(scr)
            assert idx < NSCR
            S.dma("sp", None, [r_WS[i]], [], lambda: nc.sync.dma_start(out=wscr[idx, :, 0:K * C], in_=flat))
            d = S.dkeys["d_st_ws%d" % i]
            scr[key] = (idx, (d.sem, d.tot, d.key))
        return view, r_WS[i]

    def wcols(wd, l, c0, C):
        return wd[l, :, c0:c0 + C].rearrange("(k p) c -> p k c", p=128)

    ev = [0]

    def eng2():
        ev[0] ^= 1
        return "act" if ev[0] else "dve"

    def copy(eng, out, in_, reads, writes):
        if eng == "act":
            return S.op("act", reads, writes, lambda: nc.scalar.copy(out=out, in_=in_))
        return S.op(eng, reads, writes, lambda: (nc.vector if eng == "dve" else nc.gpsimd).tensor_copy(out=out, in_=in_))

    S.dma("sp", d_par, [], [r_pp], lambda: nc.sync.dma_start(out=pp[:], in_=pp_d))
    S.dma("sp", d_par, [], [r_pp64], lambda: nc.sync.dma_start(out=pp64[:], in_=pp64_d))
    S.dma("sp", d_par, [], [r_rows], lambda: nc.sync.dma_start(out=rows[:], in_=rows_d))
    S.dma("sp", d_par, [], [r_cst], lambda: nc.sync.dma_start(out=cst[:], in_=cst_d))
    S.op("dve", [r_cst], [r_cst], lambda: nc.vector.tensor_copy(out=identb[:], in_=ident))
    S.op("dve", [], [r_cst], lambda: nc.vector.memset(onesb[:], 1.0))
    dupI = sb("dupI", [64, 128], BF16)
    S.op("dve", [r_cst], [r_cst], lambda: nc.vector.tensor_copy(out=dupI[:, 0:64], in_=ident[0:64, 0:64]))
    S.op("dve", [r_cst], [r_cst], lambda: nc.vector.tensor_copy(out=dupI[:, 64:128], in_=ident[0:64, 0:64]))
    ones32 = sb("ones32", [128, 128])
    S.op("dve", [], [r_cst], lambda: nc.vector.memset(ones32[:], 1.0))
    S.op("act", [r_rows], [r_rows], lambda: nc.scalar.activation(out=rows[:, :, 16:32], in_=rows[:, :, 16:32], func=AF.Exp))
    S.op("dve", [r_rows], [r_rows], lambda: nc.vector.tensor_scalar(out=rows[:, :, 16:32], in0=rows[:, :, 16:32], scalar1=-1.0, scalar2=None, op0=ALU.mult))
    for t_, r_ in ((ST, r_ST), (cx, r_cx), (cbc, r_cbc), (csc, r_csc), (cff, r_cff)):
        S.op("dve", [], list(r_), lambda t_=t_: nc.vector.memset(t_[:], 0.0))

    NCc = 1
    cTt = sb("cTt", [128, 8, NCc]); r_cT = Res("cT")
    cTb = sb("cTb", [128, 8, NCc], BF16)
    S.dma("sp", d_par, [], [r_cT], lambda: nc.sync.dma_start(out=cTt[:], in_=cT_d))
    S.op("act", [r_cT], [r_cT], lambda: nc.scalar.activation(out=cTb[:], in_=cTt[:], func=AF.Silu))

    def ada_layer(l, cb, r_cb, ncol, mod_t, r_mod_t, gm_t, r_gm_t, lidx):
        for blk in range(24):
            wv, wr = wload(wcols(w_ada_d, l, blk * 256, 256), 8, 256)
            ps, pr = psum()

            def f(wv=wv, ps=ps):
                for i in range(2):
                    for k in range(8):
                        last = nc.tensor.matmul(ps[:, i * ncol:(i + 1) * ncol], lhsT=wv[:, k, i * 128:(i + 1) * 128],
                                                rhs=cb[:, k, :], start=(k == 0), stop=(k == 7))
                return last
            S.op("pe", [wr, r_cb], [pr], f)
            S.op("dve", [pr, r_pp], [r_mod_t], lambda ps=ps, blk=blk: nc.vector.tensor_tensor(
                out=mod_t[:, lidx, blk * 2:blk * 2 + 2, :], in0=ps[:, 0:2 * ncol].rearrange("p (i j) -> p i j", i=2),
                in1=pp[:, l, PP_BADA + blk * 2:PP_BADA + blk * 2 + 2].unsqueeze(2).to_broadcast([128, 2, ncol]), op=ALU.add))
        for which, (sc_i, gcol) in enumerate(((1, PP_GMIX), (4, PP_GFFN))):
            S.op("dve", [r_mod_t, r_pp], [r_gm_t], lambda which=which, sc_i=sc_i, gcol=gcol: nc.vector.scalar_tensor_tensor(
                out=gm_t[:, lidx, which, :, :], in0=mod_t[:, lidx, sc_i * 8:sc_i * 8 + 8, :], scalar=1.0,
                in1=pp[:, l, gcol:gcol + 8].unsqueeze(2).to_broadcast([128, 8, ncol]), op0=ALU.add, op1=ALU.mult))

    for l in range(L):
        ada_layer(l, cTb, r_cT, 1, mod, r_mod, gm, r_gm, l)

    epsc = sb("epsc", [128, 1])
    S.op("dve", [], [r_cst], lambda: nc.vector.memset(epsc[:], EPS))

    def rstd_of(ss_ap, n, out_ap, reads, writes, scale):
        S.op("act", reads, writes, lambda: nc.scalar.activation(out=out_ap, in_=ss_ap, func=AF.Ln, bias=epsc[0:n, :], scale=scale))
        S.op("act", writes, writes, lambda: nc.scalar.activation(out=out_ap, in_=out_ap, func=AF.Exp, scale=-0.5))

    def mod_norm(l, which, N):
        sh_i = 0 if which == 0 else 3
        ps, pr = psum()
        sq = G[5]
        S.op("act", [r_x], [r_G[5]], lambda: nc.scalar.activation(out=sq[:, :, 0:N], in_=xT[:, :, 0:N], func=AF.Square))

        def f():
            for k in range(8):
                last = nc.tensor.matmul(ps[:, 0:N], lhsT=onesb[:], rhs=sq[:, k, 0:N], start=(k == 0), stop=(k == 7))
            return last
        S.op("pe", [r_G[5], r_cst], [pr], f)
        rs = acc[2]
        rstd_of(ps[:, 0:N], 128, rs[:, 0:N], [pr, r_cst], [r_acc[2]], 1.0 / D)
        for c in range(8):
            S.op("dve", [r_x, r_gm, r_acc[2]], [r_acc[c % 2]], lambda c=c: nc.vector.scalar_tensor_tensor(
                out=acc[c % 2][:, 0:N], in0=xT[:, c, 0:N], scalar=gm[:, l, which, c, 0:1], in1=rs[:, 0:N],
                op0=ALU.mult, op1=ALU.mult))
            S.op("act", [r_acc[c % 2], r_mod], [r_h], lambda c=c: nc.scalar.activation(
                out=hB[:, c, 0:N], in_=acc[c % 2][:, 0:N], func=AF.Identity, bias=mod[:, l, sh_i * 8 + c, 0:1], scale=1.0))

    def proj_ws(wv, wr, col, M, N, rhs=None, rres=None, nk=8):
        rhs = hB if rhs is None else rhs
        rres = r_h if rres is None else rres
        ps, pr = psum()

        def f():
            for k in range(nk):
                last = nc.tensor.matmul(ps[0:M, 0:N], lhsT=wv[:, k, col:col + M], rhs=rhs[:, k, 0:N],
                                        start=(k == 0), stop=(k == nk - 1))
            return last
        S.op("pe", [wr, rres], [pr], f)
        return ps, pr

    def proj_as(wlist, C, t0, T):
        ps, pr = psum()

        def f():
            for bi, (wv, wr) in enumerate(wlist):
                cw = min(256, C - bi * 256)
                for k in range(8):
                    last = nc.tensor.matmul(ps[0:T, bi * 256:bi * 256 + cw], lhsT=hB[:, k, t0:t0 + T], rhs=wv[:, k, 0:cw],
                                            start=(k == 0), stop=(k == 7))
            return last
        S.op("pe", [w[1] for w in wlist] + [r_h], [pr], f)
        return ps, pr

    def conv_fm(P, ps, pr, N, carry_ap, r_carry, taps, bias_ap, preads, si):
        K = len(taps)
        CW = K - 1
        st, rst = stg[si], r_stg[si]
        ac, rac = acc[si], r_acc[si]
        S.op("act", [pr], [rst], lambda: nc.scalar.copy(out=st[0:P, CW:CW + N], in_=ps[0:P, 0:N]))
        S.op("dve", [r_carry], [rst], lambda: nc.vector.tensor_copy(out=st[0:P, 0:CW], in_=carry_ap))
        S.op("dve", [rst], [r_carry], lambda: nc.vector.tensor_copy(out=carry_ap, in_=st[0:P, N:N + CW]))
        if bias_ap is not None:
            S.op("dve", [rst] + preads, [rac], lambda: nc.vector.tensor_scalar(
                out=ac[0:P, 0:N], in0=st[0:P, 0:N], scalar1=taps[0], scalar2=bias_ap, op0=ALU.mult, op1=ALU.add))
        else:
            S.op("dve", [rst] + preads, [rac], lambda: nc.vector.tensor_scalar(
                out=ac[0:P, 0:N], in0=st[0:P, 0:N], scalar1=taps[0], scalar2=None, op0=ALU.mult))
        for j in range(1, K):
            S.op("dve", [rst, rac] + preads, [rac], lambda j=j: nc.vector.scalar_tensor_tensor(
                out=ac[0:P, 0:N], in0=st[0:P, j:j + N], scalar=taps[j], in1=ac[0:P, 0:N], op0=ALU.mult, op1=ALU.add))
        return ac, rac

    def win(l, c0, C=256):
        return wload(wcols(w_in_d, l, c0, C), 8, C, key=("in", l, c0))

    def prompt_tile_layer(ti, l):
        N = TT
        last_tile = (ti == NT - 1)
        S.dma("sp", d_rowb, [], [r_rowb], lambda: nc.sync.dma_start(out=rowb[:, 0:1024], in_=rowb_d[l, :, 0:1024]))
        normg = rowb[:, 0:1024]
        lng = rowb[:, 0:1024]
        lnb = rowb[:, 1024:2048]
        bsrow = rowb[:, 2048:3072]
        mod_norm(l, 0, N)
        xsT, r_xsT = G[3], r_G[3]
        for bi in range(4):
            wv, wr = win(l, XBC0 + bi * 256)
            for cc in range(2):
                c = bi * 2 + cc
                ps, pr = proj_ws(wv, wr, cc * 128, 128, N)
                taps = [pp[:, l, PP_SCW + j * 8 + c:PP_SCW + j * 8 + c + 1] for j in range(4)]
                ac, rac = conv_fm(128, ps, pr, N, cx[:, l, c, 0:3], r_cx[l], taps, pp[:, l, PP_SCB + c:PP_SCB + c + 1], [r_pp], c % 2)
                S.op("act", [rac], [r_xsT], lambda ac=ac, c=c: nc.scalar.activation(out=xsT[:, c, 0:N], in_=ac[:, 0:N], func=AF.Silu))
        for bi in range(2):
            wv, wr = win(l, XBC0 + 1024 + bi * 256)
            for ee in range(4):
                e = bi * 4 + ee
                ps, pr = proj_ws(wv, wr, ee * 64, 64, N)
                taps = [pp64[:, l, j * 8 + e:j * 8 + e + 1] for j in range(4)]
                ac, rac = conv_fm(64, ps, pr, N, cbc[:, l, e, 0:3], r_cbc[l], taps, pp64[:, l, 32 + e:33 + e], [r_pp64], e % 2)
                S.op("act", [rac], [r_BCT], lambda ac=ac, e=e: nc.scalar.activation(out=BCT[:, e, 0:N], in_=ac[0:64, 0:N], func=AF.Silu))
        if DBG_STOP <= 1:
            return
        wz = [win(l, i * 256) for i in range(4)]
        wdt = [win(l, DT0, 256)]
        S.op("act", [r_ST[l]], [r_STb1], lambda: nc.scalar.copy(out=STb[:, :, :], in_=ST[:, l, :, :]))
        yaT, r_yaT = G[1], r_G[1]
        for j in range(NCH):
            ssd_chunk(l, j * 128, 128, wz, wdt, xsT, r_xsT, yaT, r_yaT, normg)
        if last_tile:
            S.dma("sp", d_out, [r_ST[l]], [], lambda: nc.sync.dma_start(out=pssm_d[l], in_=ST[:, l, :, :]))
            S.dma("sp", d_out, [r_cx[l]], [], lambda: nc.sync.dma_start(out=pxbc_d[l], in_=cx[:, l, :, :]))
            S.dma("sp", d_out, [r_cbc[l]], [], lambda: nc.sync.dma_start(out=pbc_d[l], in_=cbc[:, l, :, :]))
        if DBG_STOP <= 2:
            return
        ybT, r_ybT = G[2], r_G[2]
        for bi in range(4):
            wb = [win(l, BCX0 + part * 1024 + bi * 256) for part in range(3)]
            for cc in range(2):
                c = bi * 2 + cc
                psB, prB = proj_ws(wb[0][0], wb[0][1], cc * 128, 128, N)
                psC, prC = proj_ws(wb[1][0], wb[1][1], cc * 128, 128, N)
                psX, prX = proj_ws(wb[2][0], wb[2][1], cc * 128, 128, N)
                st, rst = stg[2], r_stg[2]
                S.op("act", [prC], [r_acc[2]], lambda psC=psC: nc.scalar.copy(out=acc[2][:, 0:N], in_=psC[:, 0:N]))
                S.op("dve", [prX, r_acc[2]], [rst], lambda psX=psX: nc.vector.tensor_tensor(
                    out=st[:, 2:2 + N], in0=psX[:, 0:N], in1=acc[2][:, 0:N], op=ALU.mult))
                S.op("dve", [r_csc[l]], [rst], lambda c=c: nc.vector.tensor_copy(out=st[:, 0:2], in_=csc[:, l, c, :]))
                S.op("dve", [rst], [r_csc[l]], lambda c=c: nc.vector.tensor_copy(out=csc[:, l, c, :], in_=st[:, N:N + 2]))
                ac, rac = acc[c % 2], r_acc[c % 2]
                taps = [pp[:, l, PP_SHW + j * 8 + c:PP_SHW + j * 8 + c + 1] for j in range(3)]
                S.op("dve", [rst, r_pp], [rac], lambda ac=ac, taps=taps: nc.vector.tensor_scalar(
                    out=ac[:, 0:N], in0=st[:, 0:N], scalar1=taps[0], scalar2=None, op0=ALU.mult))
                for jj in (1, 2):
                    S.op("dve", [rst, rac, r_pp], [rac], lambda ac=ac, taps=taps, jj=jj: nc.vector.scalar_tensor_tensor(
                        out=ac[:, 0:N], in0=st[:, jj:jj + N], scalar=taps[jj], in1=ac[:, 0:N], op0=ALU.mult, op1=ALU.add))
                S.op("dve", [prB, rac], [r_ybT], lambda ac=ac, psB=psB, c=c: nc.vector.tensor_tensor(
                    out=ybT[:, c, 0:N], in0=psB[:, 0:N], in1=ac[:, 0:N], op=ALU.mult))
        if last_tile:
            S.dma("sp", d_out, [r_csc[l]], [], lambda: nc.sync.dma_start(out=psc_d[l], in_=csc[:, l, :, :]))
        if DBG_STOP <= 3:
            return
        qT, r_qT = G[5], r_G[5]
        wk = win(l, K0)
        wvv = win(l, V0)
        for hk in range(4):
            ps, pr = proj_ws(wk[0], wk[1], hk * 64, 64, N)
            ktmp, r_ktmp = tC, r_tC
            copy("act", ktmp[0:64, 0:N], ps[0:64, 0:N], [pr], [r_ktmp])
            ps2, pr2 = psum()
            S.op("pe", [r_ktmp, r_cst], [pr2], lambda ps2=ps2: nc.tensor.matmul(ps2[:, 0:N], lhsT=dupI[:, :], rhs=ktmp[0:64, 0:N], start=True, stop=True))
            copy("dve", kT[:, hk, 0:N], ps2[:, 0:N], [pr2], [r_kT])
        if DBG_ATTP <= 1:
            return
        for j in range(NCH):
            ps, pr = proj_as([wk, wvv], 512, j * 128, 128)
            if last_tile and j == NCH - 1:
                S.op("act", [pr], [r_ktok], lambda ps=ps: nc.scalar.copy(out=ktok[:, 0:512], in_=ps[:, :]))
                S.dma("sp", d_out, [r_ktok], [], lambda: nc.sync.dma_start(out=pk_d[l], in_=ktok[:, 0:256]))
                S.dma("sp", d_out, [r_ktok], [], lambda: nc.sync.dma_start(out=pv_d[l], in_=ktok[:, 256:512]))
            S.op("dve", [pr], [r_vtok], lambda j=j, ps=ps: nc.vector.tensor_copy(out=vtok[:, j, :], in_=ps[:, 256:512]))
        if DBG_ATTP <= 2:
            return
        for bi in range(4):
            wv, wr = win(l, Q0 + bi * 256)
            for cc in range(2):
                c = bi * 2 + cc
                ps, pr = proj_ws(wv, wr, cc * 128, 128, N)
                S.op("act", [pr], [r_qT], lambda ps=ps, c=c: nc.scalar.activation(out=qT[:, c, 0:N], in_=ps[:, 0:N], func=AF.Identity, scale=0.125))
        ycT, r_ycT = G[3], r_G[3]
        if DBG_ATTP <= 3:
            return
        for j in range(NCH):
            if DBG_ATT >= 2:
                attn_chunk(l, ti, j, qT, r_qT, ycT, r_ycT)
        S.op("dve", [r_kT], [r_kprev[l]], lambda: nc.vector.tensor_copy(out=kprev[:, l, :, :], in_=kT[:, :, N - 128:N]))
        S.op("dve", [r_vtok], [r_vprev[l]], lambda: nc.vector.tensor_copy(out=vprev[:, l, :], in_=vtok[:, NCH - 1, :]))
        if DBG_STOP <= 4:
            return
        ydT, r_ydT = G[4], r_G[4]
        S.dma("sp", d_rowb, [], [r_rowb], lambda: nc.sync.dma_start(out=rowb[:, :], in_=rowb_d[l, :, 1024:4096]))
        S.dma("pool", d_in, [], [r_gmraw], lambda: nc.gpsimd.dma_start(out=gmw_raw[:], in_=gmw_d[l].rearrange("g t s -> t g s")))
        for g in range(8):
            ps, pr = psum()
            psb = ps[:, 0:64].bitcast(BF16)
            S.op("pe", [r_gmraw, r_cst], [pr], lambda psb=psb, g=g: nc.tensor.transpose(psb[:, 0:128], gmw_raw[:, g, :], identb[:]))
            S.op("dve", [pr, r_cst], [r_gmw], lambda psb=psb, g=g: nc.vector.tensor_tensor(
                out=gmwT[:, g, :], in0=psb[:, 0:128], in1=Umat, op=ALU.mult))
        for bi in range(4):
            wv, wr = win(l, UV0 + bi * 256)
            for cc in range(2):
                c = bi * 2 + cc
                ps, pr = proj_ws(wv, wr, cc * 128, 128, N)
                S.op("act", [pr], [r_ydT], lambda ps=ps, c=c: nc.scalar.activation(out=ydT[:, c, 0:N], in_=ps[:, 0:N], func=AF.Gelu_apprx_tanh))
        wv_ = [win(l, UV0 + 1024 + i * 256) for i in range(4)]
        for j in range(NCH):
            gmlp_chunk(l, j * 128, 128, wv_, ydT, r_ydT, lng, lnb, bsrow, gmwT, None)
        if DBG_STOP <= 5:
            return
        mT, r_mT = G[5], r_G[5]
        brs = ((G[1], r_G[1]), (G[2], r_G[2]), (G[3], r_G[3]), (G[4], r_G[4]))
        macc = (acc[0], acc[1])
        r_macc = (r_acc[0], r_acc[1])
        for bi in range(4):
            for i in range(4):
                wg = win(l, GT0 + i * 1024 + bi * 256)
                wb = wload(w_br_d[l, i, :, bi * 256:bi * 256 + 256].rearrange("(k p) c -> p k c", p=128), 8, 256, key=("br", l, i, bi))
                for cc in range(2):
                    c = bi * 2 + cc
                    psg, prg = proj_ws(wg[0], wg[1], cc * 128, 128, N)
                    psp, prp = proj_ws(wb[0], wb[1], cc * 128, 128, N, rhs=brs[i][0], rres=brs[i][1])
                    S.op("act", [prg], [r_stg[2]], lambda psg=psg: nc.scalar.activation(out=stg[2][:, 0:N], in_=psg[:, 0:N], func=AF.Sigmoid))
                    if i == 0:
                        S.op("dve", [prp, r_stg[2]], [r_macc[cc]], lambda psp=psp, cc=cc: nc.vector.tensor_tensor(
                            out=macc[cc][:, 0:N], in0=psp[:, 0:N], in1=stg[2][:, 0:N], op=ALU.mult))
                    else:
                        S.op("dve", [prp, r_stg[2]], [r_acc[2]], lambda psp=psp: nc.vector.tensor_tensor(
                            out=acc[2][:, 0:N], in0=psp[:, 0:N], in1=stg[2][:, 0:N], op=ALU.mult))
                        if i < 3:
                            S.op("dve", [r_macc[cc], r_acc[2]], [r_macc[cc]], lambda cc=cc: nc.vector.tensor_tensor(
                                out=macc[cc][:, 0:N], in0=macc[cc][:, 0:N], in1=acc[2][:, 0:N], op=ALU.add))
                        else:
                            S.op("dve", [r_macc[cc], r_acc[2]], [r_mT], lambda c=c, cc=cc: nc.vector.tensor_tensor(
                                out=mT[:, c, 0:N], in0=macc[cc][:, 0:N], in1=acc[2][:, 0:N], op=ALU.add))
        if DBG_STOP <= 6:
            return
        for bi in range(4):
            wv, wr = wload(wcols(w_o_d, l, bi * 256, 256), 8, 256, key=("o", l, bi))
            for cc in range(2):
                c = bi * 2 + cc
                ps, pr = proj_ws(wv, wr, cc * 128, 128, N, rhs=mT, rres=r_mT)
                S.op("dve", [pr, r_mod, r_x], [r_x], lambda ps=ps, c=c: nc.vector.scalar_tensor_tensor(
                    out=xT[:, c, 0:N], in0=ps[:, 0:N], scalar=mod[:, l, 16 + c, 0:1], in1=xT[:, c, 0:N], op0=ALU.mult, op1=ALU.add))
        if DBG_STOP <= 7:
            return
        mod_norm(l, 1, N)
        gat = (G[1], G[2], G[3])
        r_gat = (r_G[1], r_G[2], r_G[3])
        for blk in range(11):
            wa = wload(wcols(w_up_d, l, blk * 256, 256), 8, 256, key=("ua", l, blk))
            wg_ = wload(wcols(w_up_d, l, DFF + blk * 256, 256), 8, 256, key=("ug", l, blk))
            for cc in range(2):
                i = blk * 2 + cc
                outs = []
                for which, (wv, wr) in enumerate((wa, wg_)):
                    ci = which * 22 + i
                    ps, pr = proj_ws(wv, wr, cc * 128, 128, N)
                    taps = [pp[:, l, PP_FW + j * 44 + ci:PP_FW + j * 44 + ci + 1] for j in range(3)]
                    ac, rac = conv_fm(128, ps, pr, N, cff[:, l, ci, :], r_cff[l], taps, pp[:, l, PP_FB + ci:PP_FB + ci + 1], [r_pp], which)
                    outs.append((ac, rac))
                S.op("act", [outs[0][1]], [r_acc[2]], lambda a=outs[0][0]: nc.scalar.activation(out=acc[2][:, 0:N], in_=a[:, 0:N], func=AF.Silu))
                S.op("dve", [r_acc[2], outs[1][1]], [r_gat[i // 8]], lambda g_=outs[1][0], i=i: nc.vector.tensor_tensor(
                    out=gat[i // 8][:, i % 8, 0:N], in0=acc[2][:, 0:N], in1=g_[:, 0:N], op=ALU.mult))
        if last_tile:
            S.dma("sp", d_out, [r_cff[l]], [], lambda: nc.sync.dma_start(out=pffn_d[l], in_=cff[:, l, :, :]))
        for c in range(8):
            wh = [wload(w_dn_d[l, hh * 1408:(hh + 1) * 1408, c * 128:(c + 1) * 128].rearrange("(k p) c -> p k c", p=128), 11, 128, key=("dn", l, c, hh)) for hh in range(2)]
            ps, pr = psum()

            def f(wh=wh, ps=ps):
                for k in range(22):
                    last = nc.tensor.matmul(ps[:, 0:N], lhsT=wh[k // 11][0][:, k % 11, :], rhs=gat[k // 8][:, k % 8, 0:N], start=(k == 0), stop=(k == 21))
                return last
            S.op("pe", [wh[0][1], wh[1][1]] + list(r_gat), [pr], f)
            S.op("dve", [pr, r_mod, r_x], [r_x], lambda ps=ps, c=c: nc.vector.scalar_tensor_tensor(
                out=xT[:, c, 0:N], in0=ps[:, 0:N], scalar=mod[:, l, 40 + c, 0:1], in1=xT[:, c, 0:N], op0=ALU.mult, op1=ALU.add))

    def ssd_chunk(l, t0, T, wz, wdt, xsT, r_xsT, yaT, r_yaT, normg):
        dtb = rows[0:T, l, 0:16]
        Arow = rows[0:T, l, 16:32]
        Drow = rows[0:T, l, 32:48]
        xs_tok, r_xs = tC, r_tC
        for c in range(8):
            ps, pr = psum()
            psb = ps[:, 0:64].bitcast(BF16)
            S.op("pe", [r_xsT, r_cst], [pr], lambda c=c, psb=psb: nc.tensor.transpose(psb[0:T, 0:128], xsT[:, c, t0:t0 + T], identb[:]))
            copy(eng2(), xs_tok[0:T, c * 128:(c + 1) * 128], psb[0:T, 0:128], [pr], [r_xs])
        for g in range(4):
            ps, pr = psum()
            psb = ps[:, 0:64].bitcast(BF16)
            S.op("pe", [r_BCT, r_cst], [pr], lambda g=g, psb=psb: nc.tensor.transpose(psb[0:T, 0:64], BCT[:, g, t0:t0 + T], identb[0:64, 0:64]))
            copy(eng2(), Btok[0:T, g, :], psb[0:T, 0:64], [pr], [r_Btok])
        ps, pr = proj_as(wdt, 16, t0, T)
        dt = sm[0:T, 0:16]
        dtA = sm[0:T, 16:32]
        Acs = sm[0:T, 32:48]
        nAcs = sm[0:T, 48:64]
        eA = sm[0:T, 64:80]
        dec = sm[0:T, 80:96]
        wdec = sm[0:T, 96:112]
        S.op("dve", [pr, r_rows], [r_sm], lambda: nc.vector.tensor_tensor(out=dt, in0=ps[0:T, 0:16], in1=dtb, op=ALU.add))
        S.op("act", [r_sm], [r_sm], lambda: nc.scalar.activation(out=dt, in_=dt, func=AF.Exp))
        S.op("act", [r_sm], [r_sm], lambda: nc.scalar.activation(out=dt, in_=dt, func=AF.Ln, bias=1.0))
        S.op("dve", [r_sm, r_rows], [r_sm], lambda: nc.vector.tensor_tensor(out=dtA, in0=dt, in1=Arow, op=ALU.mult))
        ps1, pr1 = psum()

        def f1():
            nc.tensor.matmul(ps1[0:T, 0:16], lhsT=Umat[0:T, 0:T], rhs=dtA, start=True, stop=True)
            return nc.tensor.matmul(ps1[0:max(T, 64), 16:32], lhsT=ones32[0:T, 0:max(T, 64)], rhs=dtA, start=True, stop=True)
        S.op("pe", [r_sm, r_cst], [pr1], f1)
        S.op("dve", [pr1], [r_sm], lambda: nc.vector.tensor_copy(out=Acs, in_=ps1[0:T, 0:16]))
        S.op("dve", [pr1], [r_sm], lambda: nc.vector.tensor_scalar(out=nAcs, in0=ps1[0:T, 0:16], scalar1=-1.0, scalar2=None, op0=ALU.mult))
        S.op("act", [pr1], [r_sm], lambda: nc.scalar.activation(out=eA, in_=ps1[0:T, 0:16], func=AF.Exp))
        S.op("dve", [pr1, r_sm], [r_sm], lambda: nc.vector.tensor_tensor(out=dec, in0=ps1[0:T, 16:32], in1=Acs, op=ALU.subtract))
        cdec = sm[0:64, 112:128]
        S.op("act", [pr1], [r_sm], lambda: nc.scalar.activation(out=cdec, in_=ps1[0:64, 16:32], func=AF.Exp))
        S.op("act", [r_sm], [r_sm], lambda: nc.scalar.activation(out=dec, in_=dec, func=AF.Exp))
        S.op("dve", [r_sm], [r_sm], lambda: nc.vector.tensor_tensor(out=wdec, in0=dec, in1=dt, op=ALU.mult))
        for hg in range(4):
            ps, pr = psum()

            def fe(ps=ps, hg=hg):
                for hh in range(4):
                    h_ = hg * 4 + hh
                    nc.tensor.matmul(ps[0:T, hh * 128:hh * 128 + T], lhsT=dtA[:, h_:h_ + 1].to_broadcast([T, T]), rhs=Umat[0:T, 0:T], start=True, stop=False)
                    last = nc.tensor.matmul(ps[0:T, hh * 128:hh * 128 + T], lhsT=ident[0:T, 0:T], rhs=NEGM[0:T, 0:T], start=False, stop=True)
                return last
            S.op("pe", [r_sm, r_cst], [pr], fe)
            for hh in range(4):
                h_ = hg * 4 + hh
                S.op("act", [pr, r_sm], [r_Eh], lambda ps=ps, hh=hh, h_=h_: nc.scalar.activation(
                    out=Eh[0:T, h_, 0:T], in_=ps[0:T, hh * 128:hh * 128 + T], func=AF.Exp, bias=nAcs[:, h_:h_ + 1], scale=1.0))
        ps, pr = psum()

        def fcb(ps=ps):
            for g in range(4):
                last = nc.tensor.matmul(ps[0:T, g * 128:g * 128 + T], lhsT=BCT[:, g, t0:t0 + T], rhs=BCT[:, 4 + g, t0:t0 + T], start=True, stop=True)
            return last
        S.op("pe", [r_BCT], [pr], fcb)
        copy("act", CBs[0:T, :, 0:T], ps[0:T, :].rearrange("p (g t) -> p g t", g=4)[:, :, 0:T], [pr], [r_CBs])
        for g in range(4):
            S.op("dve", [r_Eh, r_CBs], [r_Mh], lambda g=g: nc.vector.tensor_tensor(
                out=Mh[0:T, g * 4:g * 4 + 4, 0:T], in0=Eh[0:T, g * 4:g * 4 + 4, 0:T],
                in1=CBs[0:T, g:g + 1, 0:T].to_broadcast([T, 4, T]), op=ALU.mult))
        xdt, r_xdt = tD, r_tD
        Xdd, r_Xdd = tE, r_tE
        S.op("dve", [r_xs, r_sm], [r_xdt], lambda: nc.vector.tensor_tensor(
            out=xdt[0:T, :].rearrange("p (h q) -> p h q", h=16), in0=xs_tok[0:T, :].rearrange("p (h q) -> p h q", h=16),
            in1=dt.unsqueeze(2).to_broadcast([T, 16, 64]), op=ALU.mult))
        S.op("dve", [r_xs, r_sm], [r_Xdd], lambda: nc.vector.tensor_tensor(
            out=Xdd[0:T, :].rearrange("p (h q) -> p h q", h=16), in0=xs_tok[0:T, :].rearrange("p (h q) -> p h q", h=16),
            in1=wdec.unsqueeze(2).to_broadcast([T, 16, 64]), op=ALU.mult))
        pso = [psum(), psum()]
        psd = [psum(), psum()]

        def foff():
            for g in range(4):
                last = nc.tensor.matmul(pso[g // 2][0][0:T, (g % 2) * 256:(g % 2) * 256 + 256], lhsT=BCT[:, 4 + g, t0:t0 + T],
                                        rhs=STb[:, g, :], start=True, stop=True)
            return last
        S.op("pe", [r_BCT, r_STb[l]], [pso[0][1], pso[1][1]], foff)

        def fdiag():
            for h_ in range(16):
                last = nc.tensor.matmul(psd[h_ // 8][0][0:T, (h_ % 8) * 64:(h_ % 8) * 64 + 64], lhsT=Mh[0:T, h_, 0:T],
                                        rhs=xdt[0:T, h_ * 64:(h_ + 1) * 64], start=True, stop=True)
            return last
        S.op("pe", [r_Mh, r_xdt], [psd[0][1], psd[1][1]], fdiag)
        y, r_y = tA, r_tA
        for hf in range(2):
            S.op("dve", [pso[hf][1], r_sm], [r_y], lambda hf=hf: nc.vector.tensor_tensor(
                out=y[0:T, hf * 512:(hf + 1) * 512].rearrange("p (h q) -> p h q", h=8),
                in0=pso[hf][0][0:T, :].rearrange("p (h q) -> p h q", h=8),
                in1=eA[:, hf * 8:hf * 8 + 8].unsqueeze(2).to_broadcast([T, 8, 64]), op=ALU.mult))
            S.op("dve", [psd[hf][1], r_y], [r_y], lambda hf=hf: nc.vector.tensor_tensor(
                out=y[0:T, hf * 512:(hf + 1) * 512], in0=psd[hf][0][0:T, :], in1=y[0:T, hf * 512:(hf + 1) * 512], op=ALU.add))
        t2, r_t2 = tB, r_tB
        S.op("dve", [r_xs, r_rows], [r_t2], lambda: nc.vector.tensor_tensor(
            out=t2[0:T, :].rearrange("p (h q) -> p h q", h=16), in0=xs_tok[0:T, :].rearrange("p (h q) -> p h q", h=16),
            in1=Drow.unsqueeze(2).to_broadcast([T, 16, 64]), op=ALU.mult))
        S.op("dve", [r_t2, r_y], [r_y], lambda: nc.vector.tensor_tensor(out=y[0:T, :], in0=y[0:T, :], in1=t2[0:T, :], op=ALU.add))
        pss = [psum(), psum()]

        def fst():
            for g in range(4):
                last = nc.tensor.matmul(pss[g // 2][0][0:64, (g % 2) * 256:(g % 2) * 256 + 256], lhsT=Btok[0:T, g, :],
                                        rhs=Xdd[0:T, g * 256:(g + 1) * 256], start=True, stop=True)
            return last
        S.op("pe", [r_Btok, r_Xdd], [pss[0][1], pss[1][1]], fst)
        S.op("dve", [r_ST[l], r_sm, r_STb[l]], [r_ST[l]], lambda: nc.vector.tensor_tensor(
            out=ST[:, l, :, :].rearrange("p g (r q) -> p (g r) q", r=4), in0=ST[:, l, :, :].rearrange("p g (r q) -> p (g r) q", r=4),
            in1=cdec.unsqueeze(2).to_broadcast([64, 16, 64]), op=ALU.mult))
        for hf in range(2):
            S.op("dve", [pss[hf][1], r_ST[l]], [r_ST[l]], lambda hf=hf: nc.vector.tensor_tensor(
                out=ST[:, l, hf * 2:hf * 2 + 2, :], in0=ST[:, l, hf * 2:hf * 2 + 2, :],
                in1=pss[hf][0][0:64, :].rearrange("p (g q) -> p g q", g=2), op=ALU.add))
        S.op("act", [r_ST[l]], [r_STb[l]], lambda: nc.scalar.copy(out=STb[:, :, :], in_=ST[:, l, :, :]))
        for hf in range(2):
            ps, pr = proj_as(wz[hf * 2:hf * 2 + 2], 512, t0, T)
            S.op("act", [pr], [r_t2], lambda ps=ps, hf=hf: nc.scalar.activation(out=t2[0:T, hf * 512:(hf + 1) * 512], in_=ps[0:T, :], func=AF.Silu))
        S.op("dve", [r_t2, r_y], [r_y], lambda: nc.vector.tensor_tensor(out=y[0:T, :], in0=y[0:T, :], in1=t2[0:T, :], op=ALU.mult))
        ssq = sm[0:T, 128:132]
        for g in range(4):
            S.op("act", [r_y], [r_t2, r_sm], lambda g=g: nc.scalar.activation(
                out=t2[0:T, g * 256:(g + 1) * 256], in_=y[0:T, g * 256:(g + 1) * 256], func=AF.Square, accum_out=ssq[:, g:g + 1]))
        rstd_of(ssq, T, ssq, [r_sm, r_cst], [r_sm], 1.0 / 256)
        S.op("dve", [r_y, r_sm], [r_y], lambda: nc.vector.tensor_tensor(
            out=y[0:T, :].rearrange("p (g q) -> p g q", g=4), in0=y[0:T, :].rearrange("p (g q) -> p g q", g=4),
            in1=ssq.unsqueeze(2).to_broadcast([T, 4, 256]), op=ALU.mult))
        ya_tok, r_ya = tC, r_tC
        S.op("dve", [r_y, r_rowb], [r_ya], lambda: nc.vector.tensor_tensor(out=ya_tok[0:T, :], in0=y[0:T, :], in1=normg[0:T, :], op=ALU.mult))
        for c in range(8):
            ps, pr = psum()
            psb = ps[:, 0:64].bitcast(BF16)
            S.op("pe", [r_ya, r_cst], [pr], lambda c=c, psb=psb: nc.tensor.transpose(psb[:, 0:T], ya_tok[0:T, c * 128:(c + 1) * 128], identb[0:T, 0:T]))
            copy(eng2(), yaT[:, c, t0:t0 + T], psb[:, 0:T], [pr], [r_yaT])

    def attn_chunk(l, ti, j, qT, r_qT, ycT, r_ycT):
        T = 128
        t0 = j * 128
        first = (ti == 0 and j == 0)
        KC = 128 if first else 256
        boff = 128 if first else 0
        sink = rows[:, l, 48:64]
        o_ps = [(PS[6], r_PS[6]), (PS[7], r_PS[7])]
        rinv = sm[:, 136:152]
        for hk in range(4):
            pss_ = [psum(), psum()]

            def fs(hk=hk, pss_=pss_):
                for hh in range(4):
                    hd = hk * 4 + hh
                    pb = (hd % 2) * 64
                    qa = qT[pb:pb + 64, hd // 2, t0:t0 + T]
                    dst = pss_[hh % 2][0][:, (hh // 2) * 256:(hh // 2) * 256 + KC]
                    if first:
                        last = nc.tensor.matmul(dst, lhsT=qa, rhs=kT[pb:pb + 64, hk, t0:t0 + T], start=True, stop=True)
                    elif j == 0:
                        nc.tensor.matmul(dst[:, 0:128], lhsT=qa, rhs=kprev[pb:pb + 64, l, hk, :], start=True, stop=True)
                        last = nc.tensor.matmul(dst[:, 128:256], lhsT=qa, rhs=kT[pb:pb + 64, hk, t0:t0 + T], start=True, stop=True)
                    else:
                        last = nc.tensor.matmul(dst, lhsT=qa, rhs=kT[pb:pb + 64, hk, t0 - 128:t0 + T], start=True, stop=True)
                return last
            S.op("pe", [r_qT, r_kT, r_kprev[l]], [pss_[0][1], pss_[1][1]], fs)
            sc = tA[:, :].rearrange("p (h k) -> p h k", h=4)
            for hh in range(4):
                S.op("dve", [pss_[hh % 2][1], r_cst], [r_tA], lambda hh=hh, hk=hk, pss_=pss_: nc.vector.scalar_tensor_tensor(
                    out=sc[:, hh, 0:KC], in0=distm[:, boff:boff + KC], scalar=-SLOPES[hk * 4 + hh],
                    in1=pss_[hh % 2][0][:, (hh // 2) * 256:(hh // 2) * 256 + KC], op0=ALU.mult, op1=ALU.add))
            if DBG_ATT <= 2:
                continue
            mx = sm[:, 152:156]
            nmx = sm[:, 156:160]
            esk = sm[:, 160:164]
            rsum = sm[:, 164:168]
            S.op("dve", [r_tA], [r_sm], lambda: nc.vector.tensor_reduce(out=mx, in_=sc[:, :, 0:KC], axis=AX.X, op=ALU.max))
            S.op("dve", [r_sm, r_rows], [r_sm], lambda hk=hk: nc.vector.tensor_tensor(out=mx, in0=mx, in1=sink[:, hk * 4:hk * 4 + 4], op=ALU.max))
            S.op("dve", [r_sm], [r_sm], lambda: nc.vector.tensor_scalar(out=nmx, in0=mx, scalar1=-1.0, scalar2=None, op0=ALU.mult))
            S.op("dve", [r_sm, r_rows], [r_sm], lambda hk=hk: nc.vector.tensor_tensor(out=esk, in0=sink[:, hk * 4:hk * 4 + 4], in1=mx, op=ALU.subtract))
            S.op("act", [r_sm], [r_sm], lambda: nc.scalar.activation(out=esk, in_=esk, func=AF.Exp))
            Pm = tD[:, :].rearrange("p (h k) -> p h k", h=4)
            for hh in range(4):
                S.op("act", [r_tA, r_sm], [r_tD, r_sm], lambda hh=hh: nc.scalar.activation(
                    out=Pm[:, hh, 0:KC], in_=sc[:, hh, 0:KC], func=AF.Exp, bias=nmx[:, hh:hh + 1], scale=1.0, accum_out=rsum[:, hh:hh + 1]))
            S.op("dve", [r_sm], [r_sm], lambda: nc.vector.tensor_tensor(out=rsum, in0=rsum, in1=esk, op=ALU.add))
            S.op("dve", [r_sm], [r_sm], lambda hk=hk: nc.vector.reciprocal(out=rinv[:, hk * 4:hk * 4 + 4], in_=rsum))
            if DBG_ATT <= 3:
                continue
            PT = tE[:, :].rearrange("p (h k) -> p h k", h=4)
            nkb = KC // 128
            for hh in range(4):
                ps, pr = psum()
                psb = ps[:, 0:128].bitcast(BF16)

                def ft(hh=hh, psb=psb):
                    for kb in range(nkb):
                        last = nc.tensor.transpose(psb[:, kb * 128:(kb + 1) * 128], Pm[:, hh, kb * 128:(kb + 1) * 128], identb[:])
                    return last
                S.op("pe", [r_tD, r_cst], [pr], ft)
                copy(eng2(), PT[:, hh, 0:KC], psb[:, 0:KC], [pr], [r_tE])

            def fo(hk=hk):
                for hh in range(4):
                    hd = hk * 4 + hh
                    dst = o_ps[hd // 8][0][:, (hd % 8) * 64:(hd % 8) * 64 + 64]
                    if first:
                        last = nc.tensor.matmul(dst, lhsT=PT[:, hh, 0:128], rhs=vtok[:, j, hk * 64:hk * 64 + 64], start=True, stop=True)
                    else:
                        vp = vprev[:, l, hk * 64:hk * 64 + 64] if j == 0 else vtok[:, j - 1, hk * 64:hk * 64 + 64]
                        nc.tensor.matmul(dst, lhsT=PT[:, hh, 0:128], rhs=vp, start=True, stop=False)
                        last = nc.tensor.matmul(dst, lhsT=PT[:, hh, 128:256], rhs=vtok[:, j, hk * 64:hk * 64 + 64], start=False, stop=True)
                return last
            S.op("pe", [r_tE, r_vtok, r_vprev[l]], [o_ps[hk // 2][1]], fo)
        yc_tok = tC
        for hf in range(2):
            S.op("dve", [o_ps[hf][1], r_sm], [r_tC], lambda hf=hf: nc.vector.tensor_tensor(
                out=yc_tok[:, hf * 512:(hf + 1) * 512].rearrange("p (h q) -> p h q", h=8),
                in0=o_ps[hf][0][:, :].rearrange("p (h q) -> p h q", h=8),
                in1=rinv[:, hf * 8:hf * 8 + 8].unsqueeze(2).to_broadcast([128, 8, 64]), op=ALU.mult))
        for c in range(8):
            ps, pr = psum()
            psb = ps[:, 0:64].bitcast(BF16)
            S.op("pe", [r_tC, r_cst], [pr], lambda c=c, psb=psb: nc.tensor.transpose(psb[:, 0:T], yc_tok[:, c * 128:(c + 1) * 128], identb[:]))
            copy(eng2(), ycT[:, c, t0:t0 + T], psb[:, 0:T], [pr], [r_ycT])

    def gmlp_chunk(l, t0, T, wv_, ydT, r_ydT, lng, lnb, bsrow, wT, vout):
        v, r_v = tA, r_tA
        ssum = sm[0:T, 168:170]
        mean = sm[0:T, 170:171]
        ssq = sm[0:T, 171:173]
        rstd = sm[0:T, 173:174]
        for hf in range(2):
            ps, pr = proj_as(wv_[hf * 2:hf * 2 + 2], 512, t0, T)
            S.op("act", [pr], [r_v, r_sm], lambda ps=ps, hf=hf: nc.scalar.activation(
                out=v[0:T, hf * 512:(hf + 1) * 512], in_=ps[0:T, :], func=AF.Gelu_apprx_tanh, accum_out=ssum[:, hf:hf + 1]))
        S.op("dve", [r_sm], [r_sm], lambda: nc.vector.tensor_tensor(out=mean, in0=ssum[:, 0:1], in1=ssum[:, 1:2], op=ALU.add))
        S.op("dve", [r_sm], [r_sm], lambda: nc.vector.tensor_scalar(out=mean, in0=mean, scalar1=1.0 / 1024, scalar2=None, op0=ALU.mult))
        S.op("dve", [r_v, r_sm], [r_v], lambda: nc.vector.tensor_scalar(out=v[0:T, :], in0=v[0:T, :], scalar1=mean, scalar2=None, op0=ALU.subtract))
        for hf in range(2):
            S.op("act", [r_v], [r_tB, r_sm], lambda hf=hf: nc.scalar.activation(
                out=tB[0:T, hf * 512:(hf + 1) * 512], in_=v[0:T, hf * 512:(hf + 1) * 512], func=AF.Square, accum_out=ssq[:, hf:hf + 1]))
        S.op("dve", [r_sm], [r_sm], lambda: nc.vector.tensor_tensor(out=rstd, in0=ssq[:, 0:1], in1=ssq[:, 1:2], op=ALU.add))
        rstd_of(rstd, T, rstd, [r_sm, r_cst], [r_sm], 1.0 / 1024)
        S.op("dve", [r_v, r_sm, r_rowb], [r_v], lambda: nc.vector.scalar_tensor_tensor(
            out=v[0:T, :], in0=v[0:T, :], scalar=rstd, in1=lng[0:T, :], op0=ALU.mult, op1=ALU.mult))
        if vout is not None:
            S.op("dve", [r_v, r_rowb], [r_tB], lambda: nc.vector.tensor_tensor(out=tB[0:T, :], in0=v[0:T, :], in1=lnb[0:T, :], op=ALU.add))
            vout(tB, r_tB)
        vn, r_vn = tC, r_tC
        S.op("dve", [r_v, r_rowb], [r_vn], lambda: nc.vector.tensor_tensor(out=vn[0:T, :], in0=v[0:T, :], in1=lnb[0:T, :], op=ALU.add))
        for hf in range(2):
            ps, pr = psum()

            def fm(ps=ps, hf=hf):
                for gg in range(4):
                    g = hf * 4 + gg
                    last = nc.tensor.matmul(ps[:, gg * 128:gg * 128 + T], lhsT=vn[0:T, g * 128:(g + 1) * 128], rhs=wT[0:T, g, 0:T], start=True, stop=True)
                return last
            S.op("pe", [r_vn, r_gmw], [pr], fm)
            mix = tB[:, 0:512].rearrange("p (g t) -> p g t", g=4)
            S.op("dve", [pr, r_rowb], [r_tB], lambda ps=ps, hf=hf: nc.vector.tensor_tensor(
                out=mix[:, :, 0:T], in0=ps[:, :].rearrange("p (g t) -> p g t", g=4)[:, :, 0:T],
                in1=bsrow[:, hf * 512:(hf + 1) * 512].rearrange("p (g t) -> p g t", g=4)[:, :, 0:T], op=ALU.add))
            S.op("dve", [r_tB, r_ydT], [r_ydT], lambda hf=hf: nc.vector.tensor_tensor(
                out=ydT[:, hf * 4:hf * 4 + 4, t0:t0 + T], in0=ydT[:, hf * 4:hf * 4 + 4, t0:t0 + T], in1=mix[:, :, 0:T], op=ALU.mult))

    if SB:
        NS = 4 * SB
        mod_s = sb("mod_s", [128, 1, 48, SB]); r_mods = Res("mod_s")
        gm_s = sb("gm_s", [128, 1, 2, 8, SB]); r_gms = Res("gm_s")
        csT = sb("csT_t", [128, 8, SB]); r_csT = Res("csT")
        csb = sb("csb", [128, 8, SB], BF16)
        sxbc = sb("sxbc", [128, 8, SB, 3]); r_sxbc = Res("sxbc")
        sbc = sb("sbc", [64, 8, SB, 3]); r_sbc = Res("sbc")
        ssc = sb("ssc", [128, 8, SB, 2]); r_ssc = Res("ssc")
        sffn = sb("sffn", [128, 44, SB, 2]); r_sffn = Res("sffn")
        KKb = sb("KKb", [128, 4, 160], BF16); r_KKb = Res("KKb")
        Vb = sb("Vb", [128, 256], BF16); r_Vb = Res("Vb")
        vnew = sb("vnew", [4, 256], BF16); r_vnew = Res("vnew")
        PTn = sb("PTn", [4, 4, 4], BF16); r_PTn = Res("PTn")
        d_cs = d_kv = d_st = None

    def conv_s(P, ps, pr, taps, bias_ap, preads, st_ap, r_st, si):
        K = len(taps)
        CW = K - 1
        W = CW + 4
        st3 = stg[si][0:P, 0:SB * W].rearrange("p (b w) -> p b w", b=SB)
        ac3 = acc[si][0:P, 0:NS].rearrange("p (b w) -> p b w", b=SB)
        rst, rac = r_stg[si], r_acc[si]
        S.op("act", [pr], [rst], lambda: nc.scalar.copy(out=st3[:, :, CW:W], in_=ps[0:P, 0:NS].rearrange("p (b w) -> p b w", b=SB)))
        S.op("dve", [r_st], [rst], lambda: nc.vector.tensor_copy(out=st3[:, :, 0:CW], in_=st_ap))
        S.op("dve", [rst], [r_st], lambda: nc.vector.tensor_copy(out=st_ap, in_=st3[:, :, 4:W]))
        if bias_ap is not None:
            S.op("dve", [rst] + preads, [rac], lambda: nc.vector.tensor_scalar(
                out=ac3, in0=st3[:, :, 0:4], scalar1=taps[0], scalar2=bias_ap, op0=ALU.mult, op1=ALU.add))
        else:
            S.op("dve", [rst] + preads, [rac], lambda: nc.vector.tensor_scalar(
                out=ac3, in0=st3[:, :, 0:4], scalar1=taps[0], scalar2=None, op0=ALU.mult))
        for j in range(1, K):
            S.op("dve", [rst, rac] + preads, [rac], lambda j=j: nc.vector.scalar_tensor_tensor(
                out=ac3, in0=st3[:, :, j:j + 4], scalar=taps[j], in1=ac3, op0=ALU.mult, op1=ALU.add))
        return acc[si], rac

    def mod_norm_s(which):
        N = NS
        sh_i = 0 if which == 0 else 3
        ps, pr = psum()
        sq = G[5]
        S.op("act", [r_x], [r_G[5]], lambda: nc.scalar.activation(out=sq[:, :, 0:N], in_=xT[:, :, 0:N], func=AF.Square))

        def f():
            for k in range(8):
                last = nc.tensor.matmul(ps[:, 0:N], lhsT=onesb[:], rhs=sq[:, k, 0:N], start=(k == 0), stop=(k == 7))
            return last
        S.op("pe", [r_G[5], r_cst], [pr], f)
        rs = acc[2]
        rstd_of(ps[:, 0:N], 128, rs[:, 0:N], [pr, r_cst], [r_acc[2]], 1.0 / D)
        for c in range(8):
            a_ = acc[c % 2]
            a3 = a_[:, 0:N].rearrange("p (b w) -> p b w", b=SB)
            S.op("dve", [r_x, r_acc[2]], [r_acc[c % 2]], lambda c=c, a_=a_: nc.vector.tensor_tensor(
                out=a_[:, 0:N], in0=xT[:, c, 0:N], in1=rs[:, 0:N], op=ALU.mult))
            S.op("dve", [r_gms, r_acc[c % 2]], [r_acc[c % 2]], lambda c=c, a3=a3: nc.vector.tensor_tensor(
                out=a3, in0=a3, in1=gm_s[:, 0, which, c, :].unsqueeze(2).to_broadcast([128, SB, 4]), op=ALU.mult))
            S.op("dve", [r_mods, r_acc[c % 2]], [r_h], lambda c=c, a3=a3: nc.vector.tensor_tensor(
                out=hB[:, c, 0:N].rearrange("p (b w) -> p b w", b=SB), in0=a3,
                in1=mod_s[:, 0, sh_i * 8 + c, :].unsqueeze(2).to_broadcast([128, SB, 4]), op=ALU.add))

    def resid_s(ps, pr, c, gi):
        N = NS
        S.op("dve", [pr, r_mods], [r_acc[2]], lambda: nc.vector.tensor_tensor(
            out=acc[2][:, 0:N].rearrange("p (b w) -> p b w", b=SB), in0=ps[:, 0:N].rearrange("p (b w) -> p b w", b=SB),
            in1=mod_s[:, 0, gi * 8 + c, :].unsqueeze(2).to_broadcast([128, SB, 4]), op=ALU.mult))
        S.op("dve", [r_acc[2], r_x], [r_x], lambda: nc.vector.tensor_tensor(
            out=xT[:, c, 0:N], in0=xT[:, c, 0:N], in1=acc[2][:, 0:N], op=ALU.add))

    def attn_s(l, b, qT, r_qT, ycT, r_ycT, wk, wvv):
        T = 4
        t0 = 4 * b
        KC = 132
        sink = rows[0:T, l, 48:64]
        o_ps = [(PS[6], r_PS[6]), (PS[7], r_PS[7])]
        rinv = sm[0:T, 136:152]
        S.dma("pool", d_kv, [], [r_KKb], lambda: nc.gpsimd.dma_start(out=KKb[:, :, 0:128], in_=ckT_d[l, b]))
        S.dma("pool", d_kv, [], [r_Vb], lambda: nc.gpsimd.dma_start(out=Vb[:, :], in_=cv_d[l, b]))
        S.op("dve", [r_kT], [r_KKb], lambda: nc.vector.tensor_copy(out=KKb[:, :, 128:132], in_=kT[:, :, t0:t0 + T]))
        ps, pr = proj_as([wk, wvv], 512, t0, T)
        S.op("act", [pr], [r_ktok], lambda: nc.scalar.copy(out=ktok[0:T, 0:512], in_=ps[0:T, :]))
        S.op("dve", [r_ktok], [r_vnew], lambda: nc.vector.tensor_copy(out=vnew[:, :], in_=ktok[0:T, 256:512]))
        S.dma("sp", d_out, [r_ktok], [], lambda: nc.sync.dma_start(out=sk_o[l, b, 124:128, :], in_=ktok[0:T, 0:256]))
        S.dma("sp", d_out, [r_ktok], [], lambda: nc.sync.dma_start(out=sv_o[l, b, 124:128, :], in_=ktok[0:T, 256:512]))
        for hk in range(4):
            pss_ = [psum(), psum()]

            def fs(hk=hk, pss_=pss_):
                for hh in range(4):
                    hd = hk * 4 + hh
                    pb = (hd % 2) * 64
                    last = nc.tensor.matmul(pss_[hh % 2][0][0:T, (hh // 2) * 256:(hh // 2) * 256 + KC],
                                            lhsT=qT[pb:pb + 64, hd // 2, t0:t0 + T], rhs=KKb[pb:pb + 64, hk, 0:132], start=True, stop=True)
                return last
            S.op("pe", [r_qT, r_KKb], [pss_[0][1], pss_[1][1]], fs)
            sc = tA[0:T, :].rearrange("p (h k) -> p h k", h=4)
            for hh in range(4):
                S.op("dve", [pss_[hh % 2][1], r_cst], [r_tA], lambda hh=hh, hk=hk, pss_=pss_: nc.vector.scalar_tensor_tensor(
                    out=sc[:, hh, 0:KC], in0=dists[0:T, :], scalar=-SLOPES[hk * 4 + hh],
                    in1=pss_[hh % 2][0][0:T, (hh // 2) * 256:(hh // 2) * 256 + KC], op0=ALU.mult, op1=ALU.add))
            mx = sm[0:T, 152:156]
            nmx = sm[0:T, 156:160]
            esk = sm[0:T, 160:164]
            rsum = sm[0:T, 164:168]
            S.op("dve", [r_tA], [r_sm], lambda: nc.vector.tensor_reduce(out=mx, in_=sc[:, :, 0:KC], axis=AX.X, op=ALU.max))
            S.op("dve", [r_sm, r_rows], [r_sm], lambda hk=hk: nc.vector.tensor_tensor(out=mx, in0=mx, in1=sink[:, hk * 4:hk * 4 + 4], op=ALU.max))
            S.op("dve", [r_sm], [r_sm], lambda: nc.vector.tensor_scalar(out=nmx, in0=mx, scalar1=-1.0, scalar2=None, op0=ALU.mult))
            S.op("dve", [r_sm, r_rows], [r_sm], lambda hk=hk: nc.vector.tensor_tensor(out=esk, in0=sink[:, hk * 4:hk * 4 + 4], in1=mx, op=ALU.subtract))
            S.op("act", [r_sm], [r_sm], lambda: nc.scalar.activation(out=esk, in_=esk, func=AF.Exp))
            Pm = tD[0:T, :].rearrange("p (h k) -> p h k", h=4)
            for hh in range(4):
                S.op("act", [r_tA, r_sm], [r_tD, r_sm], lambda hh=hh: nc.scalar.activation(
                    out=Pm[:, hh, 0:KC], in_=sc[:, hh, 0:KC], func=AF.Exp, bias=nmx[:, hh:hh + 1], scale=1.0, accum_out=rsum[:, hh:hh + 1]))
            S.op("dve", [r_sm], [r_sm], lambda: nc.vector.tensor_tensor(out=rsum, in0=rsum, in1=esk, op=ALU.add))
            S.op("dve", [r_sm], [r_sm], lambda hk=hk: nc.vector.reciprocal(out=rinv[:, hk * 4:hk * 4 + 4], in_=rsum))
            PT = tE[:, :].rearrange("p (h k) -> p h k", h=4)
            ps, pr = psum()
            psb = ps[:, 0:64].bitcast(BF16)

            def ft(psb=psb):
                for hh in range(4):
                    nc.tensor.transpose(psb[:, hh * 8:hh * 8 + 4], Pm[:, hh, 0:128], identb[0:T, 0:T])
                    last = nc.tensor.transpose(psb[0:T, hh * 8 + 4:hh * 8 + 8], Pm[:, hh, 128:132], identb[0:T, 0:T])
                return last
            S.op("pe", [r_tD, r_cst], [pr], ft)
            pv = psb[:, 0:32].rearrange("p (h k) -> p h k", h=4)
            copy("dve", PT[:, :, 0:4], pv[:, :, 0:4], [pr], [r_tE])
            copy("dve", PTn[:, :, :], pv[0:T, :, 4:8], [pr], [r_PTn])

            def fo(hk=hk):
                for hh in range(4):
                    hd = hk * 4 + hh
                    dst = o_ps[hd // 8][0][0:T, (hd % 8) * 64:(hd % 8) * 64 + 64]
                    nc.tensor.matmul(dst, lhsT=PT[:, hh, 0:4], rhs=Vb[:, hk * 64:hk * 64 + 64], start=True, stop=False)
                    last = nc.tensor.matmul(dst, lhsT=PTn[:, hh, :], rhs=vnew[:, hk * 64:hk * 64 + 64], start=False, stop=True)
                return last
            S.op("pe", [r_tE, r_PTn, r_Vb, r_vnew], [o_ps[hk // 2][1]], fo)
        yc_tok = tC
        for hf in range(2):
            S.op("dve", [o_ps[hf][1], r_sm], [r_tC], lambda hf=hf: nc.vector.tensor_tensor(
                out=yc_tok[0:T, hf * 512:(hf + 1) * 512].rearrange("p (h q) -> p h q", h=8),
                in0=o_ps[hf][0][0:T, :].rearrange("p (h q) -> p h q", h=8),
                in1=rinv[:, hf * 8:hf * 8 + 8].unsqueeze(2).to_broadcast([T, 8, 64]), op=ALU.mult))
        for c in range(8):
            ps, pr = psum()
            psb = ps[:, 0:64].bitcast(BF16)
            S.op("pe", [r_tC, r_cst], [pr], lambda c=c, psb=psb: nc.tensor.transpose(psb[:, 0:T], yc_tok[0:T, c * 128:(c + 1) * 128], identb[0:T, 0:T]))
            copy(eng2(), ycT[:, c, t0:t0 + T], psb[:, 0:T], [pr], [r_ycT])

    def sample_tile_layer(l):
        N = NS
        S.dma("sp", d_rowb, [], [r_rowb], lambda: nc.sync.dma_start(out=rowb[:, 0:1024], in_=rowb_d[l, :, 0:1024]))
        normg = rowb[:, 0:1024]
        lng = rowb[:, 0:1024]
        lnb = rowb[:, 1024:2048]
        bsrow = rowb[:, 2048:3072]
        S.dma("sp", d_cs, [], [r_sxbc], lambda: nc.sync.dma_start(out=sxbc[:], in_=sxbc_d[l]))
        S.dma("sp", d_cs, [], [r_sbc], lambda: nc.sync.dma_start(out=sbc[:], in_=sbc_d[l]))
        S.dma("sp", d_cs, [], [r_ssc], lambda: nc.sync.dma_start(out=ssc[:], in_=ssc_d[l]))
        S.dma("sp", d_cs, [], [r_sffn], lambda: nc.sync.dma_start(out=sffn[:], in_=sffn_d[l]))
        S.dma("sp", d_out, [], [], lambda: nc.sync.dma_start(out=sk_o[l, :, 0:124, :], in_=ck_d[l, :, 4:128, :]))
        S.dma("sp", d_out, [], [], lambda: nc.sync.dma_start(out=sv_o[l, :, 0:124, :], in_=cv_d[l, :, 4:128, :]))
        if DBG_S <= 0:
            return
        ada_layer(l, csb, r_csT, SB, mod_s, r_mods, gm_s, r_gms, 0)
        if DBG_S <= 1:
            return
        mod_norm_s(0)
        if DBG_S <= 2:
            return
        xsT, r_xsT = G[3], r_G[3]
        for bi in range(4):
            wv, wr = win(l, XBC0 + bi * 256)
            for cc in range(2):
                c = bi * 2 + cc
                ps, pr = proj_ws(wv, wr, cc * 128, 128, N)
                taps = [pp[:, l, PP_SCW + j * 8 + c:PP_SCW + j * 8 + c + 1] for j in range(4)]
                ac, rac = conv_s(128, ps, pr, taps, pp[:, l, PP_SCB + c:PP_SCB + c + 1], [r_pp], sxbc[:, c, :, :], r_sxbc, c % 2)
                S.op("act", [rac], [r_xsT], lambda ac=ac, c=c: nc.scalar.activation(out=xsT[:, c, 0:N], in_=ac[:, 0:N], func=AF.Silu))
        for bi in range(2):
            wv, wr = win(l, XBC0 + 1024 + bi * 256)
            for ee in range(4):
                e = bi * 4 + ee
                ps, pr = proj_ws(wv, wr, ee * 64, 64, N)
                taps = [pp64[:, l, j * 8 + e:j * 8 + e + 1] for j in range(4)]
                ac, rac = conv_s(64, ps, pr, taps, pp64[:, l, 32 + e:33 + e], [r_pp64], sbc[:, e, :, :], r_sbc, e % 2)
                S.op("act", [rac], [r_BCT], lambda ac=ac, e=e: nc.scalar.activation(out=BCT[:, e, 0:N], in_=ac[0:64, 0:N], func=AF.Silu))
        S.dma("sp", d_out, [r_sxbc], [], lambda: nc.sync.dma_start(out=sxbc_o[l], in_=sxbc[:]))
        S.dma("sp", d_out, [r_sbc], [], lambda: nc.sync.dma_start(out=sbc_o[l], in_=sbc[:]))
        if DBG_S <= 3:
            return
        wz = [win(l, i * 256) for i in range(4)]
        wdt = [win(l, DT0, 256)]
        yaT, r_yaT = G[1], r_G[1]
        for b in range(SB):
            S.dma("sp", d_st, [], [r_ST[l]], lambda b=b: nc.sync.dma_start(out=ST[:, l, :, :], in_=sssm_d[l, b]))
            S.op("act", [r_ST[l]], [r_STb1], lambda: nc.scalar.copy(out=STb[:, :, :], in_=ST[:, l, :, :]))
            ssd_chunk(l, 4 * b, 4, wz, wdt, xsT, r_xsT, yaT, r_yaT, normg)
            S.dma("sp", d_out, [r_ST[l]], [], lambda b=b: nc.sync.dma_start(out=sssm_o[l, b], in_=ST[:, l, :, :]))
        if DBG_S <= 4:
            return
        ybT, r_ybT = G[2], r_G[2]
        for bi in range(4):
            wb = [win(l, BCX0 + part * 1024 + bi * 256) for part in range(3)]
            for cc in range(2):
                c = bi * 2 + cc
                psB, prB = proj_ws(wb[0][0], wb[0][1], cc * 128, 128, N)
                psC, prC = proj_ws(wb[1][0], wb[1][1], cc * 128, 128, N)
                psX, prX = proj_ws(wb[2][0], wb[2][1], cc * 128, 128, N)
                st3 = stg[2][:, 0:SB * 6].rearrange("p (b w) -> p b w", b=SB)
                rst = r_stg[2]
                S.op("act", [prC], [r_acc[2]], lambda psC=psC: nc.scalar.copy(out=acc[2][:, 0:N], in_=psC[:, 0:N]))
                S.op("dve", [prX, r_acc[2]], [rst], lambda psX=psX, st3=st3: nc.vector.tensor_tensor(
                    out=st3[:, :, 2:6], in0=psX[:, 0:N].rearrange("p (b w) -> p b w", b=SB),
                    in1=acc[2][:, 0:N].rearrange("p (b w) -> p b w", b=SB), op=ALU.mult))
                S.op("dve", [r_ssc], [rst], lambda c=c, st3=st3: nc.vector.tensor_copy(out=st3[:, :, 0:2], in_=ssc[:, c, :, :]))
                S.op("dve", [rst], [r_ssc], lambda c=c, st3=st3: nc.vector.tensor_copy(out=ssc[:, c, :, :], in_=st3[:, :, 4:6]))
                ac, rac = acc[c % 2], r_acc[c % 2]
                ac3 = ac[:, 0:N].rearrange("p (b w) -> p b w", b=SB)
                taps = [pp[:, l, PP_SHW + j * 8 + c:PP_SHW + j * 8 + c + 1] for j in range(3)]
                S.op("dve", [rst, r_pp], [rac], lambda ac3=ac3, taps=taps, st3=st3: nc.vector.tensor_scalar(
                    out=ac3, in0=st3[:, :, 0:4], scalar1=taps[0], scalar2=None, op0=ALU.mult))
                for jj in (1, 2):
                    S.op("dve", [rst, rac, r_pp], [rac], lambda ac3=ac3, taps=taps, jj=jj, st3=st3: nc.vector.scalar_tensor_tensor(
                        out=ac3, in0=st3[:, :, jj:jj + 4], scalar=taps[jj], in1=ac3, op0=ALU.mult, op1=ALU.add))
                S.op("dve", [prB, rac], [r_ybT], lambda ac=ac, psB=psB, c=c: nc.vector.tensor_tensor(
                    out=ybT[:, c, 0:N], in0=psB[:, 0:N], in1=ac[:, 0:N], op=ALU.mult))
        S.dma("sp", d_out, [r_ssc], [], lambda: nc.sync.dma_start(out=ssc_o[l], in_=ssc[:]))
        if DBG_S <= 5:
            return
        qT, r_qT = G[5], r_G[5]
        wk = win(l, K0)
        wvv = win(l, V0)
        for hk in range(4):
            ps, pr = proj_ws(wk[0], wk[1], hk * 64, 64, N)
            ktmp, r_ktmp = tC, r_tC
            copy("act", ktmp[0:64, 0:N], ps[0:64, 0:N], [pr], [r_ktmp])
            ps2, pr2 = psum()
            S.op("pe", [r_ktmp, r_cst], [pr2], lambda ps2=ps2: nc.tensor.matmul(ps2[:, 0:N], lhsT=dupI[:, :], rhs=ktmp[0:64, 0:N], start=True, stop=True))
            copy("dve", kT[:, hk, 0:N], ps2[:, 0:N], [pr2], [r_kT])
        for bi in range(4):
            wv, wr = win(l, Q0 + bi * 256)
            for cc in range(2):
                c = bi * 2 + cc
                ps, pr = proj_ws(wv, wr, cc * 128, 128, N)
                S.op("act", [pr], [r_qT], lambda ps=ps, c=c: nc.scalar.activation(out=qT[:, c, 0:N], in_=ps[:, 0:N], func=AF.Identity, scale=0.125))
        ycT, r_ycT = G[3], r_G[3]
        for b in range(SB):
            attn_s(l, b, qT, r_qT, ycT, r_ycT, wk, wvv)
        if DBG_S <= 6:
            return
        ydT, r_ydT = G[4], r_G[4]
        S.dma("sp", d_rowb, [], [r_rowb], lambda: nc.sync.dma_start(out=rowb[:, :], in_=rowb_d[l, :, 1024:4096]))
        S.dma("pool", d_in, [], [r_gmraw], lambda: nc.gpsimd.dma_start(out=gmw_raw[:], in_=gmw_d[l].rearrange("g t s -> t g s")))
        for g in range(8):
            ps, pr = psum()
            psb = ps[:, 0:64].bitcast(BF16)
            S.op("pe", [r_gmraw, r_cst], [pr], lambda psb=psb, g=g: nc.tensor.transpose(psb[:, 0:128], gmw_raw[:, g, :], identb[:]))
            S.op("dve", [pr, r_cst], [r_gmw], lambda psb=psb, g=g: nc.vector.tensor_tensor(
                out=gmwT[:, g, :], in0=psb[:, 0:128], in1=Umat, op=ALU.mult))
        for bi in range(4):
            wv, wr = win(l, UV0 + bi * 256)
            for cc in range(2):
                c = bi * 2 + cc
                ps, pr = proj_ws(wv, wr, cc * 128, 128, N)
                S.op("act", [pr], [r_ydT], lambda ps=ps, c=c: nc.scalar.activation(out=ydT[:, c, 0:N], in_=ps[:, 0:N], func=AF.Gelu_apprx_tanh))
        wv_ = [win(l, UV0 + 1024 + i * 256) for i in range(4)]
        for b in range(SB):
            def vout(tt, rtt, b=b):
                S.dma("sp", d_out, [rtt], [], lambda: nc.sync.dma_start(out=sgmv_o[l, b], in_=tt[0:4, :]))
            gmlp_chunk(l, 4 * b, 4, wv_, ydT, r_ydT, lng, lnb, bsrow, gmwT, vout)
        if DBG_S <= 7:
            return
        mT, r_mT = G[5], r_G[5]
        brs = ((G[1], r_G[1]), (G[2], r_G[2]), (G[3], r_G[3]), (G[4], r_G[4]))
        macc = (acc[0], acc[1])
        r_macc = (r_acc[0], r_acc[1])
        for bi in range(4):
            for i in range(4):
                wg = win(l, GT0 + i * 1024 + bi * 256)
                wb = wload(w_br_d[l, i, :, bi * 256:bi * 256 + 256].rearrange("(k p) c -> p k c", p=128), 8, 256, key=("br", l, i, bi))
                for cc in range(2):
                    c = bi * 2 + cc
                    psg, prg = proj_ws(wg[0], wg[1], cc * 128, 128, N)
                    psp, prp = proj_ws(wb[0], wb[1], cc * 128, 128, N, rhs=brs[i][0], rres=brs[i][1])
                    S.op("act", [prg], [r_stg[2]], lambda psg=psg: nc.scalar.activation(out=stg[2][:, 0:N], in_=psg[:, 0:N], func=AF.Sigmoid))
                    if i == 0:
                        S.op("dve", [prp, r_stg[2]], [r_macc[cc]], lambda psp=psp, cc=cc: nc.vector.tensor_tensor(
                            out=macc[cc][:, 0:N], in0=psp[:, 0:N], in1=stg[2][:, 0:N], op=ALU.mult))
                    else:
                        S.op("dve", [prp, r_stg[2]], [r_acc[2]], lambda psp=psp: nc.vector.tensor_tensor(
                            out=acc[2][:, 0:N], in0=psp[:, 0:N], in1=stg[2][:, 0:N], op=ALU.mult))
                        if i < 3:
                            S.op("dve", [r_macc[cc], r_acc[2]], [r_macc[cc]], lambda cc=cc: nc.vector.tensor_tensor(
                                out=macc[cc][:, 0:N], in0=macc[cc][:, 0:N], in1=acc[2][:, 0:N], op=ALU.add))
                        else:
                            S.op("dve", [r_macc[cc], r_acc[2]], [r_mT], lambda c=c, cc=cc: nc.vector.tensor_tensor(
                                out=mT[:, c, 0:N], in0=macc[cc][:, 0:N], in1=acc[2][:, 0:N], op=ALU.add))
        for bi in range(4):
            wv, wr = wload(wcols(w_o_d, l, bi * 256, 256), 8, 256, key=("o", l, bi))
            for cc in range(2):
                c = bi * 2 + cc
                ps, pr = proj_ws(wv, wr, cc * 128, 128, N, rhs=mT, rres=r_mT)
                resid_s(ps, pr, c, 2)
        if DBG_S <= 8:
            return
        mod_norm_s(1)
        gat = (G[1], G[2], G[3])
        r_gat = (r_G[1], r_G[2], r_G[3])
        for blk in range(11):
            wa = wload(wcols(w_up_d, l, blk * 256, 256), 8, 256, key=("ua", l, blk))
            wg_ = wload(wcols(w_up_d, l, DFF + blk * 256, 256), 8, 256, key=("ug", l, blk))
            for cc in range(2):
                i = blk * 2 + cc
                outs = []
                for which, (wv, wr) in enumerate((wa, wg_)):
                    ci = which * 22 + i
                    ps, pr = proj_ws(wv, wr, cc * 128, 128, N)
                    taps = [pp[:, l, PP_FW + j * 44 + ci:PP_FW + j * 44 + ci + 1] for j in range(3)]
                    ac, rac = conv_s(128, ps, pr, taps, pp[:, l, PP_FB + ci:PP_FB + ci + 1], [r_pp], sffn[:, ci, :, :], r_sffn, which)
                    outs.append((ac, rac))
                S.op("act", [outs[0][1]], [r_acc[2]], lambda a=outs[0][0]: nc.scalar.activation(out=acc[2][:, 0:N], in_=a[:, 0:N], func=AF.Silu))
                S.op("dve", [r_acc[2], outs[1][1]], [r_gat[i // 8]], lambda g_=outs[1][0], i=i: nc.vector.tensor_tensor(
                    out=gat[i // 8][:, i % 8, 0:N], in0=acc[2][:, 0:N], in1=g_[:, 0:N], op=ALU.mult))
        S.dma("sp", d_out, [r_sffn], [], lambda: nc.sync.dma_start(out=sffn_o[l], in_=sffn[:]))
        for c in range(8):
            wh = [wload(w_dn_d[l, hh * 1408:(hh + 1) * 1408, c * 128:(c + 1) * 128].rearrange("(k p) c -> p k c", p=128), 11, 128, key=("dn", l, c, hh)) for hh in range(2)]
            ps, pr = psum()

            def f(wh=wh, ps=ps):
                for k in range(22):
                    last = nc.tensor.matmul(ps[:, 0:N], lhsT=wh[k // 11][0][:, k % 11, :], rhs=gat[k // 8][:, k % 8, 0:N], start=(k == 0), stop=(k == 21))
                return last
            S.op("pe", [wh[0][1], wh[1][1]] + list(r_gat), [pr], f)
            resid_s(ps, pr, c, 5)

    for ti in range(NT):
        S.dma("sp", d_x, [], [r_x], lambda ti=ti: nc.sync.dma_start(
            out=xT[:], in_=xT_d[:, ti * TT:(ti + 1) * TT].rearrange("(c p) t -> p c t", p=128)))
        for l in range(L):
            prompt_tile_layer(ti, l)
        ps, pr = psum()
        sq = G[5]
        S.op("act", [r_x], [r_G[5]], lambda: nc.scalar.activation(out=sq[:, :, :], in_=xT[:, :, :], func=AF.Square))

        def ff(ps=ps):
            for k in range(8):
                last = nc.tensor.matmul(ps[:, 0:TT], lhsT=onesb[:], rhs=sq[:, k, :], start=(k == 0), stop=(k == 7))
            return last
        S.op("pe", [r_G[5], r_cst], [pr], ff)
        rstd_of(ps[:, 0:TT], 128, acc[2][:, 0:TT], [pr, r_cst], [r_acc[2]], 1.0 / D)
        for c in range(8):
            yo, ryo = acc[c % 2], r_acc[c % 2]
            S.op("dve", [r_x, r_pp, r_acc[2]], [ryo], lambda c=c, yo=yo: nc.vector.scalar_tensor_tensor(
                out=yo[:, 0:TT], in0=xT[:, c, :], scalar=pp[:, 0, PP_GFIN + c:PP_GFIN + c + 1], in1=acc[2][:, 0:TT], op0=ALU.mult, op1=ALU.mult))
            S.dma("sp", d_out, [ryo], [], lambda ti=ti, yo=yo, c=c: nc.sync.dma_start(
                out=yT_d[c * 128:(c + 1) * 128, ti * TT:(ti + 1) * TT], in_=yo[:, 0:TT]))
    if SB:
        NS = 4 * SB
        S.dma("sp", d_par, [], [r_csT], lambda: nc.sync.dma_start(out=csT[:], in_=csT_d))
        S.op("act", [r_csT], [r_csT], lambda: nc.scalar.activation(out=csb[:], in_=csT[:], func=AF.Silu))
        S.dma("sp", d_x, [], [r_x], lambda: nc.sync.dma_start(out=xT[:, :, 0:NS], in_=xsT_d.rearrange("(c p) t -> p c t", p=128)))
        for l in range(L):
            sample_tile_layer(l)
        ps, pr = psum()
        sq = G[5]
        S.op("act", [r_x], [r_G[5]], lambda: nc.scalar.activation(out=sq[:, :, 0:NS], in_=xT[:, :, 0:NS], func=AF.Square))

        def ffs(ps=ps):
            for k in range(8):
                last = nc.tensor.matmul(ps[:, 0:NS], lhsT=onesb[:], rhs=sq[:, k, 0:NS], start=(k == 0), stop=(k == 7))
            return last
        S.op("pe", [r_G[5], r_cst], [pr], ffs)
        rstd_of(ps[:, 0:NS], 128, acc[2][:, 0:NS], [pr, r_cst], [r_acc[2]], 1.0 / D)
        for c in range(8):
            yo, ryo = acc[c % 2], r_acc[c % 2]
            S.op("dve", [r_x, r_pp, r_acc[2]], [ryo], lambda c=c, yo=yo: nc.vector.scalar_tensor_tensor(
                out=yo[:, 0:NS], in0=xT[:, c, 0:NS], scalar=pp[:, 0, PP_GFIN + c:PP_GFIN + c + 1], in1=acc[2][:, 0:NS], op0=ALU.mult, op1=ALU.mult))
            S.dma("sp", d_out, [ryo], [], lambda yo=yo, c=c: nc.sync.dma_start(out=ysT_d[c * 128:(c + 1) * 128, :], in_=yo[:, 0:NS]))
    S.finish()
    return nc


def _consts():
    cst = np.zeros((128, 772), np.float32)
    jj = np.arange(132)[None, :]
    ds = 128 + np.arange(128)[:, None] - jj
    cst[:, 640:772] = np.where((ds >= 0) & (ds <= 128), ds, 1e6)
    cst[:, 0:128] = np.eye(128, dtype=np.float32)
    s_ = np.arange(128)[:, None]
    t_ = np.arange(128)[None, :]
    cst[:, 128:256] = (s_ <= t_).astype(np.float32)
    cst[:, 256:384] = np.where(s_ > t_, NEG, 0.0)
    kpos = np.arange(256)[None, :] - 128
    dist = s_ - kpos
    cst[:, 384:640] = np.where((dist >= 0) & (dist <= 128), dist, 1e6)
    return cst


def _pp(w, L):
    pp = np.zeros((128, L, NPP), np.float32)
    pp64 = np.zeros((64, L, 40), np.float32)
    for l in range(L):
        pp[:, l, PP_BADA:PP_BADA + 48] = w["b_ada"][l].reshape(48, 128).T
        pp[:, l, PP_GMIX:PP_GMIX + 8] = w["g_norm_mix"][l].reshape(8, 128).T
        pp[:, l, PP_GFFN:PP_GFFN + 8] = w["g_norm_ffn"][l].reshape(8, 128).T
        for j in range(4):
            pp[:, l, PP_SCW + j * 8:PP_SCW + j * 8 + 8] = w["ssd_conv_w"][l][j, :1024].reshape(8, 128).T
            pp64[:, l, j * 8:j * 8 + 8] = w["ssd_conv_w"][l][j, 1024:].reshape(8, 64).T
        pp[:, l, PP_SCB:PP_SCB + 8] = w["ssd_conv_b"][l][:1024].reshape(8, 128).T
        pp64[:, l, 32:40] = w["ssd_conv_b"][l][1024:].reshape(8, 64).T
        for j in range(3):
            pp[:, l, PP_SHW + j * 8:PP_SHW + j * 8 + 8] = w["sc_conv_w"][l][j].reshape(8, 128).T
            pp[:, l, PP_FW + j * 44:PP_FW + j * 44 + 44] = w["ffn_conv_w"][l][j].reshape(44, 128).T
        pp[:, l, PP_FB:PP_FB + 44] = w["ffn_conv_b"][l].reshape(44, 128).T
        pp[:, l, PP_GFIN:PP_GFIN + 8] = w["g_final"].reshape(8, 128).T
    return pp, pp64


def _rows(w, L):
    rows = np.zeros((128, L, 64), np.float32)
    rowb = np.zeros((L, 128, 4096), np.float32)
    for l in range(L):
        rows[:, l, 0:16] = w["ssd_dt_bias"][l][None]
        rows[:, l, 16:32] = w["ssd_a_log"][l][None]
        rows[:, l, 32:48] = w["ssd_d"][l][None]
        rows[:, l, 48:64] = w["attn_sinks"][l][None]
        rowb[l, :, 0:1024] = w["ssd_norm_g"][l][None]
        rowb[l, :, 1024:2048] = w["gm_ln_g"][l][None]
        rowb[l, :, 2048:3072] = w["gm_ln_b"][l][None]
        rowb[l, :, 3072:4096] = w["gm_b_s"][l].reshape(1024)[None]
    return rows, rowb


def shared_maps(w, L):
    pp, pp64 = _pp(w, L)
    rows, rowb = _rows(w, L)
    m = {"pp": pp, "pp64": pp64, "rows": rows, "rowb": rowb, "cst": _consts()}
    for k in ("w_ada", "w_in", "w_branch", "w_o", "ffn_w_up", "ffn_w_down", "gm_w_s"):
        m[k] = np.ascontiguousarray(w[k][:L], dtype=np.float32)
    return m


def core_map(shared, xp_b, cp_b):
    m = dict(shared)
    m["xT"] = np.ascontiguousarray(xp_b.T)
    m["cT"] = np.ascontiguousarray(cp_b.reshape(8, 128).T)[:, :, None].copy()
    return m


def unpack_prompt(r, L):
    o = {}
    o["y"] = np.ascontiguousarray(r["yT"].T)
    o["ssm"] = np.ascontiguousarray(r["p_ssm"].reshape(L, 64, 4, 4, 64).transpose(0, 2, 3, 4, 1)).reshape(L, 16, 64, 64)
    xs = r["p_xbc"][..., 0:3].transpose(0, 3, 2, 1).reshape(L, 3, 1024)
    bc = r["p_bc"][..., 0:3].transpose(0, 3, 2, 1).reshape(L, 3, 512)
    o["ssd_conv"] = np.concatenate([xs, bc], axis=2)
    o["sc_conv"] = r["p_sc"].transpose(0, 3, 2, 1).reshape(L, 2, 1024)
    o["k"] = r["p_k"].reshape(L, 128, 4, 64)
    o["v"] = r["p_v"].reshape(L, 128, 4, 64)
    o["ffn_conv"] = r["p_ffn"][:, :, 0:44, :].transpose(0, 3, 2, 1).reshape(L, 2, 5632)
    return o


def sample_map(m, inp, bs, L):
    SBn = bs.stop - bs.start
    m["xsT"] = np.ascontiguousarray(inp["x_sample"][bs].reshape(4 * SBn, D).T)
    m["csT"] = np.ascontiguousarray(inp["c_sample"][bs].reshape(SBn, 8, 128).transpose(2, 1, 0))
    st = inp["state_ssm"][:L, bs]
    m["s_ssm_in"] = np.ascontiguousarray(st.reshape(L, SBn, 4, 4, 64, 64).transpose(0, 1, 5, 2, 3, 4)).reshape(L, SBn, 64, 4, 256)
    sc = inp["state_ssd_conv"][:L, bs]
    m["s_xbc_in"] = np.ascontiguousarray(sc[..., :1024].reshape(L, SBn, 3, 8, 128).transpose(0, 4, 3, 1, 2))
    m["s_bc_in"] = np.ascontiguousarray(sc[..., 1024:].reshape(L, SBn, 3, 8, 64).transpose(0, 4, 3, 1, 2))
    m["s_sc_in"] = np.ascontiguousarray(inp["state_sc_conv"][:L, bs].reshape(L, SBn, 2, 8, 128).transpose(0, 4, 3, 1, 2))
    m["s_ffn_in"] = np.ascontiguousarray(inp["state_ffn_conv"][:L, bs].reshape(L, SBn, 2, 44, 128).transpose(0, 4, 3, 1, 2))
    ck = inp["cache_k"][:L, bs]
    kt = ck.transpose(0, 1, 4, 3, 2)
    m["ckT"] = np.ascontiguousarray(np.concatenate([kt, kt], axis=2))
    m["ck"] = np.ascontiguousarray(ck.reshape(L, SBn, 128, 256))
    m["cv"] = np.ascontiguousarray(inp["cache_v"][:L, bs].reshape(L, SBn, 128, 256))
    return m


def unpack_sample(r, L, SBn):
    o = {}
    o["y"] = np.ascontiguousarray(r["ysT"].T).reshape(SBn, 4, D)
    o["ssm"] = np.ascontiguousarray(r["s_ssm_o"].reshape(L, SBn, 64, 4, 4, 64).transpose(0, 1, 3, 4, 5, 2)).reshape(L, SBn, 16, 64, 64)
    xs = r["s_xbc_o"].transpose(0, 3, 4, 2, 1).reshape(L, SBn, 3, 1024)
    bc = r["s_bc_o"].transpose(0, 3, 4, 2, 1).reshape(L, SBn, 3, 512)
    o["ssd_conv"] = np.concatenate([xs, bc], axis=3)
    o["sc_conv"] = r["s_sc_o"].transpose(0, 3, 4, 2, 1).reshape(L, SBn, 2, 1024)
    o["k"] = r["s_k_o"].reshape(L, SBn, 128, 4, 64)
    o["v"] = r["s_v_o"].reshape(L, SBn, 128, 4, 64)
    o["ffn_conv"] = r["s_ffn_o"].transpose(0, 3, 4, 2, 1).reshape(L, SBn, 2, 5632)
    o["gm_v"] = r["s_gmv_o"].reshape(L, SBn, 4, 1024)
    return o


def kernel(**inputs):
    inp = {k: np.asarray(v) for k, v in inputs.items()}
    L = 4
    B = inp["x_prompt"].shape[0]
    SEQ = inp["x_prompt"].shape[1]
    NT = SEQ // TT
    DB = inp["x_sample"].shape[0]
    SBn = DB // 8
    shared = shared_maps(inp, L)
    nc = build(NT, L, SBn)
    in_maps = []
    for core in range(8):
        b = core % B
        m = core_map(shared, inp["x_prompt"][b], inp["c_prompt"][b])
        sample_map(m, inp, slice(core * SBn, (core + 1) * SBn), L)
        in_maps.append(m)
    res = run_bass_kernel_spmd(nc, in_maps, core_ids=list(range(8)))
    rs = [{k: np.asarray(v) for k, v in r.items()} for r in res.results]
    po = [unpack_prompt(rs[b], L) for b in range(B)]
    so = [unpack_sample(rs[c], L, SBn) for c in range(8)]
    f32 = np.float32
    y_prompt = np.stack([p["y"] for p in po], 0).astype(f32)
    y_sample = np.concatenate([s_["y"] for s_ in so], 0).astype(f32)

    def pst(k):
        return np.ascontiguousarray(np.stack([p[k] for p in po], 1)).astype(f32)

    def sst(k):
        return np.ascontiguousarray(np.concatenate([s_[k] for s_ in so], 1)).astype(f32)
    return (y_prompt, y_sample, pst("ssm"), pst("ssd_conv"), pst("sc_conv"), pst("k"), pst("v"), pst("ffn_conv"),
            sst("ssm"), sst("ssd_conv"), sst("sc_conv"), sst("k"), sst("v"), sst("ffn_conv"), sst("gm_v"))
```

```python
import os
import numpy as np
import concourse.bass as bass
import concourse.mybir as mybir
from concourse.bass_utils import run_bass_kernel_spmd

F32 = mybir.dt.float32
BF16 = mybir.dt.bfloat16
AF = mybir.ActivationFunctionType
ALU = mybir.AluOpType
AX = mybir.AxisListType

D = 1024
TT = 512
NCH = TT // 128
WBC = 256
SLOPES = [2.0 ** (-8.0 * (h + 1) / 16) for h in range(16)]
EPS = 1e-6
XBC0, DT0, BCX0, Q0, K0, V0, UV0, GT0, DIN = 1024, 2560, 2576, 5648, 6672, 6928, 7184, 9232, 13328
DFF = 2816
PP_BADA, PP_GMIX, PP_GFFN, PP_SCW, PP_SCB, PP_SHW, PP_FW, PP_FB, PP_GFIN = 0, 48, 56, 64, 96, 104, 128, 260, 304
NPP = 312
NEG = -30000.0
DBG_STOP = int(os.environ.get('DBG_STOP', '99'))
DBG_ATT = int(os.environ.get('DBG_ATT', '99'))
DBG_ATTP = int(os.environ.get('DBG_ATTP', '99'))
DBG_S = int(os.environ.get('DBG_S', '99'))


class Res:
    __slots__ = ("name", "w", "r", "excl")

    def __init__(self, name, excl=False):
        self.name = name
        self.w = None
        self.r = {}
        self.excl = excl


class DSem:
    __slots__ = ("sem", "tot", "key", "keep")

    def __init__(self, nc, name):
        self.sem = nc.alloc_semaphore(name)
        self.tot = 0
        self.key = name
        self.keep = False


class Sched:
    def __init__(self, nc):
        self.nc = nc
        self.eng = {}
        self.seen = {}
        self.dsems = []
        self.dkeys = {}
        for name, h in (("pe", nc.tensor), ("act", nc.scalar), ("dve", nc.vector),
                        ("pool", nc.gpsimd), ("sp", nc.sync)):
            self.eng[name] = [h, nc.alloc_semaphore("s_" + name), 0]

    def dsem(self, name):
        d = DSem(self.nc, "d_" + name)
        self.dsems.append(d)
        self.dkeys[d.key] = d
        return d

    def _waits(self, eng, reads, writes):
        need = {}

        def add(ent):
            key, sem, val = ent
            if key in self.dkeys:
                val = self.dkeys[key].tot
            if key not in need or need[key][1] < val:
                need[key] = (sem, val)

        for r in reads:
            if r.w is not None:
                add(r.w)
            if r.excl:
                for key, (sem, val) in r.r.items():
                    if key != eng:
                        add((key, sem, val))
        for w in writes:
            if w.w is not None:
                add(w.w)
            for key, (sem, val) in w.r.items():
                add((key, sem, val))
        h = self.eng[eng][0]
        for key, (sem, val) in need.items():
            if self.seen.get((eng, key), 0) < val:
                h.wait_ge(sem, val)
                self.seen[(eng, key)] = val

    def op(self, eng, reads, writes, fn):
        E = self.eng[eng]
        self._waits(eng, reads, writes)
        inst = fn()
        E[2] += 1
        inst.then_inc(E[1], 1)
        for r in reads:
            r.r[eng] = (E[1], E[2])
        for w in writes:
            w.w = (eng, E[1], E[2])
            w.r = {}
        return inst

    def _auto_dsem(self, reads, writes):
        if writes:
            name = "ld_" + writes[0].name
        elif reads:
            name = "st_" + reads[0].name
        else:
            name = "dd"
        d = self.dkeys.get("d_" + name)
        if d is None:
            d = self.dsem(name)
        return d

    def dma(self, q, ds, reads, writes, fn, extra=()):
        if not getattr(ds, "keep", False):
            ds = self._auto_dsem(reads, writes)
        self._waits(q, reads, writes)
        for (sem_, val_, key_) in extra:
            if self.seen.get((q, key_), 0) < val_:
                self.eng[q][0].wait_ge(sem_, val_)
                self.seen[(q, key_)] = val_
        inst = fn()
        ds.tot += 16
        inst.then_inc(ds.sem, 16)
        for r in reads:
            r.r[ds.key] = (ds.sem, ds.tot)
        for w in writes:
            w.w = (ds.key, ds.sem, ds.tot)
            w.r = {}
        return inst

    def finish(self, eng="sp"):
        h = self.eng[eng][0]
        for d in self.dsems:
            if d.tot > 0:
                h.wait_ge(d.sem, d.tot)


def build(NT, L, SB, dbg=False):
    nc = bass.Bass("TRN2", target_bir_lowering=False)
    S = Sched(nc)
    NTOK = NT * TT

    def din(name, shape):
        return nc.dram_tensor(name, list(shape), F32, kind="ExternalInput").ap()

    def dout(name, shape):
        return nc.dram_tensor(name, list(shape), F32, kind="ExternalOutput").ap()

    xT_d = din("xT", [D, NTOK])
    cT_d = din("cT", [128, 8, 1])
    w_ada_d = din("w_ada", [L, D, 6 * D])
    w_in_d = din("w_in", [L, D, DIN])
    w_br_d = din("w_branch", [L, 4, D, D])
    w_o_d = din("w_o", [L, D, D])
    w_up_d = din("ffn_w_up", [L, D, 2 * DFF])
    w_dn_d = din("ffn_w_down", [L, DFF, D])
    pp_d = din("pp", [128, L, NPP])
    pp64_d = din("pp64", [64, L, 40])
    rows_d = din("rows", [128, L, 64])
    rowb_d = din("rowb", [L, 128, 4096])
    gmw_d = din("gm_w_s", [L, 8, 128, 128])
    cst_d = din("cst", [128, 772])

    yT_d = dout("yT", [D, NTOK])
    pssm_d = dout("p_ssm", [L, 64, 4, 256])
    pxbc_d = dout("p_xbc", [L, 128, 8, 4])
    pbc_d = dout("p_bc", [L, 64, 8, 4])
    psc_d = dout("p_sc", [L, 128, 8, 2])
    pk_d = dout("p_k", [L, 128, 256])
    pv_d = dout("p_v", [L, 128, 256])
    pffn_d = dout("p_ffn", [L, 128, 48, 2])

    if SB:
        xsT_d = din("xsT", [D, 4 * SB])
        csT_d = din("csT", [128, 8, SB])
        sssm_d = din("s_ssm_in", [L, SB, 64, 4, 256])
        sxbc_d = din("s_xbc_in", [L, 128, 8, SB, 3])
        sbc_d = din("s_bc_in", [L, 64, 8, SB, 3])
        ssc_d = din("s_sc_in", [L, 128, 8, SB, 2])
        sffn_d = din("s_ffn_in", [L, 128, 44, SB, 2])
        ckT_d = din("ckT", [L, SB, 128, 4, 128])
        ck_d = din("ck", [L, SB, 128, 256])
        cv_d = din("cv", [L, SB, 128, 256])
        ysT_d = dout("ysT", [D, 4 * SB])
        sssm_o = dout("s_ssm_o", [L, SB, 64, 4, 256])
        sxbc_o = dout("s_xbc_o", [L, 128, 8, SB, 3])
        sbc_o = dout("s_bc_o", [L, 64, 8, SB, 3])
        ssc_o = dout("s_sc_o", [L, 128, 8, SB, 2])
        sffn_o = dout("s_ffn_o", [L, 128, 44, SB, 2])
        sk_o = dout("s_k_o", [L, SB, 128, 256])
        sv_o = dout("s_v_o", [L, SB, 128, 256])
        sgmv_o = dout("s_gmv_o", [L, SB, 4, 1024])

    def sb(name, shape, dt=F32):
        return nc.alloc_sbuf_tensor(name, list(shape), dt)

    xT = sb("xTt", [128, 8, TT]); r_x = Res("x")
    mod = sb("mod", [128, L, 48, 1]); r_mod = Res("mod")
    gm = sb("gm", [128, L, 2, 8, 1]); r_gm = Res("gm")
    pp = sb("ppt", [128, L, NPP]); r_pp = Res("pp")
    pp64 = sb("pp64t", [64, L, 40]); r_pp64 = Res("pp64")
    rows = sb("rowst", [128, L, 64]); r_rows = Res("rows")
    rowb = sb("rowbt", [128, 3072]); r_rowb = Res("rowb")
    cst = sb("cstt", [128, 772]); r_cst = Res("cst")
    identb = sb("identb", [128, 128], BF16)
    onesb = sb("onesb", [128, 128], BF16)
    ident = cst[:, 0:128]
    Umat = cst[:, 128:256]
    NEGM = cst[:, 256:384]
    distm = cst[:, 384:640]
    dists = cst[:, 640:772]
    gmwT = sb("gmwT", [128, 8, 128], BF16); r_gmw = Res("gmwT")
    gmw_raw = sb("gmw_raw", [128, 8, 128], BF16); r_gmraw = Res("gmw_raw")
    ST = sb("ST", [64, L, 4, 256]); r_ST = [Res("ST%d" % l) for l in range(L)]
    STb = sb("STb", [64, 4, 256], BF16); r_STb1 = Res("STb"); r_STb = [r_STb1] * L
    cx = sb("cx", [128, L, 8, 4]); r_cx = [Res("cx%d" % l) for l in range(L)]
    cbc = sb("cbc", [64, L, 8, 4]); r_cbc = [Res("cbc%d" % l) for l in range(L)]
    csc = sb("csc", [128, L, 8, 2]); r_csc = [Res("csc%d" % l) for l in range(L)]
    cff = sb("cff", [128, L, 48, 2]); r_cff = [Res("cff%d" % l) for l in range(L)]
    kprev = sb("kprev", [128, L, 4, 128], BF16); r_kprev = [Res("kp%d" % l) for l in range(L)]
    vprev = sb("vprev", [128, L, 256], BF16); r_vprev = [Res("vp%d" % l) for l in range(L)]
    G = [sb("G%d" % i, [128, 8, TT], BF16) for i in range(6)]
    r_G = [Res("G%d" % i) for i in range(6)]
    hB, r_h = G[0], r_G[0]
    stg = [sb("stg%d" % i, [128, TT + 4]) for i in range(3)]
    r_stg = [Res("stg%d" % i) for i in range(3)]
    acc = [sb("acc%d" % i, [128, TT]) for i in range(3)]
    r_acc = [Res("acc%d" % i) for i in range(3)]
    BCT = G[2][0:64, :, :]; r_BCT = r_G[2]
    kT = sb("kT", [128, 4, TT], BF16); r_kT = Res("kT")
    vtok = sb("vtok", [128, NCH, 256], BF16); r_vtok = Res("vtok")
    tA = sb("tA", [128, 1024]); r_tA = Res("tA")
    tB = sb("tB", [128, 1024]); r_tB = Res("tB")
    tC = sb("tC", [128, 1024], BF16); r_tC = Res("tC")
    tD = sb("tD", [128, 1024], BF16); r_tD = Res("tD")
    tE = sb("tE", [128, 1024], BF16); r_tE = Res("tE")
    Eh = sb("Eh", [128, 16, 128], BF16); r_Eh = Res("Eh")
    Mh, r_Mh = Eh, r_Eh
    CBs = sb("CBs", [128, 4, 128], BF16); r_CBs = Res("CBs")
    Btok = sb("Btok", [128, 4, 64], BF16); r_Btok = Res("Btok")
    sm = sb("sm", [128, 256]); r_sm = Res("sm")
    r_aS, r_aP, r_aT = Res("attS"), Res("attP"), Res("attT")
    RV = {k: Res("sm_" + k) for k in ("dt", "dtA", "Acs", "nAcs", "eA", "dec", "wdec", "cdec", "ssq",
                                      "rinv", "mx", "nmx", "esk", "rsum", "ssum", "mean", "gssq", "rstd")}
    ktok, r_ktok = tB, r_tB

    d_in = d_par = d_rowb = d_out = d_x = None

    PS = [nc.alloc_psum_tensor("ps%d" % i, [128, 512], F32) for i in range(8)]
    r_PS = [Res("ps%d" % i, excl=True) for i in range(8)]
    psi = [0]

    def psum():
        i = psi[0]
        psi[0] = (i + 1) % 6
        return PS[i], r_PS[i]

    NSLOT = 7
    WS = [sb("ws%d" % i, [128, 2048], BF16) for i in range(NSLOT)]
    r_WS = [Res("ws%d" % i) for i in range(NSLOT)]
    d_WS = [S.dsem("ws%d" % i) for i in range(NSLOT)]
    for d_ in d_WS:
        d_.keep = True
    wsi = [0]

    NSCR = 111 * L + 2
    wscr = nc.dram_tensor("wscr", [NSCR, 128, 2048], BF16, kind="Internal").ap()
    scr = {}

    def wload(src, K, C, key=None):
        i = wsi[0]
        wsi[0] = (i + 1) % NSLOT
        flat = WS[i][:, 0:K * C]
        view = flat.rearrange("p (k c) -> p k c", k=K)
        if key is not None and key in scr:
            idx, ent = scr[key]
            S.dma("sp", d_WS[i], [], [r_WS[i]], lambda: nc.sync.dma_start(out=flat, in_=wscr[idx, :, 0:K * C]), extra=[ent])
            return view, r_WS[i]
        S.dma("pool", d_WS[i], [], [r_WS[i]], lambda: nc.gpsimd.dma_start(out=view, in_=src))
        if key is not None:
            idx = len(scr)
            assert idx < NSCR
            S.dma("sp", None, [r_WS[i]], [], lambda: nc.sync.dma_start(out=wscr[idx, :, 0:K * C], in_=flat))
            d = S.dkeys["d_st_ws%d" % i]
            scr[key] = (idx, (d.sem, d.tot, d.key))
        return view, r_WS[i]

    def wcols(wd, l, c0, C):
        return wd[l, :, c0:c0 + C].rearrange("(k p) c -> p k c", p=128)

    ev = [0]

    def eng2():
        ev[0] ^= 1
        return "act" if ev[0] else "dve"

    def copy(eng, out, in_, reads, writes):
        if eng == "act":
            return S.op("act", reads, writes, lambda: nc.scalar.copy(out=out, in_=in_))
        return S.op(eng, reads, writes, lambda: (nc.vector if eng == "dve" else nc.gpsimd).tensor_copy(out=out, in_=in_))

    S.dma("sp", d_par, [], [r_pp], lambda: nc.sync.dma_start(out=pp[:], in_=pp_d))
    S.dma("sp", d_par, [], [r_pp64], lambda: nc.sync.dma_start(out=pp64[:], in_=pp64_d))
    S.dma("sp", d_par, [], [r_rows], lambda: nc.sync.dma_start(out=rows[:], in_=rows_d))
    S.dma("sp", d_par, [], [r_cst], lambda: nc.sync.dma_start(out=cst[:], in_=cst_d))
    S.op("dve", [r_cst], [r_cst], lambda: nc.vector.tensor_copy(out=identb[:], in_=ident))
    S.op("dve", [], [r_cst], lambda: nc.vector.memset(onesb[:], 1.0))
    dupI = sb("dupI", [64, 128], BF16)
    S.op("dve", [r_cst], [r_cst], lambda: nc.vector.tensor_copy(out=dupI[:, 0:64], in_=ident[0:64, 0:64]))
    S.op("dve", [r_cst], [r_cst], lambda: nc.vector.tensor_copy(out=dupI[:, 64:128], in_=ident[0:64, 0:64]))
    ones32 = sb("ones32", [128, 128])
    S.op("dve", [], [r_cst], lambda: nc.vector.memset(ones32[:], 1.0))
    S.op("act", [r_rows], [r_rows], lambda: nc.scalar.activation(out=rows[:, :, 16:32], in_=rows[:, :, 16:32], func=AF.Exp))
    S.op("dve", [r_rows], [r_rows], lambda: nc.vector.tensor_scalar(out=rows[:, :, 16:32], in0=rows[:, :, 16:32], scalar1=-1.0, scalar2=None, op0=ALU.mult))
    for t_, r_ in ((ST, r_ST), (cx, r_cx), (cbc, r_cbc), (csc, r_csc), (cff, r_cff)):
        S.op("dve", [], list(r_), lambda t_=t_: nc.vector.memset(t_[:], 0.0))

    NCc = 1
    cTt = sb("cTt", [128, 8, NCc]); r_cT = Res("cT")
    cTb = sb("cTb", [128, 8, NCc], BF16)
    S.dma("sp", d_par, [], [r_cT], lambda: nc.sync.dma_start(out=cTt[:], in_=cT_d))
    S.op("act", [r_cT], [r_cT], lambda: nc.scalar.activation(out=cTb[:], in_=cTt[:], func=AF.Silu))

    def ada_layer(l, cb, r_cb, ncol, mod_t, r_mod_t, gm_t, r_gm_t, lidx):
        for blk in range(24):
            wv, wr = wload(wcols(w_ada_d, l, blk * 256, 256), 8, 256)
            ps, pr = psum()

            def f(wv=wv, ps=ps):
                for i in range(2):
                    for k in range(8):
                        last = nc.tensor.matmul(ps[:, i * ncol:(i + 1) * ncol], lhsT=wv[:, k, i * 128:(i + 1) * 128],
                                                rhs=cb[:, k, :], start=(k == 0), stop=(k == 7))
                return last
            S.op("pe", [wr, r_cb], [pr], f)
            S.op("dve", [pr, r_pp], [r_mod_t], lambda ps=ps, blk=blk: nc.vector.tensor_tensor(
                out=mod_t[:, lidx, blk * 2:blk * 2 + 2, :], in0=ps[:, 0:2 * ncol].rearrange("p (i j) -> p i j", i=2),
                in1=pp[:, l, PP_BADA + blk * 2:PP_BADA + blk * 2 + 2].unsqueeze(2).to_broadcast([128, 2, ncol]), op=ALU.add))
        for which, (sc_i, gcol) in enumerate(((1, PP_GMIX), (4, PP_GFFN))):
            S.op("dve", [r_mod_t, r_pp], [r_gm_t], lambda which=which, sc_i=sc_i, gcol=gcol: nc.vector.scalar_tensor_tensor(
                out=gm_t[:, lidx, which, :, :], in0=mod_t[:, lidx, sc_i * 8:sc_i * 8 + 8, :], scalar=1.0,
                in1=pp[:, l, gcol:gcol + 8].unsqueeze(2).to_broadcast([128, 8, ncol]), op0=ALU.add, op1=ALU.mult))

    for l in range(L):
        ada_layer(l, cTb, r_cT, 1, mod, r_mod, gm, r_gm, l)

    epsc = sb("epsc", [128, 1])
    S.op("dve", [], [r_cst], lambda: nc.vector.memset(epsc[:], EPS))

    def rstd_of(ss_ap, n, out_ap, reads, writes, scale):
        S.op("act", reads, writes, lambda: nc.scalar.activation(out=out_ap, in_=ss_ap, func=AF.Ln, bias=epsc[0:n, :], scale=scale))
        S.op("act", writes, writes, lambda: nc.scalar.activation(out=out_ap, in_=out_ap, func=AF.Exp, scale=-0.5))

    def mod_norm(l, which, N):
        sh_i = 0 if which == 0 else 3
        ps, pr = psum()
        sq = G[5]
        S.op("act", [r_x], [r_G[5]], lambda: nc.scalar.activation(out=sq[:, :, 0:N], in_=xT[:, :, 0:N], func=AF.Square))

        def f():
            for k in range(8):
                last = nc.tensor.matmul(ps[:, 0:N], lhsT=onesb[:], rhs=sq[:, k, 0:N], start=(k == 0), stop=(k == 7))
            return last
        S.op("pe", [r_G[5], r_cst], [pr], f)
        rs = acc[2]
        rstd_of(ps[:, 0:N], 128, rs[:, 0:N], [pr, r_cst], [r_acc[2]], 1.0 / D)
        for c in range(8):
            S.op("dve", [r_x, r_gm, r_acc[2]], [r_acc[c % 2]], lambda c=c: nc.vector.scalar_tensor_tensor(
                out=acc[c % 2][:, 0:N], in0=xT[:, c, 0:N], scalar=gm[:, l, which, c, 0:1], in1=rs[:, 0:N],
                op0=ALU.mult, op1=ALU.mult))
            S.op("act", [r_acc[c % 2], r_mod], [r_h], lambda c=c: nc.scalar.activation(
                out=hB[:, c, 0:N], in_=acc[c % 2][:, 0:N], func=AF.Identity, bias=mod[:, l, sh_i * 8 + c, 0:1], scale=1.0))

    def proj_ws(wv, wr, col, M, N, rhs=None, rres=None, nk=8):
        rhs = hB if rhs is None else rhs
        rres = r_h if rres is None else rres
        ps, pr = psum()

        def f():
            for k in range(nk):
                last = nc.tensor.matmul(ps[0:M, 0:N], lhsT=wv[:, k, col:col + M], rhs=rhs[:, k, 0:N],
                                        start=(k == 0), stop=(k == nk - 1))
            return last
        S.op("pe", [wr, rres], [pr], f)
        return ps, pr

    def proj_as(wlist, C, t0, T):
        ps, pr = psum()

        def f():
            for bi, (wv, wr) in enumerate(wlist):
                cw = min(256, C - bi * 256)
                for k in range(8):
                    last = nc.tensor.matmul(ps[0:T, bi * 256:bi * 256 + cw], lhsT=hB[:, k, t0:t0 + T], rhs=wv[:, k, 0:cw],
                                            start=(k == 0), stop=(k == 7))
            return last
        S.op("pe", [w[1] for w in wlist] + [r_h], [pr], f)
        return ps, pr

    def conv_fm(P, ps, pr, N, carry_ap, r_carry, taps, bias_ap, preads, si):
        K = len(taps)
        CW = K - 1
        st, rst = stg[si], r_stg[si]
        ac, rac = acc[si], r_acc[si]
        S.op("act", [pr], [rst], lambda: nc.scalar.copy(out=st[0:P, CW:CW + N], in_=ps[0:P, 0:N]))
        S.op("dve", [r_carry], [rst], lambda: nc.vector.tensor_copy(out=st[0:P, 0:CW], in_=carry_ap))
        S.op("dve", [rst], [r_carry], lambda: nc.vector.tensor_copy(out=carry_ap, in_=st[0:P, N:N + CW]))
        if bias_ap is not None:
            S.op("dve", [rst] + preads, [rac], lambda: nc.vector.tensor_scalar(
                out=ac[0:P, 0:N], in0=st[0:P, 0:N], scalar1=taps[0], scalar2=bias_ap, op0=ALU.mult, op1=ALU.add))
        else:
            S.op("dve", [rst] + preads, [rac], lambda: nc.vector.tensor_scalar(
                out=ac[0:P, 0:N], in0=st[0:P, 0:N], scalar1=taps[0], scalar2=None, op0=ALU.mult))
        for j in range(1, K):
            S.op("dve", [rst, rac] + preads, [rac], lambda j=j: nc.vector.scalar_tensor_tensor(
                out=ac[0:P, 0:N], in0=st[0:P, j:j + N], scalar=taps[j], in1=ac[0:P, 0:N], op0=ALU.mult, op1=ALU.add))
        return ac, rac

    def win(l, c0, C=256):
        return wload(wcols(w_in_d, l, c0, C), 8, C, key=("in", l, c0))

    def run(g):
        for _ in g:
            pass

    def interleave(g1, g2):
        live = [g1, g2]
        while live:
            for g in list(live):
                try:
                    next(g)
                except StopIteration:
                    live.remove(g)

    def prompt_tile_layer(ti, l):
        N = TT
        last_tile = (ti == NT - 1)
        S.dma("sp", d_rowb, [], [r_rowb], lambda: nc.sync.dma_start(out=rowb[:, 0:1024], in_=rowb_d[l, :, 0:1024]))
        normg = rowb[:, 0:1024]
        lng = rowb[:, 0:1024]
        lnb = rowb[:, 1024:2048]
        bsrow = rowb[:, 2048:3072]
        mod_norm(l, 0, N)
        xsT, r_xsT = G[3], r_G[3]
        for bi in range(4):
            wv, wr = win(l, XBC0 + bi * 256)
            for cc in range(2):
                c = bi * 2 + cc
                ps, pr = proj_ws(wv, wr, cc * 128, 128, N)
                taps = [pp[:, l, PP_SCW + j * 8 + c:PP_SCW + j * 8 + c + 1] for j in range(4)]
                ac, rac = conv_fm(128, ps, pr, N, cx[:, l, c, 0:3], r_cx[l], taps, pp[:, l, PP_SCB + c:PP_SCB + c + 1], [r_pp], c % 2)
                S.op("act", [rac], [r_xsT], lambda ac=ac, c=c: nc.scalar.activation(out=xsT[:, c, 0:N], in_=ac[:, 0:N], func=AF.Silu))
        for bi in range(2):
            wv, wr = win(l, XBC0 + 1024 + bi * 256)
            for ee in range(4):
                e = bi * 4 + ee
                ps, pr = proj_ws(wv, wr, ee * 64, 64, N)
                taps = [pp64[:, l, j * 8 + e:j * 8 + e + 1] for j in range(4)]
                ac, rac = conv_fm(64, ps, pr, N, cbc[:, l, e, 0:3], r_cbc[l], taps, pp64[:, l, 32 + e:33 + e], [r_pp64], e % 2)
                S.op("act", [rac], [r_BCT], lambda ac=ac, e=e: nc.scalar.activation(out=BCT[:, e, 0:N], in_=ac[0:64, 0:N], func=AF.Silu))
        qT, r_qT = G[5], r_G[5]
        wk = win(l, K0)
        wvv = win(l, V0)
        for hk in range(4):
            ps, pr = proj_ws(wk[0], wk[1], hk * 64, 64, N)
            ktmp, r_ktmp = tC, r_tC
            copy("act", ktmp[0:64, 0:N], ps[0:64, 0:N], [pr], [r_ktmp])
            ps2, pr2 = psum()
            S.op("pe", [r_ktmp, r_cst], [pr2], lambda ps2=ps2: nc.tensor.matmul(ps2[:, 0:N], lhsT=dupI[:, :], rhs=ktmp[0:64, 0:N], start=True, stop=True))
            copy("dve", kT[:, hk, 0:N], ps2[:, 0:N], [pr2], [r_kT])
        for j in range(NCH):
            ps, pr = proj_as([wk, wvv], 512, j * 128, 128)
            if last_tile and j == NCH - 1:
                S.op("act", [pr], [r_ktok], lambda ps=ps: nc.scalar.copy(out=ktok[:, 0:512], in_=ps[:, :]))
                S.dma("sp", d_out, [r_ktok], [], lambda: nc.sync.dma_start(out=pk_d[l], in_=ktok[:, 0:256]))
                S.dma("sp", d_out, [r_ktok], [], lambda: nc.sync.dma_start(out=pv_d[l], in_=ktok[:, 256:512]))
            S.op("dve", [pr], [r_vtok], lambda j=j, ps=ps: nc.vector.tensor_copy(out=vtok[:, j, :], in_=ps[:, 256:512]))
        for bi in range(4):
            wv, wr = win(l, Q0 + bi * 256)
            for cc in range(2):
                c = bi * 2 + cc
                ps, pr = proj_ws(wv, wr, cc * 128, 128, N)
                S.op("act", [pr], [r_qT], lambda ps=ps, c=c: nc.scalar.activation(out=qT[:, c, 0:N], in_=ps[:, 0:N], func=AF.Identity, scale=0.125))
        ycT, r_ycT = G[3], r_G[3]
        wz = [win(l, i * 256) for i in range(4)]
        wdt = [win(l, DT0, 256)]
        S.op("act", [r_ST[l]], [r_STb1], lambda: nc.scalar.copy(out=STb[:, :, :], in_=ST[:, l, :, :]))
        yaT, r_yaT = G[1], r_G[1]
        for j in range(NCH):
            interleave(ssd_chunk(l, j * 128, 128, wz, wdt, xsT, r_xsT, yaT, r_yaT, normg),
                       attn_chunk(l, ti, j, qT, r_qT, ycT, r_ycT))
        S.op("dve", [r_kT], [r_kprev[l]], lambda: nc.vector.tensor_copy(out=kprev[:, l, :, :], in_=kT[:, :, N - 128:N]))
        S.op("dve", [r_vtok], [r_vprev[l]], lambda: nc.vector.tensor_copy(out=vprev[:, l, :], in_=vtok[:, NCH - 1, :]))
        if last_tile:
            S.dma("sp", d_out, [r_ST[l]], [], lambda: nc.sync.dma_start(out=pssm_d[l], in_=ST[:, l, :, :]))
            S.dma("sp", d_out, [r_cx[l]], [], lambda: nc.sync.dma_start(out=pxbc_d[l], in_=cx[:, l, :, :]))
            S.dma("sp", d_out, [r_cbc[l]], [], lambda: nc.sync.dma_start(out=pbc_d[l], in_=cbc[:, l, :, :]))
        ybT, r_ybT = G[2], r_G[2]
        for bi in range(4):
            wb = [win(l, BCX0 + part * 1024 + bi * 256) for part in range(3)]
            for cc in range(2):
                c = bi * 2 + cc
                psB, prB = proj_ws(wb[0][0], wb[0][1], cc * 128, 128, N)
                psC, prC = proj_ws(wb[1][0], wb[1][1], cc * 128, 128, N)
                psX, prX = proj_ws(wb[2][0], wb[2][1], cc * 128, 128, N)
                st, rst = stg[2], r_stg[2]
                S.op("act", [prC], [r_acc[2]], lambda psC=psC: nc.scalar.copy(out=acc[2][:, 0:N], in_=psC[:, 0:N]))
                S.op("dve", [prX, r_acc[2]], [rst], lambda psX=psX: nc.vector.tensor_tensor(
                    out=st[:, 2:2 + N], in0=psX[:, 0:N], in1=acc[2][:, 0:N], op=ALU.mult))
                S.op("dve", [r_csc[l]], [rst], lambda c=c: nc.vector.tensor_copy(out=st[:, 0:2], in_=csc[:, l, c, :]))
                S.op("dve", [rst], [r_csc[l]], lambda c=c: nc.vector.tensor_copy(out=csc[:, l, c, :], in_=st[:, N:N + 2]))
                ac, rac = acc[c % 2], r_acc[c % 2]
                taps = [pp[:, l, PP_SHW + j * 8 + c:PP_SHW + j * 8 + c + 1] for j in range(3)]
                S.op("dve", [rst, r_pp], [rac], lambda ac=ac, taps=taps: nc.vector.tensor_scalar(
                    out=ac[:, 0:N], in0=st[:, 0:N], scalar1=taps[0], scalar2=None, op0=ALU.mult))
                for jj in (1, 2):
                    S.op("dve", [rst, rac, r_pp], [rac], lambda ac=ac, taps=taps, jj=jj: nc.vector.scalar_tensor_tensor(
                        out=ac[:, 0:N], in0=st[:, jj:jj + N], scalar=taps[jj], in1=ac[:, 0:N], op0=ALU.mult, op1=ALU.add))
                S.op("dve", [prB, rac], [r_ybT], lambda ac=ac, psB=psB, c=c: nc.vector.tensor_tensor(
                    out=ybT[:, c, 0:N], in0=psB[:, 0:N], in1=ac[:, 0:N], op=ALU.mult))
        if last_tile:
            S.dma("sp", d_out, [r_csc[l]], [], lambda: nc.sync.dma_start(out=psc_d[l], in_=csc[:, l, :, :]))
        ydT, r_ydT = G[4], r_G[4]
        S.dma("sp", d_rowb, [], [r_rowb], lambda: nc.sync.dma_start(out=rowb[:, :], in_=rowb_d[l, :, 1024:4096]))
        S.dma("pool", d_in, [], [r_gmraw], lambda: nc.gpsimd.dma_start(out=gmw_raw[:], in_=gmw_d[l].rearrange("g t s -> t g s")))
        for g in range(8):
            ps, pr = psum()
            psb = ps[:, 0:64].bitcast(BF16)
            S.op("pe", [r_gmraw, r_cst], [pr], lambda psb=psb, g=g: nc.tensor.transpose(psb[:, 0:128], gmw_raw[:, g, :], identb[:]))
            S.op("dve", [pr, r_cst], [r_gmw], lambda psb=psb, g=g: nc.vector.tensor_tensor(
                out=gmwT[:, g, :], in0=psb[:, 0:128], in1=Umat, op=ALU.mult))
        for bi in range(4):
            wv, wr = win(l, UV0 + bi * 256)
            for cc in range(2):
                c = bi * 2 + cc
                ps, pr = proj_ws(wv, wr, cc * 128, 128, N)
                S.op("act", [pr], [r_ydT], lambda ps=ps, c=c: nc.scalar.activation(out=ydT[:, c, 0:N], in_=ps[:, 0:N], func=AF.Gelu_apprx_tanh))
        wv_ = [win(l, UV0 + 1024 + i * 256) for i in range(4)]
        for j in range(NCH):
            gmlp_chunk(l, j * 128, 128, wv_, ydT, r_ydT, lng, lnb, bsrow, gmwT, None)
        mT, r_mT = G[5], r_G[5]
        brs = ((G[1], r_G[1]), (G[2], r_G[2]), (G[3], r_G[3]), (G[4], r_G[4]))
        macc = (acc[0], acc[1])
        r_macc = (r_acc[0], r_acc[1])
        for bi in range(4):
            for i in range(4):
                wg = win(l, GT0 + i * 1024 + bi * 256)
                wb = wload(w_br_d[l, i, :, bi * 256:bi * 256 + 256].rearrange("(k p) c -> p k c", p=128), 8, 256, key=("br", l, i, bi))
                for cc in range(2):
                    c = bi * 2 + cc
                    psg, prg = proj_ws(wg[0], wg[1], cc * 128, 128, N)
                    psp, prp = proj_ws(wb[0], wb[1], cc * 128, 128, N, rhs=brs[i][0], rres=brs[i][1])
                    S.op("act", [prg], [r_stg[2]], lambda psg=psg: nc.scalar.activation(out=stg[2][:, 0:N], in_=psg[:, 0:N], func=AF.Sigmoid))
                    if i == 0:
                        S.op("dve", [prp, r_stg[2]], [r_macc[cc]], lambda psp=psp, cc=cc: nc.vector.tensor_tensor(
                            out=macc[cc][:, 0:N], in0=psp[:, 0:N], in1=stg[2][:, 0:N], op=ALU.mult))
                    else:
                        S.op("dve", [prp, r_stg[2]], [r_acc[2]], lambda psp=psp: nc.vector.tensor_tensor(
                            out=acc[2][:, 0:N], in0=psp[:, 0:N], in1=stg[2][:, 0:N], op=ALU.mult))
                        if i < 3:
                            S.op("dve", [r_macc[cc], r_acc[2]], [r_macc[cc]], lambda cc=cc: nc.vector.tensor_tensor(
                                out=macc[cc][:, 0:N], in0=macc[cc][:, 0:N], in1=acc[2][:, 0:N], op=ALU.add))
                        else:
                            S.op("dve", [r_macc[cc], r_acc[2]], [r_mT], lambda c=c, cc=cc: nc.vector.tensor_tensor(
                                out=mT[:, c, 0:N], in0=macc[cc][:, 0:N], in1=acc[2][:, 0:N], op=ALU.add))
        for bi in range(4):
            wv, wr = wload(wcols(w_o_d, l, bi * 256, 256), 8, 256, key=("o", l, bi))
            for cc in range(2):
                c = bi * 2 + cc
                ps, pr = proj_ws(wv, wr, cc * 128, 128, N, rhs=mT, rres=r_mT)
                S.op("dve", [pr, r_mod, r_x], [r_x], lambda ps=ps, c=c: nc.vector.scalar_tensor_tensor(
                    out=xT[:, c, 0:N], in0=ps[:, 0:N], scalar=mod[:, l, 16 + c, 0:1], in1=xT[:, c, 0:N], op0=ALU.mult, op1=ALU.add))
        mod_norm(l, 1, N)
        gat = (G[1], G[2], G[3])
        r_gat = (r_G[1], r_G[2], r_G[3])
        for blk in range(11):
            wa = wload(wcols(w_up_d, l, blk * 256, 256), 8, 256, key=("ua", l, blk))
            wg_ = wload(wcols(w_up_d, l, DFF + blk * 256, 256), 8, 256, key=("ug", l, blk))
            for cc in range(2):
                i = blk * 2 + cc
                outs = []
                for which, (wv, wr) in enumerate((wa, wg_)):
                    ci = which * 22 + i
                    ps, pr = proj_ws(wv, wr, cc * 128, 128, N)
                    taps = [pp[:, l, PP_FW + j * 44 + ci:PP_FW + j * 44 + ci + 1] for j in range(3)]
                    ac, rac = conv_fm(128, ps, pr, N, cff[:, l, ci, :], r_cff[l], taps, pp[:, l, PP_FB + ci:PP_FB + ci + 1], [r_pp], which)
                    outs.append((ac, rac))
                S.op("act", [outs[0][1]], [r_acc[2]], lambda a=outs[0][0]: nc.scalar.activation(out=acc[2][:, 0:N], in_=a[:, 0:N], func=AF.Silu))
                S.op("dve", [r_acc[2], outs[1][1]], [r_gat[i // 8]], lambda g_=outs[1][0], i=i: nc.vector.tensor_tensor(
                    out=gat[i // 8][:, i % 8, 0:N], in0=acc[2][:, 0:N], in1=g_[:, 0:N], op=ALU.mult))
        if last_tile:
            S.dma("sp", d_out, [r_cff[l]], [], lambda: nc.sync.dma_start(out=pffn_d[l], in_=cff[:, l, :, :]))
        for c in range(8):
            wh = [wload(w_dn_d[l, hh * 1408:(hh + 1) * 1408, c * 128:(c + 1) * 128].rearrange("(k p) c -> p k c", p=128), 11, 128, key=("dn", l, c, hh)) for hh in range(2)]
            ps, pr = psum()

            def f(wh=wh, ps=ps):
                for k in range(22):
                    last = nc.tensor.matmul(ps[:, 0:N], lhsT=wh[k // 11][0][:, k % 11, :], rhs=gat[k // 8][:, k % 8, 0:N], start=(k == 0), stop=(k == 21))
                return last
            S.op("pe", [wh[0][1], wh[1][1]] + list(r_gat), [pr], f)
            S.op("dve", [pr, r_mod, r_x], [r_x], lambda ps=ps, c=c: nc.vector.scalar_tensor_tensor(
                out=xT[:, c, 0:N], in0=ps[:, 0:N], scalar=mod[:, l, 40 + c, 0:1], in1=xT[:, c, 0:N], op0=ALU.mult, op1=ALU.add))

    def ssd_chunk(l, t0, T, wz, wdt, xsT, r_xsT, yaT, r_yaT, normg):
        dtb = rows[0:T, l, 0:16]
        Arow = rows[0:T, l, 16:32]
        Drow = rows[0:T, l, 32:48]
        xs_tok, r_xs = tC, r_tC
        for c in range(8):
            ps, pr = psum()
            psb = ps[:, 0:64].bitcast(BF16)
            S.op("pe", [r_xsT, r_cst], [pr], lambda c=c, psb=psb: nc.tensor.transpose(psb[0:T, 0:128], xsT[:, c, t0:t0 + T], identb[:]))
            yield
            copy(eng2(), xs_tok[0:T, c * 128:(c + 1) * 128], psb[0:T, 0:128], [pr], [r_xs])
            yield
        for g in range(4):
            ps, pr = psum()
            psb = ps[:, 0:64].bitcast(BF16)
            S.op("pe", [r_BCT, r_cst], [pr], lambda g=g, psb=psb: nc.tensor.transpose(psb[0:T, 0:64], BCT[:, g, t0:t0 + T], identb[0:64, 0:64]))
            yield
            copy(eng2(), Btok[0:T, g, :], psb[0:T, 0:64], [pr], [r_Btok])
            yield
        ps, pr = proj_as(wdt, 16, t0, T)
        dt = sm[0:T, 0:16]
        dtA = sm[0:T, 16:32]
        Acs = sm[0:T, 32:48]
        nAcs = sm[0:T, 48:64]
        eA = sm[0:T, 64:80]
        dec = sm[0:T, 80:96]
        wdec = sm[0:T, 96:112]
        S.op("dve", [pr, r_rows], [RV["dt"]], lambda: nc.vector.tensor_tensor(out=dt, in0=ps[0:T, 0:16], in1=dtb, op=ALU.add))
        yield
        S.op("act", [RV["dt"]], [RV["dt"]], lambda: nc.scalar.activation(out=dt, in_=dt, func=AF.Exp))
        yield
        S.op("act", [RV["dt"]], [RV["dt"]], lambda: nc.scalar.activation(out=dt, in_=dt, func=AF.Ln, bias=1.0))
        yield
        S.op("dve", [RV["dt"], r_rows], [RV["dtA"]], lambda: nc.vector.tensor_tensor(out=dtA, in0=dt, in1=Arow, op=ALU.mult))
        yield
        ps1, pr1 = psum()

        def f1():
            nc.tensor.matmul(ps1[0:T, 0:16], lhsT=Umat[0:T, 0:T], rhs=dtA, start=True, stop=True)
            return nc.tensor.matmul(ps1[0:max(T, 64), 16:32], lhsT=ones32[0:T, 0:max(T, 64)], rhs=dtA, start=True, stop=True)
        S.op("pe", [RV["dtA"], r_cst], [pr1], f1)
        yield
        S.op("dve", [pr1], [RV["Acs"]], lambda: nc.vector.tensor_copy(out=Acs, in_=ps1[0:T, 0:16]))
        yield
        S.op("dve", [pr1], [RV["nAcs"]], lambda: nc.vector.tensor_scalar(out=nAcs, in0=ps1[0:T, 0:16], scalar1=-1.0, scalar2=None, op0=ALU.mult))
        yield
        S.op("dve", [pr1, RV["Acs"]], [RV["dec"]], lambda: nc.vector.tensor_tensor(out=dec, in0=ps1[0:T, 16:32], in1=Acs, op=ALU.subtract))
        yield
        S.op("act", [pr1], [RV["eA"]], lambda: nc.scalar.activation(out=eA, in_=ps1[0:T, 0:16], func=AF.Exp))
        yield
        cdec = sm[0:64, 112:128]
        S.op("act", [pr1], [RV["cdec"]], lambda: nc.scalar.activation(out=cdec, in_=ps1[0:64, 16:32], func=AF.Exp))
        yield
        S.op("act", [RV["dec"]], [RV["dec"]], lambda: nc.scalar.activation(out=dec, in_=dec, func=AF.Exp))
        yield
        S.op("dve", [RV["dec"], RV["dt"]], [RV["wdec"]], lambda: nc.vector.tensor_tensor(out=wdec, in0=dec, in1=dt, op=ALU.mult))
        yield
        for hg in range(4):
            ps, pr = psum()

            def fe(ps=ps, hg=hg):
                for hh in range(4):
                    h_ = hg * 4 + hh
                    nc.tensor.matmul(ps[0:T, hh * 128:hh * 128 + T], lhsT=dtA[:, h_:h_ + 1].to_broadcast([T, T]), rhs=Umat[0:T, 0:T], start=True, stop=False)
                    last = nc.tensor.matmul(ps[0:T, hh * 128:hh * 128 + T], lhsT=ident[0:T, 0:T], rhs=NEGM[0:T, 0:T], start=False, stop=True)
                return last
            S.op("pe", [RV["dtA"], r_cst], [pr], fe)
            yield
            for hh in range(4):
                h_ = hg * 4 + hh
                S.op("act", [pr, RV["nAcs"]], [r_Eh], lambda ps=ps, hh=hh, h_=h_: nc.scalar.activation(
                    out=Eh[0:T, h_, 0:T], in_=ps[0:T, hh * 128:hh * 128 + T], func=AF.Exp, bias=nAcs[:, h_:h_ + 1], scale=1.0))
                yield
        ps, pr = psum()

        def fcb(ps=ps):
            for g in range(4):
                last = nc.tensor.matmul(ps[0:T, g * 128:g * 128 + T], lhsT=BCT[:, g, t0:t0 + T], rhs=BCT[:, 4 + g, t0:t0 + T], start=True, stop=True)
            return last
        S.op("pe", [r_BCT], [pr], fcb)
        yield
        copy("act", CBs[0:T, :, 0:T], ps[0:T, :].rearrange("p (g t) -> p g t", g=4)[:, :, 0:T], [pr], [r_CBs])
        yield
        for g in range(4):
            S.op("dve", [r_Eh, r_CBs], [r_Mh], lambda g=g: nc.vector.tensor_tensor(
                out=Mh[0:T, g * 4:g * 4 + 4, 0:T], in0=Eh[0:T, g * 4:g * 4 + 4, 0:T],
                in1=CBs[0:T, g:g + 1, 0:T].to_broadcast([T, 4, T]), op=ALU.mult))
            yield
        xdt, r_xdt = tD, r_tD
        Xdd, r_Xdd = tE, r_tE
        S.op("dve", [r_xs, RV["dt"]], [r_xdt], lambda: nc.vector.tensor_tensor(
            out=xdt[0:T, :].rearrange("p (h q) -> p h q", h=16), in0=xs_tok[0:T, :].rearrange("p (h q) -> p h q", h=16),
            in1=dt.unsqueeze(2).to_broadcast([T, 16, 64]), op=ALU.mult))
        yield
        S.op("dve", [r_xs, RV["wdec"]], [r_Xdd], lambda: nc.vector.tensor_tensor(
            out=Xdd[0:T, :].rearrange("p (h q) -> p h q", h=16), in0=xs_tok[0:T, :].rearrange("p (h q) -> p h q", h=16),
            in1=wdec.unsqueeze(2).to_broadcast([T, 16, 64]), op=ALU.mult))
        yield
        pso = [psum(), psum()]
        psd = [psum(), psum()]

        def foff():
            for g in range(4):
                last = nc.tensor.matmul(pso[g // 2][0][0:T, (g % 2) * 256:(g % 2) * 256 + 256], lhsT=BCT[:, 4 + g, t0:t0 + T],
                                        rhs=STb[:, g, :], start=True, stop=True)
            return last
        S.op("pe", [r_BCT, r_STb[l]], [pso[0][1], pso[1][1]], foff)
        yield

        def fdiag():
            for h_ in range(16):
                last = nc.tensor.matmul(psd[h_ // 8][0][0:T, (h_ % 8) * 64:(h_ % 8) * 64 + 64], lhsT=Mh[0:T, h_, 0:T],
                                        rhs=xdt[0:T, h_ * 64:(h_ + 1) * 64], start=True, stop=True)
            return last
        S.op("pe", [r_Mh, r_xdt], [psd[0][1], psd[1][1]], fdiag)
        yield
        y, r_y = tA, r_tA
        for hf in range(2):
            S.op("dve", [pso[hf][1], RV["eA"]], [r_y], lambda hf=hf: nc.vector.tensor_tensor(
                out=y[0:T, hf * 512:(hf + 1) * 512].rearrange("p (h q) -> p h q", h=8),
                in0=pso[hf][0][0:T, :].rearrange("p (h q) -> p h q", h=8),
                in1=eA[:, hf * 8:hf * 8 + 8].unsqueeze(2).to_broadcast([T, 8, 64]), op=ALU.mult))
            yield
            S.op("dve", [psd[hf][1], r_y], [r_y], lambda hf=hf: nc.vector.tensor_tensor(
                out=y[0:T, hf * 512:(hf + 1) * 512], in0=psd[hf][0][0:T, :], in1=y[0:T, hf * 512:(hf + 1) * 512], op=ALU.add))
            yield
        t2, r_t2 = tB, r_tB
        S.op("dve", [r_xs, r_rows], [r_t2], lambda: nc.vector.tensor_tensor(
            out=t2[0:T, :].rearrange("p (h q) -> p h q", h=16), in0=xs_tok[0:T, :].rearrange("p (h q) -> p h q", h=16),
            in1=Drow.unsqueeze(2).to_broadcast([T, 16, 64]), op=ALU.mult))
        yield
        S.op("dve", [r_t2, r_y], [r_y], lambda: nc.vector.tensor_tensor(out=y[0:T, :], in0=y[0:T, :], in1=t2[0:T, :], op=ALU.add))
        yield
        pss = [psum(), psum()]

        def fst():
            for g in range(4):
                last = nc.tensor.matmul(pss[g // 2][0][0:64, (g % 2) * 256:(g % 2) * 256 + 256], lhsT=Btok[0:T, g, :],
                                        rhs=Xdd[0:T, g * 256:(g + 1) * 256], start=True, stop=True)
            return last
        S.op("pe", [r_Btok, r_Xdd], [pss[0][1], pss[1][1]], fst)
        yield
        S.op("dve", [r_ST[l], RV["cdec"], r_STb[l]], [r_ST[l]], lambda: nc.vector.tensor_tensor(
            out=ST[:, l, :, :].rearrange("p g (r q) -> p (g r) q", r=4), in0=ST[:, l, :, :].rearrange("p g (r q) -> p (g r) q", r=4),
            in1=cdec.unsqueeze(2).to_broadcast([64, 16, 64]), op=ALU.mult))
        yield
        for hf in range(2):
            S.op("dve", [pss[hf][1], r_ST[l]], [r_ST[l]], lambda hf=hf: nc.vector.tensor_tensor(
                out=ST[:, l, hf * 2:hf * 2 + 2, :], in0=ST[:, l, hf * 2:hf * 2 + 2, :],
                in1=pss[hf][0][0:64, :].rearrange("p (g q) -> p g q", g=2), op=ALU.add))
            yield
        S.op("act", [r_ST[l]], [r_STb[l]], lambda: nc.scalar.copy(out=STb[:, :, :], in_=ST[:, l, :, :]))
        yield
        for hf in range(2):
            ps, pr = proj_as(wz[hf * 2:hf * 2 + 2], 512, t0, T)
            S.op("act", [pr], [r_t2], lambda ps=ps, hf=hf: nc.scalar.activation(out=t2[0:T, hf * 512:(hf + 1) * 512], in_=ps[0:T, :], func=AF.Silu))
            yield
        S.op("dve", [r_t2, r_y], [r_y], lambda: nc.vector.tensor_tensor(out=y[0:T, :], in0=y[0:T, :], in1=t2[0:T, :], op=ALU.mult))
        yield
        ssq = sm[0:T, 128:132]
        for g in range(4):
            S.op("act", [r_y], [r_t2, RV["ssq"]], lambda g=g: nc.scalar.activation(
                out=t2[0:T, g * 256:(g + 1) * 256], in_=y[0:T, g * 256:(g + 1) * 256], func=AF.Square, accum_out=ssq[:, g:g + 1]))
            yield
        rstd_of(ssq, T, ssq, [RV["ssq"], r_cst], [RV["ssq"]], 1.0 / 256)
        yield
        S.op("dve", [r_y, RV["ssq"]], [r_y], lambda: nc.vector.tensor_tensor(
            out=y[0:T, :].rearrange("p (g q) -> p g q", g=4), in0=y[0:T, :].rearrange("p (g q) -> p g q", g=4),
            in1=ssq.unsqueeze(2).to_broadcast([T, 4, 256]), op=ALU.mult))
        yield
        ya_tok, r_ya = tC, r_tC
        S.op("dve", [r_y, r_rowb], [r_ya], lambda: nc.vector.tensor_tensor(out=ya_tok[0:T, :], in0=y[0:T, :], in1=normg[0:T, :], op=ALU.mult))
        yield
        for c in range(8):
            ps, pr = psum()
            psb = ps[:, 0:64].bitcast(BF16)
            S.op("pe", [r_ya, r_cst], [pr], lambda c=c, psb=psb: nc.tensor.transpose(psb[:, 0:T], ya_tok[0:T, c * 128:(c + 1) * 128], identb[0:T, 0:T]))
            yield
            copy(eng2(), yaT[:, c, t0:t0 + T], psb[:, 0:T], [pr], [r_yaT])
            yield

    def attn_chunk(l, ti, j, qT, r_qT, ycT, r_ycT):
        T = 128
        t0 = j * 128
        first = (ti == 0 and j == 0)
        KC = 128 if first else 256
        boff = 128 if first else 0
        sink = rows[:, l, 48:64]
        scA = G[4][:, 0:4, :].bitcast(F32)
        PmA = G[4][:, 4:6, :]
        PTA = G[4][:, 6:8, :]
        o_ps = [(PS[6], r_PS[6]), (PS[7], r_PS[7])]
        rinv = sm[:, 136:152]
        for hk in range(4):
            pss_ = [psum(), psum()]

            def fs(hk=hk, pss_=pss_):
                for hh in range(4):
                    hd = hk * 4 + hh
                    pb = (hd % 2) * 64
                    qa = qT[pb:pb + 64, hd // 2, t0:t0 + T]
                    dst = pss_[hh % 2][0][:, (hh // 2) * 256:(hh // 2) * 256 + KC]
                    if first:
                        last = nc.tensor.matmul(dst, lhsT=qa, rhs=kT[pb:pb + 64, hk, t0:t0 + T], start=True, stop=True)
                    elif j == 0:
                        nc.tensor.matmul(dst[:, 0:128], lhsT=qa, rhs=kprev[pb:pb + 64, l, hk, :], start=True, stop=True)
                        last = nc.tensor.matmul(dst[:, 128:256], lhsT=qa, rhs=kT[pb:pb + 64, hk, t0:t0 + T], start=True, stop=True)
                    else:
                        last = nc.tensor.matmul(dst, lhsT=qa, rhs=kT[pb:pb + 64, hk, t0 - 128:t0 + T], start=True, stop=True)
                return last
            S.op("pe", [r_qT, r_kT, r_kprev[l]], [pss_[0][1], pss_[1][1]], fs)
            yield
            sc = scA
            for hh in range(4):
                S.op("dve", [pss_[hh % 2][1], r_cst], [r_aS], lambda hh=hh, hk=hk, pss_=pss_: nc.vector.scalar_tensor_tensor(
                    out=sc[:, hh, 0:KC], in0=distm[:, boff:boff + KC], scalar=-SLOPES[hk * 4 + hh],
                    in1=pss_[hh % 2][0][:, (hh // 2) * 256:(hh // 2) * 256 + KC], op0=ALU.mult, op1=ALU.add))
                yield
            if DBG_ATT <= 2:
                continue
            mx = sm[:, 152:156]
            nmx = sm[:, 156:160]
            esk = sm[:, 160:164]
            rsum = sm[:, 164:168]
            S.op("dve", [r_aS], [r_sm], lambda: nc.vector.tensor_reduce(out=mx, in_=sc[:, :, 0:KC], axis=AX.X, op=ALU.max))
            yield
            S.op("dve", [r_sm, r_rows], [r_sm], lambda hk=hk: nc.vector.tensor_tensor(out=mx, in0=mx, in1=sink[:, hk * 4:hk * 4 + 4], op=ALU.max))
            yield
            S.op("dve", [r_sm], [r_sm], lambda: nc.vector.tensor_scalar(out=nmx, in0=mx, scalar1=-1.0, scalar2=None, op0=ALU.mult))
            yield
            S.op("dve", [r_sm, r_rows], [r_sm], lambda hk=hk: nc.vector.tensor_tensor(out=esk, in0=sink[:, hk * 4:hk * 4 + 4], in1=mx, op=ALU.subtract))
            yield
            S.op("act", [r_sm], [r_sm], lambda: nc.scalar.activation(out=esk, in_=esk, func=AF.Exp))
            yield
            Pm = PmA.rearrange("p c (h k) -> p (c h) k", h=2)
            for hh in range(4):
                S.op("act", [r_aS, r_sm], [r_aP, r_sm], lambda hh=hh: nc.scalar.activation(
                    out=Pm[:, hh, 0:KC], in_=sc[:, hh, 0:KC], func=AF.Exp, bias=nmx[:, hh:hh + 1], scale=1.0, accum_out=rsum[:, hh:hh + 1]))
                yield
            S.op("dve", [r_sm], [r_sm], lambda: nc.vector.tensor_tensor(out=rsum, in0=rsum, in1=esk, op=ALU.add))
            yield
            S.op("dve", [r_sm], [r_sm], lambda hk=hk: nc.vector.reciprocal(out=rinv[:, hk * 4:hk * 4 + 4], in_=rsum))
            yield
            if DBG_ATT <= 3:
                continue
            PT = PTA.rearrange("p c (h k) -> p (c h) k", h=2)
            nkb = KC // 128
            for hh in range(4):
                ps, pr = psum()
                psb = ps[:, 0:128].bitcast(BF16)

                def ft(hh=hh, psb=psb):
                    for kb in range(nkb):
                        last = nc.tensor.transpose(psb[:, kb * 128:(kb + 1) * 128], Pm[:, hh, kb * 128:(kb + 1) * 128], identb[:])
                    return last
                S.op("pe", [r_aP, r_cst], [pr], ft)
                yield
                copy(eng2(), PT[:, hh, 0:KC], psb[:, 0:KC], [pr], [r_aT])
                yield

            def fo(hk=hk):
                for hh in range(4):
                    hd = hk * 4 + hh
                    dst = o_ps[hd // 8][0][:, (hd % 8) * 64:(hd % 8) * 64 + 64]
                    if first:
                        last = nc.tensor.matmul(dst, lhsT=PT[:, hh, 0:128], rhs=vtok[:, j, hk * 64:hk * 64 + 64], start=True, stop=True)
                    else:
                        vp = vprev[:, l, hk * 64:hk * 64 + 64] if j == 0 else vtok[:, j - 1, hk * 64:hk * 64 + 64]
                        nc.tensor.matmul(dst, lhsT=PT[:, hh, 0:128], rhs=vp, start=True, stop=False)
                        last = nc.tensor.matmul(dst, lhsT=PT[:, hh, 128:256], rhs=vtok[:, j, hk * 64:hk * 64 + 64], start=False, stop=True)
                return last
            S.op("pe", [r_aT, r_vtok, r_vprev[l]], [o_ps[hk // 2][1]], fo)
            yield
        yc_tok = PmA.rearrange("p c t -> p (c t)")
        for hf in range(2):
            S.op("dve", [o_ps[hf][1], r_sm], [r_aP], lambda hf=hf: nc.vector.tensor_tensor(
                out=yc_tok[:, hf * 512:(hf + 1) * 512].rearrange("p (h q) -> p h q", h=8),
                in0=o_ps[hf][0][:, :].rearrange("p (h q) -> p h q", h=8),
                in1=rinv[:, hf * 8:hf * 8 + 8].unsqueeze(2).to_broadcast([128, 8, 64]), op=ALU.mult))
            yield
        for c in range(8):
            ps, pr = psum()
            psb = ps[:, 0:64].bitcast(BF16)
            S.op("pe", [r_aP, r_cst], [pr], lambda c=c, psb=psb: nc.tensor.transpose(psb[:, 0:T], yc_tok[:, c * 128:(c + 1) * 128], identb[:]))
            yield
            copy(eng2(), ycT[:, c, t0:t0 + T], psb[:, 0:T], [pr], [r_ycT])
            yield

    def gmlp_chunk(l, t0, T, wv_, ydT, r_ydT, lng, lnb, bsrow, wT, vout):
        v, r_v = tA, r_tA
        ssum = sm[0:T, 168:170]
        mean = sm[0:T, 170:171]
        ssq = sm[0:T, 171:173]
        rstd = sm[0:T, 173:174]
        for hf in range(2):
            ps, pr = proj_as(wv_[hf * 2:hf * 2 + 2], 512, t0, T)
            S.op("act", [pr], [r_v, r_sm], lambda ps=ps, hf=hf: nc.scalar.activation(
                out=v[0:T, hf * 512:(hf + 1) * 512], in_=ps[0:T, :], func=AF.Gelu_apprx_tanh, accum_out=ssum[:, hf:hf + 1]))
        S.op("dve", [r_sm], [r_sm], lambda: nc.vector.tensor_tensor(out=mean, in0=ssum[:, 0:1], in1=ssum[:, 1:2], op=ALU.add))
        S.op("dve", [r_sm], [r_sm], lambda: nc.vector.tensor_scalar(out=mean, in0=mean, scalar1=1.0 / 1024, scalar2=None, op0=ALU.mult))
        S.op("dve", [r_v, r_sm], [r_v], lambda: nc.vector.tensor_scalar(out=v[0:T, :], in0=v[0:T, :], scalar1=mean, scalar2=None, op0=ALU.subtract))
        for hf in range(2):
            S.op("act", [r_v], [r_tB, r_sm], lambda hf=hf: nc.scalar.activation(
                out=tB[0:T, hf * 512:(hf + 1) * 512], in_=v[0:T, hf * 512:(hf + 1) * 512], func=AF.Square, accum_out=ssq[:, hf:hf + 1]))
        S.op("dve", [r_sm], [r_sm], lambda: nc.vector.tensor_tensor(out=rstd, in0=ssq[:, 0:1], in1=ssq[:, 1:2], op=ALU.add))
        rstd_of(rstd, T, rstd, [r_sm, r_cst], [r_sm], 1.0 / 1024)
        S.op("dve", [r_v, r_sm, r_rowb], [r_v], lambda: nc.vector.scalar_tensor_tensor(
            out=v[0:T, :], in0=v[0:T, :], scalar=rstd, in1=lng[0:T, :], op0=ALU.mult, op1=ALU.mult))
        if vout is not None:
            S.op("dve", [r_v, r_rowb], [r_tB], lambda: nc.vector.tensor_tensor(out=tB[0:T, :], in0=v[0:T, :], in1=lnb[0:T, :], op=ALU.add))
            vout(tB, r_tB)
        vn, r_vn = tC, r_tC
        S.op("dve", [r_v, r_rowb], [r_vn], lambda: nc.vector.tensor_tensor(out=vn[0:T, :], in0=v[0:T, :], in1=lnb[0:T, :], op=ALU.add))
        for hf in range(2):
            ps, pr = psum()

            def fm(ps=ps, hf=hf):
                for gg in range(4):
                    g = hf * 4 + gg
                    last = nc.tensor.matmul(ps[:, gg * 128:gg * 128 + T], lhsT=vn[0:T, g * 128:(g + 1) * 128], rhs=wT[0:T, g, 0:T], start=True, stop=True)
                return last
            S.op("pe", [r_vn, r_gmw], [pr], fm)
            mix = tB[:, 0:512].rearrange("p (g t) -> p g t", g=4)
            S.op("dve", [pr, r_rowb], [r_tB], lambda ps=ps, hf=hf: nc.vector.tensor_tensor(
                out=mix[:, :, 0:T], in0=ps[:, :].rearrange("p (g t) -> p g t", g=4)[:, :, 0:T],
                in1=bsrow[:, hf * 512:(hf + 1) * 512].rearrange("p (g t) -> p g t", g=4)[:, :, 0:T], op=ALU.add))
            S.op("dve", [r_tB, r_ydT], [r_ydT], lambda hf=hf: nc.vector.tensor_tensor(
                out=ydT[:, hf * 4:hf * 4 + 4, t0:t0 + T], in0=ydT[:, hf * 4:hf * 4 + 4, t0:t0 + T], in1=mix[:, :, 0:T], op=ALU.mult))

    if SB:
        NS = 4 * SB
        mod_s = sb("mod_s", [128, 1, 48, SB]); r_mods = Res("mod_s")
        gm_s = sb("gm_s", [128, 1, 2, 8, SB]); r_gms = Res("gm_s")
        csT = sb("csT_t", [128, 8, SB]); r_csT = Res("csT")
        csb = sb("csb", [128, 8, SB], BF16)
        sxbc = sb("sxbc", [128, 8, SB, 3]); r_sxbc = Res("sxbc")
        sbc = sb("sbc", [64, 8, SB, 3]); r_sbc = Res("sbc")
        ssc = sb("ssc", [128, 8, SB, 2]); r_ssc = Res("ssc")
        sffn = sb("sffn", [128, 44, SB, 2]); r_sffn = Res("sffn")
        KKb = sb("KKb", [128, 4, 160], BF16); r_KKb = Res("KKb")
        Vb = sb("Vb", [128, 256], BF16); r_Vb = Res("Vb")
        vnew = sb("vnew", [4, 256], BF16); r_vnew = Res("vnew")
        PTn = sb("PTn", [4, 4, 4], BF16); r_PTn = Res("PTn")
        d_cs = d_kv = d_st = None

    def conv_s(P, ps, pr, taps, bias_ap, preads, st_ap, r_st, si):
        K = len(taps)
        CW = K - 1
        W = CW + 4
        st3 = stg[si][0:P, 0:SB * W].rearrange("p (b w) -> p b w", b=SB)
        ac3 = acc[si][0:P, 0:NS].rearrange("p (b w) -> p b w", b=SB)
        rst, rac = r_stg[si], r_acc[si]
        S.op("act", [pr], [rst], lambda: nc.scalar.copy(out=st3[:, :, CW:W], in_=ps[0:P, 0:NS].rearrange("p (b w) -> p b w", b=SB)))
        S.op("dve", [r_st], [rst], lambda: nc.vector.tensor_copy(out=st3[:, :, 0:CW], in_=st_ap))
        S.op("dve", [rst], [r_st], lambda: nc.vector.tensor_copy(out=st_ap, in_=st3[:, :, 4:W]))
        if bias_ap is not None:
            S.op("dve", [rst] + preads, [rac], lambda: nc.vector.tensor_scalar(
                out=ac3, in0=st3[:, :, 0:4], scalar1=taps[0], scalar2=bias_ap, op0=ALU.mult, op1=ALU.add))
        else:
            S.op("dve", [rst] + preads, [rac], lambda: nc.vector.tensor_scalar(
                out=ac3, in0=st3[:, :, 0:4], scalar1=taps[0], scalar2=None, op0=ALU.mult))
        for j in range(1, K):
            S.op("dve", [rst, rac] + preads, [rac], lambda j=j: nc.vector.scalar_tensor_tensor(
                out=ac3, in0=st3[:, :, j:j + 4], scalar=taps[j], in1=ac3, op0=ALU.mult, op1=ALU.add))
        return acc[si], rac

    def mod_norm_s(which):
        N = NS
        sh_i = 0 if which == 0 else 3
        ps, pr = psum()
        sq = G[5]
        S.op("act", [r_x], [r_G[5]], lambda: nc.scalar.activation(out=sq[:, :, 0:N], in_=xT[:, :, 0:N], func=AF.Square))

        def f():
            for k in range(8):
                last = nc.tensor.matmul(ps[:, 0:N], lhsT=onesb[:], rhs=sq[:, k, 0:N], start=(k == 0), stop=(k == 7))
            return last
        S.op("pe", [r_G[5], r_cst], [pr], f)
        rs = acc[2]
        rstd_of(ps[:, 0:N], 128, rs[:, 0:N], [pr, r_cst], [r_acc[2]], 1.0 / D)
        for c in range(8):
            a_ = acc[c % 2]
            a3 = a_[:, 0:N].rearrange("p (b w) -> p b w", b=SB)
            S.op("dve", [r_x, r_acc[2]], [r_acc[c % 2]], lambda c=c, a_=a_: nc.vector.tensor_tensor(
                out=a_[:, 0:N], in0=xT[:, c, 0:N], in1=rs[:, 0:N], op=ALU.mult))
            S.op("dve", [r_gms, r_acc[c % 2]], [r_acc[c % 2]], lambda c=c, a3=a3: nc.vector.tensor_tensor(
                out=a3, in0=a3, in1=gm_s[:, 0, which, c, :].unsqueeze(2).to_broadcast([128, SB, 4]), op=ALU.mult))
            S.op("dve", [r_mods, r_acc[c % 2]], [r_h], lambda c=c, a3=a3: nc.vector.tensor_tensor(
                out=hB[:, c, 0:N].rearrange("p (b w) -> p b w", b=SB), in0=a3,
                in1=mod_s[:, 0, sh_i * 8 + c, :].unsqueeze(2).to_broadcast([128, SB, 4]), op=ALU.add))

    def resid_s(ps, pr, c, gi):
        N = NS
        S.op("dve", [pr, r_mods], [r_acc[2]], lambda: nc.vector.tensor_tensor(
            out=acc[2][:, 0:N].rearrange("p (b w) -> p b w", b=SB), in0=ps[:, 0:N].rearrange("p (b w) -> p b w", b=SB),
            in1=mod_s[:, 0, gi * 8 + c, :].unsqueeze(2).to_broadcast([128, SB, 4]), op=ALU.mult))
        S.op("dve", [r_acc[2], r_x], [r_x], lambda: nc.vector.tensor_tensor(
            out=xT[:, c, 0:N], in0=xT[:, c, 0:N], in1=acc[2][:, 0:N], op=ALU.add))

    def attn_s(l, b, qT, r_qT, ycT, r_ycT, wk, wvv):
        T = 4
        t0 = 4 * b
        KC = 132
        sink = rows[0:T, l, 48:64]
        scA = G[4][:, 0:4, :].bitcast(F32)
        PmA = G[4][:, 4:6, :]
        PTA = G[4][:, 6:8, :]
        ktok, r_ktok = acc[2], r_acc[2]
        o_ps = [(PS[6], r_PS[6]), (PS[7], r_PS[7])]
        rinv = sm[0:T, 136:152]
        S.dma("pool", d_kv, [], [r_KKb], lambda: nc.gpsimd.dma_start(out=KKb[:, :, 0:128], in_=ckT_d[l, b]))
        yield
        S.dma("pool", d_kv, [], [r_Vb], lambda: nc.gpsimd.dma_start(out=Vb[:, :], in_=cv_d[l, b]))
        yield
        S.op("dve", [r_kT], [r_KKb], lambda: nc.vector.tensor_copy(out=KKb[:, :, 128:132], in_=kT[:, :, t0:t0 + T]))
        yield
        ps, pr = proj_as([wk, wvv], 512, t0, T)
        S.op("act", [pr], [r_ktok], lambda: nc.scalar.copy(out=ktok[0:T, 0:512], in_=ps[0:T, :]))
        yield
        S.op("dve", [r_ktok], [r_vnew], lambda: nc.vector.tensor_copy(out=vnew[:, :], in_=ktok[0:T, 256:512]))
        yield
        S.dma("sp", d_out, [r_ktok], [], lambda: nc.sync.dma_start(out=sk_o[l, b, 124:128, :], in_=ktok[0:T, 0:256]))
        yield
        S.dma("sp", d_out, [r_ktok], [], lambda: nc.sync.dma_start(out=sv_o[l, b, 124:128, :], in_=ktok[0:T, 256:512]))
        yield
        for hk in range(4):
            pss_ = [psum(), psum()]

            def fs(hk=hk, pss_=pss_):
                for hh in range(4):
                    hd = hk * 4 + hh
                    pb = (hd % 2) * 64
                    last = nc.tensor.matmul(pss_[hh % 2][0][0:T, (hh // 2) * 256:(hh // 2) * 256 + KC],
                                            lhsT=qT[pb:pb + 64, hd // 2, t0:t0 + T], rhs=KKb[pb:pb + 64, hk, 0:132], start=True, stop=True)
                return last
            S.op("pe", [r_qT, r_KKb], [pss_[0][1], pss_[1][1]], fs)
            yield
            sc = scA[0:T]
            for hh in range(4):
                S.op("dve", [pss_[hh % 2][1], r_cst], [r_aS], lambda hh=hh, hk=hk, pss_=pss_: nc.vector.scalar_tensor_tensor(
                    out=sc[:, hh, 0:KC], in0=dists[0:T, :], scalar=-SLOPES[hk * 4 + hh],
                    in1=pss_[hh % 2][0][0:T, (hh // 2) * 256:(hh // 2) * 256 + KC], op0=ALU.mult, op1=ALU.add))
                yield
            mx = sm[0:T, 152:156]
            nmx = sm[0:T, 156:160]
            esk = sm[0:T, 160:164]
            rsum = sm[0:T, 164:168]
            S.op("dve", [r_aS], [r_sm], lambda: nc.vector.tensor_reduce(out=mx, in_=sc[:, :, 0:KC], axis=AX.X, op=ALU.max))
            yield
            S.op("dve", [r_sm, r_rows], [r_sm], lambda hk=hk: nc.vector.tensor_tensor(out=mx, in0=mx, in1=sink[:, hk * 4:hk * 4 + 4], op=ALU.max))
            yield
            S.op("dve", [r_sm], [r_sm], lambda: nc.vector.tensor_scalar(out=nmx, in0=mx, scalar1=-1.0, scalar2=None, op0=ALU.mult))
            yield
            S.op("dve", [r_sm, r_rows], [r_sm], lambda hk=hk: nc.vector.tensor_tensor(out=esk, in0=sink[:, hk * 4:hk * 4 + 4], in1=mx, op=ALU.subtract))
            yield
            S.op("act", [r_sm], [r_sm], lambda: nc.scalar.activation(out=esk, in_=esk, func=AF.Exp))
            yield
            Pm = PmA[0:T].rearrange("p c (h k) -> p (c h) k", h=2)
            for hh in range(4):
                S.op("act", [r_aS, r_sm], [r_aP, r_sm], lambda hh=hh: nc.scalar.activation(
                    out=Pm[:, hh, 0:KC], in_=sc[:, hh, 0:KC], func=AF.Exp, bias=nmx[:, hh:hh + 1], scale=1.0, accum_out=rsum[:, hh:hh + 1]))
                yield
            S.op("dve", [r_sm], [r_sm], lambda: nc.vector.tensor_tensor(out=rsum, in0=rsum, in1=esk, op=ALU.add))
            yield
            S.op("dve", [r_sm], [r_sm], lambda hk=hk: nc.vector.reciprocal(out=rinv[:, hk * 4:hk * 4 + 4], in_=rsum))
            yield
            PT = PTA.rearrange("p c (h k) -> p (c h) k", h=2)
            ps, pr = psum()
            psb = ps[:, 0:64].bitcast(BF16)

            def ft(psb=psb):
                for hh in range(4):
                    nc.tensor.transpose(psb[:, hh * 8:hh * 8 + 4], Pm[:, hh, 0:128], identb[0:T, 0:T])
                    last = nc.tensor.transpose(psb[0:T, hh * 8 + 4:hh * 8 + 8], Pm[:, hh, 128:132], identb[0:T, 0:T])
                return last
            S.op("pe", [r_aP, r_cst], [pr], ft)
            yield
            pv = psb[:, 0:32].rearrange("p (h k) -> p h k", h=4)
            copy("dve", PT[:, :, 0:4], pv[:, :, 0:4], [pr], [r_aT])
            yield
            copy("dve", PTn[:, :, :], pv[0:T, :, 4:8], [pr], [r_PTn])
            yield

            def fo(hk=hk):
                for hh in range(4):
                    hd = hk * 4 + hh
                    dst = o_ps[hd // 8][0][0:T, (hd % 8) * 64:(hd % 8) * 64 + 64]
                    nc.tensor.matmul(dst, lhsT=PT[:, hh, 0:4], rhs=Vb[:, hk * 64:hk * 64 + 64], start=True, stop=False)
                    last = nc.tensor.matmul(dst, lhsT=PTn[:, hh, :], rhs=vnew[:, hk * 64:hk * 64 + 64], start=False, stop=True)
                return last
            S.op("pe", [r_aT, r_PTn, r_Vb, r_vnew], [o_ps[hk // 2][1]], fo)
            yield
        yc_tok = PmA.rearrange("p c t -> p (c t)")
        for hf in range(2):
            S.op("dve", [o_ps[hf][1], r_sm], [r_aP], lambda hf=hf: nc.vector.tensor_tensor(
                out=yc_tok[0:T, hf * 512:(hf + 1) * 512].rearrange("p (h q) -> p h q", h=8),
                in0=o_ps[hf][0][0:T, :].rearrange("p (h q) -> p h q", h=8),
                in1=rinv[:, hf * 8:hf * 8 + 8].unsqueeze(2).to_broadcast([T, 8, 64]), op=ALU.mult))
            yield
        for c in range(8):
            ps, pr = psum()
            psb = ps[:, 0:64].bitcast(BF16)
            S.op("pe", [r_aP, r_cst], [pr], lambda c=c, psb=psb: nc.tensor.transpose(psb[:, 0:T], yc_tok[0:T, c * 128:(c + 1) * 128], identb[0:T, 0:T]))
            yield
            copy(eng2(), ycT[:, c, t0:t0 + T], psb[:, 0:T], [pr], [r_ycT])
            yield

    def sample_tile_layer(l):
        N = NS
        S.dma("sp", d_rowb, [], [r_rowb], lambda: nc.sync.dma_start(out=rowb[:, 0:1024], in_=rowb_d[l, :, 0:1024]))
        normg = rowb[:, 0:1024]
        lng = rowb[:, 0:1024]
        lnb = rowb[:, 1024:2048]
        bsrow = rowb[:, 2048:3072]
        S.dma("sp", d_cs, [], [r_sxbc], lambda: nc.sync.dma_start(out=sxbc[:], in_=sxbc_d[l]))
        S.dma("sp", d_cs, [], [r_sbc], lambda: nc.sync.dma_start(out=sbc[:], in_=sbc_d[l]))
        S.dma("sp", d_cs, [], [r_ssc], lambda: nc.sync.dma_start(out=ssc[:], in_=ssc_d[l]))
        S.dma("sp", d_cs, [], [r_sffn], lambda: nc.sync.dma_start(out=sffn[:], in_=sffn_d[l]))
        S.dma("sp", d_out, [], [], lambda: nc.sync.dma_start(out=sk_o[l, :, 0:124, :], in_=ck_d[l, :, 4:128, :]))
        S.dma("sp", d_out, [], [], lambda: nc.sync.dma_start(out=sv_o[l, :, 0:124, :], in_=cv_d[l, :, 4:128, :]))
        ada_layer(l, csb, r_csT, SB, mod_s, r_mods, gm_s, r_gms, 0)
        mod_norm_s(0)
        xsT, r_xsT = G[3], r_G[3]
        for bi in range(4):
            wv, wr = win(l, XBC0 + bi * 256)
            for cc in range(2):
                c = bi * 2 + cc
                ps, pr = proj_ws(wv, wr, cc * 128, 128, N)
                taps = [pp[:, l, PP_SCW + j * 8 + c:PP_SCW + j * 8 + c + 1] for j in range(4)]
                ac, rac = conv_s(128, ps, pr, taps, pp[:, l, PP_SCB + c:PP_SCB + c + 1], [r_pp], sxbc[:, c, :, :], r_sxbc, c % 2)
                S.op("act", [rac], [r_xsT], lambda ac=ac, c=c: nc.scalar.activation(out=xsT[:, c, 0:N], in_=ac[:, 0:N], func=AF.Silu))
        for bi in range(2):
            wv, wr = win(l, XBC0 + 1024 + bi * 256)
            for ee in range(4):
                e = bi * 4 + ee
                ps, pr = proj_ws(wv, wr, ee * 64, 64, N)
                taps = [pp64[:, l, j * 8 + e:j * 8 + e + 1] for j in range(4)]
                ac, rac = conv_s(64, ps, pr, taps, pp64[:, l, 32 + e:33 + e], [r_pp64], sbc[:, e, :, :], r_sbc, e % 2)
                S.op("act", [rac], [r_BCT], lambda ac=ac, e=e: nc.scalar.activation(out=BCT[:, e, 0:N], in_=ac[0:64, 0:N], func=AF.Silu))
        S.dma("sp", d_out, [r_sxbc], [], lambda: nc.sync.dma_start(out=sxbc_o[l], in_=sxbc[:]))
        S.dma("sp", d_out, [r_sbc], [], lambda: nc.sync.dma_start(out=sbc_o[l], in_=sbc[:]))
        qT, r_qT = G[5], r_G[5]
        for bi in range(4):
            wv, wr = win(l, Q0 + bi * 256)
            for cc in range(2):
                c = bi * 2 + cc
                ps, pr = proj_ws(wv, wr, cc * 128, 128, N)
                S.op("act", [pr], [r_qT], lambda ps=ps, c=c: nc.scalar.activation(out=qT[:, c, 0:N], in_=ps[:, 0:N], func=AF.Identity, scale=0.125))
        wk = win(l, K0)
        wvv = win(l, V0)
        for hk in range(4):
            ps, pr = proj_ws(wk[0], wk[1], hk * 64, 64, N)
            ktmp, r_ktmp = tC, r_tC
            copy("act", ktmp[0:64, 0:N], ps[0:64, 0:N], [pr], [r_ktmp])
            ps2, pr2 = psum()
            S.op("pe", [r_ktmp, r_cst], [pr2], lambda ps2=ps2: nc.tensor.matmul(ps2[:, 0:N], lhsT=dupI[:, :], rhs=ktmp[0:64, 0:N], start=True, stop=True))
            copy("dve", kT[:, hk, 0:N], ps2[:, 0:N], [pr2], [r_kT])
        ycT, r_ycT = G[3], r_G[3]
        wz = [win(l, i * 256) for i in range(4)]
        wdt = [win(l, DT0, 256)]
        yaT, r_yaT = G[1], r_G[1]

        def ssd_b(b):
            S.dma("sp", d_st, [], [r_ST[l]], lambda b=b: nc.sync.dma_start(out=ST[:, l, :, :], in_=sssm_d[l, b]))
            S.op("act", [r_ST[l]], [r_STb1], lambda: nc.scalar.copy(out=STb[:, :, :], in_=ST[:, l, :, :]))
            yield
            for _ in ssd_chunk(l, 4 * b, 4, wz, wdt, xsT, r_xsT, yaT, r_yaT, normg):
                yield
            S.dma("sp", d_out, [r_ST[l]], [], lambda b=b: nc.sync.dma_start(out=sssm_o[l, b], in_=ST[:, l, :, :]))
        for b in range(SB):
            interleave(ssd_b(b), attn_s(l, b, qT, r_qT, ycT, r_ycT, wk, wvv))
        ybT, r_ybT = G[2], r_G[2]
        for bi in range(4):
            wb = [win(l, BCX0 + part * 1024 + bi * 256) for part in range(3)]
            for cc in range(2):
                c = bi * 2 + cc
                psB, prB = proj_ws(wb[0][0], wb[0][1], cc * 128, 128, N)
                psC, prC = proj_ws(wb[1][0], wb[1][1], cc * 128, 128, N)
                psX, prX = proj_ws(wb[2][0], wb[2][1], cc * 128, 128, N)
                st3 = stg[2][:, 0:SB * 6].rearrange("p (b w) -> p b w", b=SB)
                rst = r_stg[2]
                S.op("act", [prC], [r_acc[2]], lambda psC=psC: nc.scalar.copy(out=acc[2][:, 0:N], in_=psC[:, 0:N]))
                S.op("dve", [prX, r_acc[2]], [rst], lambda psX=psX, st3=st3: nc.vector.tensor_tensor(
                    out=st3[:, :, 2:6], in0=psX[:, 0:N].rearrange("p (b w) -> p b w", b=SB),
                    in1=acc[2][:, 0:N].rearrange("p (b w) -> p b w", b=SB), op=ALU.mult))
                S.op("dve", [r_ssc], [rst], lambda c=c, st3=st3: nc.vector.tensor_copy(out=st3[:, :, 0:2], in_=ssc[:, c, :, :]))
                S.op("dve", [rst], [r_ssc], lambda c=c, st3=st3: nc.vector.tensor_copy(out=ssc[:, c, :, :], in_=st3[:, :, 4:6]))
                ac, rac = acc[c % 2], r_acc[c % 2]
                ac3 = ac[:, 0:N].rearrange("p (b w) -> p b w", b=SB)
                taps = [pp[:, l, PP_SHW + j * 8 + c:PP_SHW + j * 8 + c + 1] for j in range(3)]
                S.op("dve", [rst, r_pp], [rac], lambda ac3=ac3, taps=taps, st3=st3: nc.vector.tensor_scalar(
                    out=ac3, in0=st3[:, :, 0:4], scalar1=taps[0], scalar2=None, op0=ALU.mult))
                for jj in (1, 2):
                    S.op("dve", [rst, rac, r_pp], [rac], lambda ac3=ac3, taps=taps, jj=jj, st3=st3: nc.vector.scalar_tensor_tensor(
                        out=ac3, in0=st3[:, :, jj:jj + 4], scalar=taps[jj], in1=ac3, op0=ALU.mult, op1=ALU.add))
                S.op("dve", [prB, rac], [r_ybT], lambda ac=ac, psB=psB, c=c: nc.vector.tensor_tensor(
                    out=ybT[:, c, 0:N], in0=psB[:, 0:N], in1=ac[:, 0:N], op=ALU.mult))
        S.dma("sp", d_out, [r_ssc], [], lambda: nc.sync.dma_start(out=ssc_o[l], in_=ssc[:]))
        ydT, r_ydT = G[4], r_G[4]
        S.dma("sp", d_rowb, [], [r_rowb], lambda: nc.sync.dma_start(out=rowb[:, :], in_=rowb_d[l, :, 1024:4096]))
        S.dma("pool", d_in, [], [r_gmraw], lambda: nc.gpsimd.dma_start(out=gmw_raw[:], in_=gmw_d[l].rearrange("g t s -> t g s")))
        for g in range(8):
            ps, pr = psum()
            psb = ps[:, 0:64].bitcast(BF16)
            S.op("pe", [r_gmraw, r_cst], [pr], lambda psb=psb, g=g: nc.tensor.transpose(psb[:, 0:128], gmw_raw[:, g, :], identb[:]))
            S.op("dve", [pr, r_cst], [r_gmw], lambda psb=psb, g=g: nc.vector.tensor_tensor(
                out=gmwT[:, g, :], in0=psb[:, 0:128], in1=Umat, op=ALU.mult))
        for bi in range(4):
            wv, wr = win(l, UV0 + bi * 256)
            for cc in range(2):
                c = bi * 2 + cc
                ps, pr = proj_ws(wv, wr, cc * 128, 128, N)
                S.op("act", [pr], [r_ydT], lambda ps=ps, c=c: nc.scalar.activation(out=ydT[:, c, 0:N], in_=ps[:, 0:N], func=AF.Gelu_apprx_tanh))
        wv_ = [win(l, UV0 + 1024 + i * 256) for i in range(4)]
        for b in range(SB):
            def vout(tt, rtt, b=b):
                S.dma("sp", d_out, [rtt], [], lambda: nc.sync.dma_start(out=sgmv_o[l, b], in_=tt[0:4, :]))
            gmlp_chunk(l, 4 * b, 4, wv_, ydT, r_ydT, lng, lnb, bsrow, gmwT, vout)
        mT, r_mT = G[5], r_G[5]
        brs = ((G[1], r_G[1]), (G[2], r_G[2]), (G[3], r_G[3]), (G[4], r_G[4]))
        macc = (acc[0], acc[1])
        r_macc = (r_acc[0], r_acc[1])
        for bi in range(4):
            for i in range(4):
                wg = win(l, GT0 + i * 1024 + bi * 256)
                wb = wload(w_br_d[l, i, :, bi * 256:bi * 256 + 256].rearrange("(k p) c -> p k c", p=128), 8, 256, key=("br", l, i, bi))
                for cc in range(2):
                    c = bi * 2 + cc
                    psg, prg = proj_ws(wg[0], wg[1], cc * 128, 128, N)
                    psp, prp = proj_ws(wb[0], wb[1], cc * 128, 128, N, rhs=brs[i][0], rres=brs[i][1])
                    S.op("act", [prg], [r_stg[2]], lambda psg=psg: nc.scalar.activation(out=stg[2][:, 0:N], in_=psg[:, 0:N], func=AF.Sigmoid))
                    if i == 0:
                        S.op("dve", [prp, r_stg[2]], [r_macc[cc]], lambda psp=psp, cc=cc: nc.vector.tensor_tensor(
                            out=macc[cc][:, 0:N], in0=psp[:, 0:N], in1=stg[2][:, 0:N], op=ALU.mult))
                    else:
                        S.op("dve", [prp, r_stg[2]], [r_acc[2]], lambda psp=psp: nc.vector.tensor_tensor(
                            out=acc[2][:, 0:N], in0=psp[:, 0:N], in1=stg[2][:, 0:N], op=ALU.mult))
                        if i < 3:
                            S.op("dve", [r_macc[cc], r_acc[2]], [r_macc[cc]], lambda cc=cc: nc.vector.tensor_tensor(
                                out=macc[cc][:, 0:N], in0=macc[cc][:, 0:N], in1=acc[2][:, 0:N], op=ALU.add))
                        else:
                            S.op("dve", [r_macc[cc], r_acc[2]], [r_mT], lambda c=c, cc=cc: nc.vector.tensor_tensor(
                                out=mT[:, c, 0:N], in0=macc[cc][:, 0:N], in1=acc[2][:, 0:N], op=ALU.add))
        for bi in range(4):
            wv, wr = wload(wcols(w_o_d, l, bi * 256, 256), 8, 256, key=("o", l, bi))
            for cc in range(2):
                c = bi * 2 + cc
                ps, pr = proj_ws(wv, wr, cc * 128, 128, N, rhs=mT, rres=r_mT)
                resid_s(ps, pr, c, 2)
        mod_norm_s(1)
        gat = (G[1], G[2], G[3])
        r_gat = (r_G[1], r_G[2], r_G[3])
        for blk in range(11):
            wa = wload(wcols(w_up_d, l, blk * 256, 256), 8, 256, key=("ua", l, blk))
            wg_ = wload(wcols(w_up_d, l, DFF + blk * 256, 256), 8, 256, key=("ug", l, blk))
            for cc in range(2):
                i = blk * 2 + cc
                outs = []
                for which, (wv, wr) in enumerate((wa, wg_)):
                    ci = which * 22 + i
                    ps, pr = proj_ws(wv, wr, cc * 128, 128, N)
                    taps = [pp[:, l, PP_FW + j * 44 + ci:PP_FW + j * 44 + ci + 1] for j in range(3)]
                    ac, rac = conv_s(128, ps, pr, taps, pp[:, l, PP_FB + ci:PP_FB + ci + 1], [r_pp], sffn[:, ci, :, :], r_sffn, which)
                    outs.append((ac, rac))
                S.op("act", [outs[0][1]], [r_acc[2]], lambda a=outs[0][0]: nc.scalar.activation(out=acc[2][:, 0:N], in_=a[:, 0:N], func=AF.Silu))
                S.op("dve", [r_acc[2], outs[1][1]], [r_gat[i // 8]], lambda g_=outs[1][0], i=i: nc.vector.tensor_tensor(
                    out=gat[i // 8][:, i % 8, 0:N], in0=acc[2][:, 0:N], in1=g_[:, 0:N], op=ALU.mult))
        S.dma("sp", d_out, [r_sffn], [], lambda: nc.sync.dma_start(out=sffn_o[l], in_=sffn[:]))
        for c in range(8):
            wh = [wload(w_dn_d[l, hh * 1408:(hh + 1) * 1408, c * 128:(c + 1) * 128].rearrange("(k p) c -> p k c", p=128), 11, 128, key=("dn", l, c, hh)) for hh in range(2)]
            ps, pr = psum()

            def f(wh=wh, ps=ps):
                for k in range(22):
                    last = nc.tensor.matmul(ps[:, 0:N], lhsT=wh[k // 11][0][:, k % 11, :], rhs=gat[k // 8][:, k % 8, 0:N], start=(k == 0), stop=(k == 21))
                return last
            S.op("pe", [wh[0][1], wh[1][1]] + list(r_gat), [pr], f)
            resid_s(ps, pr, c, 5)

    for ti in range(NT):
        S.dma("sp", d_x, [], [r_x], lambda ti=ti: nc.sync.dma_start(
            out=xT[:], in_=xT_d[:, ti * TT:(ti + 1) * TT].rearrange("(c p) t -> p c t", p=128)))
        for l in range(L):
            prompt_tile_layer(ti, l)
        ps, pr = psum()
        sq = G[5]
        S.op("act", [r_x], [r_G[5]], lambda: nc.scalar.activation(out=sq[:, :, :], in_=xT[:, :, :], func=AF.Square))

        def ff(ps=ps):
            for k in range(8):
                last = nc.tensor.matmul(ps[:, 0:TT], lhsT=onesb[:], rhs=sq[:, k, :], start=(k == 0), stop=(k == 7))
            return last
        S.op("pe", [r_G[5], r_cst], [pr], ff)
        rstd_of(ps[:, 0:TT], 128, acc[2][:, 0:TT], [pr, r_cst], [r_acc[2]], 1.0 / D)
        for c in range(8):
            yo, ryo = acc[c % 2], r_acc[c % 2]
            S.op("dve", [r_x, r_pp, r_acc[2]], [ryo], lambda c=c, yo=yo: nc.vector.scalar_tensor_tensor(
                out=yo[:, 0:TT], in0=xT[:, c, :], scalar=pp[:, 0, PP_GFIN + c:PP_GFIN + c + 1], in1=acc[2][:, 0:TT], op0=ALU.mult, op1=ALU.mult))
            S.dma("sp", d_out, [ryo], [], lambda ti=ti, yo=yo, c=c: nc.sync.dma_start(
                out=yT_d[c * 128:(c + 1) * 128, ti * TT:(ti + 1) * TT], in_=yo[:, 0:TT]))
    if SB:
        NS = 4 * SB
        S.dma("sp", d_par, [], [r_csT], lambda: nc.sync.dma_start(out=csT[:], in_=csT_d))
        S.op("act", [r_csT], [r_csT], lambda: nc.scalar.activation(out=csb[:], in_=csT[:], func=AF.Silu))
        S.dma("sp", d_x, [], [r_x], lambda: nc.sync.dma_start(out=xT[:, :, 0:NS], in_=xsT_d.rearrange("(c p) t -> p c t", p=128)))
        for l in range(L):
            sample_tile_layer(l)
        ps, pr = psum()
        sq = G[5]
        S.op("act", [r_x], [r_G[5]], lambda: nc.scalar.activation(out=sq[:, :, 0:NS], in_=xT[:, :, 0:NS], func=AF.Square))

        def ffs(ps=ps):
            for k in range(8):
                last = nc.tensor.matmul(ps[:, 0:NS], lhsT=onesb[:], rhs=sq[:, k, 0:NS], start=(k == 0), stop=(k == 7))
            return last
        S.op("pe", [r_G[5], r_cst], [pr], ffs)
        rstd_of(ps[:, 0:NS], 128, acc[2][:, 0:NS], [pr, r_cst], [r_acc[2]], 1.0 / D)
        for c in range(8):
            yo, ryo = acc[c % 2], r_acc[c % 2]
            S.op("dve", [r_x, r_pp, r_acc[2]], [ryo], lambda c=c, yo=yo: nc.vector.scalar_tensor_tensor(
                out=yo[:, 0:NS], in0=xT[:, c, 0:NS], scalar=pp[:, 0, PP_GFIN + c:PP_GFIN + c + 1], in1=acc[2][:, 0:NS], op0=ALU.mult, op1=ALU.mult))
            S.dma("sp", d_out, [ryo], [], lambda yo=yo, c=c: nc.sync.dma_start(out=ysT_d[c * 128:(c + 1) * 128, :], in_=yo[:, 0:NS]))
    S.finish()
    return nc


def _consts():
    cst = np.zeros((128, 772), np.float32)
    jj = np.arange(132)[None, :]
    ds = 128 + np.arange(128)[:, None] - jj
    cst[:, 640:772] = np.where((ds >= 0) & (ds <= 128), ds, 1e6)
    cst[:, 0:128] = np.eye(128, dtype=np.float32)
    s_ = np.arange(128)[:, None]
    t_ = np.arange(128)[None, :]
    cst[:, 128:256] = (s_ <= t_).astype(np.float32)
    cst[:, 256:384] = np.where(s_ > t_, NEG, 0.0)
    kpos = np.arange(256)[None, :] - 128
    dist = s_ - kpos
    cst[:, 384:640] = np.where((dist >= 0) & (dist <= 128), dist, 1e6)
    return cst


def _pp(w, L):
    pp = np.zeros((128, L, NPP), np.float32)
    pp64 = np.zeros((64, L, 40), np.float32)
    for l in range(L):
        pp[:, l, PP_BADA:PP_BADA + 48] = w["b_ada"][l].reshape(48, 128).T
        pp[:, l, PP_GMIX:PP_GMIX + 8] = w["g_norm_mix"][l].reshape(8, 128).T
        pp[:, l, PP_GFFN:PP_GFFN + 8] = w["g_norm_ffn"][l].reshape(8, 128).T
        for j in range(4):
            pp[:, l, PP_SCW + j * 8:PP_SCW + j * 8 + 8] = w["ssd_conv_w"][l][j, :1024].reshape(8, 128).T
            pp64[:, l, j * 8:j * 8 + 8] = w["ssd_conv_w"][l][j, 1024:].reshape(8, 64).T
        pp[:, l, PP_SCB:PP_SCB + 8] = w["ssd_conv_b"][l][:1024].reshape(8, 128).T
        pp64[:, l, 32:40] = w["ssd_conv_b"][l][1024:].reshape(8, 64).T
        for j in range(3):
            pp[:, l, PP_SHW + j * 8:PP_SHW + j * 8 + 8] = w["sc_conv_w"][l][j].reshape(8, 128).T
            pp[:, l, PP_FW + j * 44:PP_FW + j * 44 + 44] = w["ffn_conv_w"][l][j].reshape(44, 128).T
        pp[:, l, PP_FB:PP_FB + 44] = w["ffn_conv_b"][l].reshape(44, 128).T
        pp[:, l, PP_GFIN:PP_GFIN + 8] = w["g_final"].reshape(8, 128).T
    return pp, pp64


def _rows(w, L):
    rows = np.zeros((128, L, 64), np.float32)
    rowb = np.zeros((L, 128, 4096), np.float32)
    for l in range(L):
        rows[:, l, 0:16] = w["ssd_dt_bias"][l][None]
        rows[:, l, 16:32] = w["ssd_a_log"][l][None]
        rows[:, l, 32:48] = w["ssd_d"][l][None]
        rows[:, l, 48:64] = w["attn_sinks"][l][None]
        rowb[l, :, 0:1024] = w["ssd_norm_g"][l][None]
        rowb[l, :, 1024:2048] = w["gm_ln_g"][l][None]
        rowb[l, :, 2048:3072] = w["gm_ln_b"][l][None]
        rowb[l, :, 3072:4096] = w["gm_b_s"][l].reshape(1024)[None]
    return rows, rowb


def shared_maps(w, L):
    pp, pp64 = _pp(w, L)
    rows, rowb = _rows(w, L)
    m = {"pp": pp, "pp64": pp64, "rows": rows, "rowb": rowb, "cst": _consts()}
    for k in ("w_ada", "w_in", "w_branch", "w_o", "ffn_w_up", "ffn_w_down", "gm_w_s"):
        m[k] = np.ascontiguousarray(w[k][:L], dtype=np.float32)
    return m


def core_map(shared, xp_b, cp_b):
    m = dict(shared)
    m["xT"] = np.ascontiguousarray(xp_b.T)
    m["cT"] = np.ascontiguousarray(cp_b.reshape(8, 128).T)[:, :, None].copy()
    return m


def unpack_prompt(r, L):
    o = {}
    o["y"] = np.ascontiguousarray(r["yT"].T)
    o["ssm"] = np.ascontiguousarray(r["p_ssm"].reshape(L, 64, 4, 4, 64).transpose(0, 2, 3, 4, 1)).reshape(L, 16, 64, 64)
    xs = r["p_xbc"][..., 0:3].transpose(0, 3, 2, 1).reshape(L, 3, 1024)
    bc = r["p_bc"][..., 0:3].transpose(0, 3, 2, 1).reshape(L, 3, 512)
    o["ssd_conv"] = np.concatenate([xs, bc], axis=2)
    o["sc_conv"] = r["p_sc"].transpose(0, 3, 2, 1).reshape(L, 2, 1024)
    o["k"] = r["p_k"].reshape(L, 128, 4, 64)
    o["v"] = r["p_v"].reshape(L, 128, 4, 64)
    o["ffn_conv"] = r["p_ffn"][:, :, 0:44, :].transpose(0, 3, 2, 1).reshape(L, 2, 5632)
    return o


def sample_map(m, inp, bs, L):
    SBn = bs.stop - bs.start
    m["xsT"] = np.ascontiguousarray(inp["x_sample"][bs].reshape(4 * SBn, D).T)
    m["csT"] = np.ascontiguousarray(inp["c_sample"][bs].reshape(SBn, 8, 128).transpose(2, 1, 0))
    st = inp["state_ssm"][:L, bs]
    m["s_ssm_in"] = np.ascontiguousarray(st.reshape(L, SBn, 4, 4, 64, 64).transpose(0, 1, 5, 2, 3, 4)).reshape(L, SBn, 64, 4, 256)
    sc = inp["state_ssd_conv"][:L, bs]
    m["s_xbc_in"] = np.ascontiguousarray(sc[..., :1024].reshape(L, SBn, 3, 8, 128).transpose(0, 4, 3, 1, 2))
    m["s_bc_in"] = np.ascontiguousarray(sc[..., 1024:].reshape(L, SBn, 3, 8, 64).transpose(0, 4, 3, 1, 2))
    m["s_sc_in"] = np.ascontiguousarray(inp["state_sc_conv"][:L, bs].reshape(L, SBn, 2, 8, 128).transpose(0, 4, 3, 1, 2))
    m["s_ffn_in"] = np.ascontiguousarray(inp["state_ffn_conv"][:L, bs].reshape(L, SBn, 2, 44, 128).transpose(0, 4, 3, 1, 2))
    ck = inp["cache_k"][:L, bs]
    kt = ck.transpose(0, 1, 4, 3, 2)
    m["ckT"] = np.ascontiguousarray(np.concatenate([kt, kt], axis=2))
    m["ck"] = np.ascontiguousarray(ck.reshape(L, SBn, 128, 256))
    m["cv"] = np.ascontiguousarray(inp["cache_v"][:L, bs].reshape(L, SBn, 128, 256))
    return m


def unpack_sample(r, L, SBn):
    o = {}
    o["y"] = np.ascontiguousarray(r["ysT"].T).reshape(SBn, 4, D)
    o["ssm"] = np.ascontiguousarray(r["s_ssm_o"].reshape(L, SBn, 64, 4, 4, 64).transpose(0, 1, 3, 4, 5, 2)).reshape(L, SBn, 16, 64, 64)
    xs = r["s_xbc_o"].transpose(0, 3, 4, 2, 1).reshape(L, SBn, 3, 1024)
    bc = r["s_bc_o"].transpose(0, 3, 4, 2, 1).reshape(L, SBn, 3, 512)
    o["ssd_conv"] = np.concatenate([xs, bc], axis=3)
    o["sc_conv"] = r["s_sc_o"].transpose(0, 3, 4, 2, 1).reshape(L, SBn, 2, 1024)
    o["k"] = r["s_k_o"].reshape(L, SBn, 128, 4, 64)
    o["v"] = r["s_v_o"].reshape(L, SBn, 128, 4, 64)
    o["ffn_conv"] = r["s_ffn_o"].transpose(0, 3, 4, 2, 1).reshape(L, SBn, 2, 5632)
    o["gm_v"] = r["s_gmv_o"].reshape(L, SBn, 4, 1024)
    return o


def kernel(**inputs):
    inp = {k: np.asarray(v) for k, v in inputs.items()}
    L = 4
    B = inp["x_prompt"].shape[0]
    SEQ = inp["x_prompt"].shape[1]
    NT = SEQ // TT
    DB = inp["x_sample"].shape[0]
    SBn = DB // 8
    shared = shared_maps(inp, L)
    nc = build(NT, L, SBn)
    in_maps = []
    for core in range(8):
        b = core % B
        m = core_map(shared, inp["x_prompt"][b], inp["c_prompt"][b])
        sample_map(m, inp, slice(core * SBn, (core + 1) * SBn), L)
        in_maps.append(m)
    res = run_bass_kernel_spmd(nc, in_maps, core_ids=list(range(8)))
    rs = [{k: np.asarray(v) for k, v in r.items()} for r in res.results]
    po = [unpack_prompt(rs[b], L) for b in range(B)]
    so = [unpack_sample(rs[c], L, SBn) for c in range(8)]
    f32 = np.float32
    y_prompt = np.stack([p["y"] for p in po], 0).astype(f32)
    y_sample = np.concatenate([s_["y"] for s_ in so], 0).astype(f32)

    def pst(k):
        return np.ascontiguousarray(np.stack([p[k] for p in po], 1)).astype(f32)

    def sst(k):
        return np.ascontiguousarray(np.concatenate([s_[k] for s_ in so], 1)).astype(f32)
    return (y_prompt, y_sample, pst("ssm"), pst("ssd_conv"), pst("sc_conv"), pst("k"), pst("v"), pst("ffn_conv"),
            sst("ssm"), sst("ssd_conv"), sst("sc_conv"), sst("k"), sst("v"), sst("ffn_conv"), sst("gm_v"))
```

```python
import os
import numpy as np
import concourse.bass as bass
import concourse.mybir as mybir
from concourse.bass_utils import run_bass_kernel_spmd

F32 = mybir.dt.float32
BF16 = mybir.dt.bfloat16
AF = mybir.ActivationFunctionType
ALU = mybir.AluOpType
AX = mybir.AxisListType

D = 1024
TT = 512
NCH = TT // 128
WBC = 256
SLOPES = [2.0 ** (-8.0 * (h + 1) / 16) for h in range(16)]
EPS = 1e-6
XBC0, DT0, BCX0, Q0, K0, V0, UV0, GT0, DIN = 1024, 2560, 2576, 5648, 6672, 6928, 7184, 9232, 13328
DFF = 2816
PP_BADA, PP_GMIX, PP_GFFN, PP_SCW, PP_SCB, PP_SHW, PP_FW, PP_FB, PP_GFIN = 0, 48, 56, 64, 96, 104, 128, 260, 304
NPP = 312
NEG = -30000.0
DBG_STOP = int(os.environ.get('DBG_STOP', '99'))
DBG_ATT = int(os.environ.get('DBG_ATT', '99'))
DBG_ATTP = int(os.environ.get('DBG_ATTP', '99'))
DBG_S = int(os.environ.get('DBG_S', '99'))


class Res:
    __slots__ = ("name", "w", "r", "excl")

    def __init__(self, name, excl=False):
        self.name = name
        self.w = None
        self.r = {}
        self.excl = excl


class DSem:
    __slots__ = ("sem", "tot", "key", "keep")

    def __init__(self, nc, name):
        self.sem = nc.alloc_semaphore(name)
        self.tot = 0
        self.key = name
        self.keep = False


class Sched:
    def __init__(self, nc):
        self.nc = nc
        self.eng = {}
        self.seen = {}
        self.dsems = []
        self.dkeys = {}
        for name, h in (("pe", nc.tensor), ("act", nc.scalar), ("dve", nc.vector),
                        ("pool", nc.gpsimd), ("sp", nc.sync)):
            self.eng[name] = [h, nc.alloc_semaphore("s_" + name), 0]

    def dsem(self, name):
        d = DSem(self.nc, "d_" + name)
        self.dsems.append(d)
        self.dkeys[d.key] = d
        return d

    def _waits(self, eng, reads, writes):
        need = {}

        def add(ent):
            key, sem, val = ent
            if key in self.dkeys:
                val = self.dkeys[key].tot
            if key not in need or need[key][1] < val:
                need[key] = (sem, val)

        for r in reads:
            if r.w is not None:
                add(r.w)
            if r.excl:
                for key, (sem, val) in r.r.items():
                    if key != eng:
                        add((key, sem, val))
        for w in writes:
            if w.w is not None:
                add(w.w)
            for key, (sem, val) in w.r.items():
                add((key, sem, val))
        h = self.eng[eng][0]
        for key, (sem, val) in need.items():
            if self.seen.get((eng, key), 0) < val:
                h.wait_ge(sem, val)
                self.seen[(eng, key)] = val

    def op(self, eng, reads, writes, fn):
        E = self.eng[eng]
        self._waits(eng, reads, writes)
        inst = fn()
        E[2] += 1
        inst.then_inc(E[1], 1)
        for r in reads:
            r.r[eng] = (E[1], E[2])
        for w in writes:
            w.w = (eng, E[1], E[2])
            w.r = {}
        return inst

    def _auto_dsem(self, reads, writes):
        if writes:
            name = "ld_" + writes[0].name
        elif reads:
            name = "st_" + reads[0].name
        else:
            name = "dd"
        d = self.dkeys.get("d_" + name)
        if d is None:
            d = self.dsem(name)
        return d

    def dma(self, q, ds, reads, writes, fn, extra=()):
        if not getattr(ds, "keep", False):
            ds = self._auto_dsem(reads, writes)
        self._waits(q, reads, writes)
        for (sem_, val_, key_) in extra:
            if self.seen.get((q, key_), 0) < val_:
                self.eng[q][0].wait_ge(sem_, val_)
                self.seen[(q, key_)] = val_
        inst = fn()
        ds.tot += 16
        inst.then_inc(ds.sem, 16)
        for r in reads:
            r.r[ds.key] = (ds.sem, ds.tot)
        for w in writes:
            w.w = (ds.key, ds.sem, ds.tot)
            w.r = {}
        return inst

    def finish(self, eng="sp"):
        h = self.eng[eng][0]
        for d in self.dsems:
            if d.tot > 0:
                h.wait_ge(d.sem, d.tot)


def build(NT, L, SB, dbg=False):
    nc = bass.Bass("TRN2", target_bir_lowering=False)
    S = Sched(nc)
    NTOK = NT * TT

    def din(name, shape):
        return nc.dram_tensor(name, list(shape), F32, kind="ExternalInput").ap()

    def dout(name, shape):
        return nc.dram_tensor(name, list(shape), F32, kind="ExternalOutput").ap()

    xT_d = din("xT", [D, NTOK])
    cT_d = din("cT", [128, 8, 1])
    w_ada_d = din("w_ada", [L, D, 6 * D])
    w_in_d = din("w_in", [L, D, DIN])
    w_br_d = din("w_branch", [L, 4, D, D])
    w_o_d = din("w_o", [L, D, D])
    w_up_d = din("ffn_w_up", [L, D, 2 * DFF])
    w_dn_d = din("ffn_w_down", [L, DFF, D])
    pp_d = din("pp", [128, L, NPP])
    pp64_d = din("pp64", [64, L, 40])
    rows_d = din("rows", [128, L, 64])
    rowb_d = din("rowb", [L, 128, 4096])
    gmw_d = din("gm_w_s", [L, 8, 128, 128])
    cst_d = din("cst", [128, 900])

    yT_d = dout("yT", [D, NTOK])
    pssm_d = dout("p_ssm", [L, 64, 4, 256])
    pxbc_d = dout("p_xbc", [L, 128, 8, 4])
    pbc_d = dout("p_bc", [L, 64, 8, 4])
    psc_d = dout("p_sc", [L, 128, 8, 2])
    pk_d = dout("p_k", [L, 128, 256])
    pv_d = dout("p_v", [L, 128, 256])
    pffn_d = dout("p_ffn", [L, 128, 48, 2])

    if SB:
        xsT_d = din("xsT", [D, 4 * SB])
        csT_d = din("csT", [128, 8, SB])
        sssm_d = din("s_ssm_in", [L, SB, 64, 4, 256])
        sxbc_d = din("s_xbc_in", [L, 128, 8, SB, 3])
        sbc_d = din("s_bc_in", [L, 64, 8, SB, 3])
        ssc_d = din("s_sc_in", [L, 128, 8, SB, 2])
        sffn_d = din("s_ffn_in", [L, 128, 44, SB, 2])
        ckT_d = din("ckT", [L, SB, 128, 4, 128])
        ck_d = din("ck", [L, SB, 128, 256])
        cv_d = din("cv", [L, SB, 128, 256])
        ysT_d = dout("ysT", [D, 4 * SB])
        sssm_o = dout("s_ssm_o", [L, SB, 64, 4, 256])
        sxbc_o = dout("s_xbc_o", [L, 128, 8, SB, 3])
        sbc_o = dout("s_bc_o", [L, 64, 8, SB, 3])
        ssc_o = dout("s_sc_o", [L, 128, 8, SB, 2])
        sffn_o = dout("s_ffn_o", [L, 128, 44, SB, 2])
        sk_o = dout("s_k_o", [L, SB, 128, 256])
        sv_o = dout("s_v_o", [L, SB, 128, 256])
        sgmv_o = dout("s_gmv_o", [L, SB, 4, 1024])

    def sb(name, shape, dt=F32):
        return nc.alloc_sbuf_tensor(name, list(shape), dt)

    xT = sb("xTt", [128, 8, TT]); r_x = Res("x")
    mod = sb("mod", [128, L, 48, 1]); r_mod = Res("mod")
    gm = sb("gm", [128, L, 2, 8, 1]); r_gm = Res("gm")
    pp = sb("ppt", [128, L, NPP]); r_pp = Res("pp")
    pp64 = sb("pp64t", [64, L, 40]); r_pp64 = Res("pp64")
    rows = sb("rowst", [128, L, 64]); r_rows = Res("rows")
    rowb = sb("rowbt", [128, 3072]); r_rowb = Res("rowb")
    cst = sb("cstt", [128, 900]); r_cst = Res("cst")
    identb = sb("identb", [128, 128], BF16)
    onesb = sb("onesb", [128, 128], BF16)
    ident = cst[:, 0:128]
    Umat = cst[:, 128:256]
    NEGM = cst[:, 256:384]
    distm = cst[:, 384:640]
    dists = cst[:, 640:772]
    Rrep = cst[0:4, 772:836]
    bmask = cst[0:64, 836:900]
    gmwT = sb("gmwT", [128, 8, 128], BF16); r_gmw = Res("gmwT")
    gmw_raw = sb("gmw_raw", [128, 8, 128], BF16); r_gmraw = Res("gmw_raw")
    ST = sb("ST", [64, L, 4, 256]); r_ST = [Res("ST%d" % l) for l in range(L)]
    STb = sb("STb", [64, 4, 256], BF16); r_STb1 = Res("STb"); r_STb = [r_STb1] * L
    cx = sb("cx", [128, L, 8, 4]); r_cx = [Res("cx%d" % l) for l in range(L)]
    cbc = sb("cbc", [64, L, 8, 4]); r_cbc = [Res("cbc%d" % l) for l in range(L)]
    csc = sb("csc", [128, L, 8, 2]); r_csc = [Res("csc%d" % l) for l in range(L)]
    cff = sb("cff", [128, L, 48, 2]); r_cff = [Res("cff%d" % l) for l in range(L)]
    kprev = sb("kprev", [128, L, 4, 128], BF16); r_kprev = [Res("kp%d" % l) for l in range(L)]
    vprev = sb("vprev", [128, L, 256], BF16); r_vprev = [Res("vp%d" % l) for l in range(L)]
    G = [sb("G%d" % i, [128, 8, TT], BF16) for i in range(6)]
    r_G = [Res("G%d" % i) for i in range(6)]
    hB, r_h = G[0], r_G[0]
    stg = [sb("stg%d" % i, [128, TT + 4]) for i in range(3)]
    r_stg = [Res("stg%d" % i) for i in range(3)]
    acc = [sb("acc%d" % i, [128, TT]) for i in range(3)]
    r_acc = [Res("acc%d" % i) for i in range(3)]
    BCT = G[2][0:64, :, :]; r_BCT = r_G[2]
    kT = sb("kT", [128, 4, TT], BF16); r_kT = Res("kT")
    vtok = sb("vtok", [128, NCH, 256], BF16); r_vtok = Res("vtok")
    tA = sb("tA", [128, 1024]); r_tA = Res("tA")
    tB = sb("tB", [128, 1024]); r_tB = Res("tB")
    tC = sb("tC", [128, 1024], BF16); r_tC = Res("tC")
    tD = sb("tD", [128, 1024], BF16); r_tD = Res("tD")
    tE = sb("tE", [128, 1024], BF16); r_tE = Res("tE")
    Eh = sb("Eh", [128, 16, 128], BF16); r_Eh = Res("Eh")
    Mh, r_Mh = Eh, r_Eh
    CBs = sb("CBs", [128, 4, 128], BF16); r_CBs = Res("CBs")
    Btok = sb("Btok", [128, 4, 64], BF16); r_Btok = Res("Btok")
    sm = sb("sm", [128, 256]); r_sm = Res("sm")
    r_aS, r_aP, r_aT = Res("attS"), Res("attP"), Res("attT")
    RV = {k: Res("sm_" + k) for k in ("dt", "dtA", "Acs", "nAcs", "eA", "dec", "wdec", "cdec", "ssq",
                                      "rinv", "mx", "nmx", "esk", "rsum", "ssum", "mean", "gssq", "rstd")}
    ktok, r_ktok = tB, r_tB

    d_in = d_par = d_rowb = d_out = d_x = None

    PS = [nc.alloc_psum_tensor("ps%d" % i, [128, 512], F32) for i in range(8)]
    r_PS = [Res("ps%d" % i, excl=True) for i in range(8)]
    psi = [0]

    def psum():
        i = psi[0]
        psi[0] = (i + 1) % 6
        return PS[i], r_PS[i]

    NSLOT = 7
    WS = [sb("ws%d" % i, [128, 2048], BF16) for i in range(NSLOT)]
    r_WS = [Res("ws%d" % i) for i in range(NSLOT)]
    d_WS = [S.dsem("ws%d" % i) for i in range(NSLOT)]
    for d_ in d_WS:
        d_.keep = True
    wsi = [0]

    NSCR = 111 * L + 2
    wscr = nc.dram_tensor("wscr", [NSCR, 128, 2048], BF16, kind="Internal").ap()
    scr = {}

    def wload(src, K, C, key=None):
        i = wsi[0]
        wsi[0] = (i + 1) % NSLOT
        flat = WS[i][:, 0:K * C]
        view = flat.rearrange("p (k c) -> p k c", k=K)
        if key is not None and key in scr:
            idx, ent = scr[key]
            S.dma("sp", d_WS[i], [], [r_WS[i]], lambda: nc.sync.dma_start(out=flat, in_=wscr[idx, :, 0:K * C]), extra=[ent])
            return view, r_WS[i]
        S.dma("pool", d_WS[i], [], [r_WS[i]], lambda: nc.gpsimd.dma_start(out=view, in_=src))
        if key is not None:
            idx = len(scr)
            assert idx < NSCR
            S.dma("sp", None, [r_WS[i]], [], lambda: nc.sync.dma_start(out=wscr[idx, :, 0:K * C], in_=flat))
            d = S.dkeys["d_st_ws%d" % i]
            scr[key] = (idx, (d.sem, d.tot, d.key))
        return view, r_WS[i]

    def wcols(wd, l, c0, C):
        return wd[l, :, c0:c0 + C].rearrange("(k p) c -> p k c", p=128)

    ev = [0]

    def eng2():
        ev[0] ^= 1
        return "act" if ev[0] else "dve"

    def copy(eng, out, in_, reads, writes):
        if eng == "act":
            return S.op("act", reads, writes, lambda: nc.scalar.copy(out=out, in_=in_))
        return S.op(eng, reads, writes, lambda: (nc.vector if eng == "dve" else nc.gpsimd).tensor_copy(out=out, in_=in_))

    S.dma("sp", d_par, [], [r_pp], lambda: nc.sync.dma_start(out=pp[:], in_=pp_d))
    S.dma("sp", d_par, [], [r_pp64], lambda: nc.sync.dma_start(out=pp64[:], in_=pp64_d))
    S.dma("sp", d_par, [], [r_rows], lambda: nc.sync.dma_start(out=rows[:], in_=rows_d))
    S.dma("sp", d_par, [], [r_cst], lambda: nc.sync.dma_start(out=cst[:], in_=cst_d))
    S.op("dve", [r_cst], [r_cst], lambda: nc.vector.tensor_copy(out=identb[:], in_=ident))
    S.op("dve", [], [r_cst], lambda: nc.vector.memset(onesb[:], 1.0))
    dupI = sb("dupI", [64, 128], BF16)
    S.op("dve", [r_cst], [r_cst], lambda: nc.vector.tensor_copy(out=dupI[:, 0:64], in_=ident[0:64, 0:64]))
    S.op("dve", [r_cst], [r_cst], lambda: nc.vector.tensor_copy(out=dupI[:, 64:128], in_=ident[0:64, 0:64]))
    ones32 = sb("ones32", [128, 128])
    S.op("dve", [], [r_cst], lambda: nc.vector.memset(ones32[:], 1.0))
    S.op("act", [r_rows], [r_rows], lambda: nc.scalar.activation(out=rows[:, :, 16:32], in_=rows[:, :, 16:32], func=AF.Exp))
    S.op("dve", [r_rows], [r_rows], lambda: nc.vector.tensor_scalar(out=rows[:, :, 16:32], in0=rows[:, :, 16:32], scalar1=-1.0, scalar2=None, op0=ALU.mult))
    for t_, r_ in ((ST, r_ST), (cx, r_cx), (cbc, r_cbc), (csc, r_csc), (cff, r_cff)):
        S.op("dve", [], list(r_), lambda t_=t_: nc.vector.memset(t_[:], 0.0))

    NCc = 1
    cTt = sb("cTt", [128, 8, NCc]); r_cT = Res("cT")
    cTb = sb("cTb", [128, 8, NCc], BF16)
    S.dma("sp", d_par, [], [r_cT], lambda: nc.sync.dma_start(out=cTt[:], in_=cT_d))
    S.op("act", [r_cT], [r_cT], lambda: nc.scalar.activation(out=cTb[:], in_=cTt[:], func=AF.Silu))

    def ada_layer(l, cb, r_cb, ncol, mod_t, r_mod_t, gm_t, r_gm_t, lidx):
        for blk in range(24):
            wv, wr = wload(wcols(w_ada_d, l, blk * 256, 256), 8, 256)
            ps, pr = psum()

            def f(wv=wv, ps=ps):
                for i in range(2):
                    for k in range(8):
                        last = nc.tensor.matmul(ps[:, i * ncol:(i + 1) * ncol], lhsT=wv[:, k, i * 128:(i + 1) * 128],
                                                rhs=cb[:, k, :], start=(k == 0), stop=(k == 7))
                return last
            S.op("pe", [wr, r_cb], [pr], f)
            S.op("dve", [pr, r_pp], [r_mod_t], lambda ps=ps, blk=blk: nc.vector.tensor_tensor(
                out=mod_t[:, lidx, blk * 2:blk * 2 + 2, :], in0=ps[:, 0:2 * ncol].rearrange("p (i j) -> p i j", i=2),
                in1=pp[:, l, PP_BADA + blk * 2:PP_BADA + blk * 2 + 2].unsqueeze(2).to_broadcast([128, 2, ncol]), op=ALU.add))
        for which, (sc_i, gcol) in enumerate(((1, PP_GMIX), (4, PP_GFFN))):
            S.op("dve", [r_mod_t, r_pp], [r_gm_t], lambda which=which, sc_i=sc_i, gcol=gcol: nc.vector.scalar_tensor_tensor(
                out=gm_t[:, lidx, which, :, :], in0=mod_t[:, lidx, sc_i * 8:sc_i * 8 + 8, :], scalar=1.0,
                in1=pp[:, l, gcol:gcol + 8].unsqueeze(2).to_broadcast([128, 8, ncol]), op0=ALU.add, op1=ALU.mult))

    for l in range(L):
        ada_layer(l, cTb, r_cT, 1, mod, r_mod, gm, r_gm, l)

    epsc = sb("epsc", [128, 1])
    S.op("dve", [], [r_cst], lambda: nc.vector.memset(epsc[:], EPS))

    def rstd_of(ss_ap, n, out_ap, reads, writes, scale):
        S.op("act", reads, writes, lambda: nc.scalar.activation(out=out_ap, in_=ss_ap, func=AF.Ln, bias=epsc[0:n, :], scale=scale))
        S.op("act", writes, writes, lambda: nc.scalar.activation(out=out_ap, in_=out_ap, func=AF.Exp, scale=-0.5))

    def mod_norm(l, which, N):
        sh_i = 0 if which == 0 else 3
        ps, pr = psum()
        sq = G[5]
        S.op("act", [r_x], [r_G[5]], lambda: nc.scalar.activation(out=sq[:, :, 0:N], in_=xT[:, :, 0:N], func=AF.Square))

        def f():
            for k in range(8):
                last = nc.tensor.matmul(ps[:, 0:N], lhsT=onesb[:], rhs=sq[:, k, 0:N], start=(k == 0), stop=(k == 7))
            return last
        S.op("pe", [r_G[5], r_cst], [pr], f)
        rs = acc[2]
        rstd_of(ps[:, 0:N], 128, rs[:, 0:N], [pr, r_cst], [r_acc[2]], 1.0 / D)
        for c in range(8):
            S.op("dve", [r_x, r_gm, r_acc[2]], [r_acc[c % 2]], lambda c=c: nc.vector.scalar_tensor_tensor(
                out=acc[c % 2][:, 0:N], in0=xT[:, c, 0:N], scalar=gm[:, l, which, c, 0:1], in1=rs[:, 0:N],
                op0=ALU.mult, op1=ALU.mult))
            S.op("act", [r_acc[c % 2], r_mod], [r_h], lambda c=c: nc.scalar.activation(
                out=hB[:, c, 0:N], in_=acc[c % 2][:, 0:N], func=AF.Identity, bias=mod[:, l, sh_i * 8 + c, 0:1], scale=1.0))

    def proj_ws(wv, wr, col, M, N, rhs=None, rres=None, nk=8):
        rhs = hB if rhs is None else rhs
        rres = r_h if rres is None else rres
        ps, pr = psum()

        def f():
            for k in range(nk):
                last = nc.tensor.matmul(ps[0:M, 0:N], lhsT=wv[:, k, col:col + M], rhs=rhs[:, k, 0:N],
                                        start=(k == 0), stop=(k == nk - 1))
            return last
        S.op("pe", [wr, rres], [pr], f)
        return ps, pr

    def proj_as(wlist, C, t0, T):
        ps, pr = psum()

        def f():
            for bi, (wv, wr) in enumerate(wlist):
                cw = min(256, C - bi * 256)
                for k in range(8):
                    last = nc.tensor.matmul(ps[0:T, bi * 256:bi * 256 + cw], lhsT=hB[:, k, t0:t0 + T], rhs=wv[:, k, 0:cw],
                                            start=(k == 0), stop=(k == 7))
            return last
        S.op("pe", [w[1] for w in wlist] + [r_h], [pr], f)
        return ps, pr

    def conv_fm(P, ps, pr, N, carry_ap, r_carry, taps, bias_ap, preads, si):
        K = len(taps)
        CW = K - 1
        st, rst = stg[si], r_stg[si]
        ac, rac = acc[si], r_acc[si]
        S.op("act", [pr], [rst], lambda: nc.scalar.copy(out=st[0:P, CW:CW + N], in_=ps[0:P, 0:N]))
        S.op("dve", [r_carry], [rst], lambda: nc.vector.tensor_copy(out=st[0:P, 0:CW], in_=carry_ap))
        S.op("dve", [rst], [r_carry], lambda: nc.vector.tensor_copy(out=carry_ap, in_=st[0:P, N:N + CW]))
        if bias_ap is not None:
            S.op("dve", [rst] + preads, [rac], lambda: nc.vector.tensor_scalar(
                out=ac[0:P, 0:N], in0=st[0:P, 0:N], scalar1=taps[0], scalar2=bias_ap, op0=ALU.mult, op1=ALU.add))
        else:
            S.op("dve", [rst] + preads, [rac], lambda: nc.vector.tensor_scalar(
                out=ac[0:P, 0:N], in0=st[0:P, 0:N], scalar1=taps[0], scalar2=None, op0=ALU.mult))
        for j in range(1, K):
            S.op("dve", [rst, rac] + preads, [rac], lambda j=j: nc.vector.scalar_tensor_tensor(
                out=ac[0:P, 0:N], in0=st[0:P, j:j + N], scalar=taps[j], in1=ac[0:P, 0:N], op0=ALU.mult, op1=ALU.add))
        return ac, rac

    def win(l, c0, C=256):
        return wload(wcols(w_in_d, l, c0, C), 8, C, key=("in", l, c0))

    def run(g):
        for _ in g:
            pass

    def interleave(g1, g2):
        live = [g1, g2]
        while live:
            for g in list(live):
                try:
                    next(g)
                except StopIteration:
                    live.remove(g)

    def prompt_tile_layer(ti, l):
        N = TT
        last_tile = (ti == NT - 1)
        S.dma("sp", d_rowb, [], [r_rowb], lambda: nc.sync.dma_start(out=rowb[:, 0:1024], in_=rowb_d[l, :, 0:1024]))
        normg = rowb[:, 0:1024]
        lng = rowb[:, 0:1024]
        lnb = rowb[:, 1024:2048]
        bsrow = rowb[:, 2048:3072]
        mod_norm(l, 0, N)
        xsT, r_xsT = G[3], r_G[3]
        for bi in range(4):
            wv, wr = win(l, XBC0 + bi * 256)
            for cc in range(2):
                c = bi * 2 + cc
                ps, pr = proj_ws(wv, wr, cc * 128, 128, N)
                taps = [pp[:, l, PP_SCW + j * 8 + c:PP_SCW + j * 8 + c + 1] for j in range(4)]
                ac, rac = conv_fm(128, ps, pr, N, cx[:, l, c, 0:3], r_cx[l], taps, pp[:, l, PP_SCB + c:PP_SCB + c + 1], [r_pp], c % 2)
                S.op("act", [rac], [r_xsT], lambda ac=ac, c=c: nc.scalar.activation(out=xsT[:, c, 0:N], in_=ac[:, 0:N], func=AF.Silu))
        for bi in range(2):
            wv, wr = win(l, XBC0 + 1024 + bi * 256)
            for ee in range(4):
                e = bi * 4 + ee
                ps, pr = proj_ws(wv, wr, ee * 64, 64, N)
                taps = [pp64[:, l, j * 8 + e:j * 8 + e + 1] for j in range(4)]
                ac, rac = conv_fm(64, ps, pr, N, cbc[:, l, e, 0:3], r_cbc[l], taps, pp64[:, l, 32 + e:33 + e], [r_pp64], e % 2)
                S.op("act", [rac], [r_BCT], lambda ac=ac, e=e: nc.scalar.activation(out=BCT[:, e, 0:N], in_=ac[0:64, 0:N], func=AF.Silu))
        qT, r_qT = G[5], r_G[5]
        wk = win(l, K0)
        wvv = win(l, V0)
        for hk in range(4):
            ps, pr = proj_ws(wk[0], wk[1], hk * 64, 64, N)
            ktmp, r_ktmp = tC, r_tC
            copy("act", ktmp[0:64, 0:N], ps[0:64, 0:N], [pr], [r_ktmp])
            ps2, pr2 = psum()
            S.op("pe", [r_ktmp, r_cst], [pr2], lambda ps2=ps2: nc.tensor.matmul(ps2[:, 0:N], lhsT=dupI[:, :], rhs=ktmp[0:64, 0:N], start=True, stop=True))
            copy("dve", kT[:, hk, 0:N], ps2[:, 0:N], [pr2], [r_kT])
        for j in range(NCH):
            ps, pr = proj_as([wk, wvv], 512, j * 128, 128)
            if last_tile and j == NCH - 1:
                S.op("act", [pr], [r_ktok], lambda ps=ps: nc.scalar.copy(out=ktok[:, 0:512], in_=ps[:, :]))
                S.dma("sp", d_out, [r_ktok], [], lambda: nc.sync.dma_start(out=pk_d[l], in_=ktok[:, 0:256]))
                S.dma("sp", d_out, [r_ktok], [], lambda: nc.sync.dma_start(out=pv_d[l], in_=ktok[:, 256:512]))
            S.op("dve", [pr], [r_vtok], lambda j=j, ps=ps: nc.vector.tensor_copy(out=vtok[:, j, :], in_=ps[:, 256:512]))
        for bi in range(4):
            wv, wr = win(l, Q0 + bi * 256)
            for cc in range(2):
                c = bi * 2 + cc
                ps, pr = proj_ws(wv, wr, cc * 128, 128, N)
                S.op("act", [pr], [r_qT], lambda ps=ps, c=c: nc.scalar.activation(out=qT[:, c, 0:N], in_=ps[:, 0:N], func=AF.Identity, scale=0.125))
        ycT, r_ycT = G[3], r_G[3]
        wz = [win(l, i * 256) for i in range(4)]
        wdt = [win(l, DT0, 256)]
        S.op("act", [r_ST[l]], [r_STb1], lambda: nc.scalar.copy(out=STb[:, :, :], in_=ST[:, l, :, :]))
        yaT, r_yaT = G[1], r_G[1]
        for j in range(NCH):
            interleave(ssd_chunk(l, j * 128, 128, wz, wdt, xsT, r_xsT, yaT, r_yaT, normg),
                       attn_chunk(l, ti, j, qT, r_qT, ycT, r_ycT))
        S.op("dve", [r_kT], [r_kprev[l]], lambda: nc.vector.tensor_copy(out=kprev[:, l, :, :], in_=kT[:, :, N - 128:N]))
        S.op("dve", [r_vtok], [r_vprev[l]], lambda: nc.vector.tensor_copy(out=vprev[:, l, :], in_=vtok[:, NCH - 1, :]))
        if last_tile:
            S.dma("sp", d_out, [r_ST[l]], [], lambda: nc.sync.dma_start(out=pssm_d[l], in_=ST[:, l, :, :]))
            S.dma("sp", d_out, [r_cx[l]], [], lambda: nc.sync.dma_start(out=pxbc_d[l], in_=cx[:, l, :, :]))
            S.dma("sp", d_out, [r_cbc[l]], [], lambda: nc.sync.dma_start(out=pbc_d[l], in_=cbc[:, l, :, :]))
        ybT, r_ybT = G[2], r_G[2]
        for bi in range(4):
            wb = [win(l, BCX0 + part * 1024 + bi * 256) for part in range(3)]
            for cc in range(2):
                c = bi * 2 + cc
                psB, prB = proj_ws(wb[0][0], wb[0][1], cc * 128, 128, N)
                psC, prC = proj_ws(wb[1][0], wb[1][1], cc * 128, 128, N)
                psX, prX = proj_ws(wb[2][0], wb[2][1], cc * 128, 128, N)
                st, rst = stg[2], r_stg[2]
                S.op("act", [prC], [r_acc[2]], lambda psC=psC: nc.scalar.copy(out=acc[2][:, 0:N], in_=psC[:, 0:N]))
                S.op("dve", [prX, r_acc[2]], [rst], lambda psX=psX: nc.vector.tensor_tensor(
                    out=st[:, 2:2 + N], in0=psX[:, 0:N], in1=acc[2][:, 0:N], op=ALU.mult))
                S.op("dve", [r_csc[l]], [rst], lambda c=c: nc.vector.tensor_copy(out=st[:, 0:2], in_=csc[:, l, c, :]))
                S.op("dve", [rst], [r_csc[l]], lambda c=c: nc.vector.tensor_copy(out=csc[:, l, c, :], in_=st[:, N:N + 2]))
                ac, rac = acc[c % 2], r_acc[c % 2]
                taps = [pp[:, l, PP_SHW + j * 8 + c:PP_SHW + j * 8 + c + 1] for j in range(3)]
                S.op("dve", [rst, r_pp], [rac], lambda ac=ac, taps=taps: nc.vector.tensor_scalar(
                    out=ac[:, 0:N], in0=st[:, 0:N], scalar1=taps[0], scalar2=None, op0=ALU.mult))
                for jj in (1, 2):
                    S.op("dve", [rst, rac, r_pp], [rac], lambda ac=ac, taps=taps, jj=jj: nc.vector.scalar_tensor_tensor(
                        out=ac[:, 0:N], in0=st[:, jj:jj + N], scalar=taps[jj], in1=ac[:, 0:N], op0=ALU.mult, op1=ALU.add))
                S.op("dve", [prB, rac], [r_ybT], lambda ac=ac, psB=psB, c=c: nc.vector.tensor_tensor(
                    out=ybT[:, c, 0:N], in0=psB[:, 0:N], in1=ac[:, 0:N], op=ALU.mult))
        if last_tile:
            S.dma("sp", d_out, [r_csc[l]], [], lambda: nc.sync.dma_start(out=psc_d[l], in_=csc[:, l, :, :]))
        ydT, r_ydT = G[4], r_G[4]
        S.dma("sp", d_rowb, [], [r_rowb], lambda: nc.sync.dma_start(out=rowb[:, :], in_=rowb_d[l, :, 1024:4096]))
        S.dma("pool", d_in, [], [r_gmraw], lambda: nc.gpsimd.dma_start(out=gmw_raw[:], in_=gmw_d[l].rearrange("g t s -> t g s")))
        for g in range(8):
            ps, pr = psum()
            psb = ps[:, 0:64].bitcast(BF16)
            S.op("pe", [r_gmraw, r_cst], [pr], lambda psb=psb, g=g: nc.tensor.transpose(psb[:, 0:128], gmw_raw[:, g, :], identb[:]))
            S.op("dve", [pr, r_cst], [r_gmw], lambda psb=psb, g=g: nc.vector.tensor_tensor(
                out=gmwT[:, g, :], in0=psb[:, 0:128], in1=Umat, op=ALU.mult))
        for bi in range(4):
            wv, wr = win(l, UV0 + bi * 256)
            for cc in range(2):
                c = bi * 2 + cc
                ps, pr = proj_ws(wv, wr, cc * 128, 128, N)
                S.op("act", [pr], [r_ydT], lambda ps=ps, c=c: nc.scalar.activation(out=ydT[:, c, 0:N], in_=ps[:, 0:N], func=AF.Gelu_apprx_tanh))
        wv_ = [win(l, UV0 + 1024 + i * 256) for i in range(4)]
        for j in range(NCH):
            gmlp_chunk(l, j * 128, 128, wv_, ydT, r_ydT, lng, lnb, bsrow, gmwT, None)
        mT, r_mT = G[5], r_G[5]
        brs = ((G[1], r_G[1]), (G[2], r_G[2]), (G[3], r_G[3]), (G[4], r_G[4]))
        macc = (acc[0], acc[1])
        r_macc = (r_acc[0], r_acc[1])
        for bi in range(4):
            for i in range(4):
                wg = win(l, GT0 + i * 1024 + bi * 256)
                wb = wload(w_br_d[l, i, :, bi * 256:bi * 256 + 256].rearrange("(k p) c -> p k c", p=128), 8, 256, key=("br", l, i, bi))
                for cc in range(2):
                    c = bi * 2 + cc
                    psg, prg = proj_ws(wg[0], wg[1], cc * 128, 128, N)
                    psp, prp = proj_ws(wb[0], wb[1], cc * 128, 128, N, rhs=brs[i][0], rres=brs[i][1])
                    S.op("act", [prg], [r_stg[2]], lambda psg=psg: nc.scalar.activation(out=stg[2][:, 0:N], in_=psg[:, 0:N], func=AF.Sigmoid))
                    if i == 0:
                        S.op("dve", [prp, r_stg[2]], [r_macc[cc]], lambda psp=psp, cc=cc: nc.vector.tensor_tensor(
                            out=macc[cc][:, 0:N], in0=psp[:, 0:N], in1=stg[2][:, 0:N], op=ALU.mult))
                    else:
                        S.op("dve", [prp, r_stg[2]], [r_acc[2]], lambda psp=psp: nc.vector.tensor_tensor(
                            out=acc[2][:, 0:N], in0=psp[:, 0:N], in1=stg[2][:, 0:N], op=ALU.mult))
                        if i < 3:
                            S.op("dve", [r_macc[cc], r_acc[2]], [r_macc[cc]], lambda cc=cc: nc.vector.tensor_tensor(
                                out=macc[cc][:, 0:N], in0=macc[cc][:, 0:N], in1=acc[2][:, 0:N], op=ALU.add))
                        else:
                            S.op("dve", [r_macc[cc], r_acc[2]], [r_mT], lambda c=c, cc=cc: nc.vector.tensor_tensor(
                                out=mT[:, c, 0:N], in0=macc[cc][:, 0:N], in1=acc[2][:, 0:N], op=ALU.add))
        for bi in range(4):
            wv, wr = wload(wcols(w_o_d, l, bi * 256, 256), 8, 256, key=("o", l, bi))
            for cc in range(2):
                c = bi * 2 + cc
                ps, pr = proj_ws(wv, wr, cc * 128, 128, N, rhs=mT, rres=r_mT)
                S.op("dve", [pr, r_mod, r_x], [r_x], lambda ps=ps, c=c: nc.vector.scalar_tensor_tensor(
                    out=xT[:, c, 0:N], in0=ps[:, 0:N], scalar=mod[:, l, 16 + c, 0:1], in1=xT[:, c, 0:N], op0=ALU.mult, op1=ALU.add))
        mod_norm(l, 1, N)
        gat = (G[1], G[2], G[3])
        r_gat = (r_G[1], r_G[2], r_G[3])
        for blk in range(11):
            wa = wload(wcols(w_up_d, l, blk * 256, 256), 8, 256, key=("ua", l, blk))
            wg_ = wload(wcols(w_up_d, l, DFF + blk * 256, 256), 8, 256, key=("ug", l, blk))
            for cc in range(2):
                i = blk * 2 + cc
                outs = []
                for which, (wv, wr) in enumerate((wa, wg_)):
                    ci = which * 22 + i
                    ps, pr = proj_ws(wv, wr, cc * 128, 128, N)
                    taps = [pp[:, l, PP_FW + j * 44 + ci:PP_FW + j * 44 + ci + 1] for j in range(3)]
                    ac, rac = conv_fm(128, ps, pr, N, cff[:, l, ci, :], r_cff[l], taps, pp[:, l, PP_FB + ci:PP_FB + ci + 1], [r_pp], which)
                    outs.append((ac, rac))
                S.op("act", [outs[0][1]], [r_acc[2]], lambda a=outs[0][0]: nc.scalar.activation(out=acc[2][:, 0:N], in_=a[:, 0:N], func=AF.Silu))
                S.op("dve", [r_acc[2], outs[1][1]], [r_gat[i // 8]], lambda g_=outs[1][0], i=i: nc.vector.tensor_tensor(
                    out=gat[i // 8][:, i % 8, 0:N], in0=acc[2][:, 0:N], in1=g_[:, 0:N], op=ALU.mult))
        if last_tile:
            S.dma("sp", d_out, [r_cff[l]], [], lambda: nc.sync.dma_start(out=pffn_d[l], in_=cff[:, l, :, :]))
        for c in range(8):
            wh = [wload(w_dn_d[l, hh * 1408:(hh + 1) * 1408, c * 128:(c + 1) * 128].rearrange("(k p) c -> p k c", p=128), 11, 128, key=("dn", l, c, hh)) for hh in range(2)]
            ps, pr = psum()

            def f(wh=wh, ps=ps):
                for k in range(22):
                    last = nc.tensor.matmul(ps[:, 0:N], lhsT=wh[k // 11][0][:, k % 11, :], rhs=gat[k // 8][:, k % 8, 0:N], start=(k == 0), stop=(k == 21))
                return last
            S.op("pe", [wh[0][1], wh[1][1]] + list(r_gat), [pr], f)
            S.op("dve", [pr, r_mod, r_x], [r_x], lambda ps=ps, c=c: nc.vector.scalar_tensor_tensor(
                out=xT[:, c, 0:N], in0=ps[:, 0:N], scalar=mod[:, l, 40 + c, 0:1], in1=xT[:, c, 0:N], op0=ALU.mult, op1=ALU.add))

    def ssd_chunk(l, t0, T, wz, wdt, xsT, r_xsT, yaT, r_yaT, normg):
        dtb = rows[0:T, l, 0:16]
        Arow = rows[0:T, l, 16:32]
        Drow = rows[0:T, l, 32:48]
        xs_tok, r_xs = tC, r_tC
        for c in range(8):
            ps, pr = psum()
            psb = ps[:, 0:64].bitcast(BF16)
            S.op("pe", [r_xsT, r_cst], [pr], lambda c=c, psb=psb: nc.tensor.transpose(psb[0:T, 0:128], xsT[:, c, t0:t0 + T], identb[:]))
            yield
            copy(eng2(), xs_tok[0:T, c * 128:(c + 1) * 128], psb[0:T, 0:128], [pr], [r_xs])
            yield
        for g in range(4):
            ps, pr = psum()
            psb = ps[:, 0:64].bitcast(BF16)
            S.op("pe", [r_BCT, r_cst], [pr], lambda g=g, psb=psb: nc.tensor.transpose(psb[0:T, 0:64], BCT[:, g, t0:t0 + T], identb[0:64, 0:64]))
            yield
            copy(eng2(), Btok[0:T, g, :], psb[0:T, 0:64], [pr], [r_Btok])
            yield
        ps, pr = proj_as(wdt, 16, t0, T)
        dt = sm[0:T, 0:16]
        dtA = sm[0:T, 16:32]
        Acs = sm[0:T, 32:48]
        nAcs = sm[0:T, 48:64]
        eA = sm[0:T, 64:80]
        dec = sm[0:T, 80:96]
        wdec = sm[0:T, 96:112]
        S.op("dve", [pr, r_rows], [RV["dt"]], lambda: nc.vector.tensor_tensor(out=dt, in0=ps[0:T, 0:16], in1=dtb, op=ALU.add))
        yield
        S.op("act", [RV["dt"]], [RV["dt"]], lambda: nc.scalar.activation(out=dt, in_=dt, func=AF.Exp))
        yield
        S.op("act", [RV["dt"]], [RV["dt"]], lambda: nc.scalar.activation(out=dt, in_=dt, func=AF.Ln, bias=1.0))
        yield
        S.op("dve", [RV["dt"], r_rows], [RV["dtA"]], lambda: nc.vector.tensor_tensor(out=dtA, in0=dt, in1=Arow, op=ALU.mult))
        yield
        ps1, pr1 = psum()

        def f1():
            nc.tensor.matmul(ps1[0:T, 0:16], lhsT=Umat[0:T, 0:T], rhs=dtA, start=True, stop=True)
            return nc.tensor.matmul(ps1[0:max(T, 64), 16:32], lhsT=ones32[0:T, 0:max(T, 64)], rhs=dtA, start=True, stop=True)
        S.op("pe", [RV["dtA"], r_cst], [pr1], f1)
        yield
        S.op("dve", [pr1], [RV["Acs"]], lambda: nc.vector.tensor_copy(out=Acs, in_=ps1[0:T, 0:16]))
        yield
        S.op("dve", [pr1], [RV["nAcs"]], lambda: nc.vector.tensor_scalar(out=nAcs, in0=ps1[0:T, 0:16], scalar1=-1.0, scalar2=None, op0=ALU.mult))
        yield
        S.op("dve", [pr1, RV["Acs"]], [RV["dec"]], lambda: nc.vector.tensor_tensor(out=dec, in0=ps1[0:T, 16:32], in1=Acs, op=ALU.subtract))
        yield
        S.op("act", [pr1], [RV["eA"]], lambda: nc.scalar.activation(out=eA, in_=ps1[0:T, 0:16], func=AF.Exp))
        yield
        cdec = sm[0:64, 112:128]
        S.op("act", [pr1], [RV["cdec"]], lambda: nc.scalar.activation(out=cdec, in_=ps1[0:64, 16:32], func=AF.Exp))
        yield
        S.op("act", [RV["dec"]], [RV["dec"]], lambda: nc.scalar.activation(out=dec, in_=dec, func=AF.Exp))
        yield
        S.op("dve", [RV["dec"], RV["dt"]], [RV["wdec"]], lambda: nc.vector.tensor_tensor(out=wdec, in0=dec, in1=dt, op=ALU.mult))
        yield
        for hg in range(4):
            ps, pr = psum()

            def fe(ps=ps, hg=hg):
                for hh in range(4):
                    h_ = hg * 4 + hh
                    nc.tensor.matmul(ps[0:T, hh * 128:hh * 128 + T], lhsT=dtA[:, h_:h_ + 1].to_broadcast([T, T]), rhs=Umat[0:T, 0:T], start=True, stop=False)
                    last = nc.tensor.matmul(ps[0:T, hh * 128:hh * 128 + T], lhsT=ident[0:T, 0:T], rhs=NEGM[0:T, 0:T], start=False, stop=True)
                return last
            S.op("pe", [RV["dtA"], r_cst], [pr], fe)
            yield
            for hh in range(4):
                h_ = hg * 4 + hh
                S.op("act", [pr, RV["nAcs"]], [r_Eh], lambda ps=ps, hh=hh, h_=h_: nc.scalar.activation(
                    out=Eh[0:T, h_, 0:T], in_=ps[0:T, hh * 128:hh * 128 + T], func=AF.Exp, bias=nAcs[:, h_:h_ + 1], scale=1.0))
                yield
        ps, pr = psum()

        def fcb(ps=ps):
            for g in range(4):
                last = nc.tensor.matmul(ps[0:T, g * 128:g * 128 + T], lhsT=BCT[:, g, t0:t0 + T], rhs=BCT[:, 4 + g, t0:t0 + T], start=True, stop=True)
            return last
        S.op("pe", [r_BCT], [pr], fcb)
        yield
        copy("act", CBs[0:T, :, 0:T], ps[0:T, :].rearrange("p (g t) -> p g t", g=4)[:, :, 0:T], [pr], [r_CBs])
        yield
        for g in range(4):
            S.op("dve", [r_Eh, r_CBs], [r_Mh], lambda g=g: nc.vector.tensor_tensor(
                out=Mh[0:T, g * 4:g * 4 + 4, 0:T], in0=Eh[0:T, g * 4:g * 4 + 4, 0:T],
                in1=CBs[0:T, g:g + 1, 0:T].to_broadcast([T, 4, T]), op=ALU.mult))
            yield
        xdt, r_xdt = tD, r_tD
        Xdd, r_Xdd = tE, r_tE
        S.op("dve", [r_xs, RV["dt"]], [r_xdt], lambda: nc.vector.tensor_tensor(
            out=xdt[0:T, :].rearrange("p (h q) -> p h q", h=16), in0=xs_tok[0:T, :].rearrange("p (h q) -> p h q", h=16),
            in1=dt.unsqueeze(2).to_broadcast([T, 16, 64]), op=ALU.mult))
        yield
        S.op("dve", [r_xs, RV["wdec"]], [r_Xdd], lambda: nc.vector.tensor_tensor(
            out=Xdd[0:T, :].rearrange("p (h q) -> p h q", h=16), in0=xs_tok[0:T, :].rearrange("p (h q) -> p h q", h=16),
            in1=wdec.unsqueeze(2).to_broadcast([T, 16, 64]), op=ALU.mult))
        yield
        pso = [psum(), psum()]
        psd = [psum(), psum()]

        def foff():
            for g in range(4):
                last = nc.tensor.matmul(pso[g // 2][0][0:T, (g % 2) * 256:(g % 2) * 256 + 256], lhsT=BCT[:, 4 + g, t0:t0 + T],
                                        rhs=STb[:, g, :], start=True, stop=True)
            return last
        S.op("pe", [r_BCT, r_STb[l]], [pso[0][1], pso[1][1]], foff)
        yield

        def fdiag():
            for h_ in range(16):
                last = nc.tensor.matmul(psd[h_ // 8][0][0:T, (h_ % 8) * 64:(h_ % 8) * 64 + 64], lhsT=Mh[0:T, h_, 0:T],
                                        rhs=xdt[0:T, h_ * 64:(h_ + 1) * 64], start=True, stop=True)
            return last
        S.op("pe", [r_Mh, r_xdt], [psd[0][1], psd[1][1]], fdiag)
        yield
        y, r_y = tA, r_tA
        for hf in range(2):
            S.op("dve", [pso[hf][1], RV["eA"]], [r_y], lambda hf=hf: nc.vector.tensor_tensor(
                out=y[0:T, hf * 512:(hf + 1) * 512].rearrange("p (h q) -> p h q", h=8),
                in0=pso[hf][0][0:T, :].rearrange("p (h q) -> p h q", h=8),
                in1=eA[:, hf * 8:hf * 8 + 8].unsqueeze(2).to_broadcast([T, 8, 64]), op=ALU.mult))
            yield
            S.op("dve", [psd[hf][1], r_y], [r_y], lambda hf=hf: nc.vector.tensor_tensor(
                out=y[0:T, hf * 512:(hf + 1) * 512], in0=psd[hf][0][0:T, :], in1=y[0:T, hf * 512:(hf + 1) * 512], op=ALU.add))
            yield
        t2, r_t2 = tB, r_tB
        S.op("dve", [r_xs, r_rows], [r_t2], lambda: nc.vector.tensor_tensor(
            out=t2[0:T, :].rearrange("p (h q) -> p h q", h=16), in0=xs_tok[0:T, :].rearrange("p (h q) -> p h q", h=16),
            in1=Drow.unsqueeze(2).to_broadcast([T, 16, 64]), op=ALU.mult))
        yield
        S.op("dve", [r_t2, r_y], [r_y], lambda: nc.vector.tensor_tensor(out=y[0:T, :], in0=y[0:T, :], in1=t2[0:T, :], op=ALU.add))
        yield
        pss = [psum(), psum()]

        def fst():
            for g in range(4):
                last = nc.tensor.matmul(pss[g // 2][0][0:64, (g % 2) * 256:(g % 2) * 256 + 256], lhsT=Btok[0:T, g, :],
                                        rhs=Xdd[0:T, g * 256:(g + 1) * 256], start=True, stop=True)
            return last
        S.op("pe", [r_Btok, r_Xdd], [pss[0][1], pss[1][1]], fst)
        yield
        S.op("dve", [r_ST[l], RV["cdec"], r_STb[l]], [r_ST[l]], lambda: nc.vector.tensor_tensor(
            out=ST[:, l, :, :].rearrange("p g (r q) -> p (g r) q", r=4), in0=ST[:, l, :, :].rearrange("p g (r q) -> p (g r) q", r=4),
            in1=cdec.unsqueeze(2).to_broadcast([64, 16, 64]), op=ALU.mult))
        yield
        for hf in range(2):
            S.op("dve", [pss[hf][1], r_ST[l]], [r_ST[l]], lambda hf=hf: nc.vector.tensor_tensor(
                out=ST[:, l, hf * 2:hf * 2 + 2, :], in0=ST[:, l, hf * 2:hf * 2 + 2, :],
                in1=pss[hf][0][0:64, :].rearrange("p (g q) -> p g q", g=2), op=ALU.add))
            yield
        S.op("act", [r_ST[l]], [r_STb[l]], lambda: nc.scalar.copy(out=STb[:, :, :], in_=ST[:, l, :, :]))
        yield
        for hf in range(2):
            ps, pr = proj_as(wz[hf * 2:hf * 2 + 2], 512, t0, T)
            S.op("act", [pr], [r_t2], lambda ps=ps, hf=hf: nc.scalar.activation(out=t2[0:T, hf * 512:(hf + 1) * 512], in_=ps[0:T, :], func=AF.Silu))
            yield
        S.op("dve", [r_t2, r_y], [r_y], lambda: nc.vector.tensor_tensor(out=y[0:T, :], in0=y[0:T, :], in1=t2[0:T, :], op=ALU.mult))
        yield
        ssq = sm[0:T, 128:132]
        for g in range(4):
            S.op("act", [r_y], [r_t2, RV["ssq"]], lambda g=g: nc.scalar.activation(
                out=t2[0:T, g * 256:(g + 1) * 256], in_=y[0:T, g * 256:(g + 1) * 256], func=AF.Square, accum_out=ssq[:, g:g + 1]))
            yield
        rstd_of(ssq, T, ssq, [RV["ssq"], r_cst], [RV["ssq"]], 1.0 / 256)
        yield
        S.op("dve", [r_y, RV["ssq"]], [r_y], lambda: nc.vector.tensor_tensor(
            out=y[0:T, :].rearrange("p (g q) -> p g q", g=4), in0=y[0:T, :].rearrange("p (g q) -> p g q", g=4),
            in1=ssq.unsqueeze(2).to_broadcast([T, 4, 256]), op=ALU.mult))
        yield
        ya_tok, r_ya = tC, r_tC
        S.op("dve", [r_y, r_rowb], [r_ya], lambda: nc.vector.tensor_tensor(out=ya_tok[0:T, :], in0=y[0:T, :], in1=normg[0:T, :], op=ALU.mult))
        yield
        for c in range(8):
            ps, pr = psum()
            psb = ps[:, 0:64].bitcast(BF16)
            S.op("pe", [r_ya, r_cst], [pr], lambda c=c, psb=psb: nc.tensor.transpose(psb[:, 0:T], ya_tok[0:T, c * 128:(c + 1) * 128], identb[0:T, 0:T]))
            yield
            copy(eng2(), yaT[:, c, t0:t0 + T], psb[:, 0:T], [pr], [r_yaT])
            yield

    def attn_chunk(l, ti, j, qT, r_qT, ycT, r_ycT):
        T = 128
        t0 = j * 128
        first = (ti == 0 and j == 0)
        KC = 128 if first else 256
        boff = 128 if first else 0
        sink = rows[:, l, 48:64]
        scA = G[4][:, 0:4, :].bitcast(F32)
        PmA = G[4][:, 4:6, :]
        PTA = G[4][:, 6:8, :]
        o_ps = [(PS[6], r_PS[6]), (PS[7], r_PS[7])]
        rinv = sm[:, 136:152]
        for hk in range(4):
            pss_ = [psum(), psum()]

            def fs(hk=hk, pss_=pss_):
                for hh in range(4):
                    hd = hk * 4 + hh
                    pb = (hd % 2) * 64
                    qa = qT[pb:pb + 64, hd // 2, t0:t0 + T]
                    dst = pss_[hh % 2][0][:, (hh // 2) * 256:(hh // 2) * 256 + KC]
                    if first:
                        last = nc.tensor.matmul(dst, lhsT=qa, rhs=kT[pb:pb + 64, hk, t0:t0 + T], start=True, stop=True)
                    elif j == 0:
                        nc.tensor.matmul(dst[:, 0:128], lhsT=qa, rhs=kprev[pb:pb + 64, l, hk, :], start=True, stop=True)
                        last = nc.tensor.matmul(dst[:, 128:256], lhsT=qa, rhs=kT[pb:pb + 64, hk, t0:t0 + T], start=True, stop=True)
                    else:
                        last = nc.tensor.matmul(dst, lhsT=qa, rhs=kT[pb:pb + 64, hk, t0 - 128:t0 + T], start=True, stop=True)
                return last
            S.op("pe", [r_qT, r_kT, r_kprev[l]], [pss_[0][1], pss_[1][1]], fs)
            yield
            sc = scA
            for hh in range(4):
                S.op("dve", [pss_[hh % 2][1], r_cst], [r_aS], lambda hh=hh, hk=hk, pss_=pss_: nc.vector.scalar_tensor_tensor(
                    out=sc[:, hh, 0:KC], in0=distm[:, boff:boff + KC], scalar=-SLOPES[hk * 4 + hh],
                    in1=pss_[hh % 2][0][:, (hh // 2) * 256:(hh // 2) * 256 + KC], op0=ALU.mult, op1=ALU.add))
                yield
            if DBG_ATT <= 2:
                continue
            mx = sm[:, 152:156]
            nmx = sm[:, 156:160]
            esk = sm[:, 160:164]
            rsum = sm[:, 164:168]
            S.op("dve", [r_aS], [r_sm], lambda: nc.vector.tensor_reduce(out=mx, in_=sc[:, :, 0:KC], axis=AX.X, op=ALU.max))
            yield
            S.op("dve", [r_sm, r_rows], [r_sm], lambda hk=hk: nc.vector.tensor_tensor(out=mx, in0=mx, in1=sink[:, hk * 4:hk * 4 + 4], op=ALU.max))
            yield
            S.op("dve", [r_sm], [r_sm], lambda: nc.vector.tensor_scalar(out=nmx, in0=mx, scalar1=-1.0, scalar2=None, op0=ALU.mult))
            yield
            S.op("dve", [r_sm, r_rows], [r_sm], lambda hk=hk: nc.vector.tensor_tensor(out=esk, in0=sink[:, hk * 4:hk * 4 + 4], in1=mx, op=ALU.subtract))
            yield
            S.op("act", [r_sm], [r_sm], lambda: nc.scalar.activation(out=esk, in_=esk, func=AF.Exp))
            yield
            Pm = PmA.rearrange("p c (h k) -> p (c h) k", h=2)
            for hh in range(4):
                S.op("act", [r_aS, r_sm], [r_aP, r_sm], lambda hh=hh: nc.scalar.activation(
                    out=Pm[:, hh, 0:KC], in_=sc[:, hh, 0:KC], func=AF.Exp, bias=nmx[:, hh:hh + 1], scale=1.0, accum_out=rsum[:, hh:hh + 1]))
                yield
            S.op("dve", [r_sm], [r_sm], lambda: nc.vector.tensor_tensor(out=rsum, in0=rsum, in1=esk, op=ALU.add))
            yield
            S.op("dve", [r_sm], [r_sm], lambda hk=hk: nc.vector.reciprocal(out=rinv[:, hk * 4:hk * 4 + 4], in_=rsum))
            yield
            if DBG_ATT <= 3:
                continue
            PT = PTA.rearrange("p c (h k) -> p (c h) k", h=2)
            nkb = KC // 128
            for hh in range(4):
                ps, pr = psum()
                psb = ps[:, 0:128].bitcast(BF16)

                def ft(hh=hh, psb=psb):
                    for kb in range(nkb):
                        last = nc.tensor.transpose(psb[:, kb * 128:(kb + 1) * 128], Pm[:, hh, kb * 128:(kb + 1) * 128], identb[:])
                    return last
                S.op("pe", [r_aP, r_cst], [pr], ft)
                yield
                copy(eng2(), PT[:, hh, 0:KC], psb[:, 0:KC], [pr], [r_aT])
                yield

            def fo(hk=hk):
                for hh in range(4):
                    hd = hk * 4 + hh
                    dst = o_ps[hd // 8][0][:, (hd % 8) * 64:(hd % 8) * 64 + 64]
                    if first:
                        last = nc.tensor.matmul(dst, lhsT=PT[:, hh, 0:128], rhs=vtok[:, j, hk * 64:hk * 64 + 64], start=True, stop=True)
                    else:
                        vp = vprev[:, l, hk * 64:hk * 64 + 64] if j == 0 else vtok[:, j - 1, hk * 64:hk * 64 + 64]
                        nc.tensor.matmul(dst, lhsT=PT[:, hh, 0:128], rhs=vp, start=True, stop=False)
                        last = nc.tensor.matmul(dst, lhsT=PT[:, hh, 128:256], rhs=vtok[:, j, hk * 64:hk * 64 + 64], start=False, stop=True)
                return last
            S.op("pe", [r_aT, r_vtok, r_vprev[l]], [o_ps[hk // 2][1]], fo)
            yield
        yc_tok = PmA.rearrange("p c t -> p (c t)")
        for hf in range(2):
            S.op("dve", [o_ps[hf][1], r_sm], [r_aP], lambda hf=hf: nc.vector.tensor_tensor(
                out=yc_tok[:, hf * 512:(hf + 1) * 512].rearrange("p (h q) -> p h q", h=8),
                in0=o_ps[hf][0][:, :].rearrange("p (h q) -> p h q", h=8),
                in1=rinv[:, hf * 8:hf * 8 + 8].unsqueeze(2).to_broadcast([128, 8, 64]), op=ALU.mult))
            yield
        for c in range(8):
            ps, pr = psum()
            psb = ps[:, 0:64].bitcast(BF16)
            S.op("pe", [r_aP, r_cst], [pr], lambda c=c, psb=psb: nc.tensor.transpose(psb[:, 0:T], yc_tok[:, c * 128:(c + 1) * 128], identb[:]))
            yield
            copy(eng2(), ycT[:, c, t0:t0 + T], psb[:, 0:T], [pr], [r_ycT])
            yield

    def gmlp_chunk(l, t0, T, wv_, ydT, r_ydT, lng, lnb, bsrow, wT, vout, bs3=False):
        v, r_v = tA, r_tA
        ssum = sm[0:T, 168:170]
        mean = sm[0:T, 170:171]
        ssq = sm[0:T, 171:173]
        rstd = sm[0:T, 173:174]
        for hf in range(2):
            ps, pr = proj_as(wv_[hf * 2:hf * 2 + 2], 512, t0, T)
            S.op("act", [pr], [r_v, r_sm], lambda ps=ps, hf=hf: nc.scalar.activation(
                out=v[0:T, hf * 512:(hf + 1) * 512], in_=ps[0:T, :], func=AF.Gelu_apprx_tanh, accum_out=ssum[:, hf:hf + 1]))
        S.op("dve", [r_sm], [r_sm], lambda: nc.vector.tensor_tensor(out=mean, in0=ssum[:, 0:1], in1=ssum[:, 1:2], op=ALU.add))
        S.op("dve", [r_sm], [r_sm], lambda: nc.vector.tensor_scalar(out=mean, in0=mean, scalar1=1.0 / 1024, scalar2=None, op0=ALU.mult))
        S.op("dve", [r_v, r_sm], [r_v], lambda: nc.vector.tensor_scalar(out=v[0:T, :], in0=v[0:T, :], scalar1=mean, scalar2=None, op0=ALU.subtract))
        for hf in range(2):
            S.op("act", [r_v], [r_tB, r_sm], lambda hf=hf: nc.scalar.activation(
                out=tB[0:T, hf * 512:(hf + 1) * 512], in_=v[0:T, hf * 512:(hf + 1) * 512], func=AF.Square, accum_out=ssq[:, hf:hf + 1]))
        S.op("dve", [r_sm], [r_sm], lambda: nc.vector.tensor_tensor(out=rstd, in0=ssq[:, 0:1], in1=ssq[:, 1:2], op=ALU.add))
        rstd_of(rstd, T, rstd, [r_sm, r_cst], [r_sm], 1.0 / 1024)
        S.op("dve", [r_v, r_sm, r_rowb], [r_v], lambda: nc.vector.scalar_tensor_tensor(
            out=v[0:T, :], in0=v[0:T, :], scalar=rstd, in1=lng[0:T, :], op0=ALU.mult, op1=ALU.mult))
        if vout is not None:
            S.op("dve", [r_v, r_rowb], [r_tB], lambda: nc.vector.tensor_tensor(out=tB[0:T, :], in0=v[0:T, :], in1=lnb[0:T, :], op=ALU.add))
            vout(tB, r_tB)
        vn, r_vn = tC, r_tC
        S.op("dve", [r_v, r_rowb], [r_vn], lambda: nc.vector.tensor_tensor(out=vn[0:T, :], in0=v[0:T, :], in1=lnb[0:T, :], op=ALU.add))
        for hf in range(2):
            ps, pr = psum()

            def fm(ps=ps, hf=hf):
                for gg in range(4):
                    g = hf * 4 + gg
                    last = nc.tensor.matmul(ps[:, gg * 128:gg * 128 + T], lhsT=vn[0:T, g * 128:(g + 1) * 128], rhs=wT[0:T, g, 0:T], start=True, stop=True)
                return last
            S.op("pe", [r_vn, r_gmw, r_wbd], [pr], fm)
            mix = tB[:, 0:512].rearrange("p (g t) -> p g t", g=4)
            bsv = (bsrow[:, hf * 4:hf * 4 + 4, 0:T] if bs3 else
                   bsrow[:, hf * 512:(hf + 1) * 512].rearrange("p (g t) -> p g t", g=4)[:, :, 0:T])
            S.op("dve", [pr, r_rowb, r_bsp], [r_tB], lambda ps=ps, hf=hf, bsv=bsv: nc.vector.tensor_tensor(
                out=mix[:, :, 0:T], in0=ps[:, :].rearrange("p (g t) -> p g t", g=4)[:, :, 0:T],
                in1=bsv, op=ALU.add))
            S.op("dve", [r_tB, r_ydT], [r_ydT], lambda hf=hf: nc.vector.tensor_tensor(
                out=ydT[:, hf * 4:hf * 4 + 4, t0:t0 + T], in0=ydT[:, hf * 4:hf * 4 + 4, t0:t0 + T], in1=mix[:, :, 0:T], op=ALU.mult))

    if SB:
        NS = 4 * SB
        mod_s = sb("mod_s", [128, 1, 48, SB]); r_mods = Res("mod_s")
        gm_s = sb("gm_s", [128, 1, 2, 8, SB]); r_gms = Res("gm_s")
        csT = sb("csT_t", [128, 8, SB]); r_csT = Res("csT")
        csb = sb("csb", [128, 8, SB], BF16)
        sxbc = sb("sxbc", [128, 8, SB, 3]); r_sxbc = Res("sxbc")
        sbc = sb("sbc", [64, 8, SB, 3]); r_sbc = Res("sbc")
        ssc = sb("ssc", [128, 8, SB, 2]); r_ssc = Res("ssc")
        sffn = sb("sffn", [128, 44, SB, 2]); r_sffn = Res("sffn")
        KKb = sb("KKb", [128, 4, 160], BF16); r_KKb = Res("KKb")
        Vb = sb("Vb", [128, 256], BF16); r_Vb = Res("Vb")
        vnew = sb("vnew", [4, 256], BF16); r_vnew = Res("vnew")
        PTn = sb("PTn", [4, 4, 4], BF16); r_PTn = Res("PTn")
        d_cs = d_kv = d_st = None

    _once = {}

    def sb_once(name, shape, dt=F32):
        if name not in _once:
            _once[name] = sb(name, shape, dt)
        return _once[name]
    r_w4, r_wbd, r_bsp = Res("w4"), Res("wbd"), Res("bsp")

    def conv_s(P, ps, pr, taps, bias_ap, preads, st_ap, r_st, si):
        K = len(taps)
        CW = K - 1
        W = CW + 4
        st3 = stg[si][0:P, 0:SB * W].rearrange("p (b w) -> p b w", b=SB)
        ac3 = acc[si][0:P, 0:NS].rearrange("p (b w) -> p b w", b=SB)
        rst, rac = r_stg[si], r_acc[si]
        S.op("act", [pr], [rst], lambda: nc.scalar.copy(out=st3[:, :, CW:W], in_=ps[0:P, 0:NS].rearrange("p (b w) -> p b w", b=SB)))
        S.op("dve", [r_st], [rst], lambda: nc.vector.tensor_copy(out=st3[:, :, 0:CW], in_=st_ap))
        S.op("dve", [rst], [r_st], lambda: nc.vector.tensor_copy(out=st_ap, in_=st3[:, :, 4:W]))
        if bias_ap is not None:
            S.op("dve", [rst] + preads, [rac], lambda: nc.vector.tensor_scalar(
                out=ac3, in0=st3[:, :, 0:4], scalar1=taps[0], scalar2=bias_ap, op0=ALU.mult, op1=ALU.add))
        else:
            S.op("dve", [rst] + preads, [rac], lambda: nc.vector.tensor_scalar(
                out=ac3, in0=st3[:, :, 0:4], scalar1=taps[0], scalar2=None, op0=ALU.mult))
        for j in range(1, K):
            S.op("dve", [rst, rac] + preads, [rac], lambda j=j: nc.vector.scalar_tensor_tensor(
                out=ac3, in0=st3[:, :, j:j + 4], scalar=taps[j], in1=ac3, op0=ALU.mult, op1=ALU.add))
        return acc[si], rac

    def mod_norm_s(which):
        N = NS
        sh_i = 0 if which == 0 else 3
        ps, pr = psum()
        sq = G[5]
        S.op("act", [r_x], [r_G[5]], lambda: nc.scalar.activation(out=sq[:, :, 0:N], in_=xT[:, :, 0:N], func=AF.Square))

        def f():
            for k in range(8):
                last = nc.tensor.matmul(ps[:, 0:N], lhsT=onesb[:], rhs=sq[:, k, 0:N], start=(k == 0), stop=(k == 7))
            return last
        S.op("pe", [r_G[5], r_cst], [pr], f)
        rs = acc[2]
        rstd_of(ps[:, 0:N], 128, rs[:, 0:N], [pr, r_cst], [r_acc[2]], 1.0 / D)
        for c in range(8):
            a_ = acc[c % 2]
            a3 = a_[:, 0:N].rearrange("p (b w) -> p b w", b=SB)
            S.op("dve", [r_x, r_acc[2]], [r_acc[c % 2]], lambda c=c, a_=a_: nc.vector.tensor_tensor(
                out=a_[:, 0:N], in0=xT[:, c, 0:N], in1=rs[:, 0:N], op=ALU.mult))
            S.op("dve", [r_gms, r_acc[c % 2]], [r_acc[c % 2]], lambda c=c, a3=a3: nc.vector.tensor_tensor(
                out=a3, in0=a3, in1=gm_s[:, 0, which, c, :].unsqueeze(2).to_broadcast([128, SB, 4]), op=ALU.mult))
            S.op("dve", [r_mods, r_acc[c % 2]], [r_h], lambda c=c, a3=a3: nc.vector.tensor_tensor(
                out=hB[:, c, 0:N].rearrange("p (b w) -> p b w", b=SB), in0=a3,
                in1=mod_s[:, 0, sh_i * 8 + c, :].unsqueeze(2).to_broadcast([128, SB, 4]), op=ALU.add))

    def resid_s(ps, pr, c, gi):
        N = NS
        S.op("dve", [pr, r_mods], [r_acc[2]], lambda: nc.vector.tensor_tensor(
            out=acc[2][:, 0:N].rearrange("p (b w) -> p b w", b=SB), in0=ps[:, 0:N].rearrange("p (b w) -> p b w", b=SB),
            in1=mod_s[:, 0, gi * 8 + c, :].unsqueeze(2).to_broadcast([128, SB, 4]), op=ALU.mult))
        S.op("dve", [r_acc[2], r_x], [r_x], lambda: nc.vector.tensor_tensor(
            out=xT[:, c, 0:N], in0=xT[:, c, 0:N], in1=acc[2][:, 0:N], op=ALU.add))

    def attn_s(l, b, qT, r_qT, ycT, r_ycT, wk, wvv):
        T = 4
        t0 = 4 * b
        KC = 132
        sink = rows[0:T, l, 48:64]
        scA = G[4][:, 0:4, :].bitcast(F32)
        PmA = G[4][:, 4:6, :]
        PTA = G[4][:, 6:8, :]
        ktok, r_ktok = acc[2], r_acc[2]
        o_ps = [(PS[6], r_PS[6]), (PS[7], r_PS[7])]
        rinv = sm[0:T, 136:152]
        S.dma("pool", d_kv, [], [r_KKb], lambda: nc.gpsimd.dma_start(out=KKb[:, :, 0:128], in_=ckT_d[l, b]))
        yield
        S.dma("pool", d_kv, [], [r_Vb], lambda: nc.gpsimd.dma_start(out=Vb[:, :], in_=cv_d[l, b]))
        yield
        S.op("dve", [r_kT], [r_KKb], lambda: nc.vector.tensor_copy(out=KKb[:, :, 128:132], in_=kT[:, :, t0:t0 + T]))
        yield
        ps, pr = proj_as([wk, wvv], 512, t0, T)
        S.op("act", [pr], [r_ktok], lambda: nc.scalar.copy(out=ktok[0:T, 0:512], in_=ps[0:T, :]))
        yield
        S.op("dve", [r_ktok], [r_vnew], lambda: nc.vector.tensor_copy(out=vnew[:, :], in_=ktok[0:T, 256:512]))
        yield
        S.dma("sp", d_out, [r_ktok], [], lambda: nc.sync.dma_start(out=sk_o[l, b, 124:128, :], in_=ktok[0:T, 0:256]))
        yield
        S.dma("sp", d_out, [r_ktok], [], lambda: nc.sync.dma_start(out=sv_o[l, b, 124:128, :], in_=ktok[0:T, 256:512]))
        yield
        for hk in range(4):
            pss_ = [psum(), psum()]

            def fs(hk=hk, pss_=pss_):
                for hh in range(4):
                    hd = hk * 4 + hh
                    pb = (hd % 2) * 64
                    last = nc.tensor.matmul(pss_[hh % 2][0][0:T, (hh // 2) * 256:(hh // 2) * 256 + KC],
                                            lhsT=qT[pb:pb + 64, hd // 2, t0:t0 + T], rhs=KKb[pb:pb + 64, hk, 0:132], start=True, stop=True)
                return last
            S.op("pe", [r_qT, r_KKb], [pss_[0][1], pss_[1][1]], fs)
            yield
            sc = scA[0:T]
            for hh in range(4):
                S.op("dve", [pss_[hh % 2][1], r_cst], [r_aS], lambda hh=hh, hk=hk, pss_=pss_: nc.vector.scalar_tensor_tensor(
                    out=sc[:, hh, 0:KC], in0=dists[0:T, :], scalar=-SLOPES[hk * 4 + hh],
                    in1=pss_[hh % 2][0][0:T, (hh // 2) * 256:(hh // 2) * 256 + KC], op0=ALU.mult, op1=ALU.add))
                yield
            mx = sm[0:T, 152:156]
            nmx = sm[0:T, 156:160]
            esk = sm[0:T, 160:164]
            rsum = sm[0:T, 164:168]
            S.op("dve", [r_aS], [r_sm], lambda: nc.vector.tensor_reduce(out=mx, in_=sc[:, :, 0:KC], axis=AX.X, op=ALU.max))
            yield
            S.op("dve", [r_sm, r_rows], [r_sm], lambda hk=hk: nc.vector.tensor_tensor(out=mx, in0=mx, in1=sink[:, hk * 4:hk * 4 + 4], op=ALU.max))
            yield
            S.op("dve", [r_sm], [r_sm], lambda: nc.vector.tensor_scalar(out=nmx, in0=mx, scalar1=-1.0, scalar2=None, op0=ALU.mult))
            yield
            S.op("dve", [r_sm, r_rows], [r_sm], lambda hk=hk: nc.vector.tensor_tensor(out=esk, in0=sink[:, hk * 4:hk * 4 + 4], in1=mx, op=ALU.subtract))
            yield
            S.op("act", [r_sm], [r_sm], lambda: nc.scalar.activation(out=esk, in_=esk, func=AF.Exp))
            yield
            Pm = PmA[0:T].rearrange("p c (h k) -> p (c h) k", h=2)
            for hh in range(4):
                S.op("act", [r_aS, r_sm], [r_aP, r_sm], lambda hh=hh: nc.scalar.activation(
                    out=Pm[:, hh, 0:KC], in_=sc[:, hh, 0:KC], func=AF.Exp, bias=nmx[:, hh:hh + 1], scale=1.0, accum_out=rsum[:, hh:hh + 1]))
                yield
            S.op("dve", [r_sm], [r_sm], lambda: nc.vector.tensor_tensor(out=rsum, in0=rsum, in1=esk, op=ALU.add))
            yield
            S.op("dve", [r_sm], [r_sm], lambda hk=hk: nc.vector.reciprocal(out=rinv[:, hk * 4:hk * 4 + 4], in_=rsum))
            yield
            PT = PTA.rearrange("p c (h k) -> p (c h) k", h=2)
            ps, pr = psum()
            psb = ps[:, 0:64].bitcast(BF16)

            def ft(psb=psb):
                for hh in range(4):
                    nc.tensor.transpose(psb[:, hh * 8:hh * 8 + 4], Pm[:, hh, 0:128], identb[0:T, 0:T])
                    last = nc.tensor.transpose(psb[0:T, hh * 8 + 4:hh * 8 + 8], Pm[:, hh, 128:132], identb[0:T, 0:T])
                return last
            S.op("pe", [r_aP, r_cst], [pr], ft)
            yield
            pv = psb[:, 0:32].rearrange("p (h k) -> p h k", h=4)
            copy("dve", PT[:, :, 0:4], pv[:, :, 0:4], [pr], [r_aT])
            yield
            copy("dve", PTn[:, :, :], pv[0:T, :, 4:8], [pr], [r_PTn])
            yield

            def fo(hk=hk):
                for hh in range(4):
                    hd = hk * 4 + hh
                    dst = o_ps[hd // 8][0][0:T, (hd % 8) * 64:(hd % 8) * 64 + 64]
                    nc.tensor.matmul(dst, lhsT=PT[:, hh, 0:4], rhs=Vb[:, hk * 64:hk * 64 + 64], start=True, stop=False)
                    last = nc.tensor.matmul(dst, lhsT=PTn[:, hh, :], rhs=vnew[:, hk * 64:hk * 64 + 64], start=False, stop=True)
                return last
            S.op("pe", [r_aT, r_PTn, r_Vb, r_vnew], [o_ps[hk // 2][1]], fo)
            yield
        yc_tok = PmA.rearrange("p c t -> p (c t)")
        for hf in range(2):
            S.op("dve", [o_ps[hf][1], r_sm], [r_aP], lambda hf=hf: nc.vector.tensor_tensor(
                out=yc_tok[0:T, hf * 512:(hf + 1) * 512].rearrange("p (h q) -> p h q", h=8),
                in0=o_ps[hf][0][0:T, :].rearrange("p (h q) -> p h q", h=8),
                in1=rinv[:, hf * 8:hf * 8 + 8].unsqueeze(2).to_broadcast([T, 8, 64]), op=ALU.mult))
            yield
        for c in range(8):
            ps, pr = psum()
            psb = ps[:, 0:64].bitcast(BF16)
            S.op("pe", [r_aP, r_cst], [pr], lambda c=c, psb=psb: nc.tensor.transpose(psb[:, 0:T], yc_tok[0:T, c * 128:(c + 1) * 128], identb[0:T, 0:T]))
            yield
            copy(eng2(), ycT[:, c, t0:t0 + T], psb[:, 0:T], [pr], [r_ycT])
            yield

    def sample_tile_layer(l):
        N = NS
        S.dma("sp", d_rowb, [], [r_rowb], lambda: nc.sync.dma_start(out=rowb[:, 0:1024], in_=rowb_d[l, :, 0:1024]))
        normg = rowb[:, 0:1024]
        lng = rowb[:, 0:1024]
        lnb = rowb[:, 1024:2048]
        bsrow = rowb[:, 2048:3072]
        S.dma("sp", d_cs, [], [r_sxbc], lambda: nc.sync.dma_start(out=sxbc[:], in_=sxbc_d[l]))
        S.dma("sp", d_cs, [], [r_sbc], lambda: nc.sync.dma_start(out=sbc[:], in_=sbc_d[l]))
        S.dma("sp", d_cs, [], [r_ssc], lambda: nc.sync.dma_start(out=ssc[:], in_=ssc_d[l]))
        S.dma("sp", d_cs, [], [r_sffn], lambda: nc.sync.dma_start(out=sffn[:], in_=sffn_d[l]))
        S.dma("sp", d_out, [], [], lambda: nc.sync.dma_start(out=sk_o[l, :, 0:124, :], in_=ck_d[l, :, 4:128, :]))
        S.dma("sp", d_out, [], [], lambda: nc.sync.dma_start(out=sv_o[l, :, 0:124, :], in_=cv_d[l, :, 4:128, :]))
        ada_layer(l, csb, r_csT, SB, mod_s, r_mods, gm_s, r_gms, 0)
        mod_norm_s(0)
        xsT, r_xsT = G[3], r_G[3]
        for bi in range(4):
            wv, wr = win(l, XBC0 + bi * 256)
            for cc in range(2):
                c = bi * 2 + cc
                ps, pr = proj_ws(wv, wr, cc * 128, 128, N)
                taps = [pp[:, l, PP_SCW + j * 8 + c:PP_SCW + j * 8 + c + 1] for j in range(4)]
                ac, rac = conv_s(128, ps, pr, taps, pp[:, l, PP_SCB + c:PP_SCB + c + 1], [r_pp], sxbc[:, c, :, :], r_sxbc, c % 2)
                S.op("act", [rac], [r_xsT], lambda ac=ac, c=c: nc.scalar.activation(out=xsT[:, c, 0:N], in_=ac[:, 0:N], func=AF.Silu))
        for bi in range(2):
            wv, wr = win(l, XBC0 + 1024 + bi * 256)
            for ee in range(4):
                e = bi * 4 + ee
                ps, pr = proj_ws(wv, wr, ee * 64, 64, N)
                taps = [pp64[:, l, j * 8 + e:j * 8 + e + 1] for j in range(4)]
                ac, rac = conv_s(64, ps, pr, taps, pp64[:, l, 32 + e:33 + e], [r_pp64], sbc[:, e, :, :], r_sbc, e % 2)
                S.op("act", [rac], [r_BCT], lambda ac=ac, e=e: nc.scalar.activation(out=BCT[:, e, 0:N], in_=ac[0:64, 0:N], func=AF.Silu))
        S.dma("sp", d_out, [r_sxbc], [], lambda: nc.sync.dma_start(out=sxbc_o[l], in_=sxbc[:]))
        S.dma("sp", d_out, [r_sbc], [], lambda: nc.sync.dma_start(out=sbc_o[l], in_=sbc[:]))
        qT, r_qT = G[5], r_G[5]
        for bi in range(4):
            wv, wr = win(l, Q0 + bi * 256)
            for cc in range(2):
                c = bi * 2 + cc
                ps, pr = proj_ws(wv, wr, cc * 128, 128, N)
                S.op("act", [pr], [r_qT], lambda ps=ps, c=c: nc.scalar.activation(out=qT[:, c, 0:N], in_=ps[:, 0:N], func=AF.Identity, scale=0.125))
        wk = win(l, K0)
        wvv = win(l, V0)
        for hk in range(4):
            ps, pr = proj_ws(wk[0], wk[1], hk * 64, 64, N)
            ktmp, r_ktmp = tC, r_tC
            copy("act", ktmp[0:64, 0:N], ps[0:64, 0:N], [pr], [r_ktmp])
            ps2, pr2 = psum()
            S.op("pe", [r_ktmp, r_cst], [pr2], lambda ps2=ps2: nc.tensor.matmul(ps2[:, 0:N], lhsT=dupI[:, :], rhs=ktmp[0:64, 0:N], start=True, stop=True))
            copy("dve", kT[:, hk, 0:N], ps2[:, 0:N], [pr2], [r_kT])
        ycT, r_ycT = G[3], r_G[3]
        wz = [win(l, i * 256) for i in range(4)]
        wdt = [win(l, DT0, 256)]
        yaT, r_yaT = G[1], r_G[1]

        def ssd_b(b):
            S.dma("sp", d_st, [], [r_ST[l]], lambda b=b: nc.sync.dma_start(out=ST[:, l, :, :], in_=sssm_d[l, b]))
            S.op("act", [r_ST[l]], [r_STb1], lambda: nc.scalar.copy(out=STb[:, :, :], in_=ST[:, l, :, :]))
            yield
            for _ in ssd_chunk(l, 4 * b, 4, wz, wdt, xsT, r_xsT, yaT, r_yaT, normg):
                yield
            S.dma("sp", d_out, [r_ST[l]], [], lambda b=b: nc.sync.dma_start(out=sssm_o[l, b], in_=ST[:, l, :, :]))
        for b in range(SB):
            interleave(ssd_b(b), attn_s(l, b, qT, r_qT, ycT, r_ycT, wk, wvv))
        ybT, r_ybT = G[2], r_G[2]
        for bi in range(4):
            wb = [win(l, BCX0 + part * 1024 + bi * 256) for part in range(3)]
            for cc in range(2):
                c = bi * 2 + cc
                psB, prB = proj_ws(wb[0][0], wb[0][1], cc * 128, 128, N)
                psC, prC = proj_ws(wb[1][0], wb[1][1], cc * 128, 128, N)
                psX, prX = proj_ws(wb[2][0], wb[2][1], cc * 128, 128, N)
                st3 = stg[2][:, 0:SB * 6].rearrange("p (b w) -> p b w", b=SB)
                rst = r_stg[2]
                S.op("act", [prC], [r_acc[2]], lambda psC=psC: nc.scalar.copy(out=acc[2][:, 0:N], in_=psC[:, 0:N]))
                S.op("dve", [prX, r_acc[2]], [rst], lambda psX=psX, st3=st3: nc.vector.tensor_tensor(
                    out=st3[:, :, 2:6], in0=psX[:, 0:N].rearrange("p (b w) -> p b w", b=SB),
                    in1=acc[2][:, 0:N].rearrange("p (b w) -> p b w", b=SB), op=ALU.mult))
                S.op("dve", [r_ssc], [rst], lambda c=c, st3=st3: nc.vector.tensor_copy(out=st3[:, :, 0:2], in_=ssc[:, c, :, :]))
                S.op("dve", [rst], [r_ssc], lambda c=c, st3=st3: nc.vector.tensor_copy(out=ssc[:, c, :, :], in_=st3[:, :, 4:6]))
                ac, rac = acc[c % 2], r_acc[c % 2]
                ac3 = ac[:, 0:N].rearrange("p (b w) -> p b w", b=SB)
                taps = [pp[:, l, PP_SHW + j * 8 + c:PP_SHW + j * 8 + c + 1] for j in range(3)]
                S.op("dve", [rst, r_pp], [rac], lambda ac3=ac3, taps=taps, st3=st3: nc.vector.tensor_scalar(
                    out=ac3, in0=st3[:, :, 0:4], scalar1=taps[0], scalar2=None, op0=ALU.mult))
                for jj in (1, 2):
                    S.op("dve", [rst, rac, r_pp], [rac], lambda ac3=ac3, taps=taps, jj=jj, st3=st3: nc.vector.scalar_tensor_tensor(
                        out=ac3, in0=st3[:, :, jj:jj + 4], scalar=taps[jj], in1=ac3, op0=ALU.mult, op1=ALU.add))
                S.op("dve", [prB, rac], [r_ybT], lambda ac=ac, psB=psB, c=c: nc.vector.tensor_tensor(
                    out=ybT[:, c, 0:N], in0=psB[:, 0:N], in1=ac[:, 0:N], op=ALU.mult))
        S.dma("sp", d_out, [r_ssc], [], lambda: nc.sync.dma_start(out=ssc_o[l], in_=ssc[:]))
        ydT, r_ydT = G[4], r_G[4]
        S.dma("sp", d_rowb, [], [r_rowb], lambda: nc.sync.dma_start(out=rowb[:, :], in_=rowb_d[l, :, 1024:4096]))
        S.dma("pool", d_in, [], [r_gmraw], lambda: nc.gpsimd.dma_start(out=gmw_raw[:], in_=gmw_d[l].rearrange("g t s -> t g s")))
        for g in range(8):
            ps, pr = psum()
            psb = ps[:, 0:64].bitcast(BF16)
            S.op("pe", [r_gmraw, r_cst], [pr], lambda psb=psb, g=g: nc.tensor.transpose(psb[:, 0:128], gmw_raw[:, g, :], identb[:]))
            S.op("dve", [pr, r_cst], [r_gmw], lambda psb=psb, g=g: nc.vector.tensor_tensor(
                out=gmwT[:, g, :], in0=psb[:, 0:128], in1=Umat, op=ALU.mult))
        for bi in range(4):
            wv, wr = win(l, UV0 + bi * 256)
            for cc in range(2):
                c = bi * 2 + cc
                ps, pr = proj_ws(wv, wr, cc * 128, 128, N)
                S.op("act", [pr], [r_ydT], lambda ps=ps, c=c: nc.scalar.activation(out=ydT[:, c, 0:N], in_=ps[:, 0:N], func=AF.Gelu_apprx_tanh))
        wv_ = [win(l, UV0 + 1024 + i * 256) for i in range(4)]
        w4 = sb_once("w4f", [4, 8, 4])
        S.op("dve", [r_gmw], [r_w4], lambda: nc.vector.tensor_copy(out=w4[:, :, :], in_=gmwT[0:4, :, 0:4]))
        psw, prw = psum()
        S.op("pe", [r_w4, r_cst], [prw], lambda: nc.tensor.matmul(
            psw[0:NS, 0:8 * NS].rearrange("p (g b t) -> p g b t", g=8, b=SB), lhsT=Rrep[:, 0:NS],
            rhs=w4[:, :, :].unsqueeze(2).to_broadcast([4, 8, SB, 4]), start=True, stop=True))
        wbd = sb_once("wbd", [64, 8, 64], BF16)
        S.op("dve", [prw, r_cst], [r_wbd], lambda: nc.vector.tensor_tensor(
            out=wbd[0:NS, :, 0:NS], in0=psw[0:NS, 0:8 * NS].rearrange("p (g t) -> p g t", g=8),
            in1=bmask[0:NS, 0:NS].unsqueeze(1).to_broadcast([NS, 8, NS]), op=ALU.mult))
        bsp = sb_once("bsp", [128, 8, 64])
        S.op("dve", [r_rowb], [r_bsp], lambda: nc.vector.tensor_copy(
            out=bsp[:, :, 0:NS].rearrange("p g (b t) -> p g b t", b=SB),
            in_=bsrow.rearrange("p (g t) -> p g t", g=8)[:, :, 0:4].unsqueeze(2).to_broadcast([128, 8, SB, 4])))

        def vout(tt, rtt):
            S.dma("sp", d_out, [rtt], [], lambda: nc.sync.dma_start(out=sgmv_o[l].rearrange("b t f -> (b t) f"), in_=tt[0:NS, :]))
        gmlp_chunk(l, 0, NS, wv_, ydT, r_ydT, lng, lnb, bsp, wbd, vout, bs3=True)
        mT, r_mT = G[5], r_G[5]
        brs = ((G[1], r_G[1]), (G[2], r_G[2]), (G[3], r_G[3]), (G[4], r_G[4]))
        macc = (acc[0], acc[1])
        r_macc = (r_acc[0], r_acc[1])
        for bi in range(4):
            for i in range(4):
                wg = win(l, GT0 + i * 1024 + bi * 256)
                wb = wload(w_br_d[l, i, :, bi * 256:bi * 256 + 256].rearrange("(k p) c -> p k c", p=128), 8, 256, key=("br", l, i, bi))
                for cc in range(2):
                    c = bi * 2 + cc
                    psg, prg = proj_ws(wg[0], wg[1], cc * 128, 128, N)
                    psp, prp = proj_ws(wb[0], wb[1], cc * 128, 128, N, rhs=brs[i][0], rres=brs[i][1])
                    S.op("act", [prg], [r_stg[2]], lambda psg=psg: nc.scalar.activation(out=stg[2][:, 0:N], in_=psg[:, 0:N], func=AF.Sigmoid))
                    if i == 0:
                        S.op("dve", [prp, r_stg[2]], [r_macc[cc]], lambda psp=psp, cc=cc: nc.vector.tensor_tensor(
                            out=macc[cc][:, 0:N], in0=psp[:, 0:N], in1=stg[2][:, 0:N], op=ALU.mult))
                    else:
                        S.op("dve", [prp, r_stg[2]], [r_acc[2]], lambda psp=psp: nc.vector.tensor_tensor(
                            out=acc[2][:, 0:N], in0=psp[:, 0:N], in1=stg[2][:, 0:N], op=ALU.mult))
                        if i < 3:
                            S.op("dve", [r_macc[cc], r_acc[2]], [r_macc[cc]], lambda cc=cc: nc.vector.tensor_tensor(
                                out=macc[cc][:, 0:N], in0=macc[cc][:, 0:N], in1=acc[2][:, 0:N], op=ALU.add))
                        else:
                            S.op("dve", [r_macc[cc], r_acc[2]], [r_mT], lambda c=c, cc=cc: nc.vector.tensor_tensor(
                                out=mT[:, c, 0:N], in0=macc[cc][:, 0:N], in1=acc[2][:, 0:N], op=ALU.add))
        for bi in range(4):
            wv, wr = wload(wcols(w_o_d, l, bi * 256, 256), 8, 256, key=("o", l, bi))
            for cc in range(2):
                c = bi * 2 + cc
                ps, pr = proj_ws(wv, wr, cc * 128, 128, N, rhs=mT, rres=r_mT)
                resid_s(ps, pr, c, 2)
        mod_norm_s(1)
        gat = (G[1], G[2], G[3])
        r_gat = (r_G[1], r_G[2], r_G[3])
        for blk in range(11):
            wa = wload(wcols(w_up_d, l, blk * 256, 256), 8, 256, key=("ua", l, blk))
            wg_ = wload(wcols(w_up_d, l, DFF + blk * 256, 256), 8, 256, key=("ug", l, blk))
            for cc in range(2):
                i = blk * 2 + cc
                outs = []
                for which, (wv, wr) in enumerate((wa, wg_)):
                    ci = which * 22 + i
                    ps, pr = proj_ws(wv, wr, cc * 128, 128, N)
                    taps = [pp[:, l, PP_FW + j * 44 + ci:PP_FW + j * 44 + ci + 1] for j in range(3)]
                    ac, rac = conv_s(128, ps, pr, taps, pp[:, l, PP_FB + ci:PP_FB + ci + 1], [r_pp], sffn[:, ci, :, :], r_sffn, which)
                    outs.append((ac, rac))
                S.op("act", [outs[0][1]], [r_acc[2]], lambda a=outs[0][0]: nc.scalar.activation(out=acc[2][:, 0:N], in_=a[:, 0:N], func=AF.Silu))
                S.op("dve", [r_acc[2], outs[1][1]], [r_gat[i // 8]], lambda g_=outs[1][0], i=i: nc.vector.tensor_tensor(
                    out=gat[i // 8][:, i % 8, 0:N], in0=acc[2][:, 0:N], in1=g_[:, 0:N], op=ALU.mult))
        S.dma("sp", d_out, [r_sffn], [], lambda: nc.sync.dma_start(out=sffn_o[l], in_=sffn[:]))
        for c in range(8):
            wh = [wload(w_dn_d[l, hh * 1408:(hh + 1) * 1408, c * 128:(c + 1) * 128].rearrange("(k p) c -> p k c", p=128), 11, 128, key=("dn", l, c, hh)) for hh in range(2)]
            ps, pr = psum()

            def f(wh=wh, ps=ps):
                for k in range(22):
                    last = nc.tensor.matmul(ps[:, 0:N], lhsT=wh[k // 11][0][:, k % 11, :], rhs=gat[k // 8][:, k % 8, 0:N], start=(k == 0), stop=(k == 21))
                return last
            S.op("pe", [wh[0][1], wh[1][1]] + list(r_gat), [pr], f)
            resid_s(ps, pr, c, 5)

    for ti in range(NT):
        S.dma("sp", d_x, [], [r_x], lambda ti=ti: nc.sync.dma_start(
            out=xT[:], in_=xT_d[:, ti * TT:(ti + 1) * TT].rearrange("(c p) t -> p c t", p=128)))
        for l in range(L):
            prompt_tile_layer(ti, l)
        ps, pr = psum()
        sq = G[5]
        S.op("act", [r_x], [r_G[5]], lambda: nc.scalar.activation(out=sq[:, :, :], in_=xT[:, :, :], func=AF.Square))

        def ff(ps=ps):
            for k in range(8):
                last = nc.tensor.matmul(ps[:, 0:TT], lhsT=onesb[:], rhs=sq[:, k, :], start=(k == 0), stop=(k == 7))
            return last
        S.op("pe", [r_G[5], r_cst], [pr], ff)
        rstd_of(ps[:, 0:TT], 128, acc[2][:, 0:TT], [pr, r_cst], [r_acc[2]], 1.0 / D)
        for c in range(8):
            yo, ryo = acc[c % 2], r_acc[c % 2]
            S.op("dve", [r_x, r_pp, r_acc[2]], [ryo], lambda c=c, yo=yo: nc.vector.scalar_tensor_tensor(
                out=yo[:, 0:TT], in0=xT[:, c, :], scalar=pp[:, 0, PP_GFIN + c:PP_GFIN + c + 1], in1=acc[2][:, 0:TT], op0=ALU.mult, op1=ALU.mult))
            S.dma("sp", d_out, [ryo], [], lambda ti=ti, yo=yo, c=c: nc.sync.dma_start(
                out=yT_d[c * 128:(c + 1) * 128, ti * TT:(ti + 1) * TT], in_=yo[:, 0:TT]))
    if SB:
        NS = 4 * SB
        S.dma("sp", d_par, [], [r_csT], lambda: nc.sync.dma_start(out=csT[:], in_=csT_d))
        S.op("act", [r_csT], [r_csT], lambda: nc.scalar.activation(out=csb[:], in_=csT[:], func=AF.Silu))
        S.dma("sp", d_x, [], [r_x], lambda: nc.sync.dma_start(out=xT[:, :, 0:NS], in_=xsT_d.rearrange("(c p) t -> p c t", p=128)))
        for l in range(L):
            sample_tile_layer(l)
        ps, pr = psum()
        sq = G[5]
        S.op("act", [r_x], [r_G[5]], lambda: nc.scalar.activation(out=sq[:, :, 0:NS], in_=xT[:, :, 0:NS], func=AF.Square))

        def ffs(ps=ps):
            for k in range(8):
                last = nc.tensor.matmul(ps[:, 0:NS], lhsT=onesb[:], rhs=sq[:, k, 0:NS], start=(k == 0), stop=(k == 7))
            return last
        S.op("pe", [r_G[5], r_cst], [pr], ffs)
        rstd_of(ps[:, 0:NS], 128, acc[2][:, 0:NS], [pr, r_cst], [r_acc[2]], 1.0 / D)
        for c in range(8):
            yo, ryo = acc[c % 2], r_acc[c % 2]
            S.op("dve", [r_x, r_pp, r_acc[2]], [ryo], lambda c=c, yo=yo: nc.vector.scalar_tensor_tensor(
                out=yo[:, 0:NS], in0=xT[:, c, 0:NS], scalar=pp[:, 0, PP_GFIN + c:PP_GFIN + c + 1], in1=acc[2][:, 0:NS], op0=ALU.mult, op1=ALU.mult))
            S.dma("sp", d_out, [ryo], [], lambda yo=yo, c=c: nc.sync.dma_start(out=ysT_d[c * 128:(c + 1) * 128, :], in_=yo[:, 0:NS]))
    S.finish()
    return nc


def _consts():
    cst = np.zeros((128, 900), np.float32)
    tok = np.arange(64)
    cst[0:4, 772:836] = (np.arange(4)[:, None] == (tok % 4)[None, :])
    cst[0:64, 836:900] = ((tok // 4)[:, None] == (tok // 4)[None, :])
    jj = np.arange(132)[None, :]
    ds = 128 + np.arange(128)[:, None] - jj
    cst[:, 640:772] = np.where((ds >= 0) & (ds <= 128), ds, 1e6)
    cst[:, 0:128] = np.eye(128, dtype=np.float32)
    s_ = np.arange(128)[:, None]
    t_ = np.arange(128)[None, :]
    cst[:, 128:256] = (s_ <= t_).astype(np.float32)
    cst[:, 256:384] = np.where(s_ > t_, NEG, 0.0)
    kpos = np.arange(256)[None, :] - 128
    dist = s_ - kpos
    cst[:, 384:640] = np.where((dist >= 0) & (dist <= 128), dist, 1e6)
    return cst


def _pp(w, L):
    pp = np.zeros((128, L, NPP), np.float32)
    pp64 = np.zeros((64, L, 40), np.float32)
    for l in range(L):
        pp[:, l, PP_BADA:PP_BADA + 48] = w["b_ada"][l].reshape(48, 128).T
        pp[:, l, PP_GMIX:PP_GMIX + 8] = w["g_norm_mix"][l].reshape(8, 128).T
        pp[:, l, PP_GFFN:PP_GFFN + 8] = w["g_norm_ffn"][l].reshape(8, 128).T
        for j in range(4):
            pp[:, l, PP_SCW + j * 8:PP_SCW + j * 8 + 8] = w["ssd_conv_w"][l][j, :1024].reshape(8, 128).T
            pp64[:, l, j * 8:j * 8 + 8] = w["ssd_conv_w"][l][j, 1024:].reshape(8, 64).T
        pp[:, l, PP_SCB:PP_SCB + 8] = w["ssd_conv_b"][l][:1024].reshape(8, 128).T
        pp64[:, l, 32:40] = w["ssd_conv_b"][l][1024:].reshape(8, 64).T
        for j in range(3):
            pp[:, l, PP_SHW + j * 8:PP_SHW + j * 8 + 8] = w["sc_conv_w"][l][j].reshape(8, 128).T
            pp[:, l, PP_FW + j * 44:PP_FW + j * 44 + 44] = w["ffn_conv_w"][l][j].reshape(44, 128).T
        pp[:, l, PP_FB:PP_FB + 44] = w["ffn_conv_b"][l].reshape(44, 128).T
        pp[:, l, PP_GFIN:PP_GFIN + 8] = w["g_final"].reshape(8, 128).T
    return pp, pp64


def _rows(w, L):
    rows = np.zeros((128, L, 64), np.float32)
    rowb = np.zeros((L, 128, 4096), np.float32)
    for l in range(L):
        rows[:, l, 0:16] = w["ssd_dt_bias"][l][None]
        rows[:, l, 16:32] = w["ssd_a_log"][l][None]
        rows[:, l, 32:48] = w["ssd_d"][l][None]
        rows[:, l, 48:64] = w["attn_sinks"][l][None]
        rowb[l, :, 0:1024] = w["ssd_norm_g"][l][None]
        rowb[l, :, 1024:2048] = w["gm_ln_g"][l][None]
        rowb[l, :, 2048:3072] = w["gm_ln_b"][l][None]
        rowb[l, :, 3072:4096] = w["gm_b_s"][l].reshape(1024)[None]
    return rows, rowb


def shared_maps(w, L):
    pp, pp64 = _pp(w, L)
    rows, rowb = _rows(w, L)
    m = {"pp": pp, "pp64": pp64, "rows": rows, "rowb": rowb, "cst": _consts()}
    for k in ("w_ada", "w_in", "w_branch", "w_o", "ffn_w_up", "ffn_w_down", "gm_w_s"):
        m[k] = np.ascontiguousarray(w[k][:L], dtype=np.float32)
    return m


def core_map(shared, xp_b, cp_b):
    m = dict(shared)
    m["xT"] = np.ascontiguousarray(xp_b.T)
    m["cT"] = np.ascontiguousarray(cp_b.reshape(8, 128).T)[:, :, None].copy()
    return m


def unpack_prompt(r, L):
    o = {}
    o["y"] = np.ascontiguousarray(r["yT"].T)
    o["ssm"] = np.ascontiguousarray(r["p_ssm"].reshape(L, 64, 4, 4, 64).transpose(0, 2, 3, 4, 1)).reshape(L, 16, 64, 64)
    xs = r["p_xbc"][..., 0:3].transpose(0, 3, 2, 1).reshape(L, 3, 1024)
    bc = r["p_bc"][..., 0:3].transpose(0, 3, 2, 1).reshape(L, 3, 512)
    o["ssd_conv"] = np.concatenate([xs, bc], axis=2)
    o["sc_conv"] = r["p_sc"].transpose(0, 3, 2, 1).reshape(L, 2, 1024)
    o["k"] = r["p_k"].reshape(L, 128, 4, 64)
    o["v"] = r["p_v"].reshape(L, 128, 4, 64)
    o["ffn_conv"] = r["p_ffn"][:, :, 0:44, :].transpose(0, 3, 2, 1).reshape(L, 2, 5632)
    return o


def sample_map(m, inp, bs, L):
    SBn = bs.stop - bs.start
    m["xsT"] = np.ascontiguousarray(inp["x_sample"][bs].reshape(4 * SBn, D).T)
    m["csT"] = np.ascontiguousarray(inp["c_sample"][bs].reshape(SBn, 8, 128).transpose(2, 1, 0))
    st = inp["state_ssm"][:L, bs]
    m["s_ssm_in"] = np.ascontiguousarray(st.reshape(L, SBn, 4, 4, 64, 64).transpose(0, 1, 5, 2, 3, 4)).reshape(L, SBn, 64, 4, 256)
    sc = inp["state_ssd_conv"][:L, bs]
    m["s_xbc_in"] = np.ascontiguousarray(sc[..., :1024].reshape(L, SBn, 3, 8, 128).transpose(0, 4, 3, 1, 2))
    m["s_bc_in"] = np.ascontiguousarray(sc[..., 1024:].reshape(L, SBn, 3, 8, 64).transpose(0, 4, 3, 1, 2))
    m["s_sc_in"] = np.ascontiguousarray(inp["state_sc_conv"][:L, bs].reshape(L, SBn, 2, 8, 128).transpose(0, 4, 3, 1, 2))
    m["s_ffn_in"] = np.ascontiguousarray(inp["state_ffn_conv"][:L, bs].reshape(L, SBn, 2, 44, 128).transpose(0, 4, 3, 1, 2))
    ck = inp["cache_k"][:L, bs]
    kt = ck.transpose(0, 1, 4, 3, 2)
    m["ckT"] = np.ascontiguousarray(np.concatenate([kt, kt], axis=2))
    m["ck"] = np.ascontiguousarray(ck.reshape(L, SBn, 128, 256))
    m["cv"] = np.ascontiguousarray(inp["cache_v"][:L, bs].reshape(L, SBn, 128, 256))
    return m


def unpack_sample(r, L, SBn):
    o = {}
    o["y"] = np.ascontiguousarray(r["ysT"].T).reshape(SBn, 4, D)
    o["ssm"] = np.ascontiguousarray(r["s_ssm_o"].reshape(L, SBn, 64, 4, 4, 64).transpose(0, 1, 3, 4, 5, 2)).reshape(L, SBn, 16, 64, 64)
    xs = r["s_xbc_o"].transpose(0, 3, 4, 2, 1).reshape(L, SBn, 3, 1024)
    bc = r["s_bc_o"].transpose(0, 3, 4, 2, 1).reshape(L, SBn, 3, 512)
    o["ssd_conv"] = np.concatenate([xs, bc], axis=3)
    o["sc_conv"] = r["s_sc_o"].transpose(0, 3, 4, 2, 1).reshape(L, SBn, 2, 1024)
    o["k"] = r["s_k_o"].reshape(L, SBn, 128, 4, 64)
    o["v"] = r["s_v_o"].reshape(L, SBn, 128, 4, 64)
    o["ffn_conv"] = r["s_ffn_o"].transpose(0, 3, 4, 2, 1).reshape(L, SBn, 2, 5632)
    o["gm_v"] = r["s_gmv_o"].reshape(L, SBn, 4, 1024)
    return o


def kernel(**inputs):
    inp = {k: np.asarray(v) for k, v in inputs.items()}
    L = 4
    B = inp["x_prompt"].shape[0]
    SEQ = inp["x_prompt"].shape[1]
    NT = SEQ // TT
    DB = inp["x_sample"].shape[0]
    SBn = DB // 8
    shared = shared_maps(inp, L)
    nc = build(NT, L, SBn)
    in_maps = []
    for core in range(8):
        b = core % B
        m = core_map(shared, inp["x_prompt"][b], inp["c_prompt"][b])
        sample_map(m, inp, slice(core * SBn, (core + 1) * SBn), L)
        in_maps.append(m)
    res = run_bass_kernel_spmd(nc, in_maps, core_ids=list(range(8)))
    rs = [{k: np.asarray(v) for k, v in r.items()} for r in res.results]
    po = [unpack_prompt(rs[b], L) for b in range(B)]
    so = [unpack_sample(rs[c], L, SBn) for c in range(8)]
    f32 = np.float32
    y_prompt = np.stack([p["y"] for p in po], 0).astype(f32)
    y_sample = np.concatenate([s_["y"] for s_ in so], 0).astype(f32)

    def pst(k):
        return np.ascontiguousarray(np.stack([p[k] for p in po], 1)).astype(f32)

    def sst(k):
        return np.ascontiguousarray(np.concatenate([s_[k] for s_ in so], 1)).astype(f32)
    return (y_prompt, y_sample, pst("ssm"), pst("ssd_conv"), pst("sc_conv"), pst("k"), pst("v"), pst("ffn_conv"),
            sst("ssm"), sst("ssd_conv"), sst("sc_conv"), sst("k"), sst("v"), sst("ffn_conv"), sst("gm_v"))
```

```python
import os
import numpy as np
import concourse.bass as bass
import concourse.mybir as mybir
from concourse.bass_utils import run_bass_kernel_spmd

F32 = mybir.dt.float32
BF16 = mybir.dt.bfloat16
AF = mybir.ActivationFunctionType
ALU = mybir.AluOpType
AX = mybir.AxisListType

D = 1024
TT = 512
NCH = TT // 128
WBC = 256
SLOPES = [2.0 ** (-8.0 * (h + 1) / 16) for h in range(16)]
EPS = 1e-6
XBC0, DT0, BCX0, Q0, K0, V0, UV0, GT0, DIN = 1024, 2560, 2576, 5648, 6672, 6928, 7184, 9232, 13328
DFF = 2816
PP_BADA, PP_GMIX, PP_GFFN, PP_SCW, PP_SCB, PP_SHW, PP_FW, PP_FB, PP_GFIN = 0, 48, 56, 64, 96, 104, 128, 260, 304
NPP = 312
NEG = -30000.0
DBG_STOP = int(os.environ.get('DBG_STOP', '99'))
DBG_ATT = int(os.environ.get('DBG_ATT', '99'))
DBG_ATTP = int(os.environ.get('DBG_ATTP', '99'))
DBG_S = int(os.environ.get('DBG_S', '99'))


class Res:
    __slots__ = ("name", "w", "r", "excl")

    def __init__(self, name, excl=False):
        self.name = name
        self.w = None
        self.r = {}
        self.excl = excl


class DSem:
    __slots__ = ("sem", "tot", "key", "keep")

    def __init__(self, nc, name):
        self.sem = nc.alloc_semaphore(name)
        self.tot = 0
        self.key = name
        self.keep = False


class Sched:
    def __init__(self, nc):
        self.nc = nc
        self.eng = {}
        self.seen = {}
        self.dsems = []
        self.dkeys = {}
        for name, h in (("pe", nc.tensor), ("act", nc.scalar), ("dve", nc.vector),
                        ("pool", nc.gpsimd), ("sp", nc.sync)):
            self.eng[name] = [h, nc.alloc_semaphore("s_" + name), 0]

    def dsem(self, name):
        d = DSem(self.nc, "d_" + name)
        self.dsems.append(d)
        self.dkeys[d.key] = d
        return d

    def _waits(self, eng, reads, writes):
        need = {}

        def add(ent):
            key, sem, val = ent
            if key in self.dkeys:
                val = self.dkeys[key].tot
            if key not in need or need[key][1] < val:
                need[key] = (sem, val)

        for r in reads:
            if r.w is not None:
                add(r.w)
            if r.excl:
                for key, (sem, val) in r.r.items():
                    if key != eng:
                        add((key, sem, val))
        for w in writes:
            if w.w is not None:
                add(w.w)
            for key, (sem, val) in w.r.items():
                add((key, sem, val))
        h = self.eng[eng][0]
        for key, (sem, val) in need.items():
            if self.seen.get((eng, key), 0) < val:
                h.wait_ge(sem, val)
                self.seen[(eng, key)] = val

    def op(self, eng, reads, writes, fn):
        E = self.eng[eng]
        self._waits(eng, reads, writes)
        inst = fn()
        E[2] += 1
        inst.then_inc(E[1], 1)
        for r in reads:
            r.r[eng] = (E[1], E[2])
        for w in writes:
            w.w = (eng, E[1], E[2])
            w.r = {}
        return inst

    def _auto_dsem(self, reads, writes):
        if writes:
            name = "ld_" + writes[0].name
        elif reads:
            name = "st_" + reads[0].name
        else:
            name = "dd"
        d = self.dkeys.get("d_" + name)
        if d is None:
            d = self.dsem(name)
        return d

    def dma(self, q, ds, reads, writes, fn, extra=()):
        if not getattr(ds, "keep", False):
            ds = self._auto_dsem(reads, writes)
        self._waits(q, reads, writes)
        for (sem_, val_, key_) in extra:
            if self.seen.get((q, key_), 0) < val_:
                self.eng[q][0].wait_ge(sem_, val_)
                self.seen[(q, key_)] = val_
        inst = fn()
        ds.tot += 16
        inst.then_inc(ds.sem, 16)
        for r in reads:
            r.r[ds.key] = (ds.sem, ds.tot)
        for w in writes:
            w.w = (ds.key, ds.sem, ds.tot)
            w.r = {}
        return inst

    def finish(self, eng="sp"):
        h = self.eng[eng][0]
        for d in self.dsems:
            if d.tot > 0:
                h.wait_ge(d.sem, d.tot)


def build(NT, L, SB, dbg=False):
    nc = bass.Bass("TRN2", target_bir_lowering=False)
    S = Sched(nc)
    NTOK = NT * TT

    def din(name, shape):
        return nc.dram_tensor(name, list(shape), F32, kind="ExternalInput").ap()

    def dout(name, shape):
        return nc.dram_tensor(name, list(shape), F32, kind="ExternalOutput").ap()

    xT_d = din("xT", [D, NTOK])
    cT_d = din("cT", [128, 8, 1])
    w_ada_d = din("w_ada", [L, D, 6 * D])
    w_in_d = din("w_in", [L, D, DIN])
    w_br_d = din("w_branch", [L, 4, D, D])
    w_o_d = din("w_o", [L, D, D])
    w_up_d = din("ffn_w_up", [L, D, 2 * DFF])
    w_dn_d = din("ffn_w_down", [L, DFF, D])
    pp_d = din("pp", [128, L, NPP])
    pp64_d = din("pp64", [64, L, 40])
    rows_d = din("rows", [128, L, 64])
    rowb_d = din("rowb", [L, 128, 4096])
    gmw_d = din("gm_w_s", [L, 8, 128, 128])
    cst_d = din("cst", [128, 900])

    yT_d = dout("yT", [D, NTOK])
    pssm_d = dout("p_ssm", [L, 64, 4, 256])
    pxbc_d = dout("p_xbc", [L, 128, 8, 4])
    pbc_d = dout("p_bc", [L, 64, 8, 4])
    psc_d = dout("p_sc", [L, 128, 8, 2])
    pk_d = dout("p_k", [L, 128, 256])
    pv_d = dout("p_v", [L, 128, 256])
    pffn_d = dout("p_ffn", [L, 128, 48, 2])

    if SB:
        xsT_d = din("xsT", [D, 4 * SB])
        csT_d = din("csT", [128, 8, SB])
        sssm_d = din("s_ssm_in", [L, SB, 64, 4, 256])
        sxbc_d = din("s_xbc_in", [L, 128, 8, SB, 3])
        sbc_d = din("s_bc_in", [L, 64, 8, SB, 3])
        ssc_d = din("s_sc_in", [L, 128, 8, SB, 2])
        sffn_d = din("s_ffn_in", [L, 128, 44, SB, 2])
        ckT_d = din("ckT", [L, SB, 128, 4, 128])
        ck_d = din("ck", [L, SB, 128, 256])
        cv_d = din("cv", [L, SB, 128, 256])
        ysT_d = dout("ysT", [D, 4 * SB])
        sssm_o = dout("s_ssm_o", [L, SB, 64, 4, 256])
        sxbc_o = dout("s_xbc_o", [L, 128, 8, SB, 3])
        sbc_o = dout("s_bc_o", [L, 64, 8, SB, 3])
        ssc_o = dout("s_sc_o", [L, 128, 8, SB, 2])
        sffn_o = dout("s_ffn_o", [L, 128, 44, SB, 2])
        sk_o = dout("s_k_o", [L, SB, 128, 256])
        sv_o = dout("s_v_o", [L, SB, 128, 256])
        sgmv_o = dout("s_gmv_o", [L, SB, 4, 1024])

    def sb(name, shape, dt=F32):
        return nc.alloc_sbuf_tensor(name, list(shape), dt)

    xT = sb("xTt", [128, 8, TT]); r_x = Res("x")
    mod = sb("mod", [128, L, 48, 1]); r_mod = Res("mod")
    gm = sb("gm", [128, L, 2, 8, 1]); r_gm = Res("gm")
    pp = sb("ppt", [128, L, NPP]); r_pp = Res("pp")
    pp64 = sb("pp64t", [64, L, 40]); r_pp64 = Res("pp64")
    rows = sb("rowst", [128, L, 64]); r_rows = Res("rows")
    rowb = sb("rowbt", [128, 3072]); r_rowb = Res("rowb")
    cst = sb("cstt", [128, 900]); r_cst = Res("cst")
    identb = sb("identb", [128, 128], BF16)
    onesb = sb("onesb", [128, 128], BF16)
    ident = cst[:, 0:128]
    Umat = cst[:, 128:256]
    NEGM = cst[:, 256:384]
    distm = cst[:, 384:640]
    dists = cst[:, 640:772]
    Rrep = cst[0:4, 772:836]
    bmask = cst[0:64, 836:900]
    gmwT = sb("gmwT", [128, 8, 128], BF16); r_gmw = Res("gmwT")
    gmw_raw = sb("gmw_raw", [128, 8, 128], BF16); r_gmraw = Res("gmw_raw")
    ST = sb("ST", [64, L, 4, 256]); r_ST = [Res("ST%d" % l) for l in range(L)]
    STb = sb("STb", [64, 4, 256], BF16); r_STb1 = Res("STb"); r_STb = [r_STb1] * L
    cx = sb("cx", [128, L, 8, 4]); r_cx = [Res("cx%d" % l) for l in range(L)]
    cbc = sb("cbc", [64, L, 8, 4]); r_cbc = [Res("cbc%d" % l) for l in range(L)]
    csc = sb("csc", [128, L, 8, 2]); r_csc = [Res("csc%d" % l) for l in range(L)]
    cff = sb("cff", [128, L, 48, 2]); r_cff = [Res("cff%d" % l) for l in range(L)]
    kprev = sb("kprev", [128, L, 4, 128], BF16); r_kprev = [Res("kp%d" % l) for l in range(L)]
    vprev = sb("vprev", [128, L, 256], BF16); r_vprev = [Res("vp%d" % l) for l in range(L)]
    G = [sb("G%d" % i, [128, 8, TT], BF16) for i in range(6)]
    r_G = [Res("G%d" % i) for i in range(6)]
    hB, r_h = G[0], r_G[0]
    stg = [sb("stg%d" % i, [128, TT + 4]) for i in range(3)]
    r_stg = [Res("stg%d" % i) for i in range(3)]
    acc = [sb("acc%d" % i, [128, TT]) for i in range(3)]
    r_acc = [Res("acc%d" % i) for i in range(3)]
    BCT = G[2][0:64, :, :]; r_BCT = r_G[2]
    kT = sb("kT", [128, 4, TT], BF16); r_kT = Res("kT")
    vtok = sb("vtok", [128, NCH, 256], BF16); r_vtok = Res("vtok")
    tA = sb("tA", [128, 1024]); r_tA = Res("tA")
    tB = sb("tB", [128, 1024]); r_tB = Res("tB")
    tC = sb("tC", [128, 1024], BF16); r_tC = Res("tC")
    tD = sb("tD", [128, 1024], BF16); r_tD = Res("tD")
    tE = sb("tE", [128, 1024], BF16); r_tE = Res("tE")
    Eh = sb("Eh", [128, 16, 128], BF16); r_Eh = Res("Eh")
    Mh, r_Mh = Eh, r_Eh
    CBs = sb("CBs", [128, 4, 128], BF16); r_CBs = Res("CBs")
    Btok = sb("Btok", [128, 4, 64], BF16); r_Btok = Res("Btok")
    sm = sb("sm", [128, 256]); r_sm = Res("sm")
    r_aS, r_aP, r_aT = Res("attS"), Res("attP"), Res("attT")
    RV = {k: Res("sm_" + k) for k in ("dt", "dtA", "Acs", "nAcs", "eA", "dec", "wdec", "cdec", "ssq",
                                      "rinv", "mx", "nmx", "esk", "rsum", "ssum", "mean", "gssq", "rstd")}
    ktok, r_ktok = tB, r_tB

    d_in = d_par = d_rowb = d_out = d_x = None

    PS = [nc.alloc_psum_tensor("ps%d" % i, [128, 512], F32) for i in range(8)]
    r_PS = [Res("ps%d" % i, excl=True) for i in range(8)]
    psi = [0]

    def psum():
        i = psi[0]
        psi[0] = (i + 1) % 6
        return PS[i], r_PS[i]

    NSLOT = 7
    WS = [sb("ws%d" % i, [128, 2048], BF16) for i in range(NSLOT)]
    r_WS = [Res("ws%d" % i) for i in range(NSLOT)]
    d_WS = [S.dsem("ws%d" % i) for i in range(NSLOT)]
    for d_ in d_WS:
        d_.keep = True
    wsi = [0]

    NSCR = 111 * L + 2
    wscr = nc.dram_tensor("wscr", [NSCR, 128, 2048], BF16, kind="Internal").ap()
    scr = {}

    def wload(src, K, C, key=None):
        i = wsi[0]
        wsi[0] = (i + 1) % NSLOT
        flat = WS[i][:, 0:K * C]
        view = flat.rearrange("p (k c) -> p k c", k=K)
        if key is not None and key in scr:
            idx, ent = scr[key]
            S.dma("sp", d_WS[i], [], [r_WS[i]], lambda: nc.sync.dma_start(out=flat, in_=wscr[idx, :, 0:K * C]), extra=[ent])
            return view, r_WS[i]
        S.dma("pool", d_WS[i], [], [r_WS[i]], lambda: nc.gpsimd.dma_start(out=view, in_=src))
        if key is not None:
            idx = len(scr)
            assert idx < NSCR
            S.dma("sp", None, [r_WS[i]], [], lambda: nc.sync.dma_start(out=wscr[idx, :, 0:K * C], in_=flat))
            d = S.dkeys["d_st_ws%d" % i]
            scr[key] = (idx, (d.sem, d.tot, d.key))
        return view, r_WS[i]

    def wcols(wd, l, c0, C):
        return wd[l, :, c0:c0 + C].rearrange("(k p) c -> p k c", p=128)

    ev = [0]

    def eng2():
        ev[0] ^= 1
        return "act" if ev[0] else "dve"

    def copy(eng, out, in_, reads, writes):
        if eng == "act":
            return S.op("act", reads, writes, lambda: nc.scalar.copy(out=out, in_=in_))
        return S.op(eng, reads, writes, lambda: (nc.vector if eng == "dve" else nc.gpsimd).tensor_copy(out=out, in_=in_))

    S.dma("sp", d_par, [], [r_pp], lambda: nc.sync.dma_start(out=pp[:], in_=pp_d))
    S.dma("sp", d_par, [], [r_pp64], lambda: nc.sync.dma_start(out=pp64[:], in_=pp64_d))
    S.dma("sp", d_par, [], [r_rows], lambda: nc.sync.dma_start(out=rows[:], in_=rows_d))
    S.dma("sp", d_par, [], [r_cst], lambda: nc.sync.dma_start(out=cst[:], in_=cst_d))
    S.op("dve", [r_cst], [r_cst], lambda: nc.vector.tensor_copy(out=identb[:], in_=ident))
    S.op("dve", [], [r_cst], lambda: nc.vector.memset(onesb[:], 1.0))
    dupI = sb("dupI", [64, 128], BF16)
    S.op("dve", [r_cst], [r_cst], lambda: nc.vector.tensor_copy(out=dupI[:, 0:64], in_=ident[0:64, 0:64]))
    S.op("dve", [r_cst], [r_cst], lambda: nc.vector.tensor_copy(out=dupI[:, 64:128], in_=ident[0:64, 0:64]))
    ones32 = sb("ones32", [128, 128])
    S.op("dve", [], [r_cst], lambda: nc.vector.memset(ones32[:], 1.0))
    S.op("act", [r_rows], [r_rows], lambda: nc.scalar.activation(out=rows[:, :, 16:32], in_=rows[:, :, 16:32], func=AF.Exp))
    S.op("dve", [r_rows], [r_rows], lambda: nc.vector.tensor_scalar(out=rows[:, :, 16:32], in0=rows[:, :, 16:32], scalar1=-1.0, scalar2=None, op0=ALU.mult))
    for t_, r_ in ((ST, r_ST), (cx, r_cx), (cbc, r_cbc), (csc, r_csc), (cff, r_cff)):
        S.op("dve", [], list(r_), lambda t_=t_: nc.vector.memset(t_[:], 0.0))

    NCc = 1
    cTt = sb("cTt", [128, 8, NCc]); r_cT = Res("cT")
    cTb = sb("cTb", [128, 8, NCc], BF16)
    S.dma("sp", d_par, [], [r_cT], lambda: nc.sync.dma_start(out=cTt[:], in_=cT_d))
    S.op("act", [r_cT], [r_cT], lambda: nc.scalar.activation(out=cTb[:], in_=cTt[:], func=AF.Silu))

    def ada_layer(l, cb, r_cb, ncol, mod_t, r_mod_t, gm_t, r_gm_t, lidx):
        for blk in range(24):
            wv, wr = wload(wcols(w_ada_d, l, blk * 256, 256), 8, 256)
            ps, pr = psum()

            def f(wv=wv, ps=ps):
                for i in range(2):
                    for k in range(8):
                        last = nc.tensor.matmul(ps[:, i * ncol:(i + 1) * ncol], lhsT=wv[:, k, i * 128:(i + 1) * 128],
                                                rhs=cb[:, k, :], start=(k == 0), stop=(k == 7))
                return last
            S.op("pe", [wr, r_cb], [pr], f)
            S.op("dve", [pr, r_pp], [r_mod_t], lambda ps=ps, blk=blk: nc.vector.tensor_tensor(
                out=mod_t[:, lidx, blk * 2:blk * 2 + 2, :], in0=ps[:, 0:2 * ncol].rearrange("p (i j) -> p i j", i=2),
                in1=pp[:, l, PP_BADA + blk * 2:PP_BADA + blk * 2 + 2].unsqueeze(2).to_broadcast([128, 2, ncol]), op=ALU.add))
        for which, (sc_i, gcol) in enumerate(((1, PP_GMIX), (4, PP_GFFN))):
            S.op("dve", [r_mod_t, r_pp], [r_gm_t], lambda which=which, sc_i=sc_i, gcol=gcol: nc.vector.scalar_tensor_tensor(
                out=gm_t[:, lidx, which, :, :], in0=mod_t[:, lidx, sc_i * 8:sc_i * 8 + 8, :], scalar=1.0,
                in1=pp[:, l, gcol:gcol + 8].unsqueeze(2).to_broadcast([128, 8, ncol]), op0=ALU.add, op1=ALU.mult))

    for l in range(L):
        ada_layer(l, cTb, r_cT, 1, mod, r_mod, gm, r_gm, l)

    epsc = sb("epsc", [128, 1])
    S.op("dve", [], [r_cst], lambda: nc.vector.memset(epsc[:], EPS))

    def rstd_of(ss_ap, n, out_ap, reads, writes, scale):
        S.op("act", reads, writes, lambda: nc.scalar.activation(out=out_ap, in_=ss_ap, func=AF.Ln, bias=epsc[0:n, :], scale=scale))
        S.op("act", writes, writes, lambda: nc.scalar.activation(out=out_ap, in_=out_ap, func=AF.Exp, scale=-0.5))

    def mod_norm(l, which, N):
        sh_i = 0 if which == 0 else 3
        ps, pr = psum()
        sq = G[5]
        S.op("act", [r_x], [r_G[5]], lambda: nc.scalar.activation(out=sq[:, :, 0:N], in_=xT[:, :, 0:N], func=AF.Square))

        def f():
            for k in range(8):
                last = nc.tensor.matmul(ps[:, 0:N], lhsT=onesb[:], rhs=sq[:, k, 0:N], start=(k == 0), stop=(k == 7))
            return last
        S.op("pe", [r_G[5], r_cst], [pr], f)
        rs = acc[2]
        rstd_of(ps[:, 0:N], 128, rs[:, 0:N], [pr, r_cst], [r_acc[2]], 1.0 / D)
        for c in range(8):
            S.op("dve", [r_x, r_gm, r_acc[2]], [r_acc[c % 2]], lambda c=c: nc.vector.scalar_tensor_tensor(
                out=acc[c % 2][:, 0:N], in0=xT[:, c, 0:N], scalar=gm[:, l, which, c, 0:1], in1=rs[:, 0:N],
                op0=ALU.mult, op1=ALU.mult))
            S.op("act", [r_acc[c % 2], r_mod], [r_h], lambda c=c: nc.scalar.activation(
                out=hB[:, c, 0:N], in_=acc[c % 2][:, 0:N], func=AF.Identity, bias=mod[:, l, sh_i * 8 + c, 0:1], scale=1.0))

    def proj_ws(wv, wr, col, M, N, rhs=None, rres=None, nk=8):
        rhs = hB if rhs is None else rhs
        rres = r_h if rres is None else rres
        ps, pr = psum()

        def f():
            for k in range(nk):
                last = nc.tensor.matmul(ps[0:M, 0:N], lhsT=wv[:, k, col:col + M], rhs=rhs[:, k, 0:N],
                                        start=(k == 0), stop=(k == nk - 1))
            return last
        S.op("pe", [wr, rres], [pr], f)
        return ps, pr

    def proj_as(wlist, C, t0, T):
        ps, pr = psum()

        def f():
            for bi, (wv, wr) in enumerate(wlist):
                cw = min(256, C - bi * 256)
                for k in range(8):
                    last = nc.tensor.matmul(ps[0:T, bi * 256:bi * 256 + cw], lhsT=hB[:, k, t0:t0 + T], rhs=wv[:, k, 0:cw],
                                            start=(k == 0), stop=(k == 7))
            return last
        S.op("pe", [w[1] for w in wlist] + [r_h], [pr], f)
        return ps, pr

    def conv_fm(P, ps, pr, N, carry_ap, r_carry, taps, bias_ap, preads, si):
        K = len(taps)
        CW = K - 1
        st, rst = stg[si], r_stg[si]
        ac, rac = acc[si], r_acc[si]
        S.op("act", [pr], [rst], lambda: nc.scalar.copy(out=st[0:P, CW:CW + N], in_=ps[0:P, 0:N]))
        S.op("dve", [r_carry], [rst], lambda: nc.vector.tensor_copy(out=st[0:P, 0:CW], in_=carry_ap))
        S.op("dve", [rst], [r_carry], lambda: nc.vector.tensor_copy(out=carry_ap, in_=st[0:P, N:N + CW]))
        if bias_ap is not None:
            S.op("dve", [rst] + preads, [rac], lambda: nc.vector.tensor_scalar(
                out=ac[0:P, 0:N], in0=st[0:P, 0:N], scalar1=taps[0], scalar2=bias_ap, op0=ALU.mult, op1=ALU.add))
        else:
            S.op("dve", [rst] + preads, [rac], lambda: nc.vector.tensor_scalar(
                out=ac[0:P, 0:N], in0=st[0:P, 0:N], scalar1=taps[0], scalar2=None, op0=ALU.mult))
        for j in range(1, K):
            S.op("dve", [rst, rac] + preads, [rac], lambda j=j: nc.vector.scalar_tensor_tensor(
                out=ac[0:P, 0:N], in0=st[0:P, j:j + N], scalar=taps[j], in1=ac[0:P, 0:N], op0=ALU.mult, op1=ALU.add))
        return ac, rac

    def win(l, c0, C=256):
        return wload(wcols(w_in_d, l, c0, C), 8, C, key=("in", l, c0))

    def run(g):
        for _ in g:
            pass

    def interleave(g1, g2):
        live = [g1, g2]
        while live:
            for g in list(live):
                try:
                    next(g)
                except StopIteration:
                    live.remove(g)

    def prompt_tile_layer(ti, l):
        N = TT
        last_tile = (ti == NT - 1)
        S.dma("sp", d_rowb, [], [r_rowb], lambda: nc.sync.dma_start(out=rowb[:, 0:1024], in_=rowb_d[l, :, 0:1024]))
        normg = rowb[:, 0:1024]
        lng = rowb[:, 0:1024]
        lnb = rowb[:, 1024:2048]
        bsrow = rowb[:, 2048:3072]
        mod_norm(l, 0, N)
        xsT, r_xsT = G[3], r_G[3]
        for bi in range(4):
            wv, wr = win(l, XBC0 + bi * 256)
            for cc in range(2):
                c = bi * 2 + cc
                ps, pr = proj_ws(wv, wr, cc * 128, 128, N)
                taps = [pp[:, l, PP_SCW + j * 8 + c:PP_SCW + j * 8 + c + 1] for j in range(4)]
                ac, rac = conv_fm(128, ps, pr, N, cx[:, l, c, 0:3], r_cx[l], taps, pp[:, l, PP_SCB + c:PP_SCB + c + 1], [r_pp], c % 2)
                S.op("act", [rac], [r_xsT], lambda ac=ac, c=c: nc.scalar.activation(out=xsT[:, c, 0:N], in_=ac[:, 0:N], func=AF.Silu))
        for bi in range(2):
            wv, wr = win(l, XBC0 + 1024 + bi * 256)
            for ee in range(4):
                e = bi * 4 + ee
                ps, pr = proj_ws(wv, wr, ee * 64, 64, N)
                taps = [pp64[:, l, j * 8 + e:j * 8 + e + 1] for j in range(4)]
                ac, rac = conv_fm(64, ps, pr, N, cbc[:, l, e, 0:3], r_cbc[l], taps, pp64[:, l, 32 + e:33 + e], [r_pp64], e % 2)
                S.op("act", [rac], [r_BCT], lambda ac=ac, e=e: nc.scalar.activation(out=BCT[:, e, 0:N], in_=ac[0:64, 0:N], func=AF.Silu))
        qT, r_qT = G[5], r_G[5]
        wk = win(l, K0)
        wvv = win(l, V0)
        for hk in range(4):
            ps, pr = proj_ws(wk[0], wk[1], hk * 64, 64, N)
            ktmp, r_ktmp = tC, r_tC
            copy("act", ktmp[0:64, 0:N], ps[0:64, 0:N], [pr], [r_ktmp])
            ps2, pr2 = psum()
            S.op("pe", [r_ktmp, r_cst], [pr2], lambda ps2=ps2: nc.tensor.matmul(ps2[:, 0:N], lhsT=dupI[:, :], rhs=ktmp[0:64, 0:N], start=True, stop=True))
            copy("dve", kT[:, hk, 0:N], ps2[:, 0:N], [pr2], [r_kT])
        for j in range(NCH):
            ps, pr = proj_as([wk, wvv], 512, j * 128, 128)
            if last_tile and j == NCH - 1:
                S.op("act", [pr], [r_ktok], lambda ps=ps: nc.scalar.copy(out=ktok[:, 0:512], in_=ps[:, :]))
                S.dma("sp", d_out, [r_ktok], [], lambda: nc.sync.dma_start(out=pk_d[l], in_=ktok[:, 0:256]))
                S.dma("sp", d_out, [r_ktok], [], lambda: nc.sync.dma_start(out=pv_d[l], in_=ktok[:, 256:512]))
            S.op("dve", [pr], [r_vtok], lambda j=j, ps=ps: nc.vector.tensor_copy(out=vtok[:, j, :], in_=ps[:, 256:512]))
        for bi in range(4):
            wv, wr = win(l, Q0 + bi * 256)
            for cc in range(2):
                c = bi * 2 + cc
                ps, pr = proj_ws(wv, wr, cc * 128, 128, N)
                S.op("act", [pr], [r_qT], lambda ps=ps, c=c: nc.scalar.activation(out=qT[:, c, 0:N], in_=ps[:, 0:N], func=AF.Identity, scale=0.125))
        ycT, r_ycT = G[3], r_G[3]
        wz = [win(l, i * 256) for i in range(4)]
        wdt = [win(l, DT0, 256)]
        S.op("act", [r_ST[l]], [r_STb1], lambda: nc.scalar.copy(out=STb[:, :, :], in_=ST[:, l, :, :]))
        yaT, r_yaT = G[1], r_G[1]
        for j in range(NCH):
            interleave(ssd_chunk(l, j * 128, 128, wz, wdt, xsT, r_xsT, yaT, r_yaT, normg),
                       attn_chunk(l, ti, j, qT, r_qT, ycT, r_ycT))
        S.op("dve", [r_kT], [r_kprev[l]], lambda: nc.vector.tensor_copy(out=kprev[:, l, :, :], in_=kT[:, :, N - 128:N]))
        S.op("dve", [r_vtok], [r_vprev[l]], lambda: nc.vector.tensor_copy(out=vprev[:, l, :], in_=vtok[:, NCH - 1, :]))
        if last_tile:
            S.dma("sp", d_out, [r_ST[l]], [], lambda: nc.sync.dma_start(out=pssm_d[l], in_=ST[:, l, :, :]))
            S.dma("sp", d_out, [r_cx[l]], [], lambda: nc.sync.dma_start(out=pxbc_d[l], in_=cx[:, l, :, :]))
            S.dma("sp", d_out, [r_cbc[l]], [], lambda: nc.sync.dma_start(out=pbc_d[l], in_=cbc[:, l, :, :]))
        ybT, r_ybT = G[2], r_G[2]
        for bi in range(4):
            wb = [win(l, BCX0 + part * 1024 + bi * 256) for part in range(3)]
            for cc in range(2):
                c = bi * 2 + cc
                psB, prB = proj_ws(wb[0][0], wb[0][1], cc * 128, 128, N)
                psC, prC = proj_ws(wb[1][0], wb[1][1], cc * 128, 128, N)
                psX, prX = proj_ws(wb[2][0], wb[2][1], cc * 128, 128, N)
                st, rst = stg[2], r_stg[2]
                S.op("act", [prC], [r_acc[2]], lambda psC=psC: nc.scalar.copy(out=acc[2][:, 0:N], in_=psC[:, 0:N]))
                S.op("dve", [prX, r_acc[2]], [rst], lambda psX=psX: nc.vector.tensor_tensor(
                    out=st[:, 2:2 + N], in0=psX[:, 0:N], in1=acc[2][:, 0:N], op=ALU.mult))
                S.op("dve", [r_csc[l]], [rst], lambda c=c: nc.vector.tensor_copy(out=st[:, 0:2], in_=csc[:, l, c, :]))
                S.op("dve", [rst], [r_csc[l]], lambda c=c: nc.vector.tensor_copy(out=csc[:, l, c, :], in_=st[:, N:N + 2]))
                ac, rac = acc[c % 2], r_acc[c % 2]
                taps = [pp[:, l, PP_SHW + j * 8 + c:PP_SHW + j * 8 + c + 1] for j in range(3)]
                S.op("dve", [rst, r_pp], [rac], lambda ac=ac, taps=taps: nc.vector.tensor_scalar(
                    out=ac[:, 0:N], in0=st[:, 0:N], scalar1=taps[0], scalar2=None, op0=ALU.mult))
                for jj in (1, 2):
                    S.op("dve", [rst, rac, r_pp], [rac], lambda ac=ac, taps=taps, jj=jj: nc.vector.scalar_tensor_tensor(
                        out=ac[:, 0:N], in0=st[:, jj:jj + N], scalar=taps[jj], in1=ac[:, 0:N], op0=ALU.mult, op1=ALU.add))
                S.op("dve", [prB, rac], [r_ybT], lambda ac=ac, psB=psB, c=c: nc.vector.tensor_tensor(
                    out=ybT[:, c, 0:N], in0=psB[:, 0:N], in1=ac[:, 0:N], op=ALU.mult))
        if last_tile:
            S.dma("sp", d_out, [r_csc[l]], [], lambda: nc.sync.dma_start(out=psc_d[l], in_=csc[:, l, :, :]))
        ydT, r_ydT = G[4], r_G[4]
        S.dma("sp", d_rowb, [], [r_rowb], lambda: nc.sync.dma_start(out=rowb[:, :], in_=rowb_d[l, :, 1024:4096]))
        S.dma("pool", d_in, [], [r_gmraw], lambda: nc.gpsimd.dma_start(out=gmw_raw[:], in_=gmw_d[l].rearrange("g t s -> t g s")))
        for g in range(8):
            ps, pr = psum()
            psb = ps[:, 0:64].bitcast(BF16)
            S.op("pe", [r_gmraw, r_cst], [pr], lambda psb=psb, g=g: nc.tensor.transpose(psb[:, 0:128], gmw_raw[:, g, :], identb[:]))
            S.op("dve", [pr, r_cst], [r_gmw], lambda psb=psb, g=g: nc.vector.tensor_tensor(
                out=gmwT[:, g, :], in0=psb[:, 0:128], in1=Umat, op=ALU.mult))
        for bi in range(4):
            wv, wr = win(l, UV0 + bi * 256)
            for cc in range(2):
                c = bi * 2 + cc
                ps, pr = proj_ws(wv, wr, cc * 128, 128, N)
                S.op("act", [pr], [r_ydT], lambda ps=ps, c=c: nc.scalar.activation(out=ydT[:, c, 0:N], in_=ps[:, 0:N], func=AF.Gelu_apprx_tanh))
        wv_ = [win(l, UV0 + 1024 + i * 256) for i in range(4)]
        for j in range(NCH):
            gmlp_chunk(l, j * 128, 128, wv_, ydT, r_ydT, lng, lnb, bsrow, gmwT, None)
        mT, r_mT = G[5], r_G[5]
        brs = ((G[1], r_G[1]), (G[2], r_G[2]), (G[3], r_G[3]), (G[4], r_G[4]))
        macc = (acc[0], acc[1])
        r_macc = (r_acc[0], r_acc[1])
        for bi in range(4):
            for i in range(4):
                wg = win(l, GT0 + i * 1024 + bi * 256)
                wb = wload(w_br_d[l, i, :, bi * 256:bi * 256 + 256].rearrange("(k p) c -> p k c", p=128), 8, 256, key=("br", l, i, bi))
                for cc in range(2):
                    c = bi * 2 + cc
                    psg, prg = proj_ws(wg[0], wg[1], cc * 128, 128, N)
                    psp, prp = proj_ws(wb[0], wb[1], cc * 128, 128, N, rhs=brs[i][0], rres=brs[i][1])
                    S.op("act", [prg], [r_stg[2]], lambda psg=psg: nc.scalar.activation(out=stg[2][:, 0:N], in_=psg[:, 0:N], func=AF.Sigmoid))
                    if i == 0:
                        S.op("dve", [prp, r_stg[2]], [r_macc[cc]], lambda psp=psp, cc=cc: nc.vector.tensor_tensor(
                            out=macc[cc][:, 0:N], in0=psp[:, 0:N], in1=stg[2][:, 0:N], op=ALU.mult))
                    else:
                        S.op("dve", [prp, r_stg[2]], [r_acc[2]], lambda psp=psp: nc.vector.tensor_tensor(
                            out=acc[2][:, 0:N], in0=psp[:, 0:N], in1=stg[2][:, 0:N], op=ALU.mult))
                        if i < 3:
                            S.op("dve", [r_macc[cc], r_acc[2]], [r_macc[cc]], lambda cc=cc: nc.vector.tensor_tensor(
                                out=macc[cc][:, 0:N], in0=macc[cc][:, 0:N], in1=acc[2][:, 0:N], op=ALU.add))
                        else:
                            S.op("dve", [r_macc[cc], r_acc[2]], [r_mT], lambda c=c, cc=cc: nc.vector.tensor_tensor(
                                out=mT[:, c, 0:N], in0=macc[cc][:, 0:N], in1=acc[2][:, 0:N], op=ALU.add))
        for bi in range(4):
            wv, wr = wload(wcols(w_o_d, l, bi * 256, 256), 8, 256, key=("o", l, bi))
            for cc in range(2):
                c = bi * 2 + cc
                ps, pr = proj_ws(wv, wr, cc * 128, 128, N, rhs=mT, rres=r_mT)
                S.op("dve", [pr, r_mod, r_x], [r_x], lambda ps=ps, c=c: nc.vector.scalar_tensor_tensor(
                    out=xT[:, c, 0:N], in0=ps[:, 0:N], scalar=mod[:, l, 16 + c, 0:1], in1=xT[:, c, 0:N], op0=ALU.mult, op1=ALU.add))
        mod_norm(l, 1, N)
        gat = (G[1], G[2], G[3])
        r_gat = (r_G[1], r_G[2], r_G[3])
        for blk in range(11):
            wa = wload(wcols(w_up_d, l, blk * 256, 256), 8, 256, key=("ua", l, blk))
            wg_ = wload(wcols(w_up_d, l, DFF + blk * 256, 256), 8, 256, key=("ug", l, blk))
            for cc in range(2):
                i = blk * 2 + cc
                outs = []
                for which, (wv, wr) in enumerate((wa, wg_)):
                    ci = which * 22 + i
                    ps, pr = proj_ws(wv, wr, cc * 128, 128, N)
                    taps = [pp[:, l, PP_FW + j * 44 + ci:PP_FW + j * 44 + ci + 1] for j in range(3)]
                    ac, rac = conv_fm(128, ps, pr, N, cff[:, l, ci, :], r_cff[l], taps, pp[:, l, PP_FB + ci:PP_FB + ci + 1], [r_pp], which)
                    outs.append((ac, rac))
                S.op("act", [outs[0][1]], [r_acc[2]], lambda a=outs[0][0]: nc.scalar.activation(out=acc[2][:, 0:N], in_=a[:, 0:N], func=AF.Silu))
                S.op("dve", [r_acc[2], outs[1][1]], [r_gat[i // 8]], lambda g_=outs[1][0], i=i: nc.vector.tensor_tensor(
                    out=gat[i // 8][:, i % 8, 0:N], in0=acc[2][:, 0:N], in1=g_[:, 0:N], op=ALU.mult))
        if last_tile:
            S.dma("sp", d_out, [r_cff[l]], [], lambda: nc.sync.dma_start(out=pffn_d[l], in_=cff[:, l, :, :]))
        for c in range(8):
            wh = [wload(w_dn_d[l, hh * 1408:(hh + 1) * 1408, c * 128:(c + 1) * 128].rearrange("(k p) c -> p k c", p=128), 11, 128, key=("dn", l, c, hh)) for hh in range(2)]
            ps, pr = psum()

            def f(wh=wh, ps=ps):
                for k in range(22):
                    last = nc.tensor.matmul(ps[:, 0:N], lhsT=wh[k // 11][0][:, k % 11, :], rhs=gat[k // 8][:, k % 8, 0:N], start=(k == 0), stop=(k == 21))
                return last
            S.op("pe", [wh[0][1], wh[1][1]] + list(r_gat), [pr], f)
            S.op("dve", [pr, r_mod, r_x], [r_x], lambda ps=ps, c=c: nc.vector.scalar_tensor_tensor(
                out=xT[:, c, 0:N], in0=ps[:, 0:N], scalar=mod[:, l, 40 + c, 0:1], in1=xT[:, c, 0:N], op0=ALU.mult, op1=ALU.add))

    def ssd_chunk(l, t0, T, wz, wdt, xsT, r_xsT, yaT, r_yaT, normg, sl=None):
        sl = l if sl is None else sl
        dtb = rows[0:T, l, 0:16]
        Arow = rows[0:T, l, 16:32]
        Drow = rows[0:T, l, 32:48]
        xs_tok, r_xs = tC, r_tC
        for c in range(8):
            ps, pr = psum()
            psb = ps[:, 0:64].bitcast(BF16)
            S.op("pe", [r_xsT, r_cst], [pr], lambda c=c, psb=psb: nc.tensor.transpose(psb[0:T, 0:128], xsT[:, c, t0:t0 + T], identb[:]))
            yield
            copy(eng2(), xs_tok[0:T, c * 128:(c + 1) * 128], psb[0:T, 0:128], [pr], [r_xs])
            yield
        for g in range(4):
            ps, pr = psum()
            psb = ps[:, 0:64].bitcast(BF16)
            S.op("pe", [r_BCT, r_cst], [pr], lambda g=g, psb=psb: nc.tensor.transpose(psb[0:T, 0:64], BCT[:, g, t0:t0 + T], identb[0:64, 0:64]))
            yield
            copy(eng2(), Btok[0:T, g, :], psb[0:T, 0:64], [pr], [r_Btok])
            yield
        ps, pr = proj_as(wdt, 16, t0, T)
        dt = sm[0:T, 0:16]
        dtA = sm[0:T, 16:32]
        Acs = sm[0:T, 32:48]
        nAcs = sm[0:T, 48:64]
        eA = sm[0:T, 64:80]
        dec = sm[0:T, 80:96]
        wdec = sm[0:T, 96:112]
        S.op("dve", [pr, r_rows], [RV["dt"]], lambda: nc.vector.tensor_tensor(out=dt, in0=ps[0:T, 0:16], in1=dtb, op=ALU.add))
        yield
        S.op("act", [RV["dt"]], [RV["dt"]], lambda: nc.scalar.activation(out=dt, in_=dt, func=AF.Exp))
        yield
        S.op("act", [RV["dt"]], [RV["dt"]], lambda: nc.scalar.activation(out=dt, in_=dt, func=AF.Ln, bias=1.0))
        yield
        S.op("dve", [RV["dt"], r_rows], [RV["dtA"]], lambda: nc.vector.tensor_tensor(out=dtA, in0=dt, in1=Arow, op=ALU.mult))
        yield
        ps1, pr1 = psum()

        def f1():
            nc.tensor.matmul(ps1[0:T, 0:16], lhsT=Umat[0:T, 0:T], rhs=dtA, start=True, stop=True)
            return nc.tensor.matmul(ps1[0:max(T, 64), 16:32], lhsT=ones32[0:T, 0:max(T, 64)], rhs=dtA, start=True, stop=True)
        S.op("pe", [RV["dtA"], r_cst], [pr1], f1)
        yield
        S.op("dve", [pr1], [RV["Acs"]], lambda: nc.vector.tensor_copy(out=Acs, in_=ps1[0:T, 0:16]))
        yield
        S.op("dve", [pr1], [RV["nAcs"]], lambda: nc.vector.tensor_scalar(out=nAcs, in0=ps1[0:T, 0:16], scalar1=-1.0, scalar2=None, op0=ALU.mult))
        yield
        S.op("dve", [pr1, RV["Acs"]], [RV["dec"]], lambda: nc.vector.tensor_tensor(out=dec, in0=ps1[0:T, 16:32], in1=Acs, op=ALU.subtract))
        yield
        S.op("act", [pr1], [RV["eA"]], lambda: nc.scalar.activation(out=eA, in_=ps1[0:T, 0:16], func=AF.Exp))
        yield
        cdec = sm[0:64, 112:128]
        S.op("act", [pr1], [RV["cdec"]], lambda: nc.scalar.activation(out=cdec, in_=ps1[0:64, 16:32], func=AF.Exp))
        yield
        S.op("act", [RV["dec"]], [RV["dec"]], lambda: nc.scalar.activation(out=dec, in_=dec, func=AF.Exp))
        yield
        S.op("dve", [RV["dec"], RV["dt"]], [RV["wdec"]], lambda: nc.vector.tensor_tensor(out=wdec, in0=dec, in1=dt, op=ALU.mult))
        yield
        for hg in range(4):
            ps, pr = psum()

            def fe(ps=ps, hg=hg):
                for hh in range(4):
                    h_ = hg * 4 + hh
                    nc.tensor.matmul(ps[0:T, hh * 128:hh * 128 + T], lhsT=dtA[:, h_:h_ + 1].to_broadcast([T, T]), rhs=Umat[0:T, 0:T], start=True, stop=False)
                    last = nc.tensor.matmul(ps[0:T, hh * 128:hh * 128 + T], lhsT=ident[0:T, 0:T], rhs=NEGM[0:T, 0:T], start=False, stop=True)
                return last
            S.op("pe", [RV["dtA"], r_cst], [pr], fe)
            yield
            for hh in range(4):
                h_ = hg * 4 + hh
                S.op("act", [pr, RV["nAcs"]], [r_Eh], lambda ps=ps, hh=hh, h_=h_: nc.scalar.activation(
                    out=Eh[0:T, h_, 0:T], in_=ps[0:T, hh * 128:hh * 128 + T], func=AF.Exp, bias=nAcs[:, h_:h_ + 1], scale=1.0))
                yield
        ps, pr = psum()

        def fcb(ps=ps):
            for g in range(4):
                last = nc.tensor.matmul(ps[0:T, g * 128:g * 128 + T], lhsT=BCT[:, g, t0:t0 + T], rhs=BCT[:, 4 + g, t0:t0 + T], start=True, stop=True)
            return last
        S.op("pe", [r_BCT], [pr], fcb)
        yield
        copy("act", CBs[0:T, :, 0:T], ps[0:T, :].rearrange("p (g t) -> p g t", g=4)[:, :, 0:T], [pr], [r_CBs])
        yield
        for g in range(4):
            S.op("dve", [r_Eh, r_CBs], [r_Mh], lambda g=g: nc.vector.tensor_tensor(
                out=Mh[0:T, g * 4:g * 4 + 4, 0:T], in0=Eh[0:T, g * 4:g * 4 + 4, 0:T],
                in1=CBs[0:T, g:g + 1, 0:T].to_broadcast([T, 4, T]), op=ALU.mult))
            yield
        xdt, r_xdt = tD, r_tD
        Xdd, r_Xdd = tE, r_tE
        S.op("dve", [r_xs, RV["dt"]], [r_xdt], lambda: nc.vector.tensor_tensor(
            out=xdt[0:T, :].rearrange("p (h q) -> p h q", h=16), in0=xs_tok[0:T, :].rearrange("p (h q) -> p h q", h=16),
            in1=dt.unsqueeze(2).to_broadcast([T, 16, 64]), op=ALU.mult))
        yield
        S.op("dve", [r_xs, RV["wdec"]], [r_Xdd], lambda: nc.vector.tensor_tensor(
            out=Xdd[0:T, :].rearrange("p (h q) -> p h q", h=16), in0=xs_tok[0:T, :].rearrange("p (h q) -> p h q", h=16),
            in1=wdec.unsqueeze(2).to_broadcast([T, 16, 64]), op=ALU.mult))
        yield
        pso = [psum(), psum()]
        psd = [psum(), psum()]

        def foff():
            for g in range(4):
                last = nc.tensor.matmul(pso[g // 2][0][0:T, (g % 2) * 256:(g % 2) * 256 + 256], lhsT=BCT[:, 4 + g, t0:t0 + T],
                                        rhs=STb[:, g, :], start=True, stop=True)
            return last
        S.op("pe", [r_BCT, r_STb[l]], [pso[0][1], pso[1][1]], foff)
        yield

        def fdiag():
            for h_ in range(16):
                last = nc.tensor.matmul(psd[h_ // 8][0][0:T, (h_ % 8) * 64:(h_ % 8) * 64 + 64], lhsT=Mh[0:T, h_, 0:T],
                                        rhs=xdt[0:T, h_ * 64:(h_ + 1) * 64], start=True, stop=True)
            return last
        S.op("pe", [r_Mh, r_xdt], [psd[0][1], psd[1][1]], fdiag)
        yield
        y, r_y = tA, r_tA
        for hf in range(2):
            S.op("dve", [pso[hf][1], RV["eA"]], [r_y], lambda hf=hf: nc.vector.tensor_tensor(
                out=y[0:T, hf * 512:(hf + 1) * 512].rearrange("p (h q) -> p h q", h=8),
                in0=pso[hf][0][0:T, :].rearrange("p (h q) -> p h q", h=8),
                in1=eA[:, hf * 8:hf * 8 + 8].unsqueeze(2).to_broadcast([T, 8, 64]), op=ALU.mult))
            yield
            S.op("dve", [psd[hf][1], r_y], [r_y], lambda hf=hf: nc.vector.tensor_tensor(
                out=y[0:T, hf * 512:(hf + 1) * 512], in0=psd[hf][0][0:T, :], in1=y[0:T, hf * 512:(hf + 1) * 512], op=ALU.add))
            yield
        t2, r_t2 = tB, r_tB
        S.op("dve", [r_xs, r_rows], [r_t2], lambda: nc.vector.tensor_tensor(
            out=t2[0:T, :].rearrange("p (h q) -> p h q", h=16), in0=xs_tok[0:T, :].rearrange("p (h q) -> p h q", h=16),
            in1=Drow.unsqueeze(2).to_broadcast([T, 16, 64]), op=ALU.mult))
        yield
        S.op("dve", [r_t2, r_y], [r_y], lambda: nc.vector.tensor_tensor(out=y[0:T, :], in0=y[0:T, :], in1=t2[0:T, :], op=ALU.add))
        yield
        pss = [psum(), psum()]

        def fst():
            for g in range(4):
                last = nc.tensor.matmul(pss[g // 2][0][0:64, (g % 2) * 256:(g % 2) * 256 + 256], lhsT=Btok[0:T, g, :],
                                        rhs=Xdd[0:T, g * 256:(g + 1) * 256], start=True, stop=True)
            return last
        S.op("pe", [r_Btok, r_Xdd], [pss[0][1], pss[1][1]], fst)
        yield
        S.op("dve", [r_ST[sl], RV["cdec"], r_STb[l]], [r_ST[sl]], lambda: nc.vector.tensor_tensor(
            out=ST[:, sl, :, :].rearrange("p g (r q) -> p (g r) q", r=4), in0=ST[:, sl, :, :].rearrange("p g (r q) -> p (g r) q", r=4),
            in1=cdec.unsqueeze(2).to_broadcast([64, 16, 64]), op=ALU.mult))
        yield
        for hf in range(2):
            S.op("dve", [pss[hf][1], r_ST[sl]], [r_ST[sl]], lambda hf=hf: nc.vector.tensor_tensor(
                out=ST[:, sl, hf * 2:hf * 2 + 2, :], in0=ST[:, sl, hf * 2:hf * 2 + 2, :],
                in1=pss[hf][0][0:64, :].rearrange("p (g q) -> p g q", g=2), op=ALU.add))
            yield
        S.op("act", [r_ST[sl]], [r_STb[l]], lambda: nc.scalar.copy(out=STb[:, :, :], in_=ST[:, sl, :, :]))
        yield
        for hf in range(2):
            ps, pr = proj_as(wz[hf * 2:hf * 2 + 2], 512, t0, T)
            S.op("act", [pr], [r_t2], lambda ps=ps, hf=hf: nc.scalar.activation(out=t2[0:T, hf * 512:(hf + 1) * 512], in_=ps[0:T, :], func=AF.Silu))
            yield
        S.op("dve", [r_t2, r_y], [r_y], lambda: nc.vector.tensor_tensor(out=y[0:T, :], in0=y[0:T, :], in1=t2[0:T, :], op=ALU.mult))
        yield
        ssq = sm[0:T, 128:132]
        for g in range(4):
            S.op("act", [r_y], [r_t2, RV["ssq"]], lambda g=g: nc.scalar.activation(
                out=t2[0:T, g * 256:(g + 1) * 256], in_=y[0:T, g * 256:(g + 1) * 256], func=AF.Square, accum_out=ssq[:, g:g + 1]))
            yield
        rstd_of(ssq, T, ssq, [RV["ssq"], r_cst], [RV["ssq"]], 1.0 / 256)
        yield
        S.op("dve", [r_y, RV["ssq"]], [r_y], lambda: nc.vector.tensor_tensor(
            out=y[0:T, :].rearrange("p (g q) -> p g q", g=4), in0=y[0:T, :].rearrange("p (g q) -> p g q", g=4),
            in1=ssq.unsqueeze(2).to_broadcast([T, 4, 256]), op=ALU.mult))
        yield
        ya_tok, r_ya = tC, r_tC
        S.op("dve", [r_y, r_rowb], [r_ya], lambda: nc.vector.tensor_tensor(out=ya_tok[0:T, :], in0=y[0:T, :], in1=normg[0:T, :], op=ALU.mult))
        yield
        for c in range(8):
            ps, pr = psum()
            psb = ps[:, 0:64].bitcast(BF16)
            S.op("pe", [r_ya, r_cst], [pr], lambda c=c, psb=psb: nc.tensor.transpose(psb[:, 0:T], ya_tok[0:T, c * 128:(c + 1) * 128], identb[0:T, 0:T]))
            yield
            copy(eng2(), yaT[:, c, t0:t0 + T], psb[:, 0:T], [pr], [r_yaT])
            yield

    def attn_chunk(l, ti, j, qT, r_qT, ycT, r_ycT):
        T = 128
        t0 = j * 128
        first = (ti == 0 and j == 0)
        KC = 128 if first else 256
        boff = 128 if first else 0
        sink = rows[:, l, 48:64]
        scA = G[4][:, 0:4, :].bitcast(F32)
        PmA = G[4][:, 4:6, :]
        PTA = G[4][:, 6:8, :]
        o_ps = [(PS[6], r_PS[6]), (PS[7], r_PS[7])]
        rinv = sm[:, 136:152]
        for hk in range(4):
            pss_ = [psum(), psum()]

            def fs(hk=hk, pss_=pss_):
                for hh in range(4):
                    hd = hk * 4 + hh
                    pb = (hd % 2) * 64
                    qa = qT[pb:pb + 64, hd // 2, t0:t0 + T]
                    dst = pss_[hh % 2][0][:, (hh // 2) * 256:(hh // 2) * 256 + KC]
                    if first:
                        last = nc.tensor.matmul(dst, lhsT=qa, rhs=kT[pb:pb + 64, hk, t0:t0 + T], start=True, stop=True)
                    elif j == 0:
                        nc.tensor.matmul(dst[:, 0:128], lhsT=qa, rhs=kprev[pb:pb + 64, l, hk, :], start=True, stop=True)
                        last = nc.tensor.matmul(dst[:, 128:256], lhsT=qa, rhs=kT[pb:pb + 64, hk, t0:t0 + T], start=True, stop=True)
                    else:
                        last = nc.tensor.matmul(dst, lhsT=qa, rhs=kT[pb:pb + 64, hk, t0 - 128:t0 + T], start=True, stop=True)
                return last
            S.op("pe", [r_qT, r_kT, r_kprev[l]], [pss_[0][1], pss_[1][1]], fs)
            yield
            sc = scA
            for hh in range(4):
                S.op("dve", [pss_[hh % 2][1], r_cst], [r_aS], lambda hh=hh, hk=hk, pss_=pss_: nc.vector.scalar_tensor_tensor(
                    out=sc[:, hh, 0:KC], in0=distm[:, boff:boff + KC], scalar=-SLOPES[hk * 4 + hh],
                    in1=pss_[hh % 2][0][:, (hh // 2) * 256:(hh // 2) * 256 + KC], op0=ALU.mult, op1=ALU.add))
                yield
            if DBG_ATT <= 2:
                continue
            mx = sm[:, 152:156]
            nmx = sm[:, 156:160]
            esk = sm[:, 160:164]
            rsum = sm[:, 164:168]
            S.op("dve", [r_aS], [r_sm], lambda: nc.vector.tensor_reduce(out=mx, in_=sc[:, :, 0:KC], axis=AX.X, op=ALU.max))
            yield
            S.op("dve", [r_sm, r_rows], [r_sm], lambda hk=hk: nc.vector.tensor_tensor(out=mx, in0=mx, in1=sink[:, hk * 4:hk * 4 + 4], op=ALU.max))
            yield
            S.op("dve", [r_sm], [r_sm], lambda: nc.vector.tensor_scalar(out=nmx, in0=mx, scalar1=-1.0, scalar2=None, op0=ALU.mult))
            yield
            S.op("dve", [r_sm, r_rows], [r_sm], lambda hk=hk: nc.vector.tensor_tensor(out=esk, in0=sink[:, hk * 4:hk * 4 + 4], in1=mx, op=ALU.subtract))
            yield
            S.op("act", [r_sm], [r_sm], lambda: nc.scalar.activation(out=esk, in_=esk, func=AF.Exp))
            yield
            Pm = PmA.rearrange("p c (h k) -> p (c h) k", h=2)
            for hh in range(4):
                S.op("act", [r_aS, r_sm], [r_aP, r_sm], lambda hh=hh: nc.scalar.activation(
                    out=Pm[:, hh, 0:KC], in_=sc[:, hh, 0:KC], func=AF.Exp, bias=nmx[:, hh:hh + 1], scale=1.0, accum_out=rsum[:, hh:hh + 1]))
                yield
            S.op("dve", [r_sm], [r_sm], lambda: nc.vector.tensor_tensor(out=rsum, in0=rsum, in1=esk, op=ALU.add))
            yield
            S.op("dve", [r_sm], [r_sm], lambda hk=hk: nc.vector.reciprocal(out=rinv[:, hk * 4:hk * 4 + 4], in_=rsum))
            yield
            if DBG_ATT <= 3:
                continue
            PT = PTA.rearrange("p c (h k) -> p (c h) k", h=2)
            nkb = KC // 128
            for hh in range(4):
                ps, pr = psum()
                psb = ps[:, 0:128].bitcast(BF16)

                def ft(hh=hh, psb=psb):
                    for kb in range(nkb):
                        last = nc.tensor.transpose(psb[:, kb * 128:(kb + 1) * 128], Pm[:, hh, kb * 128:(kb + 1) * 128], identb[:])
                    return last
                S.op("pe", [r_aP, r_cst], [pr], ft)
                yield
                copy(eng2(), PT[:, hh, 0:KC], psb[:, 0:KC], [pr], [r_aT])
                yield

            def fo(hk=hk):
                for hh in range(4):
                    hd = hk * 4 + hh
                    dst = o_ps[hd // 8][0][:, (hd % 8) * 64:(hd % 8) * 64 + 64]
                    if first:
                        last = nc.tensor.matmul(dst, lhsT=PT[:, hh, 0:128], rhs=vtok[:, j, hk * 64:hk * 64 + 64], start=True, stop=True)
                    else:
                        vp = vprev[:, l, hk * 64:hk * 64 + 64] if j == 0 else vtok[:, j - 1, hk * 64:hk * 64 + 64]
                        nc.tensor.matmul(dst, lhsT=PT[:, hh, 0:128], rhs=vp, start=True, stop=False)
                        last = nc.tensor.matmul(dst, lhsT=PT[:, hh, 128:256], rhs=vtok[:, j, hk * 64:hk * 64 + 64], start=False, stop=True)
                return last
            S.op("pe", [r_aT, r_vtok, r_vprev[l]], [o_ps[hk // 2][1]], fo)
            yield
        yc_tok = PmA.rearrange("p c t -> p (c t)")
        for hf in range(2):
            S.op("dve", [o_ps[hf][1], r_sm], [r_aP], lambda hf=hf: nc.vector.tensor_tensor(
                out=yc_tok[:, hf * 512:(hf + 1) * 512].rearrange("p (h q) -> p h q", h=8),
                in0=o_ps[hf][0][:, :].rearrange("p (h q) -> p h q", h=8),
                in1=rinv[:, hf * 8:hf * 8 + 8].unsqueeze(2).to_broadcast([128, 8, 64]), op=ALU.mult))
            yield
        for c in range(8):
            ps, pr = psum()
            psb = ps[:, 0:64].bitcast(BF16)
            S.op("pe", [r_aP, r_cst], [pr], lambda c=c, psb=psb: nc.tensor.transpose(psb[:, 0:T], yc_tok[:, c * 128:(c + 1) * 128], identb[:]))
            yield
            copy(eng2(), ycT[:, c, t0:t0 + T], psb[:, 0:T], [pr], [r_ycT])
            yield

    def gmlp_chunk(l, t0, T, wv_, ydT, r_ydT, lng, lnb, bsrow, wT, vout, bs3=False):
        v, r_v = tA, r_tA
        ssum = sm[0:T, 168:170]
        mean = sm[0:T, 170:171]
        ssq = sm[0:T, 171:173]
        rstd = sm[0:T, 173:174]
        for hf in range(2):
            ps, pr = proj_as(wv_[hf * 2:hf * 2 + 2], 512, t0, T)
            S.op("act", [pr], [r_v, r_sm], lambda ps=ps, hf=hf: nc.scalar.activation(
                out=v[0:T, hf * 512:(hf + 1) * 512], in_=ps[0:T, :], func=AF.Gelu_apprx_tanh, accum_out=ssum[:, hf:hf + 1]))
        S.op("dve", [r_sm], [r_sm], lambda: nc.vector.tensor_tensor(out=mean, in0=ssum[:, 0:1], in1=ssum[:, 1:2], op=ALU.add))
        S.op("dve", [r_sm], [r_sm], lambda: nc.vector.tensor_scalar(out=mean, in0=mean, scalar1=1.0 / 1024, scalar2=None, op0=ALU.mult))
        S.op("dve", [r_v, r_sm], [r_v], lambda: nc.vector.tensor_scalar(out=v[0:T, :], in0=v[0:T, :], scalar1=mean, scalar2=None, op0=ALU.subtract))
        for hf in range(2):
            S.op("act", [r_v], [r_tB, r_sm], lambda hf=hf: nc.scalar.activation(
                out=tB[0:T, hf * 512:(hf + 1) * 512], in_=v[0:T, hf * 512:(hf + 1) * 512], func=AF.Square, accum_out=ssq[:, hf:hf + 1]))
        S.op("dve", [r_sm], [r_sm], lambda: nc.vector.tensor_tensor(out=rstd, in0=ssq[:, 0:1], in1=ssq[:, 1:2], op=ALU.add))
        rstd_of(rstd, T, rstd, [r_sm, r_cst], [r_sm], 1.0 / 1024)
        S.op("dve", [r_v, r_sm, r_rowb], [r_v], lambda: nc.vector.scalar_tensor_tensor(
            out=v[0:T, :], in0=v[0:T, :], scalar=rstd, in1=lng[0:T, :], op0=ALU.mult, op1=ALU.mult))
        if vout is not None:
            S.op("dve", [r_v, r_rowb], [r_tB], lambda: nc.vector.tensor_tensor(out=tB[0:T, :], in0=v[0:T, :], in1=lnb[0:T, :], op=ALU.add))
            vout(tB, r_tB)
        vn, r_vn = tC, r_tC
        S.op("dve", [r_v, r_rowb], [r_vn], lambda: nc.vector.tensor_tensor(out=vn[0:T, :], in0=v[0:T, :], in1=lnb[0:T, :], op=ALU.add))
        for hf in range(2):
            ps, pr = psum()

            def fm(ps=ps, hf=hf):
                for gg in range(4):
                    g = hf * 4 + gg
                    last = nc.tensor.matmul(ps[:, gg * 128:gg * 128 + T], lhsT=vn[0:T, g * 128:(g + 1) * 128], rhs=wT[0:T, g, 0:T], start=True, stop=True)
                return last
            S.op("pe", [r_vn, r_gmw, r_wbd], [pr], fm)
            mix = tB[:, 0:512].rearrange("p (g t) -> p g t", g=4)
            bsv = (bsrow[:, hf * 4:hf * 4 + 4, 0:T] if bs3 else
                   bsrow[:, hf * 512:(hf + 1) * 512].rearrange("p (g t) -> p g t", g=4)[:, :, 0:T])
            S.op("dve", [pr, r_rowb, r_bsp], [r_tB], lambda ps=ps, hf=hf, bsv=bsv: nc.vector.tensor_tensor(
                out=mix[:, :, 0:T], in0=ps[:, :].rearrange("p (g t) -> p g t", g=4)[:, :, 0:T],
                in1=bsv, op=ALU.add))
            S.op("dve", [r_tB, r_ydT], [r_ydT], lambda hf=hf: nc.vector.tensor_tensor(
                out=ydT[:, hf * 4:hf * 4 + 4, t0:t0 + T], in0=ydT[:, hf * 4:hf * 4 + 4, t0:t0 + T], in1=mix[:, :, 0:T], op=ALU.mult))

    if SB:
        NS = 4 * SB
        mod_s = sb("mod_s", [128, 1, 48, SB]); r_mods = Res("mod_s")
        gm_s = sb("gm_s", [128, 1, 2, 8, SB]); r_gms = Res("gm_s")
        csT = sb("csT_t", [128, 8, SB]); r_csT = Res("csT")
        csb = sb("csb", [128, 8, SB], BF16)
        sxbc = sb("sxbc", [128, 8, SB, 3]); r_sxbc = Res("sxbc")
        sbc = sb("sbc", [64, 8, SB, 3]); r_sbc = Res("sbc")
        ssc = sb("ssc", [128, 8, SB, 2]); r_ssc = Res("ssc")
        sffn = sb("sffn", [128, 44, SB, 2]); r_sffn = Res("sffn")
        KKb = sb("KKb", [128, 4, 160], BF16); r_KKb = Res("KKb")
        Vb = sb("Vb", [128, 256], BF16); r_Vb = Res("Vb")
        vnew = sb("vnew", [4, 256], BF16); r_vnew = Res("vnew")
        PTn = sb("PTn", [4, 4, 4], BF16); r_PTn = Res("PTn")
        d_cs = d_kv = d_st = None

    _once = {}

    def sb_once(name, shape, dt=F32):
        if name not in _once:
            _once[name] = sb(name, shape, dt)
        return _once[name]
    r_w4, r_wbd, r_bsp = Res("w4"), Res("wbd"), Res("bsp")

    def conv_s(P, ps, pr, taps, bias_ap, preads, st_ap, r_st, si):
        K = len(taps)
        CW = K - 1
        W = CW + 4
        st3 = stg[si][0:P, 0:SB * W].rearrange("p (b w) -> p b w", b=SB)
        ac3 = acc[si][0:P, 0:NS].rearrange("p (b w) -> p b w", b=SB)
        rst, rac = r_stg[si], r_acc[si]
        S.op("act", [pr], [rst], lambda: nc.scalar.copy(out=st3[:, :, CW:W], in_=ps[0:P, 0:NS].rearrange("p (b w) -> p b w", b=SB)))
        S.op("dve", [r_st], [rst], lambda: nc.vector.tensor_copy(out=st3[:, :, 0:CW], in_=st_ap))
        S.op("dve", [rst], [r_st], lambda: nc.vector.tensor_copy(out=st_ap, in_=st3[:, :, 4:W]))
        if bias_ap is not None:
            S.op("dve", [rst] + preads, [rac], lambda: nc.vector.tensor_scalar(
                out=ac3, in0=st3[:, :, 0:4], scalar1=taps[0], scalar2=bias_ap, op0=ALU.mult, op1=ALU.add))
        else:
            S.op("dve", [rst] + preads, [rac], lambda: nc.vector.tensor_scalar(
                out=ac3, in0=st3[:, :, 0:4], scalar1=taps[0], scalar2=None, op0=ALU.mult))
        for j in range(1, K):
            S.op("dve", [rst, rac] + preads, [rac], lambda j=j: nc.vector.scalar_tensor_tensor(
                out=ac3, in0=st3[:, :, j:j + 4], scalar=taps[j], in1=ac3, op0=ALU.mult, op1=ALU.add))
        return acc[si], rac

    def mod_norm_s(which):
        N = NS
        sh_i = 0 if which == 0 else 3
        ps, pr = psum()
        sq = G[5]
        S.op("act", [r_x], [r_G[5]], lambda: nc.scalar.activation(out=sq[:, :, 0:N], in_=xT[:, :, 0:N], func=AF.Square))

        def f():
            for k in range(8):
                last = nc.tensor.matmul(ps[:, 0:N], lhsT=onesb[:], rhs=sq[:, k, 0:N], start=(k == 0), stop=(k == 7))
            return last
        S.op("pe", [r_G[5], r_cst], [pr], f)
        rs = acc[2]
        rstd_of(ps[:, 0:N], 128, rs[:, 0:N], [pr, r_cst], [r_acc[2]], 1.0 / D)
        for c in range(8):
            a_ = acc[c % 2]
            a3 = a_[:, 0:N].rearrange("p (b w) -> p b w", b=SB)
            S.op("dve", [r_x, r_acc[2]], [r_acc[c % 2]], lambda c=c, a_=a_: nc.vector.tensor_tensor(
                out=a_[:, 0:N], in0=xT[:, c, 0:N], in1=rs[:, 0:N], op=ALU.mult))
            S.op("dve", [r_gms, r_acc[c % 2]], [r_acc[c % 2]], lambda c=c, a3=a3: nc.vector.tensor_tensor(
                out=a3, in0=a3, in1=gm_s[:, 0, which, c, :].unsqueeze(2).to_broadcast([128, SB, 4]), op=ALU.mult))
            S.op("dve", [r_mods, r_acc[c % 2]], [r_h], lambda c=c, a3=a3: nc.vector.tensor_tensor(
                out=hB[:, c, 0:N].rearrange("p (b w) -> p b w", b=SB), in0=a3,
                in1=mod_s[:, 0, sh_i * 8 + c, :].unsqueeze(2).to_broadcast([128, SB, 4]), op=ALU.add))

    def resid_s(ps, pr, c, gi):
        N = NS
        S.op("dve", [pr, r_mods], [r_acc[2]], lambda: nc.vector.tensor_tensor(
            out=acc[2][:, 0:N].rearrange("p (b w) -> p b w", b=SB), in0=ps[:, 0:N].rearrange("p (b w) -> p b w", b=SB),
            in1=mod_s[:, 0, gi * 8 + c, :].unsqueeze(2).to_broadcast([128, SB, 4]), op=ALU.mult))
        S.op("dve", [r_acc[2], r_x], [r_x], lambda: nc.vector.tensor_tensor(
            out=xT[:, c, 0:N], in0=xT[:, c, 0:N], in1=acc[2][:, 0:N], op=ALU.add))

    def attn_s(l, b, qT, r_qT, ycT, r_ycT, wk, wvv):
        T = 4
        t0 = 4 * b
        KC = 132
        sink = rows[0:T, l, 48:64]
        scA = G[4][:, 0:4, :].bitcast(F32)
        PmA = G[4][:, 4:6, :]
        PTA = G[4][:, 6:8, :]
        ktok, r_ktok = acc[2], r_acc[2]
        o_ps = [(PS[6], r_PS[6]), (PS[7], r_PS[7])]
        rinv = sm[0:T, 136:152]
        S.dma("pool", d_kv, [], [r_KKb], lambda: nc.gpsimd.dma_start(out=KKb[:, :, 0:128], in_=ckT_d[l, b]))
        yield
        S.dma("pool", d_kv, [], [r_Vb], lambda: nc.gpsimd.dma_start(out=Vb[:, :], in_=cv_d[l, b]))
        yield
        S.op("dve", [r_kT], [r_KKb], lambda: nc.vector.tensor_copy(out=KKb[:, :, 128:132], in_=kT[:, :, t0:t0 + T]))
        yield
        ps, pr = proj_as([wk, wvv], 512, t0, T)
        S.op("act", [pr], [r_ktok], lambda: nc.scalar.copy(out=ktok[0:T, 0:512], in_=ps[0:T, :]))
        yield
        S.op("dve", [r_ktok], [r_vnew], lambda: nc.vector.tensor_copy(out=vnew[:, :], in_=ktok[0:T, 256:512]))
        yield
        S.dma("sp", d_out, [r_ktok], [], lambda: nc.sync.dma_start(out=sk_o[l, b, 124:128, :], in_=ktok[0:T, 0:256]))
        yield
        S.dma("sp", d_out, [r_ktok], [], lambda: nc.sync.dma_start(out=sv_o[l, b, 124:128, :], in_=ktok[0:T, 256:512]))
        yield
        for hk in range(4):
            pss_ = [psum(), psum()]

            def fs(hk=hk, pss_=pss_):
                for hh in range(4):
                    hd = hk * 4 + hh
                    pb = (hd % 2) * 64
                    last = nc.tensor.matmul(pss_[hh % 2][0][0:T, (hh // 2) * 256:(hh // 2) * 256 + KC],
                                            lhsT=qT[pb:pb + 64, hd // 2, t0:t0 + T], rhs=KKb[pb:pb + 64, hk, 0:132], start=True, stop=True)
                return last
            S.op("pe", [r_qT, r_KKb], [pss_[0][1], pss_[1][1]], fs)
            yield
            sc = scA[0:T]
            for hh in range(4):
                S.op("dve", [pss_[hh % 2][1], r_cst], [r_aS], lambda hh=hh, hk=hk, pss_=pss_: nc.vector.scalar_tensor_tensor(
                    out=sc[:, hh, 0:KC], in0=dists[0:T, :], scalar=-SLOPES[hk * 4 + hh],
                    in1=pss_[hh % 2][0][0:T, (hh // 2) * 256:(hh // 2) * 256 + KC], op0=ALU.mult, op1=ALU.add))
                yield
            mx = sm[0:T, 152:156]
            nmx = sm[0:T, 156:160]
            esk = sm[0:T, 160:164]
            rsum = sm[0:T, 164:168]
            S.op("dve", [r_aS], [r_sm], lambda: nc.vector.tensor_reduce(out=mx, in_=sc[:, :, 0:KC], axis=AX.X, op=ALU.max))
            yield
            S.op("dve", [r_sm, r_rows], [r_sm], lambda hk=hk: nc.vector.tensor_tensor(out=mx, in0=mx, in1=sink[:, hk * 4:hk * 4 + 4], op=ALU.max))
            yield
            S.op("dve", [r_sm], [r_sm], lambda: nc.vector.tensor_scalar(out=nmx, in0=mx, scalar1=-1.0, scalar2=None, op0=ALU.mult))
            yield
            S.op("dve", [r_sm, r_rows], [r_sm], lambda hk=hk: nc.vector.tensor_tensor(out=esk, in0=sink[:, hk * 4:hk * 4 + 4], in1=mx, op=ALU.subtract))
            yield
            S.op("act", [r_sm], [r_sm], lambda: nc.scalar.activation(out=esk, in_=esk, func=AF.Exp))
            yield
            Pm = PmA[0:T].rearrange("p c (h k) -> p (c h) k", h=2)
            for hh in range(4):
                S.op("act", [r_aS, r_sm], [r_aP, r_sm], lambda hh=hh: nc.scalar.activation(
                    out=Pm[:, hh, 0:KC], in_=sc[:, hh, 0:KC], func=AF.Exp, bias=nmx[:, hh:hh + 1], scale=1.0, accum_out=rsum[:, hh:hh + 1]))
                yield
            S.op("dve", [r_sm], [r_sm], lambda: nc.vector.tensor_tensor(out=rsum, in0=rsum, in1=esk, op=ALU.add))
            yield
            S.op("dve", [r_sm], [r_sm], lambda hk=hk: nc.vector.reciprocal(out=rinv[:, hk * 4:hk * 4 + 4], in_=rsum))
            yield
            PT = PTA.rearrange("p c (h k) -> p (c h) k", h=2)
            ps, pr = psum()
            psb = ps[:, 0:64].bitcast(BF16)

            def ft(psb=psb):
                for hh in range(4):
                    nc.tensor.transpose(psb[:, hh * 8:hh * 8 + 4], Pm[:, hh, 0:128], identb[0:T, 0:T])
                    last = nc.tensor.transpose(psb[0:T, hh * 8 + 4:hh * 8 + 8], Pm[:, hh, 128:132], identb[0:T, 0:T])
                return last
            S.op("pe", [r_aP, r_cst], [pr], ft)
            yield
            pv = psb[:, 0:32].rearrange("p (h k) -> p h k", h=4)
            copy("dve", PT[:, :, 0:4], pv[:, :, 0:4], [pr], [r_aT])
            yield
            copy("dve", PTn[:, :, :], pv[0:T, :, 4:8], [pr], [r_PTn])
            yield

            def fo(hk=hk):
                for hh in range(4):
                    hd = hk * 4 + hh
                    dst = o_ps[hd // 8][0][0:T, (hd % 8) * 64:(hd % 8) * 64 + 64]
                    nc.tensor.matmul(dst, lhsT=PT[:, hh, 0:4], rhs=Vb[:, hk * 64:hk * 64 + 64], start=True, stop=False)
                    last = nc.tensor.matmul(dst, lhsT=PTn[:, hh, :], rhs=vnew[:, hk * 64:hk * 64 + 64], start=False, stop=True)
                return last
            S.op("pe", [r_aT, r_PTn, r_Vb, r_vnew], [o_ps[hk // 2][1]], fo)
            yield
        yc_tok = PmA.rearrange("p c t -> p (c t)")
        for hf in range(2):
            S.op("dve", [o_ps[hf][1], r_sm], [r_aP], lambda hf=hf: nc.vector.tensor_tensor(
                out=yc_tok[0:T, hf * 512:(hf + 1) * 512].rearrange("p (h q) -> p h q", h=8),
                in0=o_ps[hf][0][0:T, :].rearrange("p (h q) -> p h q", h=8),
                in1=rinv[:, hf * 8:hf * 8 + 8].unsqueeze(2).to_broadcast([T, 8, 64]), op=ALU.mult))
            yield
        for c in range(8):
            ps, pr = psum()
            psb = ps[:, 0:64].bitcast(BF16)
            S.op("pe", [r_aP, r_cst], [pr], lambda c=c, psb=psb: nc.tensor.transpose(psb[:, 0:T], yc_tok[0:T, c * 128:(c + 1) * 128], identb[0:T, 0:T]))
            yield
            copy(eng2(), ycT[:, c, t0:t0 + T], psb[:, 0:T], [pr], [r_ycT])
            yield

    def sample_tile_layer(l):
        N = NS
        S.dma("sp", d_rowb, [], [r_rowb], lambda: nc.sync.dma_start(out=rowb[:, 0:1024], in_=rowb_d[l, :, 0:1024]))
        normg = rowb[:, 0:1024]
        lng = rowb[:, 0:1024]
        lnb = rowb[:, 1024:2048]
        bsrow = rowb[:, 2048:3072]
        S.dma("sp", d_cs, [], [r_sxbc], lambda: nc.sync.dma_start(out=sxbc[:], in_=sxbc_d[l]))
        S.dma("sp", d_cs, [], [r_sbc], lambda: nc.sync.dma_start(out=sbc[:], in_=sbc_d[l]))
        S.dma("sp", d_cs, [], [r_ssc], lambda: nc.sync.dma_start(out=ssc[:], in_=ssc_d[l]))
        S.dma("sp", d_cs, [], [r_sffn], lambda: nc.sync.dma_start(out=sffn[:], in_=sffn_d[l]))
        S.dma("sp", d_out, [], [], lambda: nc.sync.dma_start(out=sk_o[l, :, 0:124, :], in_=ck_d[l, :, 4:128, :]))
        S.dma("sp", d_out, [], [], lambda: nc.sync.dma_start(out=sv_o[l, :, 0:124, :], in_=cv_d[l, :, 4:128, :]))
        ada_layer(l, csb, r_csT, SB, mod_s, r_mods, gm_s, r_gms, 0)
        mod_norm_s(0)
        xsT, r_xsT = G[3], r_G[3]
        for bi in range(4):
            wv, wr = win(l, XBC0 + bi * 256)
            for cc in range(2):
                c = bi * 2 + cc
                ps, pr = proj_ws(wv, wr, cc * 128, 128, N)
                taps = [pp[:, l, PP_SCW + j * 8 + c:PP_SCW + j * 8 + c + 1] for j in range(4)]
                ac, rac = conv_s(128, ps, pr, taps, pp[:, l, PP_SCB + c:PP_SCB + c + 1], [r_pp], sxbc[:, c, :, :], r_sxbc, c % 2)
                S.op("act", [rac], [r_xsT], lambda ac=ac, c=c: nc.scalar.activation(out=xsT[:, c, 0:N], in_=ac[:, 0:N], func=AF.Silu))
        for bi in range(2):
            wv, wr = win(l, XBC0 + 1024 + bi * 256)
            for ee in range(4):
                e = bi * 4 + ee
                ps, pr = proj_ws(wv, wr, ee * 64, 64, N)
                taps = [pp64[:, l, j * 8 + e:j * 8 + e + 1] for j in range(4)]
                ac, rac = conv_s(64, ps, pr, taps, pp64[:, l, 32 + e:33 + e], [r_pp64], sbc[:, e, :, :], r_sbc, e % 2)
                S.op("act", [rac], [r_BCT], lambda ac=ac, e=e: nc.scalar.activation(out=BCT[:, e, 0:N], in_=ac[0:64, 0:N], func=AF.Silu))
        S.dma("sp", d_out, [r_sxbc], [], lambda: nc.sync.dma_start(out=sxbc_o[l], in_=sxbc[:]))
        S.dma("sp", d_out, [r_sbc], [], lambda: nc.sync.dma_start(out=sbc_o[l], in_=sbc[:]))
        qT, r_qT = G[5], r_G[5]
        for bi in range(4):
            wv, wr = win(l, Q0 + bi * 256)
            for cc in range(2):
                c = bi * 2 + cc
                ps, pr = proj_ws(wv, wr, cc * 128, 128, N)
                S.op("act", [pr], [r_qT], lambda ps=ps, c=c: nc.scalar.activation(out=qT[:, c, 0:N], in_=ps[:, 0:N], func=AF.Identity, scale=0.125))
        wk = win(l, K0)
        wvv = win(l, V0)
        for hk in range(4):
            ps, pr = proj_ws(wk[0], wk[1], hk * 64, 64, N)
            ktmp, r_ktmp = tC, r_tC
            copy("act", ktmp[0:64, 0:N], ps[0:64, 0:N], [pr], [r_ktmp])
            ps2, pr2 = psum()
            S.op("pe", [r_ktmp, r_cst], [pr2], lambda ps2=ps2: nc.tensor.matmul(ps2[:, 0:N], lhsT=dupI[:, :], rhs=ktmp[0:64, 0:N], start=True, stop=True))
            copy("dve", kT[:, hk, 0:N], ps2[:, 0:N], [pr2], [r_kT])
        ycT, r_ycT = G[3], r_G[3]
        wz = [win(l, i * 256) for i in range(4)]
        wdt = [win(l, DT0, 256)]
        yaT, r_yaT = G[1], r_G[1]

        def ssd_b(b):
            sl = b % L
            S.dma("sp", d_st, [], [r_ST[sl]], lambda b=b: nc.sync.dma_start(out=ST[:, sl, :, :], in_=sssm_d[l, b]))
            S.op("act", [r_ST[sl]], [r_STb1], lambda: nc.scalar.copy(out=STb[:, :, :], in_=ST[:, sl, :, :]))
            yield
            for _ in ssd_chunk(l, 4 * b, 4, wz, wdt, xsT, r_xsT, yaT, r_yaT, normg, sl=sl):
                yield
            S.dma("sp", d_out, [r_ST[sl]], [], lambda b=b: nc.sync.dma_start(out=sssm_o[l, b], in_=ST[:, sl, :, :]))
        for b in range(SB):
            interleave(ssd_b(b), attn_s(l, b, qT, r_qT, ycT, r_ycT, wk, wvv))
        ybT, r_ybT = G[2], r_G[2]
        for bi in range(4):
            wb = [win(l, BCX0 + part * 1024 + bi * 256) for part in range(3)]
            for cc in range(2):
                c = bi * 2 + cc
                psB, prB = proj_ws(wb[0][0], wb[0][1], cc * 128, 128, N)
                psC, prC = proj_ws(wb[1][0], wb[1][1], cc * 128, 128, N)
                psX, prX = proj_ws(wb[2][0], wb[2][1], cc * 128, 128, N)
                st3 = stg[2][:, 0:SB * 6].rearrange("p (b w) -> p b w", b=SB)
                rst = r_stg[2]
                S.op("act", [prC], [r_acc[2]], lambda psC=psC: nc.scalar.copy(out=acc[2][:, 0:N], in_=psC[:, 0:N]))
                S.op("dve", [prX, r_acc[2]], [rst], lambda psX=psX, st3=st3: nc.vector.tensor_tensor(
                    out=st3[:, :, 2:6], in0=psX[:, 0:N].rearrange("p (b w) -> p b w", b=SB),
                    in1=acc[2][:, 0:N].rearrange("p (b w) -> p b w", b=SB), op=ALU.mult))
                S.op("dve", [r_ssc], [rst], lambda c=c, st3=st3: nc.vector.tensor_copy(out=st3[:, :, 0:2], in_=ssc[:, c, :, :]))
                S.op("dve", [rst], [r_ssc], lambda c=c, st3=st3: nc.vector.tensor_copy(out=ssc[:, c, :, :], in_=st3[:, :, 4:6]))
                ac, rac = acc[c % 2], r_acc[c % 2]
                ac3 = ac[:, 0:N].rearrange("p (b w) -> p b w", b=SB)
                taps = [pp[:, l, PP_SHW + j * 8 + c:PP_SHW + j * 8 + c + 1] for j in range(3)]
                S.op("dve", [rst, r_pp], [rac], lambda ac3=ac3, taps=taps, st3=st3: nc.vector.tensor_scalar(
                    out=ac3, in0=st3[:, :, 0:4], scalar1=taps[0], scalar2=None, op0=ALU.mult))
                for jj in (1, 2):
                    S.op("dve", [rst, rac, r_pp], [rac], lambda ac3=ac3, taps=taps, jj=jj, st3=st3: nc.vector.scalar_tensor_tensor(
                        out=ac3, in0=st3[:, :, jj:jj + 4], scalar=taps[jj], in1=ac3, op0=ALU.mult, op1=ALU.add))
                S.op("dve", [prB, rac], [r_ybT], lambda ac=ac, psB=psB, c=c: nc.vector.tensor_tensor(
                    out=ybT[:, c, 0:N], in0=psB[:, 0:N], in1=ac[:, 0:N], op=ALU.mult))
        S.dma("sp", d_out, [r_ssc], [], lambda: nc.sync.dma_start(out=ssc_o[l], in_=ssc[:]))
        ydT, r_ydT = G[4], r_G[4]
        S.dma("sp", d_rowb, [], [r_rowb], lambda: nc.sync.dma_start(out=rowb[:, :], in_=rowb_d[l, :, 1024:4096]))
        S.dma("pool", d_in, [], [r_gmraw], lambda: nc.gpsimd.dma_start(out=gmw_raw[:], in_=gmw_d[l].rearrange("g t s -> t g s")))
        for g in range(8):
            ps, pr = psum()
            psb = ps[:, 0:64].bitcast(BF16)
            S.op("pe", [r_gmraw, r_cst], [pr], lambda psb=psb, g=g: nc.tensor.transpose(psb[:, 0:128], gmw_raw[:, g, :], identb[:]))
            S.op("dve", [pr, r_cst], [r_gmw], lambda psb=psb, g=g: nc.vector.tensor_tensor(
                out=gmwT[:, g, :], in0=psb[:, 0:128], in1=Umat, op=ALU.mult))
        for bi in range(4):
            wv, wr = win(l, UV0 + bi * 256)
            for cc in range(2):
                c = bi * 2 + cc
                ps, pr = proj_ws(wv, wr, cc * 128, 128, N)
                S.op("act", [pr], [r_ydT], lambda ps=ps, c=c: nc.scalar.activation(out=ydT[:, c, 0:N], in_=ps[:, 0:N], func=AF.Gelu_apprx_tanh))
        wv_ = [win(l, UV0 + 1024 + i * 256) for i in range(4)]
        w4 = sb_once("w4f", [4, 8, 4])
        S.op("dve", [r_gmw], [r_w4], lambda: nc.vector.tensor_copy(out=w4[:, :, :], in_=gmwT[0:4, :, 0:4]))
        psw, prw = psum()
        S.op("pe", [r_w4, r_cst], [prw], lambda: nc.tensor.matmul(
            psw[0:NS, 0:8 * NS].rearrange("p (g b t) -> p g b t", g=8, b=SB), lhsT=Rrep[:, 0:NS],
            rhs=w4[:, :, :].unsqueeze(2).to_broadcast([4, 8, SB, 4]), start=True, stop=True))
        wbd = sb_once("wbd", [64, 8, 64], BF16)
        S.op("dve", [prw, r_cst], [r_wbd], lambda: nc.vector.tensor_tensor(
            out=wbd[0:NS, :, 0:NS], in0=psw[0:NS, 0:8 * NS].rearrange("p (g t) -> p g t", g=8),
            in1=bmask[0:NS, 0:NS].unsqueeze(1).to_broadcast([NS, 8, NS]), op=ALU.mult))
        bsp = sb_once("bsp", [128, 8, 64])
        S.op("dve", [r_rowb], [r_bsp], lambda: nc.vector.tensor_copy(
            out=bsp[:, :, 0:NS].rearrange("p g (b t) -> p g b t", b=SB),
            in_=bsrow.rearrange("p (g t) -> p g t", g=8)[:, :, 0:4].unsqueeze(2).to_broadcast([128, 8, SB, 4])))

        def vout(tt, rtt):
            S.dma("sp", d_out, [rtt], [], lambda: nc.sync.dma_start(out=sgmv_o[l].rearrange("b t f -> (b t) f"), in_=tt[0:NS, :]))
        gmlp_chunk(l, 0, NS, wv_, ydT, r_ydT, lng, lnb, bsp, wbd, vout, bs3=True)
        mT, r_mT = G[5], r_G[5]
        brs = ((G[1], r_G[1]), (G[2], r_G[2]), (G[3], r_G[3]), (G[4], r_G[4]))
        macc = (acc[0], acc[1])
        r_macc = (r_acc[0], r_acc[1])
        for bi in range(4):
            for i in range(4):
                wg = win(l, GT0 + i * 1024 + bi * 256)
                wb = wload(w_br_d[l, i, :, bi * 256:bi * 256 + 256].rearrange("(k p) c -> p k c", p=128), 8, 256, key=("br", l, i, bi))
                for cc in range(2):
                    c = bi * 2 + cc
                    psg, prg = proj_ws(wg[0], wg[1], cc * 128, 128, N)
                    psp, prp = proj_ws(wb[0], wb[1], cc * 128, 128, N, rhs=brs[i][0], rres=brs[i][1])
                    S.op("act", [prg], [r_stg[2]], lambda psg=psg: nc.scalar.activation(out=stg[2][:, 0:N], in_=psg[:, 0:N], func=AF.Sigmoid))
                    if i == 0:
                        S.op("dve", [prp, r_stg[2]], [r_macc[cc]], lambda psp=psp, cc=cc: nc.vector.tensor_tensor(
                            out=macc[cc][:, 0:N], in0=psp[:, 0:N], in1=stg[2][:, 0:N], op=ALU.mult))
                    else:
                        S.op("dve", [prp, r_stg[2]], [r_acc[2]], lambda psp=psp: nc.vector.tensor_tensor(
                            out=acc[2][:, 0:N], in0=psp[:, 0:N], in1=stg[2][:, 0:N], op=ALU.mult))
                        if i < 3:
                            S.op("dve", [r_macc[cc], r_acc[2]], [r_macc[cc]], lambda cc=cc: nc.vector.tensor_tensor(
                                out=macc[cc][:, 0:N], in0=macc[cc][:, 0:N], in1=acc[2][:, 0:N], op=ALU.add))
                        else:
                            S.op("dve", [r_macc[cc], r_acc[2]], [r_mT], lambda c=c, cc=cc: nc.vector.tensor_tensor(
                                out=mT[:, c, 0:N], in0=macc[cc][:, 0:N], in1=acc[2][:, 0:N], op=ALU.add))
        for bi in range(4):
            wv, wr = wload(wcols(w_o_d, l, bi * 256, 256), 8, 256, key=("o", l, bi))
            for cc in range(2):
                c = bi * 2 + cc
                ps, pr = proj_ws(wv, wr, cc * 128, 128, N, rhs=mT, rres=r_mT)
                resid_s(ps, pr, c, 2)
        mod_norm_s(1)
        gat = (G[1], G[2], G[3])
        r_gat = (r_G[1], r_G[2], r_G[3])
        for blk in range(11):
            wa = wload(wcols(w_up_d, l, blk * 256, 256), 8, 256, key=("ua", l, blk))
            wg_ = wload(wcols(w_up_d, l, DFF + blk * 256, 256), 8, 256, key=("ug", l, blk))
            for cc in range(2):
                i = blk * 2 + cc
                outs = []
                for which, (wv, wr) in enumerate((wa, wg_)):
                    ci = which * 22 + i
                    ps, pr = proj_ws(wv, wr, cc * 128, 128, N)
                    taps = [pp[:, l, PP_FW + j * 44 + ci:PP_FW + j * 44 + ci + 1] for j in range(3)]
                    ac, rac = conv_s(128, ps, pr, taps, pp[:, l, PP_FB + ci:PP_FB + ci + 1], [r_pp], sffn[:, ci, :, :], r_sffn, which)
                    outs.append((ac, rac))
                S.op("act", [outs[0][1]], [r_acc[2]], lambda a=outs[0][0]: nc.scalar.activation(out=acc[2][:, 0:N], in_=a[:, 0:N], func=AF.Silu))
                S.op("dve", [r_acc[2], outs[1][1]], [r_gat[i // 8]], lambda g_=outs[1][0], i=i: nc.vector.tensor_tensor(
                    out=gat[i // 8][:, i % 8, 0:N], in0=acc[2][:, 0:N], in1=g_[:, 0:N], op=ALU.mult))
        S.dma("sp", d_out, [r_sffn], [], lambda: nc.sync.dma_start(out=sffn_o[l], in_=sffn[:]))
        for c in range(8):
            wh = [wload(w_dn_d[l, hh * 1408:(hh + 1) * 1408, c * 128:(c + 1) * 128].rearrange("(k p) c -> p k c", p=128), 11, 128, key=("dn", l, c, hh)) for hh in range(2)]
            ps, pr = psum()

            def f(wh=wh, ps=ps):
                for k in range(22):
                    last = nc.tensor.matmul(ps[:, 0:N], lhsT=wh[k // 11][0][:, k % 11, :], rhs=gat[k // 8][:, k % 8, 0:N], start=(k == 0), stop=(k == 21))
                return last
            S.op("pe", [wh[0][1], wh[1][1]] + list(r_gat), [pr], f)
            resid_s(ps, pr, c, 5)

    for ti in range(NT):
        S.dma("sp", d_x, [], [r_x], lambda ti=ti: nc.sync.dma_start(
            out=xT[:], in_=xT_d[:, ti * TT:(ti + 1) * TT].rearrange("(c p) t -> p c t", p=128)))
        for l in range(L):
            prompt_tile_layer(ti, l)
        ps, pr = psum()
        sq = G[5]
        S.op("act", [r_x], [r_G[5]], lambda: nc.scalar.activation(out=sq[:, :, :], in_=xT[:, :, :], func=AF.Square))

        def ff(ps=ps):
            for k in range(8):
                last = nc.tensor.matmul(ps[:, 0:TT], lhsT=onesb[:], rhs=sq[:, k, :], start=(k == 0), stop=(k == 7))
            return last
        S.op("pe", [r_G[5], r_cst], [pr], ff)
        rstd_of(ps[:, 0:TT], 128, acc[2][:, 0:TT], [pr, r_cst], [r_acc[2]], 1.0 / D)
        for c in range(8):
            yo, ryo = acc[c % 2], r_acc[c % 2]
            S.op("dve", [r_x, r_pp, r_acc[2]], [ryo], lambda c=c, yo=yo: nc.vector.scalar_tensor_tensor(
                out=yo[:, 0:TT], in0=xT[:, c, :], scalar=pp[:, 0, PP_GFIN + c:PP_GFIN + c + 1], in1=acc[2][:, 0:TT], op0=ALU.mult, op1=ALU.mult))
            S.dma("sp", d_out, [ryo], [], lambda ti=ti, yo=yo, c=c: nc.sync.dma_start(
                out=yT_d[c * 128:(c + 1) * 128, ti * TT:(ti + 1) * TT], in_=yo[:, 0:TT]))
    if SB:
        NS = 4 * SB
        S.dma("sp", d_par, [], [r_csT], lambda: nc.sync.dma_start(out=csT[:], in_=csT_d))
        S.op("act", [r_csT], [r_csT], lambda: nc.scalar.activation(out=csb[:], in_=csT[:], func=AF.Silu))
        S.dma("sp", d_x, [], [r_x], lambda: nc.sync.dma_start(out=xT[:, :, 0:NS], in_=xsT_d.rearrange("(c p) t -> p c t", p=128)))
        for l in range(L):
            sample_tile_layer(l)
        ps, pr = psum()
        sq = G[5]
        S.op("act", [r_x], [r_G[5]], lambda: nc.scalar.activation(out=sq[:, :, 0:NS], in_=xT[:, :, 0:NS], func=AF.Square))

        def ffs(ps=ps):
            for k in range(8):
                last = nc.tensor.matmul(ps[:, 0:NS], lhsT=onesb[:], rhs=sq[:, k, 0:NS], start=(k == 0), stop=(k == 7))
            return last
        S.op("pe", [r_G[5], r_cst], [pr], ffs)
        rstd_of(ps[:, 0:NS], 128, acc[2][:, 0:NS], [pr, r_cst], [r_acc[2]], 1.0 / D)
        for c in range(8):
            yo, ryo = acc[c % 2], r_acc[c % 2]
            S.op("dve", [r_x, r_pp, r_acc[2]], [ryo], lambda c=c, yo=yo: nc.vector.scalar_tensor_tensor(
                out=yo[:, 0:NS], in0=xT[:, c, 0:NS], scalar=pp[:, 0, PP_GFIN + c:PP_GFIN + c + 1], in1=acc[2][:, 0:NS], op0=ALU.mult, op1=ALU.mult))
            S.dma("sp", d_out, [ryo], [], lambda yo=yo, c=c: nc.sync.dma_start(out=ysT_d[c * 128:(c + 1) * 128, :], in_=yo[:, 0:NS]))
    S.finish()
    return nc


def _consts():
    cst = np.zeros((128, 900), np.float32)
    tok = np.arange(64)
    cst[0:4, 772:836] = (np.arange(4)[:, None] == (tok % 4)[None, :])
    cst[0:64, 836:900] = ((tok // 4)[:, None] == (tok // 4)[None, :])
    jj = np.arange(132)[None, :]
    ds = 128 + np.arange(128)[:, None] - jj
    cst[:, 640:772] = np.where((ds >= 0) & (ds <= 128), ds, 1e6)
    cst[:, 0:128] = np.eye(128, dtype=np.float32)
    s_ = np.arange(128)[:, None]
    t_ = np.arange(128)[None, :]
    cst[:, 128:256] = (s_ <= t_).astype(np.float32)
    cst[:, 256:384] = np.where(s_ > t_, NEG, 0.0)
    kpos = np.arange(256)[None, :] - 128
    dist = s_ - kpos
    cst[:, 384:640] = np.where((dist >= 0) & (dist <= 128), dist, 1e6)
    return cst


def _pp(w, L):
    pp = np.zeros((128, L, NPP), np.float32)
    pp64 = np.zeros((64, L, 40), np.float32)
    for l in range(L):
        pp[:, l, PP_BADA:PP_BADA + 48] = w["b_ada"][l].reshape(48, 128).T
        pp[:, l, PP_GMIX:PP_GMIX + 8] = w["g_norm_mix"][l].reshape(8, 128).T
        pp[:, l, PP_GFFN:PP_GFFN + 8] = w["g_norm_ffn"][l].reshape(8, 128).T
        for j in range(4):
            pp[:, l, PP_SCW + j * 8:PP_SCW + j * 8 + 8] = w["ssd_conv_w"][l][j, :1024].reshape(8, 128).T
            pp64[:, l, j * 8:j * 8 + 8] = w["ssd_conv_w"][l][j, 1024:].reshape(8, 64).T
        pp[:, l, PP_SCB:PP_SCB + 8] = w["ssd_conv_b"][l][:1024].reshape(8, 128).T
        pp64[:, l, 32:40] = w["ssd_conv_b"][l][1024:].reshape(8, 64).T
        for j in range(3):
            pp[:, l, PP_SHW + j * 8:PP_SHW + j * 8 + 8] = w["sc_conv_w"][l][j].reshape(8, 128).T
            pp[:, l, PP_FW + j * 44:PP_FW + j * 44 + 44] = w["ffn_conv_w"][l][j].reshape(44, 128).T
        pp[:, l, PP_FB:PP_FB + 44] = w["ffn_conv_b"][l].reshape(44, 128).T
        pp[:, l, PP_GFIN:PP_GFIN + 8] = w["g_final"].reshape(8, 128).T
    return pp, pp64


def _rows(w, L):
    rows = np.zeros((128, L, 64), np.float32)
    rowb = np.zeros((L, 128, 4096), np.float32)
    for l in range(L):
        rows[:, l, 0:16] = w["ssd_dt_bias"][l][None]
        rows[:, l, 16:32] = w["ssd_a_log"][l][None]
        rows[:, l, 32:48] = w["ssd_d"][l][None]
        rows[:, l, 48:64] = w["attn_sinks"][l][None]
        rowb[l, :, 0:1024] = w["ssd_norm_g"][l][None]
        rowb[l, :, 1024:2048] = w["gm_ln_g"][l][None]
        rowb[l, :, 2048:3072] = w["gm_ln_b"][l][None]
        rowb[l, :, 3072:4096] = w["gm_b_s"][l].reshape(1024)[None]
    return rows, rowb


def shared_maps(w, L):
    pp, pp64 = _pp(w, L)
    rows, rowb = _rows(w, L)
    m = {"pp": pp, "pp64": pp64, "rows": rows, "rowb": rowb, "cst": _consts()}
    for k in ("w_ada", "w_in", "w_branch", "w_o", "ffn_w_up", "ffn_w_down", "gm_w_s"):
        m[k] = np.ascontiguousarray(w[k][:L], dtype=np.float32)
    return m


def core_map(shared, xp_b, cp_b):
    m = dict(shared)
    m["xT"] = np.ascontiguousarray(xp_b.T)
    m["cT"] = np.ascontiguousarray(cp_b.reshape(8, 128).T)[:, :, None].copy()
    return m


def unpack_prompt(r, L):
    o = {}
    o["y"] = np.ascontiguousarray(r["yT"].T)
    o["ssm"] = np.ascontiguousarray(r["p_ssm"].reshape(L, 64, 4, 4, 64).transpose(0, 2, 3, 4, 1)).reshape(L, 16, 64, 64)
    xs = r["p_xbc"][..., 0:3].transpose(0, 3, 2, 1).reshape(L, 3, 1024)
    bc = r["p_bc"][..., 0:3].transpose(0, 3, 2, 1).reshape(L, 3, 512)
    o["ssd_conv"] = np.concatenate([xs, bc], axis=2)
    o["sc_conv"] = r["p_sc"].transpose(0, 3, 2, 1).reshape(L, 2, 1024)
    o["k"] = r["p_k"].reshape(L, 128, 4, 64)
    o["v"] = r["p_v"].reshape(L, 128, 4, 64)
    o["ffn_conv"] = r["p_ffn"][:, :, 0:44, :].transpose(0, 3, 2, 1).reshape(L, 2, 5632)
    return o


def sample_map(m, inp, bs, L):
    SBn = bs.stop - bs.start
    m["xsT"] = np.ascontiguousarray(inp["x_sample"][bs].reshape(4 * SBn, D).T)
    m["csT"] = np.ascontiguousarray(inp["c_sample"][bs].reshape(SBn, 8, 128).transpose(2, 1, 0))
    st = inp["state_ssm"][:L, bs]
    m["s_ssm_in"] = np.ascontiguousarray(st.reshape(L, SBn, 4, 4, 64, 64).transpose(0, 1, 5, 2, 3, 4)).reshape(L, SBn, 64, 4, 256)
    sc = inp["state_ssd_conv"][:L, bs]
    m["s_xbc_in"] = np.ascontiguousarray(sc[..., :1024].reshape(L, SBn, 3, 8, 128).transpose(0, 4, 3, 1, 2))
    m["s_bc_in"] = np.ascontiguousarray(sc[..., 1024:].reshape(L, SBn, 3, 8, 64).transpose(0, 4, 3, 1, 2))
    m["s_sc_in"] = np.ascontiguousarray(inp["state_sc_conv"][:L, bs].reshape(L, SBn, 2, 8, 128).transpose(0, 4, 3, 1, 2))
    m["s_ffn_in"] = np.ascontiguousarray(inp["state_ffn_conv"][:L, bs].reshape(L, SBn, 2, 44, 128).transpose(0, 4, 3, 1, 2))
    ck = inp["cache_k"][:L, bs]
    kt = ck.transpose(0, 1, 4, 3, 2)
    m["ckT"] = np.ascontiguousarray(np.concatenate([kt, kt], axis=2))
    m["ck"] = np.ascontiguousarray(ck.reshape(L, SBn, 128, 256))
    m["cv"] = np.ascontiguousarray(inp["cache_v"][:L, bs].reshape(L, SBn, 128, 256))
    return m


def unpack_sample(r, L, SBn):
    o = {}
    o["y"] = np.ascontiguousarray(r["ysT"].T).reshape(SBn, 4, D)
    o["ssm"] = np.ascontiguousarray(r["s_ssm_o"].reshape(L, SBn, 64, 4, 4, 64).transpose(0, 1, 3, 4, 5, 2)).reshape(L, SBn, 16, 64, 64)
    xs = r["s_xbc_o"].transpose(0, 3, 4, 2, 1).reshape(L, SBn, 3, 1024)
    bc = r["s_bc_o"].transpose(0, 3, 4, 2, 1).reshape(L, SBn, 3, 512)
    o["ssd_conv"] = np.concatenate([xs, bc], axis=3)
    o["sc_conv"] = r["s_sc_o"].transpose(0, 3, 4, 2, 1).reshape(L, SBn, 2, 1024)
    o["k"] = r["s_k_o"].reshape(L, SBn, 128, 4, 64)
    o["v"] = r["s_v_o"].reshape(L, SBn, 128, 4, 64)
    o["ffn_conv"] = r["s_ffn_o"].transpose(0, 3, 4, 2, 1).reshape(L, SBn, 2, 5632)
    o["gm_v"] = r["s_gmv_o"].reshape(L, SBn, 4, 1024)
    return o


def kernel(**inputs):
    inp = {k: np.asarray(v) for k, v in inputs.items()}
    L = 4
    B = inp["x_prompt"].shape[0]
    SEQ = inp["x_prompt"].shape[1]
    NT = SEQ // TT
    DB = inp["x_sample"].shape[0]
    SBn = DB // 8
    shared = shared_maps(inp, L)
    nc = build(NT, L, SBn)
    in_maps = []
    for core in range(8):
        b = core % B
        m = core_map(shared, inp["x_prompt"][b], inp["c_prompt"][b])
        sample_map(m, inp, slice(core * SBn, (core + 1) * SBn), L)
        in_maps.append(m)
    res = run_bass_kernel_spmd(nc, in_maps, core_ids=list(range(8)))
    rs = [{k: np.asarray(v) for k, v in r.items()} for r in res.results]
    po = [unpack_prompt(rs[b], L) for b in range(B)]
    so = [unpack_sample(rs[c], L, SBn) for c in range(8)]
    f32 = np.float32
    y_prompt = np.stack([p["y"] for p in po], 0).astype(f32)
    y_sample = np.concatenate([s_["y"] for s_ in so], 0).astype(f32)

    def pst(k):
        return np.ascontiguousarray(np.stack([p[k] for p in po], 1)).astype(f32)

    def sst(k):
        return np.ascontiguousarray(np.concatenate([s_[k] for s_ in so], 1)).astype(f32)
    return (y_prompt, y_sample, pst("ssm"), pst("ssd_conv"), pst("sc_conv"), pst("k"), pst("v"), pst("ffn_conv"),
            sst("ssm"), sst("ssd_conv"), sst("sc_conv"), sst("k"), sst("v"), sst("ffn_conv"), sst("gm_v"))
```
